# Optimizing a Trainium2 kernel written in Bass

```python
import math
import jax, jax.numpy as jnp
from jax import lax
import numpy as np

D_MODEL = 1024
BATCH = 4
SEQ = 8192
DEPTH = 1
DEC_BATCH = 32
DEC_SEQ = 4
PAST_LEN = 16384
PAGE_SIZE = 128

HEAD_DIM = 64
HEADS_PER_GROUP = 4
DIL_PATTERNS = ((128, 1), (512, 4), (2048, 16))
N_DIL_GROUPS = 3
N_ATTN_HEADS = HEADS_PER_GROUP * N_DIL_GROUPS
QK_WIDTH = N_ATTN_HEADS * HEAD_DIM
ATTN_OUT_WIDTH = HEADS_PER_GROUP * HEAD_DIM
QBLOCK = 128
SSM_GROUP = 16
SSM_STATE = 64
SSM_WIDTH = D_MODEL // 2
N_SSM_GROUPS = SSM_WIDTH // SSM_GROUP
DT_MIN = 0.001
DT_MAX = 0.1
D_FF = 2816
CONV_W = 3
N_BUCKETS = 32
MAX_DISTANCE = 2048
NORM_EPS = 1e-6
NEG_INF = -1e30
U_START = 3 * QK_WIDTH
GA_START = U_START + SSM_WIDTH
GS_START = GA_START + D_MODEL
IN_WIDTH = GS_START + D_MODEL

kernel_name = 'dilated_attn_s5_convffn_hybrid_step'


def rmsnorm(x, g):
    xf = x.astype(jnp.float32)
    y = xf * lax.rsqrt(jnp.mean(xf * xf, axis=-1, keepdims=True) + NORM_EPS)
    return (y * g.astype(jnp.float32)).astype(x.dtype)


def rel_bucket(dist):
    max_exact = N_BUCKETS // 2
    n = jnp.maximum(dist, 0)
    nf = jnp.maximum(n, 1).astype(jnp.float32)
    large = max_exact + (jnp.log(nf / max_exact) / math.log(MAX_DISTANCE / max_exact)
                         * (N_BUCKETS - max_exact)).astype(jnp.int32)
    large = jnp.minimum(large, N_BUCKETS - 1)
    return jnp.where(n < max_exact, n, large)


def dilated_attn_prompt(q, k, v, bias_tab, window, dil):
    b, s, h, e = q.shape
    n_dist = window // dil
    L = s // dil
    nb = -(-L // QBLOCK)
    lp = nb * QBLOCK

    def to_residue(t):
        t = t.astype(jnp.float32).reshape(b, L, dil, h, e).transpose(0, 2, 1, 3, 4)
        return jnp.pad(t, ((0, 0), (0, 0), (0, lp - L), (0, 0), (0, 0)))

    def band(t):
        t = jnp.pad(t, ((0, 0), (0, 0), (QBLOCK, 0), (0, 0), (0, 0)))
        prev = t[:, :, :lp].reshape(b, dil, nb, QBLOCK, h, e)
        cur = t[:, :, QBLOCK:].reshape(b, dil, nb, QBLOCK, h, e)
        return jnp.concatenate([prev, cur], axis=3)

    qb = to_residue(q).reshape(b, dil, nb, QBLOCK, h, e)
    kb = band(to_residue(k))
    vb = band(to_residue(v))
    qi = jnp.arange(QBLOCK)[:, None]
    ki = jnp.arange(2 * QBLOCK)[None, :]
    j = qi + QBLOCK - ki
    kpos = jnp.arange(nb)[:, None, None] * QBLOCK + ki[None] - QBLOCK
    valid = ((j >= 0) & (j <= n_dist))[None] & (kpos >= 0)
    bias = bias_tab.astype(jnp.float32)[rel_bucket(jnp.clip(j, 0, n_dist) * dil)].transpose(2, 0, 1)
    logits = jnp.einsum('bdnqhe,bdnkhe->bdnhqk', qb, kb) * (HEAD_DIM ** -0.5) + bias
    logits = jnp.where(valid[:, None], logits, NEG_INF)
    m = jnp.max(logits, axis=-1, keepdims=True)
    p = jnp.exp(logits - m)
    den = jnp.sum(p, axis=-1)
    o = jnp.einsum('bdnhqk,bdnkhe->bdnqhe', p, vb) / den.transpose(0, 1, 2, 4, 3)[..., None]
    lse = (m[..., 0] + jnp.log(den)).transpose(0, 1, 2, 4, 3)
    o = o.reshape(b, dil, lp, h, e)[:, :, :L].transpose(0, 2, 1, 3, 4).reshape(b, s, h, e)
    lse = lse.reshape(b, dil, lp, h)[:, :, :L].transpose(0, 2, 1, 3).reshape(b, s, h)
    return o, lse


def dilated_attn_sample(q, k, v, kv_buf, bias_tab, window, dil):
    t = q.shape[1]
    lb = kv_buf.shape[1]
    n_dist = window // dil
    f32 = jnp.float32
    k_all = jnp.concatenate([kv_buf[:, :, 0].astype(f32), k.astype(f32)], axis=1)
    v_all = jnp.concatenate([kv_buf[:, :, 1].astype(f32), v.astype(f32)], axis=1)
    dist = jnp.arange(n_dist + 1)
    idx = lb + jnp.arange(t)[:, None] - dist[None, :] * dil
    valid = idx >= 0
    idx = jnp.maximum(idx, 0)
    kg = k_all[:, idx]
    vg = v_all[:, idx]
    bias = bias_tab.astype(f32)[rel_bucket(dist * dil)].T
    logits = jnp.einsum('bthe,btjhe->bthj', q.astype(f32), kg) * (HEAD_DIM ** -0.5) + bias
    logits = jnp.where(valid[:, None, :], logits, NEG_INF)
    m = jnp.max(logits, axis=-1, keepdims=True)
    p = jnp.exp(logits - m)
    den = jnp.sum(p, axis=-1)
    o = jnp.einsum('bthj,btjhe->bthe', p, vg) / den[..., None]
    lse = m[..., 0] + jnp.log(den)
    return o, lse


def _complex_scan_combine(e1, e2):
    a1r, a1i, b1r, b1i = e1
    a2r, a2i, b2r, b2i = e2
    return (a2r * a1r - a2i * a1i,
            a2r * a1i + a2i * a1r,
            a2r * b1r - a2i * b1i + b2r,
            a2r * b1i + a2i * b1r + b2i)


def s5_branch(u, h0_re, h0_im, log_dt, lam_re, lam_im, b_re, b_im, c_re, c_im, d, w_glu, b_glu):
    b, s, _ = u.shape
    f32 = jnp.float32
    uf = u.astype(f32)
    ug = uf.reshape(b, s, N_SSM_GROUPS, SSM_GROUP)
    dt = jnp.exp(log_dt.astype(f32))[:, None]
    lr, li = lam_re.astype(f32), lam_im.astype(f32)
    mag = jnp.exp(lr * dt)
    ab_re, ab_im = mag * jnp.cos(li * dt), mag * jnp.sin(li * dt)
    den = lr * lr + li * li
    nr, ni = ab_re - 1.0, ab_im
    coef_re = (nr * lr + ni * li) / den
    coef_im = (ni * lr - nr * li) / den
    br, bi = b_re.astype(f32), b_im.astype(f32)
    bb_re = coef_re[..., None] * br - coef_im[..., None] * bi
    bb_im = coef_re[..., None] * bi + coef_im[..., None] * br
    bu_re = jnp.einsum('bsgp,gnp->bsgn', ug, bb_re)
    bu_im = jnp.einsum('bsgp,gnp->bsgn', ug, bb_im)
    a_re = jnp.broadcast_to(ab_re, bu_re.shape)
    a_im = jnp.broadcast_to(ab_im, bu_im.shape)
    acc_re, acc_im, s_re, s_im = lax.associative_scan(
        _complex_scan_combine, (a_re, a_im, bu_re, bu_im), axis=1)
    h0r = h0_re.astype(f32)[:, None]
    h0i = h0_im.astype(f32)[:, None]
    h_re = acc_re * h0r - acc_im * h0i + s_re
    h_im = acc_re * h0i + acc_im * h0r + s_im
    y = (jnp.einsum('bsgn,gpn->bsgp', h_re, c_re.astype(f32))
         - jnp.einsum('bsgn,gpn->bsgp', h_im, c_im.astype(f32)))
    y = y.reshape(b, s, SSM_WIDTH) + d.astype(f32) * uf
    y = jax.nn.gelu(y)
    y = y * jax.nn.sigmoid(y @ w_glu.astype(f32) + b_glu.astype(f32))
    return y.astype(u.dtype), h_re[:, -1], h_im[:, -1]


def trunk_layer(x, kv_bufs, ssm_h_re, ssm_h_im, conv_buf, rel_bias,
                norm1_g, w_in, ssm_log_dt, ssm_lambda_re, ssm_lambda_im,
                ssm_b_re, ssm_b_im, ssm_c_re, ssm_c_im, ssm_d, w_glu, b_glu,
                w_branch_attn, w_branch_ssm, w_out, norm2_g, w_up, conv_w, conv_b, w_down):
    b, s, _ = x.shape
    proj = rmsnorm(x, norm1_g) @ w_in
    q = proj[..., :QK_WIDTH].reshape(b, s, N_ATTN_HEADS, HEAD_DIM)
    k = proj[..., QK_WIDTH:2 * QK_WIDTH].reshape(b, s, N_ATTN_HEADS, HEAD_DIM)
    v = proj[..., 2 * QK_WIDTH:U_START].reshape(b, s, N_ATTN_HEADS, HEAD_DIM)
    u = proj[..., U_START:GA_START]
    gate_a = proj[..., GA_START:GS_START]
    gate_s = proj[..., GS_START:]

    outs, lses, kv_new = [], [], []
    for g, (window, dil) in enumerate(DIL_PATTERNS):
        hs = slice(g * HEADS_PER_GROUP, (g + 1) * HEADS_PER_GROUP)
        qg, kg, vg = q[:, :, hs], k[:, :, hs], v[:, :, hs]
        tab = rel_bias[:, hs]
        rows = jnp.stack([kg, vg], axis=2)
        if kv_bufs is None:
            o, lse = dilated_attn_prompt(qg, kg, vg, tab, window, dil)
            kv_new.append(rows[:, -min(window, s):])
        else:
            o, lse = dilated_attn_sample(qg, kg, vg, kv_bufs[g], tab, window, dil)
            kv_new.append(rows)
        outs.append(o)
        lses.append(lse)
    wts = jax.nn.softmax(jnp.stack(lses, axis=0), axis=0)
    merged = jnp.sum(wts[..., None] * jnp.stack(outs, axis=0), axis=0)
    attn_out = merged.reshape(b, s, ATTN_OUT_WIDTH).astype(x.dtype)

    ssm_out, h_re, h_im = s5_branch(u, ssm_h_re, ssm_h_im, ssm_log_dt, ssm_lambda_re, ssm_lambda_im,
                                    ssm_b_re, ssm_b_im, ssm_c_re, ssm_c_im, ssm_d, w_glu, b_glu)
    mix = (jax.nn.sigmoid(gate_a) * (attn_out @ w_branch_attn)
           + jax.nn.sigmoid(gate_s) * (ssm_out @ w_branch_ssm))
    x = x + mix @ w_out

    up = rmsnorm(x, norm2_g) @ w_up
    a, val = up[..., :D_FF], up[..., D_FF:]
    a_ext = jnp.concatenate([conv_buf.astype(a.dtype), a], axis=1)
    a_conv = conv_b
    for i in range(CONV_W):
        a_conv = a_conv + conv_w[i] * a_ext[:, i:i + s]
    conv_state = a_ext[:, s:]
    x = x + (jax.nn.silu(a_conv) * val) @ w_down
    return x, (kv_new[0], kv_new[1], kv_new[2], h_re, h_im, conv_state)


def _stack_layers(states, i):
    return jnp.stack([st[i] for st in states], axis=0)


def setup_inputs(seed: int = 0) -> dict:
    key = jax.random.key(seed)
    ks = iter(jax.random.split(key, 32))
    f32 = jnp.float32

    def nrm(shape, scale=1.0):
        return jax.random.normal(next(ks), shape, f32) * scale

    def kv_shape(window):
        return (DEPTH, DEC_BATCH, min(window, PAST_LEN), 2, HEADS_PER_GROUP, HEAD_DIM)

    G, N, P = N_SSM_GROUPS, SSM_STATE, SSM_GROUP
    return {
        'x_prompt': nrm((BATCH, SEQ, D_MODEL)),
        'x_sample': nrm((DEC_BATCH, DEC_SEQ, D_MODEL)),
        'cache_kv_w128': nrm(kv_shape(DIL_PATTERNS[0][0])),
        'cache_kv_w512': nrm(kv_shape(DIL_PATTERNS[1][0])),
        'cache_kv_w2048': nrm(kv_shape(DIL_PATTERNS[2][0])),
        'state_ssm_re': nrm((DEPTH, DEC_BATCH, G, N), 0.3),
        'state_ssm_im': nrm((DEPTH, DEC_BATCH, G, N), 0.3),
        'state_ffn_conv': nrm((DEPTH, DEC_BATCH, CONV_W - 1, D_FF)),
        'rel_bias': nrm((N_BUCKETS, N_ATTN_HEADS), 0.5),
        'norm1_g': 1.0 + nrm((DEPTH, D_MODEL), 0.01),
        'w_in': nrm((DEPTH, D_MODEL, IN_WIDTH), D_MODEL ** -0.5),
        'ssm_log_dt': jax.random.uniform(next(ks), (DEPTH, G), f32, math.log(DT_MIN), math.log(DT_MAX)),
        'ssm_lambda_re': -0.5 + nrm((DEPTH, G, N), 0.01),
        'ssm_lambda_im': math.pi * jnp.arange(N, dtype=f32) + nrm((DEPTH, G, N), 0.01),
        'ssm_b_re': nrm((DEPTH, G, N, P), (2 * P) ** -0.5),
        'ssm_b_im': nrm((DEPTH, G, N, P), (2 * P) ** -0.5),
        'ssm_c_re': nrm((DEPTH, G, P, N), (2 * N) ** -0.5),
        'ssm_c_im': nrm((DEPTH, G, P, N), (2 * N) ** -0.5),
        'ssm_d': nrm((DEPTH, SSM_WIDTH)),
        'w_glu': nrm((DEPTH, SSM_WIDTH, SSM_WIDTH), SSM_WIDTH ** -0.5),
        'b_glu': nrm((DEPTH, SSM_WIDTH), 0.01),
        'w_branch_attn': nrm((DEPTH, ATTN_OUT_WIDTH, D_MODEL), ATTN_OUT_WIDTH ** -0.5),
        'w_branch_ssm': nrm((DEPTH, SSM_WIDTH, D_MODEL), SSM_WIDTH ** -0.5),
        'w_out': nrm((DEPTH, D_MODEL, D_MODEL), D_MODEL ** -0.5),
        'norm2_g': 1.0 + nrm((DEPTH, D_MODEL), 0.01),
        'w_up': nrm((DEPTH, D_MODEL, 2 * D_FF), D_MODEL ** -0.5),
        'conv_w': nrm((DEPTH, CONV_W, D_FF), CONV_W ** -0.5),
        'conv_b': nrm((DEPTH, D_FF), 0.01),
        'w_down': nrm((DEPTH, D_FF, D_MODEL), D_FF ** -0.5),
        'norm_f_g': 1.0 + nrm((D_MODEL,), 0.01),
    }


def reference(x_prompt, x_sample, cache_kv_w128, cache_kv_w512, cache_kv_w2048,
              state_ssm_re, state_ssm_im, state_ffn_conv, rel_bias, norm1_g, w_in,
              ssm_log_dt, ssm_lambda_re, ssm_lambda_im, ssm_b_re, ssm_b_im, ssm_c_re, ssm_c_im,
              ssm_d, w_glu, b_glu, w_branch_attn, w_branch_ssm, w_out, norm2_g, w_up,
              conv_w, conv_b, w_down, norm_f_g):
    hp, hs = x_prompt, x_sample
    bp = x_prompt.shape[0]
    st_p, st_s = [], []
    for l in range(DEPTH):
        lw = (norm1_g[l], w_in[l], ssm_log_dt[l], ssm_lambda_re[l], ssm_lambda_im[l],
              ssm_b_re[l], ssm_b_im[l], ssm_c_re[l], ssm_c_im[l], ssm_d[l], w_glu[l], b_glu[l],
              w_branch_attn[l], w_branch_ssm[l], w_out[l], norm2_g[l], w_up[l],
              conv_w[l], conv_b[l], w_down[l])
        zero_h = jnp.zeros((bp, N_SSM_GROUPS, SSM_STATE), jnp.float32)
        zero_conv = jnp.zeros((bp, CONV_W - 1, D_FF), x_prompt.dtype)
        hp, sp = trunk_layer(hp, None, zero_h, zero_h, zero_conv, rel_bias, *lw)
        hs, ss = trunk_layer(hs, (cache_kv_w128[l], cache_kv_w512[l], cache_kv_w2048[l]),
                             state_ssm_re[l], state_ssm_im[l], state_ffn_conv[l], rel_bias, *lw)
        st_p.append(sp)
        st_s.append(ss)
    y_prompt = rmsnorm(hp, norm_f_g)
    y_sample = rmsnorm(hs, norm_f_g)
    kv128_p = _stack_layers(st_p, 0)
    kv512_p = _stack_layers(st_p, 1)
    kv2048_p = _stack_layers(st_p, 2)
    ssm_re_p = _stack_layers(st_p, 3)
    ssm_im_p = _stack_layers(st_p, 4)
    conv_p = _stack_layers(st_p, 5)
    kv128_s = _stack_layers(st_s, 0)
    kv512_s = _stack_layers(st_s, 1)
    kv2048_s = _stack_layers(st_s, 2)
    ssm_re_s = _stack_layers(st_s, 3)
    ssm_im_s = _stack_layers(st_s, 4)
    conv_s = _stack_layers(st_s, 5)
    return (y_prompt, y_sample, kv128_p, kv512_p, kv2048_p, ssm_re_p, ssm_im_p, conv_p,
            kv128_s, kv512_s, kv2048_s, ssm_re_s, ssm_im_s, conv_s)
```

```python
import math
import os
from contextlib import ExitStack

import numpy as np
import concourse.bass as bass
import concourse.mybir as mybir
from concourse.bass_utils import run_bass_kernel_spmd

F32 = mybir.dt.float32
BF16 = mybir.dt.bfloat16
AF = mybir.ActivationFunctionType
ALU = mybir.AluOpType
AX = mybir.AxisListType

NCORES = 8
D = 1024
INW = 4864
DFF = 2816
TN = 384
NT_ALL = 22
T_ALL = NT_ALL * TN
T_MAIN0 = 11 * TN
T_MAIN = 11 * TN
KV_T0 = 5 * TN
T_KV = T_ALL - KV_T0
EPS = 1e-6
WINS = (128, 512, 2048)
DILS = (1, 4, 16)
LCH = 16
_DBG = int(os.environ.get("K_DBG", "9"))
_DBG2 = int(os.environ.get("K_DBG2", "0"))
_PHASES = set(int(v) for v in os.environ.get("K_PH", "1,2,3,4,5").split(","))
_P3B = int(os.environ.get("K_P3B", "9"))
_ADEPTH = int(os.environ.get("K_ADEPTH", "2"))
_TR = tuple(int(v) for v in os.environ.get("K_TILES", "0,22").split(","))


class Sem:
    _n = 0

    def __init__(self, h):
        self.h = h
        Sem._n += 1
        self.id = Sem._n


class Buf:
    __slots__ = ("name", "w", "r", "sem", "semcnt", "excl")

    def __init__(self, name, excl=False):
        self.name = name
        self.excl = excl
        self.w = None
        self.r = []
        self.sem = None
        self.semcnt = 0


class Sched:
    def __init__(self, nc, es):
        self.nc = nc
        self.es = es
        self.engs = {"pe": nc.tensor, "act": nc.scalar, "dve": nc.vector,
                     "pool": nc.gpsimd, "sp": nc.sync}
        self.ops = {k: [] for k in self.engs}
        self.sem = {k: Sem(es.enter_context(nc.semaphore("sem_" + k)))
                    for k in ("pe", "act", "dve", "pool")}
        self.cnt = {k: 0 for k in self.sem}
        self.seen = {k: {} for k in self.engs}
        self.nsem = 4
        self.pending_noinc = {k: False for k in self.sem}
        self.free_sems = []
        self.dma_bufs = []

    def _need(self, eng, r, w):
        deps = []
        for b in r:
            if b.w is not None:
                deps.append(b.w)
        for b in w:
            if b.w is not None:
                deps.append(b.w)
            deps.extend(b.r)
        waits = {}
        for (s, v) in deps:
            if eng == "pe" and s is self.sem["pe"]:
                continue
            if waits.get(s, (None, 0))[1] < v:
                waits[s] = (s, v)
        need = []
        seen = self.seen[eng]
        for s, v in waits.values():
            if seen.get(s.id, 0) < v:
                seen[s.id] = v
                need.append((s.h, v))
        return need

    def _record(self, tk, r, w):
        for b in r:
            b.r.append(tk)
        for b in w:
            b.w = tk
            b.r = []

    def op(self, eng, fn, r=(), w=(), inc=True):
        if eng != "pe":
            ex = [b for b in r if b.excl]
            if ex:
                r = [b for b in r if not b.excl]
                w = list(w) + ex
        need = self._need(eng, r, w)
        s = self.sem[eng]
        if inc:
            self.cnt[eng] += 1
            tk = (s, self.cnt[eng])
            self.ops[eng].append((need, fn, s.h, 1))
            self.pending_noinc[eng] = False
        else:
            tk = (s, self.cnt[eng] + 1)
            self.ops[eng].append((need, fn, None, 0))
            self.pending_noinc[eng] = True
        self._record(tk, r, w)
        return tk

    def dma(self, q, out, in_, key, r=(), w=(), **kw):
        need = self._need(q, r, w)
        if key.sem is None:
            if self.free_sems:
                key.sem, key.semcnt = self.free_sems.pop()
            else:
                key.sem = Sem(self.es.enter_context(self.nc.semaphore("dsem_%d" % self.nsem)))
                self.nsem += 1
            self.dma_bufs.append(key)
        key.semcnt += 16
        tk = (key.sem, key.semcnt)
        self.ops[q].append((need, lambda e: e.dma_start(out=out, in_=in_, **kw), key.sem.h, 16))
        self._record(tk, r, w)
        return tk

    def drain_dmas(self, eng="sp"):
        need = []
        for b in self.dma_bufs:
            if self.seen[eng].get(b.sem.id, 0) < b.semcnt:
                self.seen[eng][b.sem.id] = b.semcnt
                need.append((b.sem.h, b.semcnt))
            self.free_sems.append((b.sem, b.semcnt))
            b.sem = None
        self.dma_bufs = []
        self.ops[eng].append((need, None, None, 0))

    def wait_all(self, eng, bufs):
        need = self._need(eng, (), bufs)
        self.ops[eng].append((need, None, None, 0))

    def emit(self, block):
        for k in self.pending_noinc:
            assert not self.pending_noinc[k], k

        def run(name):
            lst = self.ops[name]

            def f(e):
                for need, fn, sh, inc in lst:
                    for (h, v) in need:
                        e.wait_ge(h, v)
                    if fn is None:
                        continue
                    ins = fn(e)
                    if sh is not None:
                        ins.then_inc(sh, inc)
            return f
        block.sync(run("sp"))
        block.tensor(run("pe"))
        block.scalar(run("act"))
        block.vector(run("dve"))
        block.gpsimd(run("pool"))
        self.ops = {k: [] for k in self.engs}


def dap(t, off, dims):
    return bass.AP(t, off, [list(d) for d in dims])


class Prog:
    def __init__(self, phases):
        self.phases = phases
        self.nc = bass.Bass("TRN2", target_bir_lowering=False)
        self.es = ExitStack()
        self.S = None
        self.din = {}
        self.dout = {}
        self.outbufs = []
        self.phn = 0

    def inp(self, name, shape, dt=F32):
        t = self.nc.dram_tensor(name, list(shape), dt, kind="ExternalInput")
        self.din[name] = t
        return t

    def outp(self, name, shape, dt=F32):
        t = self.nc.dram_tensor(name, list(shape), dt, kind="ExternalOutput")
        self.dout[name] = t
        return t

    def scr(self, name, shape, dt):
        return self.nc.dram_tensor("d_" + name, list(shape), dt)

    def sb(self, name, shape, dt, glob=False):
        es = self.es if glob else self.pes
        return es.enter_context(self.nc.sbuf_tensor("s%d_%s" % (self.phn, name), list(shape), dt))

    def run_phase(self, fn):
        self.phn += 1
        with ExitStack() as pes:
            self.pes = pes
            fn()
            self.S.drain_dmas("sp")
            with self.nc.Block() as block:
                self.S.emit(block)
        self.pes = self.es

    def ps(self, name, shape, dt):
        return self.es.enter_context(self.nc.psum_tensor("p_" + name, list(shape), dt))

    def build(self):
        nc, es = self.nc, self.es
        with es:
            self._declare_io()
            self.S = Sched(nc, es)
            self.pes = es
            self._consts()
            for ph, fn in ((1, self.phase1), (2, self.phase2), (31, self.phase3a), (32, self.phase3b), (33, self.phase3c), (4, self.phase4), (5, self.phase5)):
                if ph in self.phases:
                    self.run_phase(fn)

            self.run_phase(lambda: None)
        return nc

    def _declare_io(self):
        self.xall = self.inp("xall", [T_ALL, D])
        self.w_in = self.inp("w_in", [D, INW])
        self.norm1_g = self.inp("norm1_g", [D])
        self.ident_in = self.inp("ident", [128, 128])
        self.okv = [self.outp("okv%d" % g, [WINS[g], 2, 256]) for g in range(3)]
        self.rel_bias = self.inp("rel_bias", [32, 12])
        self.bucket_oh = self.inp("bucket_oh", [32, 3 * 129])
        self.hv_in = self.inp("hv", [128, 1])
        self.antiident = self.inp("antiident", [128, 128])
        self.EXT = self.scr("EXT", [12, 385], F32)
        self.log_dt = self.inp("ssm_log_dt", [32])
        self.lam_re = self.inp("ssm_lambda_re", [32, 64])
        self.lam_im = self.inp("ssm_lambda_im", [32, 64])
        self.b_re = self.inp("ssm_b_re", [32, 64, 16])
        self.b_im = self.inp("ssm_b_im", [32, 64, 16])
        self.c_re = self.inp("ssm_c_re", [32, 16, 64])
        self.c_im = self.inp("ssm_c_im", [32, 16, 64])
        self.ssm_d = self.inp("ssm_d", [512])
        self.w_glu = self.inp("w_glu", [512, 512])
        self.b_glu = self.inp("b_glu", [512])
        self.w_ba = self.inp("w_branch_attn", [256, D])
        self.w_bs = self.inp("w_branch_ssm", [512, D])
        self.w_out = self.inp("w_out", [D, D])
        self.norm2_g = self.inp("norm2_g", [D])
        self.w_up = self.inp("w_up", [D, 2 * DFF])
        self.conv_w = self.inp("conv_w", [3, DFF])
        self.conv_b = self.inp("conv_b", [DFF])
        self.w_down = self.inp("w_down", [DFF, D])
        self.norm_f_g = self.inp("norm_f_g", [D])
        self.oy = self.outp("oy", [4096, D])
        self.ocv = self.outp("ocv", [2, DFF])
        if _DBG2:
            self.X1s = self.outp("X1s", [T_MAIN, D], F32)
        else:
            self.X1s = self.scr("X1s", [T_MAIN, D], F32)
        self.xsamp = self.inp("xsamp", [16, D])
        self.XN2s = self.scr("XN2s", [8, 128, T_MAIN], BF16)
        self.XN2ss = self.scr("XN2ss", [8, 128, 16], BF16)
        self.HBs = self.scr("HBs", [2, 128, 16, T_MAIN // LCH], BF16)
        self.st_conv = self.inp("st_conv", [4, 2, DFF])
        self.ocvs = self.outp("ocvs", [4, 2, DFF])
        self.oys = self.outp("oys", [16, D])
        self.X1ss = self.scr("X1ss", [16, D], F32)
        self.st_re = self.inp("st_re", [4, 32, 64])
        self.st_im = self.inp("st_im", [4, 32, 64])
        self.ossm_s_re = self.outp("ossm_s_re", [4, 32, 64])
        self.ossm_s_im = self.outp("ossm_s_im", [4, 32, 64])
        self.YSs = self.scr("YSs", [4, 128, 16], BF16)
        self.caches = [self.inp("cache%d" % g, [4, WINS[g], 512]) for g in range(3)]
        self.ATTs = self.scr("ATTs", [4, 64, 16], BF16)
        self.QTs = self.scr("QTs", [6, 128, 16], BF16)
        self.KTs = self.scr("KTs", [6, 128, 16], BF16)
        self.USs = self.scr("USs", [4, 128, 16], BF16)
        self.GSs = self.scr("GSs", [16, 128, 16], BF16)
        self.VSs = self.scr("VSs", [16, 768], BF16)
        self.okvs = [self.outp("okvs%d" % g, [16, 2, 256]) for g in range(3)]
        self.ossm_re = self.outp("ossm_re", [32, 64])
        self.ossm_im = self.outp("ossm_im", [32, 64])
        if _DBG2:
            self.YS = self.outp("YS", [4, 128, T_MAIN], BF16)
        else:
            self.YS = self.scr("YS", [4, 128, T_MAIN], BF16)
        self.VBs = self.scr("VBs", [2, 128, (LCH + 1) * 16 * 32], BF16)
        self.KBs = self.scr("KBs", [128, LCH, 512], BF16)
        self.WSs = self.scr("WSs", [2, 128, LCH, 512], BF16)
        if _DBG2:
            self.TAB = self.outp("TAB", [128, 4, 16], F32)
        else:
            self.TAB = self.scr("TAB", [128, 4, 16], F32)
        if _DBG2:
            self.ATT = self.outp("ATT", [4, 64, T_MAIN], BF16)
        else:
            self.ATT = self.scr("ATT", [4, 64, T_MAIN], BF16)
        self.QT = self.scr("QT", [6, 128, T_MAIN], BF16)
        self.KT = self.scr("KT", [6, 128, T_KV], BF16)
        self.VS = self.scr("VS", [T_KV, 768], BF16)
        self.US = self.scr("US", [4, 128, T_ALL], BF16)
        self.GS = self.scr("GS", [16, 128, T_MAIN], BF16)

    def _consts(self):
        S = self.S
        self.ident_f = self.sb("ident_f", [128, 128], F32, glob=True)
        self.ident = self.sb("ident", [128, 128], BF16, glob=True)
        self.b_ident = Buf("ident")
        S.dma("sp", self.ident_f[:], self.ident_in.ap(), self.b_ident, w=[self.b_ident])
        S.op("dve", lambda e: e.tensor_copy(self.ident[:], self.ident_f[:]),
             r=[self.b_ident], w=[self.b_ident])
        self.psb = [self.ps("psb%d" % i, [128, 512], F32) for i in range(6)]
        self.b_psb = [Buf("psb%d" % i, True) for i in range(6)]
        self.pst = [self.ps("pst%d" % i, [128, 1024], BF16) for i in range(2)]
        self.b_pst = [Buf("pst%d" % i, True) for i in range(2)]
        self.psi = 0

    def next_ps(self):
        i = self.psi % 6
        self.psi += 1
        return self.psb[i], self.b_psb[i]

    def load_weight_bf16(self, dst, dst_buf, src_dram, nk, ncols, gcol, stg, stg_bufs, nsplit=1, q="sp", row0=0,
                         order=None):
        S = self.S
        cw = ncols // nsplit
        i = 0
        pieces = [(sp, kc) for sp in (order if order is not None else range(nsplit)) for kc in range(nk)] \
            if isinstance(dst_buf, list) else [(sp, kc) for kc in range(nk) for sp in range(nsplit)]
        if not hasattr(self, "_stg_prev"):
            self._stg_prev = {}
        slots = []
        for t_, tb_ in zip(stg, stg_bufs):
            width = t_.shape[-1]
            prev = self._stg_prev.get(tb_.name + str(self.phn), [tb_])
            mine = []
            for k_ in range(max(1, width // cw)):
                sbf = Buf("%s_s%d" % (tb_.name, k_))
                slots.append([t_[:, k_ * cw:(k_ + 1) * cw], sbf, list(prev)])
                mine.append(sbf)
            self._stg_prev[tb_.name + str(self.phn)] = mine
        for (sp, kc) in pieces:
            if True:
                dbuf = dst_buf[sp] if isinstance(dst_buf, list) else dst_buf
                dst_buf_ = dbuf
                slot = slots[i % len(slots)]
                sbuf_t, sb_b = slot[0], slot[1]
                extra = slot[2]
                slot[2] = []
                i += 1
                c0 = sp * cw
                src = src_dram.ap()[row0 + kc * 128:row0 + (kc + 1) * 128, c0:c0 + cw]
                S.dma(q, sbuf_t[:, 0:cw], src, sb_b, w=[sb_b] + extra)
                eng = "dve" if (i % 2 == 0) else "act"
                if gcol is not None:
                    gc, gb = gcol
                    if eng == "dve":
                        S.op("dve", lambda e, kc=kc, sbuf_t=sbuf_t, gc=gc, c0=c0: e.tensor_scalar(
                            dst[:, kc, c0:c0 + cw], sbuf_t[:, 0:cw], gc[:, kc:kc + 1], None, ALU.mult),
                            r=[sb_b, gb], w=[dst_buf_])
                    else:
                        S.op("act", lambda e, kc=kc, sbuf_t=sbuf_t, gc=gc, c0=c0: e.activation(
                            dst[:, kc, c0:c0 + cw], sbuf_t[:, 0:cw], AF.Copy, scale=gc[:, kc:kc + 1]),
                            r=[sb_b, gb], w=[dst_buf_])
                else:
                    if eng == "dve":
                        S.op("dve", lambda e, kc=kc, sbuf_t=sbuf_t, c0=c0: e.tensor_copy(
                            dst[:, kc, c0:c0 + cw], sbuf_t[:, 0:cw]), r=[sb_b], w=[dst_buf_])
                    else:
                        S.op("act", lambda e, kc=kc, sbuf_t=sbuf_t, c0=c0: e.activation(
                            dst[:, kc, c0:c0 + cw], sbuf_t[:, 0:cw], AF.Copy), r=[sb_b], w=[dst_buf_])

    def norm_transpose(self, xt, b_xt, nsub, rows, xnT, b_xnT, tok0=0):
        S = self.S
        ss, b_ss = self.ss, self.b_ss
        for s in range(nsub):
            S.op("act", lambda e, s=s: e.activation(
                self.junk[:rows, :], xt[:rows, s, :], AF.Square, accum_out=ss[:rows, s:s + 1]),
                r=[b_xt], w=[self.b_junk, b_ss])
        S.op("act", lambda e: e.activation(
            self.sq[:rows, 0:nsub], ss[:rows, 0:nsub], AF.Sqrt, bias=self.epsc[:rows, :], scale=1.0 / D),
            r=[b_ss, self.b_epsc], w=[self.b_sq])
        S.op("dve", lambda e: e.reciprocal(self.rstd[:rows, 0:nsub], self.sq[:rows, 0:nsub]),
             r=[self.b_sq], w=[self.b_rstd])
        for s in range(nsub):
            S.op("dve", lambda e, s=s: e.tensor_scalar(
                self.xs[:rows, s, :], xt[:rows, s, :], self.rstd[:rows, s:s + 1], None, ALU.mult),
                r=[b_xt, self.b_rstd], w=[self.b_xs[s]])
        for s in range(nsub):
            pt, b_pt = self.pst[s % 2], self.b_pst[s % 2]
            for kc in range(8):
                S.op("pe", lambda e, s=s, kc=kc, pt=pt: e.transpose(
                    pt[:, kc * 128:kc * 128 + rows], self.xs[:rows, s, kc * 128:(kc + 1) * 128],
                    self.ident[:rows, :rows]),
                    r=[self.b_xs[s], self.b_ident], w=[b_pt], inc=(kc == 7))
            S.op("act", lambda e, s=s, pt=pt: e.activation(
                xnT[:, :, tok0 + s * 128:tok0 + s * 128 + rows],
                pt[:, :].rearrange("p (k t) -> p k t", k=8)[:, :, 0:rows], AF.Copy),
                r=[b_pt], w=[b_xnT])

    def phase1(self):
        S = self.S
        sb = self.sb
        self.g1c = sb("g1c", [128, 8], F32)
        self.b_g1c = Buf("g1c")
        S.dma("sp", self.g1c[:], dap(self.norm1_g, 0, [[1, 128], [128, 8]]), self.b_g1c,
              w=[self.b_g1c], allow_slow_non_contiguous=True)
        self.epsc = sb("epsc", [128, 1], F32)
        self.b_epsc = Buf("epsc")
        S.op("dve", lambda e: e.memset(self.epsc[:], EPS), w=[self.b_epsc])
        Wi = sb("Wi", [128, 8, INW], BF16)
        b_Wis = [Buf("Wi%d" % i) for i in range(19)]

        def wib(c0, n):
            return b_Wis[c0 // 256:(c0 + n - 1) // 256 + 1]
        stg = [sb("wstg%d" % i, [128, INW // 2], F32) for i in range(2)]
        b_stg = [Buf("wstg%d" % i) for i in range(2)]
        if _DBG >= 1:
            self.load_weight_bf16(Wi, b_Wis, self.w_in, 8, INW, (self.g1c, self.b_g1c), stg, b_stg, nsplit=19,
                                  order=[9, 10, 3, 4, 5, 6, 7, 8, 0, 1, 2] + list(range(11, 19)))
        self.junk = sb("junk", [128, D], BF16)
        self.b_junk = Buf("junk")
        self.ss = sb("ss", [128, 4], F32)
        self.b_ss = Buf("ss")
        self.sq = sb("sq", [128, 4], F32)
        self.b_sq = Buf("sq")
        self.rstd = sb("rstd", [128, 4], F32)
        self.b_rstd = Buf("rstd")
        self.xs = sb("xs", [128, 3, D], BF16)
        self.b_xs = [Buf("xs%d" % i) for i in range(3)]
        xt = [sb("xt%d" % i, [128, 3, D], F32) for i in range(2)]
        b_xt = [Buf("xt%d" % i) for i in range(2)]
        xnT = [sb("xnT%d" % i, [128, 8, TN], BF16) for i in range(2)]
        b_xnT = [Buf("xnT%d" % i) for i in range(2)]
        fst = [sb("fst%d" % i, [128, 4, TN], BF16) for i in range(2)]
        b_fst = [Buf("fst%d" % i) for i in range(2)]
        vst = [sb("vst%d" % i, [128, 768], BF16) for i in range(2)]
        b_vst = [Buf("vst%d" % i) for i in range(2)]
        kvst = [sb("kvst%d" % i, [128, 2, 768], F32) for i in range(2)]
        b_kvst = [Buf("kvst%d" % i) for i in range(2)]
        b_okv = [Buf("okv%d" % g) for g in range(3)]
        self.outbufs += b_kvst
        fcount = 0
        vcount = 0

        def load_x(ti):
            S.dma("sp", xt[ti % 2][:],
                  self.xall.ap()[ti * TN:(ti + 1) * TN, :].rearrange("(s p) d -> p s d", p=128),
                  b_xt[ti % 2], w=[b_xt[ti % 2]])

        load_x(_TR[0])
        if _TR[0] + 1 < _TR[1]:
            load_x(_TR[0] + 1)
        self.norm_transpose(xt[_TR[0] % 2], b_xt[_TR[0] % 2], 3, 128, xnT[_TR[0] % 2], b_xnT[_TR[0] % 2])
        for ti in range(_TR[0], _TR[1] if _DBG >= 2 else _TR[0]):
            xn, b_xn = xnT[ti % 2], b_xnT[ti % 2]
            did_next = False

            def prep_next(ti=ti):
                if ti + 1 < _TR[1]:
                    if ti + 2 < _TR[1]:
                        load_x(ti + 2)
                    self.norm_transpose(xt[(ti + 1) % 2], b_xt[(ti + 1) % 2], 3, 128,
                                        xnT[(ti + 1) % 2], b_xnT[(ti + 1) % 2])
            is_main = ti >= 11
            has_kv = ti >= 5
            if _DBG < 3:
                continue
            fm = []
            for m in range(4):
                fm.append((2304 + m * 128, self.US, m, ti * TN))
            if has_kv:
                for m in range(6):
                    fm.append((768 + m * 128, self.KT, m, ti * TN - KV_T0))
            if is_main:
                for m in range(6):
                    fm.append((m * 128, self.QT, m, ti * TN - T_MAIN0))
                for m in range(16):
                    fm.append((2816 + m * 128, self.GS, m, ti * TN - T_MAIN0))
            i = 0
            while i < len(fm):
                if not did_next and i >= len(fm) // 2:
                    prep_next()
                    did_next = True
                grp = [fm[i]]
                while len(grp) < 4 and i + len(grp) < len(fm) and fm[i + len(grp)][1] is grp[0][1]:
                    grp.append(fm[i + len(grp)])
                st, b_st = fst[fcount % 2], b_fst[fcount % 2]
                fcount += 1
                for j, (c0, dst, m, t0) in enumerate(grp):
                    pt, b_pt = self.next_ps()
                    for kc in range(8):
                        S.op("pe", lambda e, kc=kc, c0=c0, pt=pt, xn=xn: e.matmul(
                            pt[:, 0:TN], Wi[:, kc, c0:c0 + 128], xn[:, kc, :],
                            start=(kc == 0), stop=(kc == 7)),
                            r=wib(c0, 128) + [b_xn], w=[b_pt], inc=(kc == 7))
                    eng = "act" if (j % 2 == 0) else "dve"
                    if dst is self.US:
                        o_ap = st[:, j, :].rearrange("p (t c) -> p c t", t=LCH)
                        i_ap = pt[:, 0:TN].rearrange("p (c t) -> p c t", t=LCH)
                    else:
                        o_ap, i_ap = st[:, j, :], pt[:, 0:TN]
                    if eng == "act":
                        S.op("act", lambda e, o_ap=o_ap, i_ap=i_ap: e.activation(o_ap, i_ap, AF.Copy),
                             r=[b_pt], w=[b_st])
                    else:
                        S.op("dve", lambda e, o_ap=o_ap, i_ap=i_ap: e.tensor_copy(o_ap, i_ap),
                             r=[b_pt], w=[b_st])
                c0, dst, m0, t0 = grp[0]
                n = len(grp)
                if not (os.environ.get("K_NOGS") and (dst is self.GS or dst is self.QT)):
                    S.dma("sp", dst.ap()[m0:m0 + n, :, t0:t0 + TN].rearrange("m p t -> p m t"),
                          st[:, 0:n, :], b_st, r=[b_st])
                i += n
            if not did_next:
                prep_next()
                did_next = True
            if has_kv and _DBG >= 4:
                need_kout = (ti + 1) * TN > T_ALL - 2048
                for s in range(3):
                    tok = ti * TN + s * 128
                    kout = tok >= T_ALL - 2048
                    vt, b_vt = vst[vcount % 2], b_vst[vcount % 2]
                    kt, b_kt = kvst[vcount % 2], b_kvst[vcount % 2]
                    vcount += 1
                    for kv in ((0, 1) if kout else (1,)):
                        for (cc, nn) in ((0, 512), (512, 256)):
                            c0 = 768 * (1 + kv) + cc
                            pt, b_pt = self.next_ps()
                            for kc in range(8):
                                S.op("pe", lambda e, kc=kc, c0=c0, nn=nn, pt=pt, xn=xn, s=s: e.matmul(
                                    pt[:, 0:nn], xn[:, kc, s * 128:(s + 1) * 128], Wi[:, kc, c0:c0 + nn],
                                    start=(kc == 0), stop=(kc == 7)),
                                    r=wib(c0, nn) + [b_xn], w=[b_pt], inc=(kc == 7))
                            S.op("dve", lambda e, cc=cc, nn=nn, pt=pt, kt=kt, kv=kv: e.tensor_copy(
                                kt[:, kv, cc:cc + nn], pt[:, 0:nn]), r=[b_pt], w=[b_kt])
                    S.op("act", lambda e, kt=kt, vt=vt: e.activation(
                        vt[:, :], kt[:, 1, :], AF.Copy), r=[b_kt], w=[b_vt])
                    if _DBG >= 5:
                        S.dma("sp", self.VS.ap()[tok - KV_T0:tok - KV_T0 + 128, :], vt[:, :], b_vt, r=[b_vt])
                    if kout and _DBG >= 6:
                        for g in range(3):
                            w0 = T_ALL - WINS[g]
                            if tok >= w0:
                                S.dma("sp", self.okv[g].ap()[tok - w0:tok - w0 + 128, :, :],
                                      kt[:, :, 256 * g:256 * (g + 1)], b_kt, r=[b_kt])


        NS = 16
        xs_t = sb("xsamp", [128, 1, D], F32)
        b_xs_t = Buf("xsamp")
        S.dma("sp", xs_t[0:NS, 0, :], self.xsamp.ap(), b_xs_t, w=[b_xs_t])
        xnS = sb("xnS", [128, 8, NS], BF16)
        b_xnS = Buf("xnS")
        self.norm_transpose(xs_t, b_xs_t, 1, NS, xnS, b_xnS)
        fsS = sb("fsS", [128, 32, NS], BF16)
        b_fsS = Buf("fsS")
        fm = [(2304 + m * 128, self.USs, m) for m in range(4)] + [(768 + m * 128, self.KTs, m) for m in range(6)] \
            + [(m * 128, self.QTs, m) for m in range(6)] + [(2816 + m * 128, self.GSs, m) for m in range(16)]
        for j, (c0, dst, m) in enumerate(fm):
            pt, b_pt = self.next_ps()
            for kc in range(8):
                S.op("pe", lambda e, kc=kc, c0=c0, pt=pt: e.matmul(
                    pt[:, 0:NS], Wi[:, kc, c0:c0 + 128], xnS[:, kc, :], start=(kc == 0), stop=(kc == 7)),
                    r=wib(c0, 128) + [b_xnS], w=[b_pt], inc=(kc == 7))
            S.op("act", lambda e, j=j, pt=pt: e.activation(fsS[:, j, :], pt[:, 0:NS], AF.Copy), r=[b_pt], w=[b_fsS])
        for (j0, n, dst) in ((0, 4, self.USs), (4, 6, self.KTs), (10, 6, self.QTs), (16, 16, self.GSs)):
            S.dma("sp", dst.ap().rearrange("m p t -> p m t"), fsS[:, j0:j0 + n, :], b_fsS, r=[b_fsS])
        kvS = sb("kvS", [128, 2, 768], F32)
        vbS = sb("vbS", [128, 768], BF16)
        b_kvS = Buf("kvS")
        for kv in range(2):
            for (cc, nn) in ((0, 512), (512, 256)):
                c0 = 768 * (1 + kv) + cc
                pt, b_pt = self.next_ps()
                for kc in range(8):
                    S.op("pe", lambda e, kc=kc, c0=c0, nn=nn, pt=pt: e.matmul(
                        pt[0:NS, 0:nn], xnS[:, kc, :], Wi[:, kc, c0:c0 + nn], start=(kc == 0), stop=(kc == 7)),
                        r=wib(c0, nn) + [b_xnS], w=[b_pt], inc=(kc == 7))
                S.op("dve", lambda e, cc=cc, nn=nn, pt=pt, kv=kv: e.tensor_copy(
                    kvS[0:NS, kv, cc:cc + nn], pt[0:NS, 0:nn]), r=[b_pt], w=[b_kvS])
        S.op("act", lambda e: e.activation(vbS[0:NS, :], kvS[0:NS, 1, :], AF.Copy), r=[b_kvS], w=[b_kvS])
        S.dma("sp", self.VSs.ap(), vbS[0:NS, :], b_kvS, r=[b_kvS])
        for g in range(3):
            S.dma("sp", self.okvs[g].ap().rearrange("t k c -> t k c"), kvS[0:NS, :, 256 * g:256 * (g + 1)], b_kvS, r=[b_kvS])

    def build_expbias(self):
        S, sb = self.S, self.sb
        EB = sb("EB", [128, 12, 2, 128], F32)
        self.EB, self.b_EB = EB, Buf("EB")
        rb = sb("rb", [32, 12], F32)
        oh = sb("oh", [32, 3 * 129], F32)
        b_rb = Buf("rb")
        S.dma("sp", rb[:], self.rel_bias.ap(), b_rb, w=[b_rb])
        S.dma("sp", oh[:], self.bucket_oh.ap(), b_rb, w=[b_rb])
        ebx = sb("ebx", [12, 3, 129], F32)
        b_ebx = Buf("ebx")
        zt = sb("zt", [12, 385], F32)
        b_zt = Buf("zt")
        S.op("dve", lambda e: e.memset(zt[:], 0.0), w=[b_zt])
        b_ext = Buf("ext")
        S.dma("sp", self.EXT.ap(), zt[:], b_zt, r=[b_zt], w=[b_ext])
        pt, b_pt = self.next_ps()
        S.op("pe", lambda e: e.matmul(pt[0:12, 0:387], rb[:, :], oh[:, :], start=True, stop=True),
             r=[b_rb], w=[b_pt])
        S.op("act", lambda e: e.activation(ebx[:, :, :].rearrange("p g j -> p (g j)"), pt[0:12, 0:387], AF.Exp),
             r=[b_pt], w=[b_ebx])
        for g in range(3):
            S.dma("sp", self.EXT.ap()[4 * g:4 * g + 4, 128:257], ebx[4 * g:4 * g + 4, g, :], b_ebx,
                  r=[b_ebx], w=[b_ext])
        TH = sb("TH", [128, 12, 2, 128], F32)
        b_TH = Buf("TH")
        aid = sb("aid", [128, 128], F32)
        b_aid = Buf("aid")
        S.dma("sp", aid[:], self.antiident.ap(), b_aid, w=[b_aid])
        for gh in range(12):
            for bi in range(2):
                S.dma("sp", TH[:, gh, bi, :], dap(self.EXT, gh * 385 + 129 - 128 * bi, [[1, 128], [1, 128]]),
                      b_TH, r=[b_ext], w=[b_TH])
        for gh in range(12):
            pt, b_pt = self.next_ps()
            S.op("pe", lambda e, gh=gh, pt=pt: e.matmul(
                pt[:, 0:256], aid[:, :], TH[:, gh, :, :].rearrange("p b q -> p (b q)"), start=True, stop=True),
                r=[b_aid, b_TH], w=[b_pt])
            S.op("act", lambda e, gh=gh, pt=pt: e.activation(
                EB[:, gh, :, :].rearrange("p b q -> p (b q)"), pt[:, 0:256], AF.Copy),
                r=[b_pt], w=[self.b_EB])

    def attn_unit(self, kp_ap, kc_ap, q_ap, vp_ap, vc_ap, nq, gh, acc_ap, b_acc, first, rb):
        S = self.S
        ps_s, b_ps = self.next_ps()
        nb = len(self.Ebuf)
        E, b_E = self.Ebuf[self.ucnt % nb], self.b_Ebuf[self.ucnt % nb]
        P, b_P = self.Pbuf[self.ucnt % nb], self.b_Pbuf[self.ucnt % nb]
        self.ucnt += 1
        S.op("pe", lambda e: e.matmul(ps_s[:, 0:nq], kp_ap, q_ap, start=True, stop=True),
             r=rb, w=[b_ps], inc=False)
        S.op("pe", lambda e: e.matmul(ps_s[0:nq, 128:128 + nq], kc_ap, q_ap, start=True, stop=True),
             r=rb, w=[b_ps])
        psv = ps_s[:, 0:256].rearrange("p (b q) -> p b q", b=2)
        if nq == 128:
            S.op("act", lambda e: e.activation(E[:, :, :], psv, AF.Exp, scale=0.125), r=[b_ps], w=[b_E])
            S.op("dve", lambda e: e.tensor_tensor(P[:, :, :], E[:, :, :], self.EB[:, gh, :, :], ALU.mult),
                 r=[b_E, self.b_EB], w=[b_P])
        else:
            S.op("act", lambda e: e.activation(E[:, 0, 0:nq], ps_s[:, 0:nq], AF.Exp, scale=0.125),
                 r=[b_ps], w=[b_E])
            S.op("act", lambda e: e.activation(E[0:nq, 1, 0:nq], ps_s[0:nq, 128:128 + nq], AF.Exp, scale=0.125),
                 r=[b_ps], w=[b_E])
            S.op("dve", lambda e: e.tensor_tensor(P[:, 0, 0:nq], E[:, 0, 0:nq], self.EB[:, gh, 0, 0:nq], ALU.mult),
                 r=[b_E, self.b_EB], w=[b_P])
            S.op("dve", lambda e: e.tensor_tensor(P[0:nq, 1, 0:nq], E[0:nq, 1, 0:nq], self.EB[0:nq, gh, 1, 0:nq],
                                                  ALU.mult), r=[b_E, self.b_EB], w=[b_P])

        def stage_b():
            ps_o, b_po = self.next_ps()
            S.op("pe", lambda e: e.matmul(ps_o[0:65, 0:nq], vp_ap, P[:, 0, 0:nq], start=True, stop=False),
                 r=rb + [b_P], w=[b_po], inc=False)
            S.op("pe", lambda e: e.matmul(ps_o[0:65, 0:nq], vc_ap, P[0:nq, 1, 0:nq], start=False, stop=True),
                 r=rb + [b_P], w=[b_po])
            if first:
                S.op("dve", lambda e: e.tensor_copy(acc_ap, ps_o[0:65, 0:nq]), r=[b_po], w=[b_acc])
            else:
                S.op("dve", lambda e: e.tensor_tensor(acc_ap, acc_ap, ps_o[0:65, 0:nq], ALU.add),
                     r=[b_po], w=[b_acc])
        self.pending_b.append(stage_b)
        while len(self.pending_b) > self.attn_depth:
            self.pending_b.pop(0)()

    def attn_flush(self):
        while self.pending_b:
            self.pending_b.pop(0)()

    def sample_attention(self, sel, b_sel, rec, b_rec):
        S, sb = self.S, self.sb
        KnT = sb("KnT", [128, 6, 16], BF16)
        QnT = sb("QnT", [128, 6, 16], BF16)
        b_kq = Buf("knq")
        S.dma("sp", KnT[:], self.KTs.ap().rearrange("m p t -> p m t"), b_kq, w=[b_kq])
        S.dma("sp", QnT[:], self.QTs.ap().rearrange("m p t -> p m t"), b_kq, w=[b_kq])
        accS = sb("accS", [65, 4, 16], F32)
        b_accS = Buf("accS")
        CK = [sb("CK%d" % i, [128, 512], F32) for i in range(2)]
        Kb = [sb("Kb16_%d" % i, [128, 256], BF16) for i in range(2)]
        KcT = [sb("KcT%d" % i, [128, 2, 128], BF16) for i in range(2)]
        VcP = [sb("VcP%d" % i, [128, 4, 65], BF16) for i in range(2)]
        VnC = [sb("VnC%d" % i, [4, 4, 65], BF16) for i in range(2)]
        b_CK = [Buf("CK%d" % i) for i in range(2)]
        b_Kb = [Buf("Kb16_%d" % i) for i in range(2)]
        b_KcT = [Buf("KcT%d" % i) for i in range(2)]
        b_VcP = [Buf("VcP%d" % i) for i in range(2)]
        b_VnC = [Buf("VnC%d" % i) for i in range(2)]
        for i in range(2):
            S.op("pool", lambda e, i=i: e.memset(VcP[i][:, :, 64:65], 1.0), w=[b_VcP[i]])
            S.op("pool", lambda e, i=i: e.memset(VnC[i][:, :, 64:65], 1.0), w=[b_VnC[i]])
        bi = 0
        for s_ in range(4):
            for g in range(3):
                d, W = DILS[g], WINS[g]
                blocks = [(0, 4)] if g == 0 else [(t, 1) for t in range(4)]
                for (t0, nq) in blocks:
                    i = bi % 2
                    bi += 1
                    row0 = 0 if g == 0 else t0
                    S.dma("sp", CK[i][:, :], dap(self.caches[g], (s_ * W + row0) * 512, [[d * 512, 128], [1, 512]]),
                          b_CK[i], w=[b_CK[i]])
                    S.op("dve", lambda e, i=i: e.tensor_copy(Kb[i][:, :], CK[i][:, 0:256]), r=[b_CK[i]], w=[b_Kb[i]])
                    S.op("pool", lambda e, i=i: e.tensor_copy(
                        VcP[i][:, :, 0:64], CK[i][:, 256:512].rearrange("p (h e) -> p h e", h=4)),
                        r=[b_CK[i]], w=[b_VcP[i]])
                    pt, b_pt = self.pst[i], self.b_pst[i]
                    for pair in range(2):
                        S.op("pe", lambda e, pt=pt, pair=pair, i=i: e.transpose(
                            pt[:, pair * 128:(pair + 1) * 128], Kb[i][:, pair * 128:(pair + 1) * 128], self.ident[:, :]),
                            r=[b_Kb[i], self.b_ident], w=[b_pt], inc=(pair == 1))
                    S.op("act", lambda e, pt=pt, i=i: e.activation(
                        KcT[i][:, :, :], pt[:, 0:256].rearrange("p (a k) -> p a k", a=2), AF.Copy),
                        r=[b_pt], w=[b_KcT[i]])
                    tk0 = 4 * s_ + t0
                    S.dma("sp", VnC[i][0:nq, :, 0:64],
                          dap(self.VSs, tk0 * 768 + 256 * g, [[768, nq], [64, 4], [1, 64]]), b_VnC[i], w=[b_VnC[i]])
                    for h in range(4):
                        pair, hh = h // 2, h % 2
                        rw = slice(64 * hh, 64 * hh + 64)
                        self.attn_unit(KcT[i][rw, pair, :], KnT[rw, 2 * g + pair, tk0:tk0 + nq],
                                       QnT[rw, 2 * g + pair, tk0:tk0 + nq], VcP[i][:, h, :], VnC[i][0:nq, h, :],
                                       nq, 4 * g + h, accS[:, h, tk0:tk0 + nq], b_accS, g == 0,
                                       [b_KcT[i], b_kq, b_VcP[i], b_VnC[i]])
        self.attn_flush()
        aS = sb("aS", [64, 4, 16], BF16)
        b_aS = Buf("aS")
        for h in range(4):
            pt, b_pt = self.next_ps()
            S.op("pe", lambda e, pt=pt, h=h: e.matmul(pt[0:64, 0:16], sel[:, :], accS[:, h, :], start=True, stop=True),
                 r=[b_sel, b_accS], w=[b_pt])
            S.op("dve", lambda e, pt=pt: e.tensor_scalar(rec[:, 0:16], pt[0:64, 0:16], 1e-30, None, ALU.max),
                 r=[b_pt], w=[b_rec])
            S.op("dve", lambda e: e.reciprocal(rec[:, 0:16], rec[:, 0:16]), r=[b_rec], w=[b_rec])
            S.op("dve", lambda e, h=h: e.tensor_tensor(aS[:, h, :], accS[0:64, h, :], rec[:, 0:16], ALU.mult),
                 r=[b_accS, b_rec], w=[b_aS])
        S.dma("sp", self.ATTs.ap().rearrange("h p t -> p h t"), aS[:, :, :], b_aS, r=[b_aS])


    def phase2(self):
        S, sb = self.S, self.sb
        self.build_expbias()
        ast = [sb("ast%d" % i, [64, 512], BF16) for i in range(2)]
        b_ast = [Buf("ast%d" % i) for i in range(2)]
        acnt = 0
        hv = sb("hv", [128, 1], F32)
        b_hv = Buf("hv")
        S.dma("sp", hv[:], self.hv_in.ap(), b_hv, w=[b_hv])
        sel = sb("sel", [65, 64], F32)
        b_sel = Buf("sel")
        S.op("dve", lambda e: e.memset(sel[:], 0.0), w=[b_sel])
        S.op("dve", lambda e: e.memset(sel[64:65, :], 1.0), w=[b_sel])
        NEP = _ADEPTH + 1
        self.Ebuf = [sb("E%d" % i, [128, 2, 128], F32) for i in range(NEP)]
        self.b_Ebuf = [Buf("E%d" % i) for i in range(NEP)]
        self.Pbuf = [sb("P%d" % i, [128, 2, 128], BF16) for i in range(NEP)]
        self.b_Pbuf = [Buf("P%d" % i) for i in range(NEP)]
        self.ucnt = 0
        self.pending_b = []
        self.attn_depth = _ADEPTH
        acc = sb("acc", [65, 2, T_MAIN], F32)
        b_acc = Buf("acc")
        NBLK = 3 * 16 + 2 * 16
        Kb = [sb("Kb%d" % i, [128, T_KV], BF16) for i in range(2)]
        Qb = [sb("Qb%d" % i, [128, T_MAIN], BF16) for i in range(2)]
        Vb = [sb("Vb%d" % i, [128, NBLK, 2, 65], BF16) for i in range(2)]
        b_Kb = [Buf("Kb%d" % i) for i in range(2)]
        b_Qb = [Buf("Qb%d" % i) for i in range(2)]
        b_Vb = [Buf("Vb%d" % i) for i in range(2)]
        for i in range(2):
            S.op("pool", lambda e, i=i: e.memset(Vb[i][:, :, :, :], 0.0), w=[b_Vb[i]])
        rec = sb("rec", [64, 512], F32)
        b_rec = Buf("rec")
        H0 = T_MAIN0 + 128
        li = 0
        for pair in range(2):
            for g in range(3):
                d = DILS[g]
                NB = 4096 // (128 * d)
                nqh = 128 // d
                K_, Q_, V_ = Kb[li % 2], Qb[li % 2], Vb[li % 2]
                bK, bQ, bV = b_Kb[li % 2], b_Qb[li % 2], b_Vb[li % 2]
                li += 1
                mt = 2 * g + pair
                S.dma("sp", K_[:, :], self.KT.ap()[mt, :, :], bK, w=[bK])
                S.dma("sp", Q_[:, :], self.QT.ap()[mt, :, :], bQ, w=[bQ])
                nblk = 3 * d + NB * d
                S.op("pool", lambda e, V_=V_, nblk=nblk: e.memset(V_[:, 0:nblk, :, 64:65], 1.0), w=[bV])
                colb = 256 * g + 128 * pair
                for ty, (lt0, npart) in enumerate(((H0 - 128 * d, 128), (T_MAIN0 - 128 * d, 128), (T_MAIN0, nqh))):
                    S.dma("sp", V_[0:npart, ty * d:(ty + 1) * d, :, 0:64],
                          dap(self.VS, (lt0 - KV_T0) * 768 + colb, [[d * 768, npart], [768, d], [64, 2], [1, 64]]),
                          bV, w=[bV])
                for n in range(NB):
                    S.dma("sp", V_[:, 3 * d + n * d:3 * d + (n + 1) * d, :, 0:64],
                          dap(self.VS, (H0 + 128 * n * d - KV_T0) * 768 + colb,
                              [[d * 768, 128], [768, d], [64, 2], [1, 64]]), bV, w=[bV])
                S.op("dve", lambda e, V_=V_, d=d: e.tensor_scalar(
                    V_[:, 0:3 * d, :, :], V_[:, 0:3 * d, :, :], hv[:, 0:1], None, ALU.mult),
                    r=[b_hv], w=[bV])
                for hh in range(2):
                    gh = 4 * g + 2 * pair + hh
                    rw = slice(64 * hh, 64 * hh + 64)
                    rb = [bK, bQ, bV]

                    def cs(st, n, d=d):
                        return slice(st, st + (n - 1) * d + 1, d)
                    for r in range(d):
                        self.attn_unit(K_[rw, cs(T_MAIN0 - 128 * d + r - KV_T0, 128)],
                                       K_[rw, cs(T_MAIN0 + r - KV_T0, nqh)], Q_[rw, cs(r, nqh)],
                                       V_[:, 1 * d + r, hh, :], V_[0:nqh, 2 * d + r, hh, :], nqh, gh,
                                       acc[:, hh, cs(r, nqh)], b_acc, g == 0, rb)
                        for n in range(NB):
                            kp = H0 + 128 * (n - 1) * d + r - KV_T0
                            kc = H0 + 128 * n * d + r - KV_T0
                            q0 = 128 + 128 * n * d + r
                            vp = (0 * d + r) if n == 0 else (3 * d + (n - 1) * d + r)
                            vc = 3 * d + n * d + r
                            self.attn_unit(K_[rw, cs(kp, 128)], K_[rw, cs(kc, 128)], Q_[rw, cs(q0, 128)],
                                           V_[:, vp, hh, :], V_[:, vc, hh, :], 128, gh,
                                           acc[:, hh, cs(q0, 128)], b_acc, g == 0, rb)
            self.attn_flush()
            for hh in range(2):
                h = 2 * pair + hh
                for c0 in range(0, T_MAIN, 512):
                    n = min(512, T_MAIN - c0)
                    pt, b_pt = self.next_ps()
                    S.op("pe", lambda e, pt=pt, hh=hh, c0=c0, n=n: e.matmul(
                        pt[0:64, 0:n], sel[:, :], acc[:, hh, c0:c0 + n], start=True, stop=True),
                        r=[b_sel, b_acc], w=[b_pt])
                    S.op("dve", lambda e, pt=pt, n=n: e.tensor_scalar(
                        rec[:, 0:n], pt[0:64, 0:n], 1e-18, None, ALU.max), r=[b_pt], w=[b_rec])
                    S.op("act", lambda e, n=n: e.activation(rec[:, 0:n], rec[:, 0:n], AF.Ln), r=[b_rec], w=[b_rec])
                    S.op("act", lambda e, n=n: e.activation(rec[:, 0:n], rec[:, 0:n], AF.Exp, scale=-1.0),
                         r=[b_rec], w=[b_rec])
                    a_, b_a = ast[acnt % 2], b_ast[acnt % 2]
                    acnt += 1
                    S.op("dve", lambda e, a_=a_, hh=hh, c0=c0, n=n: e.tensor_tensor(
                        a_[:, 0:n], acc[0:64, hh, c0:c0 + n], rec[:, 0:n], ALU.mult),
                        r=[b_acc, b_rec], w=[b_a])
                    S.dma("sp", self.ATT.ap()[h, :, c0:c0 + n], a_[:, 0:n], b_a, r=[b_a])
        self.sample_attention(sel, b_sel, rec, b_rec)


    def cmul(self, eng, out_r, out_i, ar, ai, br, bi, t, bufs_r, bufs_w):
        S = self.S
        t1, t2 = t
        S.op(eng, lambda e: e.tensor_tensor(t1, ar, br, ALU.mult), r=bufs_r, w=[self.b_ct])
        S.op(eng, lambda e: e.tensor_tensor(t2, ai, bi, ALU.mult), r=bufs_r, w=[self.b_ct])
        S.op(eng, lambda e: e.tensor_tensor(out_r, t1, t2, ALU.subtract), r=[self.b_ct], w=bufs_w)
        S.op(eng, lambda e: e.tensor_tensor(t1, ar, bi, ALU.mult), r=bufs_r + bufs_w, w=[self.b_ct])
        S.op(eng, lambda e: e.tensor_tensor(t2, ai, br, ALU.mult), r=bufs_r, w=[self.b_ct])
        S.op(eng, lambda e: e.tensor_tensor(out_i, t1, t2, ALU.add), r=[self.b_ct], w=bufs_w)

    def phase3a(self):
        S, sb = self.S, self.sb
        L = LCH
        b_in = Buf("ssm_in")
        LR = sb("LR", [128, 16], F32)
        LI = sb("LI", [128, 16], F32)
        LDT = sb("LDT", [128, 16], F32)
        BR = sb("BR", [128, 16, 16], F32)
        BI = sb("BI", [128, 16, 16], F32)
        CR = sb("CR", [128, 16, 16], F32)
        CI = sb("CI", [128, 16, 16], F32)
        Dc = sb("Dc", [128, 4], F32)
        S.dma("sp", LR[:], dap(self.lam_re, 0, [[1, 128], [128, 16]]), b_in, w=[b_in], allow_slow_non_contiguous=True)
        S.dma("sp", LI[:], dap(self.lam_im, 0, [[1, 128], [128, 16]]), b_in, w=[b_in], allow_slow_non_contiguous=True)
        for j in range(2):
            S.dma("sp", LDT[64 * j:64 * j + 64, :], dap(self.log_dt, j, [[0, 64], [2, 16]]), b_in, w=[b_in],
                  allow_slow_non_contiguous=True)
        S.dma("sp", BR[:], dap(self.b_re, 0, [[16, 128], [2048, 16], [1, 16]]), b_in, w=[b_in])
        S.dma("sp", BI[:], dap(self.b_im, 0, [[16, 128], [2048, 16], [1, 16]]), b_in, w=[b_in])
        b_cn = Buf("Cnat")
        for nm, src, dstC in (("r", self.c_re, CR), ("i", self.c_im, CI)):
            CTf = sb("CTf" + nm, [16, 32, 64], F32)
            CTb = sb("CTb" + nm, [16, 32, 64], BF16)
            S.dma("sp", CTf[:], dap(src, 0, [[64, 16], [1024, 32], [1, 64]]), b_cn, w=[b_cn])
            S.op("dve", lambda e, CTf=CTf, CTb=CTb: e.tensor_copy(CTb[:], CTf[:]), r=[b_cn], w=[b_cn])
            pt, b_pt = self.next_ps()
            for pr in range(16):
                for j in range(2):
                    S.op("pe", lambda e, pt=pt, pr=pr, j=j, CTb=CTb: e.matmul(
                        pt[64 * j:64 * j + 64, 16 * pr:16 * pr + 16], CTb[0:16, 2 * pr + j, :],
                        self.ident[0:16, 0:16], start=True, stop=True),
                        r=[b_cn, self.b_ident], w=[b_pt], inc=(pr == 15 and j == 1))
            S.op("act", lambda e, pt=pt, dstC=dstC: e.activation(
                dstC[:].rearrange("p r c -> p (r c)"), pt[:, 0:256], AF.Copy), r=[b_pt], w=[b_in])
        S.dma("sp", Dc[:], dap(self.ssm_d, 0, [[1, 128], [128, 4]]), b_in, w=[b_in], allow_slow_non_contiguous=True)
        halfpi = sb("halfpi", [128, 1], F32)
        b_w = Buf("ssm_work")
        self.b_ct = Buf("ct")
        S.op("dve", lambda e: e.memset(halfpi[:], math.pi / 2), w=[b_w])
        dt = sb("dt", [128, 16], F32)
        S.op("act", lambda e: e.activation(dt[:], LDT[:], AF.Exp), r=[b_in], w=[b_w])
        t1 = sb("t1", [128, 16], F32)
        t2 = sb("t2", [128, 16], F32)
        t3 = sb("t3", [128, 16], F32)
        ar = sb("ar", [128, 16], F32)
        ai = sb("ai", [128, 16], F32)
        wr = sb("wr", [128, 16], F32)
        wi = sb("wi", [128, 16], F32)
        zr = sb("zr", [128, 16], F32)
        zi = sb("zi", [128, 16], F32)
        pr_ = sb("pr_", [128, 16], F32)
        pi_ = sb("pi_", [128, 16], F32)
        qr_ = sb("qr_", [128, 16], F32)
        qi_ = sb("qi_", [128, 16], F32)
        MSQ = 8
        S.op("dve", lambda e: e.scalar_tensor_tensor(zr[:], LR[:], 1.0 / (1 << MSQ), dt[:], ALU.mult, ALU.mult),
             r=[b_in, b_w], w=[b_w])
        S.op("dve", lambda e: e.scalar_tensor_tensor(zi[:], LI[:], 1.0 / (1 << MSQ), dt[:], ALU.mult, ALU.mult),
             r=[b_in, b_w], w=[b_w])
        S.op("dve", lambda e: e.tensor_scalar(pr_[:], zr[:], 1.0 / 5, 1.0, ALU.mult, ALU.add), r=[b_w], w=[b_w])
        S.op("dve", lambda e: e.tensor_scalar(pi_[:], zi[:], 1.0 / 5, None, ALU.mult), r=[b_w], w=[b_w])
        for dv in (4.0, 3.0, 2.0):
            self.cmul("dve", qr_[:], qi_[:], zr[:], zi[:], pr_[:], pi_[:], (t1[:], t2[:]), [b_w], [b_w])
            S.op("dve", lambda e, dv=dv: e.tensor_scalar(pr_[:], qr_[:], 1.0 / dv, 1.0, ALU.mult, ALU.add),
                 r=[b_w], w=[b_w])
            S.op("dve", lambda e, dv=dv: e.tensor_scalar(pi_[:], qi_[:], 1.0 / dv, None, ALU.mult), r=[b_w], w=[b_w])
        self.cmul("dve", wr[:], wi[:], zr[:], zi[:], pr_[:], pi_[:], (t1[:], t2[:]), [b_w], [b_w])
        for _ in range(MSQ):
            S.op("dve", lambda e: e.tensor_tensor(t1[:], wr[:], wr[:], ALU.mult), r=[b_w], w=[self.b_ct])
            S.op("dve", lambda e: e.tensor_tensor(t2[:], wi[:], wi[:], ALU.mult), r=[b_w], w=[self.b_ct])
            S.op("dve", lambda e: e.tensor_tensor(t3[:], wr[:], wi[:], ALU.mult), r=[b_w], w=[self.b_ct])
            S.op("dve", lambda e: e.tensor_tensor(t1[:], t1[:], t2[:], ALU.subtract), r=[self.b_ct], w=[self.b_ct])
            S.op("dve", lambda e: e.tensor_tensor(t3[:], t3[:], wi[:], ALU.add), r=[self.b_ct, b_w], w=[self.b_ct])
            S.op("dve", lambda e: e.scalar_tensor_tensor(wr[:], wr[:], 2.0, t1[:], ALU.mult, ALU.add),
                 r=[self.b_ct, b_w], w=[b_w])
            S.op("dve", lambda e: e.tensor_scalar(wi[:], t3[:], 2.0, None, ALU.mult), r=[self.b_ct], w=[b_w])
        S.op("dve", lambda e: e.tensor_scalar(ar[:], wr[:], 1.0, None, ALU.add), r=[b_w], w=[b_w])
        S.op("dve", lambda e: e.tensor_copy(ai[:], wi[:]), r=[b_w], w=[b_w])
        APr = sb("APr", [128, L + 1, 16], F32)
        APi = sb("APi", [128, L + 1, 16], F32)
        b_ap = Buf("AP")
        S.op("dve", lambda e: e.memset(APr[:, 0, :], 1.0), w=[b_ap])
        S.op("dve", lambda e: e.memset(APi[:, 0, :], 0.0), w=[b_ap])
        for ee in range(1, L + 1):
            self.cmul("dve", APr[:, ee, :], APi[:, ee, :], APr[:, ee - 1, :], APi[:, ee - 1, :], ar[:], ai[:],
                      (t1[:], t2[:]), [b_ap, b_w], [b_ap])
        nr = sb("nr", [128, 16], F32)
        cr = sb("cr", [128, 16], F32)
        ci = sb("ci", [128, 16], F32)
        S.op("dve", lambda e: e.tensor_copy(nr[:], wr[:]), r=[b_w], w=[b_w])
        S.op("dve", lambda e: e.tensor_tensor(t1[:], LR[:], LR[:], ALU.mult), r=[b_in], w=[self.b_ct])
        S.op("dve", lambda e: e.tensor_tensor(t2[:], LI[:], LI[:], ALU.mult), r=[b_in], w=[self.b_ct])
        S.op("dve", lambda e: e.tensor_tensor(t1[:], t1[:], t2[:], ALU.add), r=[self.b_ct], w=[self.b_ct])
        S.op("dve", lambda e: e.reciprocal(t3[:], t1[:]), r=[self.b_ct], w=[self.b_ct])
        S.op("dve", lambda e: e.tensor_tensor(t1[:], nr[:], LR[:], ALU.mult), r=[b_w, b_in], w=[self.b_ct])
        S.op("dve", lambda e: e.tensor_tensor(t2[:], ai[:], LI[:], ALU.mult), r=[b_w, b_in], w=[self.b_ct])
        S.op("dve", lambda e: e.tensor_tensor(t1[:], t1[:], t2[:], ALU.add), r=[self.b_ct], w=[self.b_ct])
        S.op("dve", lambda e: e.tensor_tensor(cr[:], t1[:], t3[:], ALU.mult), r=[self.b_ct], w=[b_w])
        S.op("dve", lambda e: e.tensor_tensor(t1[:], ai[:], LR[:], ALU.mult), r=[b_w, b_in], w=[self.b_ct])
        S.op("dve", lambda e: e.tensor_tensor(t2[:], nr[:], LI[:], ALU.mult), r=[b_w, b_in], w=[self.b_ct])
        S.op("dve", lambda e: e.tensor_tensor(t1[:], t1[:], t2[:], ALU.subtract), r=[self.b_ct], w=[self.b_ct])
        S.op("dve", lambda e: e.tensor_tensor(ci[:], t1[:], t3[:], ALU.mult), r=[self.b_ct], w=[b_w])
        Bbr = sb("Bbr", [128, 16, 16], F32)
        Bbi = sb("Bbi", [128, 16, 16], F32)
        T1 = sb("T1", [128, 16, 16], F32)
        T2 = sb("T2", [128, 16, 16], F32)
        b_bb = Buf("Bb")

        def bc(x):
            return x.unsqueeze(2).to_broadcast([128, 16, 16])
        self.cmul("dve", Bbr[:], Bbi[:], bc(cr[:]), bc(ci[:]), BR[:], BI[:], (T1[:], T2[:]), [b_w, b_in], [b_bb])
        PBr = sb("PBr", [128, L, 16, 32], BF16)
        PBi = sb("PBi", [128, L, 16, 32], BF16)
        CBr = sb("CBr", [128, 16, 32], BF16)
        CBin = sb("CBin", [128, 16, 32], BF16)
        VBr = sb("VBr", [128, L + 1, 16, 32], BF16)
        VBin = sb("VBin", [128, L + 1, 16, 32], BF16)
        b_pb, b_cb, b_vb = Buf("PB"), Buf("CB"), Buf("VB")
        for tt, bb in ((PBr, b_pb), (PBi, b_pb), (CBr, b_cb), (CBin, b_cb), (VBr, b_vb), (VBin, b_vb)):
            S.op("pool", lambda e, tt=tt: e.memset(tt[:], 0.0), w=[bb])
        for j in range(2):
            ps_, cs_ = slice(64 * j, 64 * j + 64), slice(16 * j, 16 * j + 16)
            S.op("dve", lambda e, ps_=ps_, cs_=cs_: e.tensor_copy(CBr[ps_, :, cs_], CR[ps_]), r=[b_in], w=[b_cb])
            S.op("dve", lambda e, ps_=ps_, cs_=cs_: e.tensor_scalar(CBin[ps_, :, cs_], CI[ps_], -1.0, None, ALU.mult),
                 r=[b_in], w=[b_cb])
        for k in range(L):
            akr, aki = bc(APr[:, k, :]), bc(APi[:, k, :])
            S.op("dve", lambda e, akr=akr: e.tensor_tensor(T1[:], akr, Bbr[:], ALU.mult), r=[b_ap, b_bb], w=[self.b_ct])
            S.op("dve", lambda e, aki=aki: e.tensor_tensor(T2[:], aki, Bbi[:], ALU.mult), r=[b_ap, b_bb], w=[self.b_ct])
            for j in range(2):
                ps_, cs_ = slice(64 * j, 64 * j + 64), slice(16 * j, 16 * j + 16)
                S.op("dve", lambda e, ps_=ps_, cs_=cs_, k=k: e.tensor_tensor(
                    PBr[ps_, k, :, cs_], T1[ps_], T2[ps_], ALU.subtract), r=[self.b_ct], w=[b_pb])
            S.op("dve", lambda e, akr=akr: e.tensor_tensor(T1[:], akr, Bbi[:], ALU.mult), r=[b_ap, b_bb, b_pb], w=[self.b_ct])
            S.op("dve", lambda e, aki=aki: e.tensor_tensor(T2[:], aki, Bbr[:], ALU.mult), r=[b_ap, b_bb], w=[self.b_ct])
            for j in range(2):
                ps_, cs_ = slice(64 * j, 64 * j + 64), slice(16 * j, 16 * j + 16)
                S.op("dve", lambda e, ps_=ps_, cs_=cs_, k=k: e.tensor_tensor(
                    PBi[ps_, k, :, cs_], T1[ps_], T2[ps_], ALU.add), r=[self.b_ct], w=[b_pb])
        for ee in range(L + 1):
            akr, aki = bc(APr[:, ee, :]), bc(APi[:, ee, :])
            S.op("dve", lambda e, akr=akr: e.tensor_tensor(T1[:], akr, CR[:], ALU.mult), r=[b_ap, b_in, b_vb, b_pb], w=[self.b_ct])
            S.op("dve", lambda e, aki=aki: e.tensor_tensor(T2[:], aki, CI[:], ALU.mult), r=[b_ap, b_in], w=[self.b_ct])
            for j in range(2):
                ps_, cs_ = slice(64 * j, 64 * j + 64), slice(16 * j, 16 * j + 16)
                S.op("dve", lambda e, ps_=ps_, cs_=cs_, ee=ee: e.tensor_tensor(
                    VBr[ps_, ee, :, cs_], T1[ps_], T2[ps_], ALU.subtract), r=[self.b_ct], w=[b_vb])
            S.op("dve", lambda e, aki=aki: e.tensor_tensor(T1[:], aki, CR[:], ALU.mult), r=[b_ap, b_in, b_vb], w=[self.b_ct])
            S.op("dve", lambda e, akr=akr: e.tensor_tensor(T2[:], akr, CI[:], ALU.mult), r=[b_ap, b_in], w=[self.b_ct])
            S.op("dve", lambda e: e.tensor_tensor(T1[:], T1[:], T2[:], ALU.add), r=[self.b_ct], w=[self.b_ct])
            for j in range(2):
                ps_, cs_ = slice(64 * j, 64 * j + 64), slice(16 * j, 16 * j + 16)
                S.op("dve", lambda e, ps_=ps_, cs_=cs_, ee=ee: e.tensor_scalar(
                    VBin[ps_, ee, :, cs_], T1[ps_], -1.0, None, ALU.mult), r=[self.b_ct], w=[b_vb])
        S.dma("sp", self.VBs.ap()[0], VBr[:].rearrange("p e r c -> p (e r c)"), b_vb, r=[b_vb])
        S.dma("sp", self.VBs.ap()[1], VBin[:].rearrange("p e r c -> p (e r c)"), b_vb, r=[b_vb])
        S.dma("sp", self.TAB.ap()[:, 0, :], APr[:, L, :], b_ap, r=[b_ap])
        S.dma("sp", self.TAB.ap()[:, 1, :], APi[:, L, :], b_ap, r=[b_ap])
        S.dma("sp", self.TAB.ap()[:, 2, :], APr[:, 1, :], b_ap, r=[b_ap])
        S.dma("sp", self.TAB.ap()[:, 3, :], APi[:, 1, :], b_ap, r=[b_ap])
        KBf = sb("KBf", [128, 4, 128], F32)
        b_kbf = Buf("KBf")
        KBo = [sb("KBo%d" % i, [128, 4, 128], BF16) for i in range(2)]
        b_kbo = [Buf("KBo%d" % i) for i in range(2)]
        S.op("pool", lambda e: e.memset(KBf[:], 0.0), w=[b_kbf])
        for lag in range(L):
            for ct in range(4):
                pt, b_pt = self.next_ps()
                for r4 in range(4):
                    pr = 4 * ct + r4
                    o_ = pt[32 * r4:32 * r4 + 32, 32 * r4:32 * r4 + 32]
                    S.op("pe", lambda e, o_=o_, lag=lag, pr=pr, r4=r4: e.matmul(
                        o_, PBr[:, lag, pr, :], CBr[:, pr, :], start=True, stop=False, tile_position=(0, 32 * r4)),
                        r=[b_pb, b_cb], w=[b_pt], inc=False)
                    S.op("pe", lambda e, o_=o_, lag=lag, pr=pr, r4=r4: e.matmul(
                        o_, PBi[:, lag, pr, :], CBin[:, pr, :], start=False, stop=True, tile_position=(0, 32 * r4)),
                        r=[b_pb, b_cb], w=[b_pt], inc=(r4 == 3))
                for r4 in range(4):
                    sl = slice(32 * r4, 32 * r4 + 32)
                    S.op("act", lambda e, sl=sl, ct=ct, pt=pt: e.activation(KBf[sl, ct, sl], pt[sl, sl], AF.Copy),
                         r=[b_pt], w=[b_kbf])
                if lag == 0:
                    S.op("dve", lambda e, ct=ct: e.scalar_tensor_tensor(
                        KBf[:, ct, :], self.ident_f[:, :], Dc[:, ct:ct + 1], KBf[:, ct, :], ALU.mult, ALU.add),
                        r=[b_in, self.b_ident], w=[b_kbf])
            ko, b_ko = KBo[lag % 2], b_kbo[lag % 2]
            S.op("dve", lambda e, ko=ko: e.tensor_copy(ko[:], KBf[:]), r=[b_kbf], w=[b_ko])
            S.dma("sp", self.KBs.ap()[:, lag, :], ko[:].rearrange("p c m -> p (c m)"), b_ko, r=[b_ko])
        WSo = [sb("WSo%d" % i, [128, 4, 128], BF16) for i in range(2)]
        b_wso = [Buf("WSo%d" % i) for i in range(2)]
        cnt = 0
        for ri, PB in enumerate((PBr, PBi)):
            for k in range(L):
                wo, b_wo = WSo[cnt % 2], b_wso[cnt % 2]
                cnt += 1
                for ct in range(4):
                    pt, b_pt = self.next_ps()
                    for r4 in range(4):
                        pr = 4 * ct + r4
                        S.op("pe", lambda e, pt=pt, r4=r4, PB=PB, k=k, pr=pr: e.matmul(
                            pt[32 * r4:32 * r4 + 32, 0:128], PB[:, k, pr, :], self.ident[:, :], start=True, stop=True,
                            tile_position=(0, 32 * r4)),
                            r=[b_pb, self.b_ident], w=[b_pt], inc=(r4 == 3))
                    S.op("act", lambda e, pt=pt, wo=wo, ct=ct: e.activation(wo[:, ct, :], pt[:, 0:128], AF.Copy),
                         r=[b_pt], w=[b_wo])
                S.dma("sp", self.WSs.ap()[ri, :, k, :], wo[:].rearrange("p c m -> p (c m)"), b_wo, r=[b_wo])


    def gelu_tanh(self, dst, src, tmp, b_src, b_tmp, b_dst):
        S = self.S
        S.op("dve", lambda e: e.tensor_tensor(tmp, src, src, ALU.mult), r=[b_src], w=[b_tmp])
        S.op("dve", lambda e: e.tensor_scalar(tmp, tmp, 0.044715, 1.0, ALU.mult, ALU.add), r=[b_tmp], w=[b_tmp])
        S.op("dve", lambda e: e.tensor_tensor(tmp, tmp, src, ALU.mult), r=[b_src, b_tmp], w=[b_tmp])
        S.op("act", lambda e: e.activation(tmp, tmp, AF.Sigmoid, scale=1.5957691216), r=[b_tmp], w=[b_tmp])
        S.op("dve", lambda e: e.tensor_tensor(dst, src, tmp, ALU.mult), r=[b_src, b_tmp], w=[b_dst])

    def sample_ssm(self, WS, VB, TAB, b_wt):
        S, sb = self.S, self.sb
        NS = 16
        b_su = Buf("s_u")
        Us = sb("Us", [128, 4, NS], BF16)
        Dcs = sb("Dcs", [128, 4], F32)
        S.dma("sp", Us[:], self.USs.ap().rearrange("c p t -> p c t"), b_su, w=[b_su])
        S.dma("sp", Dcs[:], dap(self.ssm_d, 0, [[1, 128], [128, 4]]), b_su, w=[b_su], allow_slow_non_contiguous=True)
        BU = [sb("BU%d" % i, [128, 16, NS], F32) for i in range(2)]
        b_BU = Buf("BU")
        for ri in range(2):
            for r4 in range(4):
                pt, b_pt = self.next_ps()
                rows = slice(32 * r4, 32 * r4 + 32)
                for ct in range(4):
                    S.op("pe", lambda e, pt=pt, ct=ct, r4=r4, rows=rows, ri=ri: e.matmul(
                        pt[:, ct * NS:(ct + 1) * NS], WS[ri][rows, 0, ct, :], Us[rows, ct, :],
                        start=True, stop=True, tile_position=(32 * r4, 0)),
                        r=[b_wt, b_su], w=[b_pt], inc=(ct == 3))
                S.op("act", lambda e, pt=pt, ri=ri, r4=r4: e.activation(
                    BU[ri][:, r4:16:4, :], pt[:, 0:4 * NS].rearrange("p (r c) -> p r c", r=4), AF.Copy),
                    r=[b_pt], w=[b_BU])
        H0 = [sb("H0_%d" % i, [128, 16, 4], F32) for i in range(2)]
        b_H0 = Buf("H0")
        for ri, src in enumerate((self.st_re, self.st_im)):
            for s_ in range(4):
                S.dma("sp", H0[ri][:, :, s_], dap(src, s_ * 2048, [[1, 128], [128, 16]]), b_H0, w=[b_H0],
                      allow_slow_non_contiguous=True)
        Hs = [sb("Hs%d" % i, [128, 16, NS], F32) for i in range(2)]
        b_Hs = Buf("Hs")
        tq = [sb("tqs%d" % i, [128, 16, 4], F32) for i in range(4)]
        b_tq = Buf("tqs")
        A1r = TAB[:, 2, :].unsqueeze(2).to_broadcast([128, 16, 4])
        A1i = TAB[:, 3, :].unsqueeze(2).to_broadcast([128, 16, 4])

        def v4(x, t):
            return x[:, :, t:t + 13:4]
        for t in range(4):
            if t == 0:
                pr_, pi_, bp = H0[0][:, :, :], H0[1][:, :, :], b_H0
            else:
                pr_, pi_, bp = v4(Hs[0], t - 1), v4(Hs[1], t - 1), b_Hs
            S.op("pool", lambda e, pr_=pr_: e.tensor_tensor(tq[0][:], A1r, pr_, ALU.mult), r=[bp, b_wt], w=[b_tq])
            S.op("pool", lambda e, pi_=pi_: e.tensor_tensor(tq[1][:], A1i, pi_, ALU.mult), r=[bp], w=[b_tq])
            S.op("pool", lambda e, pi_=pi_: e.tensor_tensor(tq[2][:], A1r, pi_, ALU.mult), r=[bp], w=[b_tq])
            S.op("pool", lambda e, pr_=pr_: e.tensor_tensor(tq[3][:], A1i, pr_, ALU.mult), r=[bp], w=[b_tq])
            S.op("pool", lambda e: e.tensor_tensor(tq[0][:], tq[0][:], tq[1][:], ALU.subtract), r=[b_tq], w=[b_tq])
            S.op("pool", lambda e: e.tensor_tensor(tq[2][:], tq[2][:], tq[3][:], ALU.add), r=[b_tq], w=[b_tq])
            S.op("pool", lambda e, t=t: e.tensor_tensor(v4(Hs[0], t), v4(BU[0], t), tq[0][:], ALU.add),
                 r=[b_tq, b_BU], w=[b_Hs])
            S.op("pool", lambda e, t=t: e.tensor_tensor(v4(Hs[1], t), v4(BU[1], t), tq[2][:], ALU.add),
                 r=[b_tq, b_BU], w=[b_Hs])
        for ri, dst in enumerate((self.ossm_s_re, self.ossm_s_im)):
            for s_ in range(4):
                S.dma("sp", dap(dst, s_ * 2048, [[1, 128], [128, 16]]), Hs[ri][:, :, 4 * s_ + 3], b_Hs, r=[b_Hs],
                      allow_slow_non_contiguous=True)
        Hb = [sb("Hbs%d" % i, [128, 16, NS], BF16) for i in range(2)]
        b_Hb = Buf("Hbs")
        for ri in range(2):
            S.op("act", lambda e, ri=ri: e.activation(Hb[ri][:], Hs[ri][:], AF.Copy), r=[b_Hs], w=[b_Hb])
        yfs = sb("yfs", [128, 4, NS], F32)
        yts = sb("yts", [128, 4, NS], F32)
        ybs = sb("ybs", [128, 4, NS], BF16)
        b_yfs, b_yts, b_ybs = Buf("yfs"), Buf("yts"), Buf("ybs")
        for ct in range(4):
            pt, b_pt = self.next_ps()
            for r4 in range(4):
                pr = 4 * ct + r4
                S.op("pe", lambda e, pt=pt, r4=r4, pr=pr: e.matmul(
                    pt[32 * r4:32 * r4 + 32, 0:NS], VB[0][:, 0, pr, :], Hb[0][:, pr, :], start=True, stop=False,
                    tile_position=(0, 32 * r4)), r=[b_wt, b_Hb], w=[b_pt], inc=False)
                S.op("pe", lambda e, pt=pt, r4=r4, pr=pr: e.matmul(
                    pt[32 * r4:32 * r4 + 32, 0:NS], VB[1][:, 0, pr, :], Hb[1][:, pr, :], start=False, stop=True,
                    tile_position=(0, 32 * r4)), r=[b_wt, b_Hb], w=[b_pt], inc=(r4 == 3))
            S.op("dve", lambda e, pt=pt, ct=ct: e.scalar_tensor_tensor(
                yfs[:, ct, :], Us[:, ct, :], Dcs[:, ct:ct + 1], pt[:, 0:NS], ALU.mult, ALU.add),
                r=[b_pt, b_su], w=[b_yfs])
        self.gelu_tanh(ybs[:], yfs[:], yts[:], b_yfs, b_yts, b_ybs)
        S.dma("sp", self.YSs.ap().rearrange("c p t -> p c t"), ybs[:], b_ybs, r=[b_ybs])


    def cmadd(self, eng, dr, di, ar, ai, xr, xi, tmps, rbufs, wbuf, b_t):
        S = self.S
        t0, t1, t2, t3 = tmps
        S.op(eng, lambda e: e.tensor_tensor(t0, ar, xr, ALU.mult), r=rbufs, w=[b_t])
        S.op(eng, lambda e: e.tensor_tensor(t1, ai, xi, ALU.mult), r=rbufs, w=[b_t])
        S.op(eng, lambda e: e.tensor_tensor(t2, ar, xi, ALU.mult), r=rbufs, w=[b_t])
        S.op(eng, lambda e: e.tensor_tensor(t3, ai, xr, ALU.mult), r=rbufs, w=[b_t])
        S.op(eng, lambda e: e.tensor_tensor(t0, t0, t1, ALU.subtract), r=[b_t], w=[b_t])
        S.op(eng, lambda e: e.tensor_tensor(t2, t2, t3, ALU.add), r=[b_t], w=[b_t])
        S.op(eng, lambda e: e.tensor_tensor(dr, dr, t0, ALU.add), r=[b_t] + rbufs, w=[wbuf])
        S.op(eng, lambda e: e.tensor_tensor(di, di, t2, ALU.add), r=[b_t] + rbufs, w=[wbuf])

    def phase3b(self):
        S, sb = self.S, self.sb
        L = LCH
        NCH = T_ALL // L
        NH = NCH // 2
        TH = T_ALL // 2
        GRP = 8
        NG = NCH // GRP
        b_wt = Buf("ssm_wt")
        WS = [sb("WS%d" % i, [128, L, 4, 128], BF16) for i in range(2)]
        VB = [sb("VB%d" % i, [128, L + 1, 16, 32], BF16) for i in range(2)]
        TAB = sb("TAB", [128, 4, 16], F32)
        for i in range(2):
            S.dma("sp", WS[i][:].rearrange("p l c m -> p l (c m)"), self.WSs.ap()[i], b_wt, w=[b_wt])
            S.dma("sp", VB[i][:].rearrange("p e r c -> p (e r c)"), self.VBs.ap()[i], b_wt, w=[b_wt])
        S.dma("sp", TAB[:], self.TAB.ap(), b_wt, w=[b_wt])
        self.sample_ssm(WS, VB, TAB, b_wt)
        Sall = [sb("Sall%d" % i, [128, 16, NCH], F32) for i in range(2)]
        b_S = Buf("Sall")
        U = sb("Uh", [128, 4, TH], BF16)
        b_U = Buf("Uh")
        for half in range(2):
            S.dma("sp", U[:], self.US.ap()[:, :, half * TH:(half + 1) * TH].rearrange("c p t -> p c t"), b_U, w=[b_U])
            for ri in range(2):
                for pr in range(16):
                    ct, r4 = pr // 4, pr % 4
                    rows = slice(32 * r4, 32 * r4 + 32)
                    pt, b_pt = self.next_ps()
                    for tau in range(L):
                        S.op("pe", lambda e, pt=pt, ct=ct, r4=r4, rows=rows, tau=tau, ri=ri: e.matmul(
                            pt[:, 0:NH], WS[ri][rows, L - 1 - tau, ct, :],
                            U[rows, ct, :].rearrange("p (n t c) -> p n t c", t=L, c=TN // L)[:, :, tau, :],
                            start=(tau == 0), stop=(tau == L - 1), tile_position=(32 * r4, 0)),
                            r=[b_wt, b_U], w=[b_pt], inc=(tau == L - 1))
                    S.op("act", lambda e, pt=pt, ri=ri, pr=pr, half=half: e.activation(
                        Sall[ri][:, pr, half * NH:(half + 1) * NH], pt[:, 0:NH], AF.Copy), r=[b_pt], w=[b_S])
        PW = [sb("PW%d" % i, [128, GRP + 1, 16], F32) for i in range(2)]
        b_PW = Buf("PW")
        tq = [sb("tq%d" % i, [128, 16, NG], F32) for i in range(4)]
        b_tq = Buf("tq")
        self.b_ct = b_tq
        S.op("dve", lambda e: e.tensor_copy(PW[0][:, 1, :], TAB[:, 0, :]), r=[b_wt], w=[b_PW])
        S.op("dve", lambda e: e.tensor_copy(PW[1][:, 1, :], TAB[:, 1, :]), r=[b_wt], w=[b_PW])
        for k in range(2, GRP + 1):
            self.cmul("dve", PW[0][:, k, :], PW[1][:, k, :], PW[0][:, k - 1, :], PW[1][:, k - 1, :],
                      PW[0][:, 1, :], PW[1][:, 1, :], (tq[0][:, :, 0], tq[1][:, :, 0]), [b_PW], [b_PW])

        def bcg(x):
            return x.unsqueeze(2).to_broadcast([128, 16, NG])

        def vw(ri, i):
            return Sall[ri][:, :, i:i + (NG - 1) * GRP + 1:GRP]
        tmps = tuple(t[:, :, :] for t in tq)
        for i in range(1, GRP):
            self.cmadd("dve", vw(0, i), vw(1, i), bcg(PW[0][:, 1, :]), bcg(PW[1][:, 1, :]), vw(0, i - 1), vw(1, i - 1),
                       tmps, [b_PW, b_S], b_S, b_tq)
        t16 = tuple(t[:, :, 0] for t in tq)
        for C in range(1, NG):
            e0, e1 = C * GRP - 1, (C + 1) * GRP - 1
            self.cmadd("dve", Sall[0][:, :, e1], Sall[1][:, :, e1], PW[0][:, GRP, :], PW[1][:, GRP, :],
                       Sall[0][:, :, e0], Sall[1][:, :, e0], t16, [b_PW, b_S], b_S, b_tq)

        def vw1(ri, i):
            return Sall[ri][:, :, GRP + i:GRP + i + (NG - 2) * GRP + 1:GRP]

        def ends(ri):
            return Sall[ri][:, :, GRP - 1:GRP - 1 + (NG - 2) * GRP + 1:GRP]

        def bcg1(x):
            return x.unsqueeze(2).to_broadcast([128, 16, NG - 1])
        tmps1 = tuple(t[:, :, 0:NG - 1] for t in tq)
        for i in range(GRP - 1):
            self.cmadd("dve", vw1(0, i), vw1(1, i), bcg1(PW[0][:, i + 1, :]), bcg1(PW[1][:, i + 1, :]), ends(0), ends(1),
                       tmps1, [b_PW, b_S], b_S, b_tq)
        S.dma("sp", dap(self.ossm_re, 0, [[1, 128], [128, 16]]), Sall[0][:, :, NCH - 1], b_S, r=[b_S],
              allow_slow_non_contiguous=True)
        S.dma("sp", dap(self.ossm_im, 0, [[1, 128], [128, 16]]), Sall[1][:, :, NCH - 1], b_S, r=[b_S],
              allow_slow_non_contiguous=True)
        HBt = sb("HBt", [128, 16, NH], BF16)
        b_HBt = Buf("HBt")
        for ri in range(2):
            S.op("act", lambda e, ri=ri: e.activation(HBt[:, :, :], Sall[ri][:, :, NH - 1:2 * NH - 1], AF.Copy),
                 r=[b_S], w=[b_HBt])
            S.dma("sp", self.HBs.ap()[ri], HBt[:, :, :], b_HBt, r=[b_HBt])

    def phase3c(self):
        S, sb = self.S, self.sb
        L = LCH
        NH = T_MAIN // L
        b_wt = Buf("ssm_wt2")
        KB = sb("KB", [128, L, 4, 128], BF16)
        VB = [sb("VB%d" % i, [128, L + 1, 16, 32], BF16) for i in range(2)]
        HB = [sb("HB%d" % i, [128, 16, NH], BF16) for i in range(2)]
        U = sb("Um", [128, 4, T_MAIN], BF16)
        for i in range(2):
            S.dma("sp", VB[i][:].rearrange("p e r c -> p (e r c)"), self.VBs.ap()[i], b_wt, w=[b_wt])
            S.dma("sp", HB[i][:], self.HBs.ap()[i], b_wt, w=[b_wt])
        S.dma("sp", KB[:].rearrange("p l c m -> p l (c m)"), self.KBs.ap(), b_wt, w=[b_wt])
        S.dma("sp", U[:], self.US.ap()[:, :, T_MAIN0:T_ALL].rearrange("c p t -> p c t"), b_wt, w=[b_wt])
        yf = sb("yf", [128, T_MAIN], F32)
        yt = sb("yt", [128, T_MAIN], F32)
        yb = sb("yb", [128, T_MAIN], BF16)
        b_yf, b_yt, b_yb = Buf("yf"), Buf("yt"), Buf("yb")
        for ct in range(4):
            for tau in range(L):
                pt, b_pt = self.next_ps()
                for lag in range(tau + 1):
                    S.op("pe", lambda e, pt=pt, tau=tau, lag=lag, ct=ct: e.matmul(
                        pt[:, 0:NH], KB[:, lag, ct, :],
                        U[:, ct, :].rearrange("p (n t c) -> p n t c", t=L, c=TN // L)[:, :, tau - lag, :],
                        start=(lag == 0), stop=False, skip_group_check=True), r=[b_wt], w=[b_pt], inc=False)
                for r4 in range(4):
                    pr = 4 * ct + r4
                    for ri in range(2):
                        last = (r4 == 3 and ri == 1)
                        S.op("pe", lambda e, pt=pt, tau=tau, r4=r4, pr=pr, ri=ri, last=last: e.matmul(
                            pt[32 * r4:32 * r4 + 32, 0:NH], VB[ri][:, tau + 1, pr, :], HB[ri][:, pr, :],
                            start=False, stop=last, skip_group_check=True, tile_position=(0, 32 * r4)),
                            r=[b_wt], w=[b_pt], inc=last)
                S.op("act", lambda e, pt=pt, tau=tau: e.activation(
                    yf[:, tau:tau + (NH - 1) * L + 1:L], pt[:, 0:NH], AF.Copy), r=[b_pt], w=[b_yf])
            self.gelu_tanh(yb[:, :], yf[:, :], yt[:, :], b_yf, b_yt, b_yb)
            S.dma("sp", self.YS.ap()[ct, :, :], yb[:, :], b_yb, r=[b_yb])

    def phase4(self):
        S, sb = self.S, self.sb
        stg = [sb("wstg%d" % i, [128, 1024], F32) for i in range(2)]
        b_stg = [Buf("wstg%d" % i) for i in range(2)]
        Wglu = sb("Wglu", [128, 4, 512], BF16)
        Wbs = sb("Wbs", [128, 4, 1024], BF16)
        Wout = sb("Wout", [128, 8, 1024], BF16)
        Wba = sb("Wba", [64, 4, 1024], BF16)
        b_W = Buf("W4")
        self.load_weight_bf16(Wglu, b_W, self.w_glu, 4, 512, None, stg, b_stg)
        self.load_weight_bf16(Wbs, b_W, self.w_bs, 4, 1024, None, stg, b_stg)
        self.load_weight_bf16(Wout, b_W, self.w_out, 8, 1024, None, stg, b_stg)
        for h in range(4):
            st_, bs_ = stg[h % 2], b_stg[h % 2]
            S.dma("sp", st_[0:64, :], self.w_ba.ap()[64 * h:64 * h + 64, :], bs_,
                  w=[bs_] + self._stg_prev.get(bs_.name + str(self.phn), []))
            S.op("dve", lambda e, st_=st_, h=h: e.tensor_copy(Wba[:, h, :], st_[0:64, :]), r=[bs_], w=[b_W])
        bg = sb("bg", [128, 4], F32)
        b_bg = Buf("bg")
        S.dma("sp", bg[:], dap(self.b_glu, 0, [[1, 128], [128, 4]]), b_bg, w=[b_bg], allow_slow_non_contiguous=True)
        AT = [sb("AT%d" % i, [64, 4, TN], BF16) for i in range(2)]
        YT = [sb("YT%d" % i, [128, 4, TN], BF16) for i in range(2)]
        G = [sb("G%d" % i, [128, 16, TN], BF16) for i in range(2)]
        X = [sb("X%d" % i, [128, 3, D], F32) for i in range(2)]
        b_in = [Buf("in4_%d" % i) for i in range(2)]
        SG = sb("SG", [128, 16, TN], BF16)
        b_SG = Buf("SG")
        sgl = sb("sgl", [128, TN], F32)
        b_sgl = Buf("sgl")
        so = sb("so", [128, 4, TN], BF16)
        b_so = Buf("so")
        mix = sb("mix", [128, 8, TN], BF16)
        b_mix = Buf("mix")
        ta = [sb("ta%d" % i, [128, TN], F32) for i in range(2)]
        b_ta = [Buf("ta%d" % i) for i in range(2)]
        X1 = [sb("X1_%d" % i, [128, 3, D], F32) for i in range(2)]
        b_X1 = [Buf("X1_%d" % i) for i in range(2)]

        def load_tile(k):
            i = k % 2
            m0 = k * TN
            S.dma("sp", AT[i][:], self.ATT.ap()[:, :, m0:m0 + TN].rearrange("h p t -> p h t"), b_in[i], w=[b_in[i]])
            S.dma("sp", YT[i][:], self.YS.ap()[:, :, m0:m0 + TN].rearrange("c p t -> p c t"), b_in[i], w=[b_in[i]])
            S.dma("sp", G[i][:], self.GS.ap()[:, :, m0:m0 + TN].rearrange("c p t -> p c t"), b_in[i], w=[b_in[i]])
            S.dma("sp", X[i][:], self.xall.ap()[T_MAIN0 + m0:T_MAIN0 + m0 + TN, :].rearrange("(s p) d -> p s d", p=128),
                  b_in[i], w=[b_in[i]])
        self.epsc = sb("epsc", [128, 1], F32)
        self.b_epsc = Buf("epsc")
        S.op("dve", lambda e: e.memset(self.epsc[:], EPS), w=[self.b_epsc])
        self.junk = sb("junk", [128, D], BF16)
        self.b_junk = Buf("junk")
        self.ss = sb("ss", [128, 4], F32)
        self.b_ss = Buf("ss")
        self.sq = sb("sq", [128, 4], F32)
        self.b_sq = Buf("sq")
        self.rstd = sb("rstd", [128, 4], F32)
        self.b_rstd = Buf("rstd")
        self.xs = sb("xs", [128, 3, D], BF16)
        self.b_xs = [Buf("xs%d" % i) for i in range(3)]
        XN2 = [sb("XN2_%d" % i, [128, 8, TN], BF16) for i in range(2)]
        b_XN2 = [Buf("XN2_%d" % i) for i in range(2)]

        def body(i, nt, nsub, rows, xo, b_xo):
            bi = b_in[i]
            S.op("act", lambda e, i=i: e.activation(SG[:, :, 0:nt], G[i][:, :, 0:nt], AF.Sigmoid), r=[bi], w=[b_SG])
            for mt in range(4):
                pt, b_pt = self.next_ps()
                for kc in range(4):
                    S.op("pe", lambda e, pt=pt, kc=kc, mt=mt, i=i: e.matmul(
                        pt[:, 0:nt], Wglu[:, kc, mt * 128:(mt + 1) * 128], YT[i][:, kc, 0:nt],
                        start=(kc == 0), stop=(kc == 3)), r=[b_W, bi], w=[b_pt], inc=(kc == 3))
                S.op("act", lambda e, pt=pt, mt=mt: e.activation(
                    sgl[:, 0:nt], pt[:, 0:nt], AF.Sigmoid, bias=bg[:, mt:mt + 1]), r=[b_pt, b_bg], w=[b_sgl])
                S.op("dve", lambda e, mt=mt, i=i: e.tensor_tensor(so[:, mt, 0:nt], YT[i][:, mt, 0:nt], sgl[:, 0:nt], ALU.mult),
                     r=[bi, b_sgl], w=[b_so])
            for mt in range(8):
                pa, b_pa = self.next_ps()
                for h in range(4):
                    S.op("pe", lambda e, pa=pa, h=h, mt=mt, i=i: e.matmul(
                        pa[:, 0:nt], Wba[:, h, mt * 128:(mt + 1) * 128], AT[i][:, h, 0:nt],
                        start=(h == 0), stop=(h == 3)), r=[b_W, bi], w=[b_pa], inc=(h == 3))
                pb, b_pb = self.next_ps()
                for kc in range(4):
                    S.op("pe", lambda e, pb=pb, kc=kc, mt=mt: e.matmul(
                        pb[:, 0:nt], Wbs[:, kc, mt * 128:(mt + 1) * 128], so[:, kc, 0:nt],
                        start=(kc == 0), stop=(kc == 3)), r=[b_W, b_so], w=[b_pb], inc=(kc == 3))
                t0_, t1_ = ta[0], ta[1]
                S.op("dve", lambda e, pa=pa, mt=mt: e.tensor_tensor(ta[0][:, 0:nt], pa[:, 0:nt], SG[:, mt, 0:nt], ALU.mult),
                     r=[b_pa, b_SG], w=[b_ta[0]])
                S.op("dve", lambda e, pb=pb, mt=mt: e.tensor_tensor(ta[1][:, 0:nt], pb[:, 0:nt], SG[:, 8 + mt, 0:nt], ALU.mult),
                     r=[b_pb, b_SG], w=[b_ta[1]])
                S.op("dve", lambda e, mt=mt: e.tensor_tensor(mix[:, mt, 0:nt], ta[0][:, 0:nt], ta[1][:, 0:nt], ALU.add),
                     r=[b_ta[0], b_ta[1]], w=[b_mix])
            for s_ in range(nsub):
                for nh in range(2):
                    pt, b_pt = self.next_ps()
                    for kc in range(8):
                        S.op("pe", lambda e, pt=pt, kc=kc, s_=s_, nh=nh: e.matmul(
                            pt[0:rows, 0:512], mix[:, kc, s_ * 128:s_ * 128 + rows], Wout[:, kc, nh * 512:(nh + 1) * 512],
                            start=(kc == 0), stop=(kc == 7)), r=[b_W, b_mix], w=[b_pt], inc=(kc == 7))
                    S.op("dve", lambda e, pt=pt, s_=s_, nh=nh, i=i, xo=xo: e.tensor_tensor(
                        xo[0:rows, s_, nh * 512:(nh + 1) * 512], X[i][0:rows, s_, nh * 512:(nh + 1) * 512], pt[0:rows, 0:512], ALU.add),
                        r=[b_pt, bi], w=[b_xo])

        NK = T_MAIN // TN
        load_tile(0)
        for k in range(NK):
            if k + 1 < NK:
                load_tile(k + 1)
            xo, b_xo = X1[k % 2], b_X1[k % 2]
            body(k % 2, TN, 3, 128, xo, b_xo)
            m0 = k * TN
            S.dma("sp", self.X1s.ap()[m0:m0 + TN, :].rearrange("(s p) d -> p s d", p=128), xo[:], b_xo, r=[b_xo])
            self.norm_transpose(xo, b_xo, 3, 128, XN2[k % 2], b_XN2[k % 2])
            S.dma("sp", self.XN2s.ap()[:, :, m0:m0 + TN].rearrange("k p t -> p k t"), XN2[k % 2][:, :, :],
                  b_XN2[k % 2], r=[b_XN2[k % 2]])
        i = NK % 2
        NS = 16
        S.dma("sp", AT[i][:, :, 0:NS], self.ATTs.ap().rearrange("h p t -> p h t"), b_in[i], w=[b_in[i]])
        S.dma("sp", YT[i][:, :, 0:NS], self.YSs.ap().rearrange("c p t -> p c t"), b_in[i], w=[b_in[i]])
        S.dma("sp", G[i][:, :, 0:NS], self.GSs.ap().rearrange("c p t -> p c t"), b_in[i], w=[b_in[i]])
        S.dma("sp", X[i][0:NS, 0, :], self.xsamp.ap(), b_in[i], w=[b_in[i]])
        xo, b_xo = X1[NK % 2], b_X1[NK % 2]
        body(i, NS, 1, NS, xo, b_xo)
        S.dma("sp", self.X1ss.ap(), xo[0:NS, 0, :], b_xo, r=[b_xo])
        self.norm_transpose(xo, b_xo, 1, NS, XN2[i], b_XN2[i])
        S.dma("sp", self.XN2ss.ap().rearrange("k p t -> p k t"), XN2[i][:, :, 0:NS], b_XN2[i], r=[b_XN2[i]])

    def phase5(self):
        S, sb = self.S, self.sb
        NM = DFF // 128
        g2c = sb("g2c", [128, 8], F32)
        b_c = Buf("c5")
        S.dma("sp", g2c[:], dap(self.norm2_g, 0, [[1, 128], [128, 8]]), b_c, w=[b_c], allow_slow_non_contiguous=True)
        self.epsc = sb("epsc", [128, 1], F32)
        self.b_epsc = Buf("epsc")
        S.op("dve", lambda e: e.memset(self.epsc[:], EPS), w=[self.b_epsc])
        cw = sb("cw", [128, 3, NM], F32)
        cb = sb("cb", [128, NM], F32)
        for i3 in range(3):
            S.dma("sp", cw[:, i3, :], dap(self.conv_w, i3 * DFF, [[1, 128], [128, NM]]), b_c, w=[b_c],
                  allow_slow_non_contiguous=True)
        S.dma("sp", cb[:], dap(self.conv_b, 0, [[1, 128], [128, NM]]), b_c, w=[b_c], allow_slow_non_contiguous=True)
        gfb = sb("gfb", [128, D], F32)
        S.dma("sp", gfb[:], dap(self.norm_f_g, 0, [[0, 128], [1, D]]), b_c, w=[b_c])
        stg = [sb("wstg%d" % i, [128, 704], F32) for i in range(2)]
        b_stg = [Buf("wstg%d" % i) for i in range(2)]
        Wup = sb("Wup", [128, 8, 2 * DFF], BF16)
        Wdn = sb("Wdn", [128, NM, D], BF16)
        b_Wup = [Buf("Wup%d" % i) for i in range(22)]
        b_Wdn = Buf("Wdn")
        self.load_weight_bf16(Wup, b_Wup, self.w_up, 8, 2 * DFF, (g2c, b_c), stg, b_stg, nsplit=22,
                              order=[v for i in range(11) for v in (i, 11 + i)])
        self.load_weight_bf16(Wdn, b_Wdn, self.w_down, NM, D, None, stg, b_stg, nsplit=4)
        self.junk = sb("junk", [128, D], BF16)
        self.b_junk = Buf("junk")
        X1 = [sb("X1_0", [128, 3, D], F32)] * 2
        b_X1 = [Buf("X1_0")] * 2
        xnTs = [sb("xnT%d" % i, [128, 8, TN], BF16) for i in range(2)]
        b_xnTs = [Buf("xnT%d" % i) for i in range(2)]
        carry = sb("carry", [128, NM, 2], F32)
        b_carry = Buf("carry")
        S.op("dve", lambda e: e.memset(carry[:], 0.0), w=[b_carry])
        ab = [sb("ab%d" % i, [128, TN + 2], F32) for i in range(2)]
        b_ab = [Buf("ab%d" % i) for i in range(2)]
        tc_ = [sb("tc%d" % i, [128, TN], F32) for i in range(2)]
        b_tc = [Buf("tc%d" % i) for i in range(2)]
        hT = sb("hT", [128, NM, TN], BF16)
        b_hT = Buf("hT")
        x2 = [sb("x2_0", [128, D], F32)] * 2
        b_x2 = [Buf("x2_0")] * 2
        yo = [sb("yo%d" % i, [128, D], F32) for i in range(2)]
        b_yo = [Buf("yo%d" % i) for i in range(2)]
        ss2 = sb("ss2", [128, 2], F32)
        sq2 = sb("sq2", [128, 2], F32)
        b_n2 = [Buf("n2_%d" % i) for i in range(2)]

        def load_tile(k):
            m0 = k * TN
            S.dma("sp", X1[k % 2][:], self.X1s.ap()[m0:m0 + TN, :].rearrange("(s p) d -> p s d", p=128),
                  b_X1[k % 2], w=[b_X1[k % 2]])

        def load_xn(k):
            m0 = k * TN
            S.dma("sp", xnTs[k % 2][:], self.XN2s.ap()[:, :, m0:m0 + TN].rearrange("k p t -> p k t"),
                  b_xnTs[k % 2], w=[b_xnTs[k % 2]])
        NK = T_MAIN // TN
        load_xn(0)
        cnt = 0
        ocnt = 0
        for k in range(NK):
            if k + 1 < NK:
                load_xn(k + 1)
            load_tile(k)
            xt, b_xt = X1[k % 2], b_X1[k % 2]
            xnT, b_xnT = xnTs[k % 2], b_xnTs[k % 2]
            for mt in range(NM):
                pa, b_pa = self.next_ps()
                pv, b_pv = self.next_ps()
                for (pp, bp, c0) in ((pa, b_pa, mt * 128), (pv, b_pv, DFF + mt * 128)):
                    for kc in range(8):
                        S.op("pe", lambda e, pp=pp, kc=kc, c0=c0, xnT=xnT: e.matmul(
                            pp[:, 0:TN], Wup[:, kc, c0:c0 + 128], xnT[:, kc, :], start=(kc == 0), stop=(kc == 7)),
                            r=[b_Wup[c0 // 256], b_xnT], w=[bp], inc=(kc == 7))
                a_, b_a = ab[cnt % 2], b_ab[cnt % 2]
                t_, b_t = tc_[cnt % 2], b_tc[cnt % 2]
                cnt += 1
                S.op("act", lambda e, a_=a_, mt=mt: e.activation(a_[:, 0:2], carry[:, mt, :], AF.Copy),
                     r=[b_carry], w=[b_a])
                S.op("act", lambda e, a_=a_, pa=pa: e.activation(a_[:, 2:TN + 2], pa[:, 0:TN], AF.Copy),
                     r=[b_pa], w=[b_a])
                S.op("dve", lambda e, a_=a_, t_=t_, mt=mt: e.tensor_scalar(
                    t_[:, :], a_[:, 2:TN + 2], cw[:, 2, mt:mt + 1], cb[:, mt:mt + 1], ALU.mult, ALU.add),
                    r=[b_a, b_c], w=[b_t])
                S.op("dve", lambda e, a_=a_, t_=t_, mt=mt: e.scalar_tensor_tensor(
                    t_[:, :], a_[:, 1:TN + 1], cw[:, 1, mt:mt + 1], t_[:, :], ALU.mult, ALU.add),
                    r=[b_a, b_c], w=[b_t])
                S.op("dve", lambda e, a_=a_, t_=t_, mt=mt: e.scalar_tensor_tensor(
                    t_[:, :], a_[:, 0:TN], cw[:, 0, mt:mt + 1], t_[:, :], ALU.mult, ALU.add),
                    r=[b_a, b_c], w=[b_t])
                S.op("dve", lambda e, a_=a_, mt=mt: e.tensor_copy(carry[:, mt, :], a_[:, TN:TN + 2]),
                     r=[b_a], w=[b_carry])
                S.op("act", lambda e, t_=t_: e.activation(t_[:, :], t_[:, :], AF.Silu), r=[b_t], w=[b_t])
                S.op("dve", lambda e, t_=t_, pv=pv, mt=mt: e.tensor_tensor(hT[:, mt, :], t_[:, :], pv[:, 0:TN], ALU.mult),
                     r=[b_t, b_pv], w=[b_hT])
            for s_ in range(3):
                j = ocnt % 2
                ocnt += 1
                for nh in range(2):
                    pt, b_pt = self.next_ps()
                    for kc in range(NM):
                        S.op("pe", lambda e, pt=pt, kc=kc, s_=s_, nh=nh: e.matmul(
                            pt[:, 0:512], hT[:, kc, s_ * 128:(s_ + 1) * 128], Wdn[:, kc, nh * 512:(nh + 1) * 512],
                            start=(kc == 0), stop=(kc == NM - 1)), r=[b_Wdn, b_hT], w=[b_pt], inc=(kc == NM - 1))
                    S.op("dve", lambda e, pt=pt, s_=s_, nh=nh, j=j, xt=xt: e.tensor_tensor(
                        x2[j][:, nh * 512:(nh + 1) * 512], xt[:, s_, nh * 512:(nh + 1) * 512], pt[:, 0:512], ALU.add),
                        r=[b_pt, b_xt], w=[b_x2[j]])
                tok = k * TN + s_ * 128
                if tok < 128:
                    continue
                S.op("act", lambda e, j=j: e.activation(self.junk[:, :], x2[j][:, :], AF.Square,
                                                        accum_out=ss2[:, j:j + 1]),
                     r=[b_x2[j]], w=[self.b_junk, b_n2[j]])
                S.op("act", lambda e, j=j: e.activation(sq2[:, j:j + 1], ss2[:, j:j + 1], AF.Sqrt,
                                                        bias=self.epsc[:, :], scale=1.0 / D),
                     r=[b_n2[j], self.b_epsc], w=[b_n2[j]])
                S.op("dve", lambda e, j=j: e.reciprocal(sq2[:, j:j + 1], sq2[:, j:j + 1]), r=[b_n2[j]], w=[b_n2[j]])
                S.op("dve", lambda e, j=j: e.scalar_tensor_tensor(
                    yo[j][:, :], x2[j][:, :], sq2[:, j:j + 1], gfb[:, :], ALU.mult, ALU.mult),
                    r=[b_x2[j], b_n2[j], b_c], w=[b_yo[j]])
                S.dma("sp", self.oy.ap()[tok - 128:tok, :], yo[j][:, :], b_yo[j], r=[b_yo[j]])
        for i2 in range(2):
            S.dma("sp", dap(self.ocv, i2 * DFF, [[1, 128], [128, NM]]), carry[:, :, i2], b_carry, r=[b_carry],
                  allow_slow_non_contiguous=True)
        NS = 16
        carS = sb("carS", [128, NM, 4, 2], F32)
        b_carS = Buf("carS")
        for s_ in range(4):
            for i2 in range(2):
                S.dma("sp", carS[:, :, s_, i2], dap(self.st_conv, (s_ * 2 + i2) * DFF, [[1, 128], [128, NM]]),
                      b_carS, w=[b_carS], allow_slow_non_contiguous=True)
        xt, b_xt = X1[0], b_X1[0]
        S.dma("sp", xt[0:NS, 0, :], self.X1ss.ap(), b_xt, w=[b_xt])
        xnT, b_xnT = xnTs[NK % 2], b_xnTs[NK % 2]
        S.dma("sp", xnT[:, :, 0:NS], self.XN2ss.ap().rearrange("k p t -> p k t"), b_xnT, w=[b_xnT])
        a3 = sb("a3", [128, 4, 6], F32)
        b_a3 = Buf("a3")
        for mt in range(NM):
            pa, b_pa = self.next_ps()
            pv, b_pv = self.next_ps()
            for (pp, bp, c0) in ((pa, b_pa, mt * 128), (pv, b_pv, DFF + mt * 128)):
                for kc in range(8):
                    S.op("pe", lambda e, pp=pp, kc=kc, c0=c0, xnT=xnT: e.matmul(
                        pp[:, 0:NS], Wup[:, kc, c0:c0 + 128], xnT[:, kc, 0:NS], start=(kc == 0), stop=(kc == 7)),
                        r=[b_Wup[c0 // 256], b_xnT], w=[bp], inc=(kc == 7))
            t_, b_t = tc_[mt % 2], b_tc[mt % 2]
            t3 = t_[:, 0:NS].rearrange("p (s t) -> p s t", s=4)
            S.op("act", lambda e, mt=mt: e.activation(a3[:, :, 0:2], carS[:, mt, :, :], AF.Copy), r=[b_carS], w=[b_a3])
            S.op("act", lambda e, pa=pa: e.activation(
                a3[:, :, 2:6], pa[:, 0:NS].rearrange("p (s t) -> p s t", s=4), AF.Copy), r=[b_pa], w=[b_a3])
            S.op("dve", lambda e, t3=t3, mt=mt: e.tensor_scalar(
                t3, a3[:, :, 2:6], cw[:, 2, mt:mt + 1], cb[:, mt:mt + 1], ALU.mult, ALU.add), r=[b_a3, b_c], w=[b_t])
            S.op("dve", lambda e, t3=t3, mt=mt: e.scalar_tensor_tensor(
                t3, a3[:, :, 1:5], cw[:, 1, mt:mt + 1], t3, ALU.mult, ALU.add), r=[b_a3, b_c], w=[b_t])
            S.op("dve", lambda e, t3=t3, mt=mt: e.scalar_tensor_tensor(
                t3, a3[:, :, 0:4], cw[:, 0, mt:mt + 1], t3, ALU.mult, ALU.add), r=[b_a3, b_c], w=[b_t])
            S.op("dve", lambda e, mt=mt: e.tensor_copy(carS[:, mt, :, :], a3[:, :, 4:6]), r=[b_a3], w=[b_carS])
            S.op("act", lambda e, t_=t_: e.activation(t_[:, 0:NS], t_[:, 0:NS], AF.Silu), r=[b_t], w=[b_t])
            S.op("dve", lambda e, t_=t_, pv=pv, mt=mt: e.tensor_tensor(hT[:, mt, 0:NS], t_[:, 0:NS], pv[:, 0:NS], ALU.mult),
                 r=[b_t, b_pv], w=[b_hT])
        for s_ in range(4):
            for i2 in range(2):
                S.dma("sp", dap(self.ocvs, (s_ * 2 + i2) * DFF, [[1, 128], [128, NM]]), carS[:, :, s_, i2],
                      b_carS, r=[b_carS], allow_slow_non_contiguous=True)
        j = 0
        for nh in range(2):
            pt, b_pt = self.next_ps()
            for kc in range(NM):
                S.op("pe", lambda e, pt=pt, kc=kc, nh=nh: e.matmul(
                    pt[0:NS, 0:512], hT[:, kc, 0:NS], Wdn[:, kc, nh * 512:(nh + 1) * 512],
                    start=(kc == 0), stop=(kc == NM - 1)), r=[b_Wdn, b_hT], w=[b_pt], inc=(kc == NM - 1))
            S.op("dve", lambda e, pt=pt, nh=nh: e.tensor_tensor(
                x2[j][0:NS, nh * 512:(nh + 1) * 512], xt[0:NS, 0, nh * 512:(nh + 1) * 512], pt[0:NS, 0:512], ALU.add),
                r=[b_pt, b_xt], w=[b_x2[j]])
        S.op("act", lambda e: e.activation(self.junk[0:NS, :], x2[j][0:NS, :], AF.Square, accum_out=ss2[0:NS, j:j + 1]),
             r=[b_x2[j]], w=[self.b_junk, b_n2[j]])
        S.op("act", lambda e: e.activation(sq2[0:NS, j:j + 1], ss2[0:NS, j:j + 1], AF.Sqrt,
                                           bias=self.epsc[0:NS, :], scale=1.0 / D),
             r=[b_n2[j], self.b_epsc], w=[b_n2[j]])
        S.op("dve", lambda e: e.reciprocal(sq2[0:NS, j:j + 1], sq2[0:NS, j:j + 1]), r=[b_n2[j]], w=[b_n2[j]])
        S.op("dve", lambda e: e.scalar_tensor_tensor(
            yo[j][0:NS, :], x2[j][0:NS, :], sq2[0:NS, j:j + 1], gfb[0:NS, :], ALU.mult, ALU.mult),
            r=[b_x2[j], b_n2[j], b_c], w=[b_yo[j]])
        S.dma("sp", self.oys.ap(), yo[j][0:NS, :], b_yo[j], r=[b_yo[j]])


_CACHE = {}


def get_prog(phases):
    key = tuple(sorted(phases))
    if key not in _CACHE:
        phases = set(phases)
        if 3 in phases:
            phases |= {31, 32, 33}
        p = Prog(phases)
        p.build()
        _CACHE[key] = p
    return _CACHE[key]


def _rel_bucket(dist):
    n = np.maximum(dist, 0)
    nf = np.maximum(n, 1).astype(np.float32)
    large = 16 + (np.log(nf / np.float32(16)) / np.float32(math.log(2048 / 16)) * np.float32(16)).astype(np.int32)
    large = np.minimum(large, 31)
    return np.where(n < 16, n, large)


def _bucket_onehot():
    oh = np.zeros((32, 3 * 129), np.float32)
    for g in range(3):
        b = _rel_bucket(np.arange(129) * DILS[g])
        oh[b, g * 129 + np.arange(129)] = 1.0
    return oh


def make_in_maps(inputs):
    xp = np.asarray(inputs["x_prompt"], dtype=np.float32)
    maps = []
    ident = np.eye(128, dtype=np.float32)
    boh = _bucket_onehot()
    for c in range(NCORES):
        b, half = c // 2, c % 2
        s0 = half * 4096
        lo = s0 - (T_ALL - 4096)
        xall = np.zeros((T_ALL, D), np.float32)
        src_lo = max(lo, 0)
        xall[src_lo - lo:, :] = xp[b, src_lo:s0 + 4096, :]
        m = {
            "xall": xall,
            "w_in": np.ascontiguousarray(inputs["w_in"][0]),
            "norm1_g": np.ascontiguousarray(inputs["norm1_g"][0]),
            "ident": ident,
            "st_conv": np.ascontiguousarray(inputs["state_ffn_conv"][0, 4 * c:4 * c + 4]),
            "st_re": np.ascontiguousarray(inputs["state_ssm_re"][0, 4 * c:4 * c + 4]),
            "st_im": np.ascontiguousarray(inputs["state_ssm_im"][0, 4 * c:4 * c + 4]),
            "cache0": np.ascontiguousarray(inputs["cache_kv_w128"][0, 4 * c:4 * c + 4].reshape(4, 128, 512)),
            "cache1": np.ascontiguousarray(inputs["cache_kv_w512"][0, 4 * c:4 * c + 4].reshape(4, 512, 512)),
            "cache2": np.ascontiguousarray(inputs["cache_kv_w2048"][0, 4 * c:4 * c + 4].reshape(4, 2048, 512)),
            "xsamp": np.ascontiguousarray(inputs["x_sample"][4 * c:4 * c + 4].reshape(16, D)),
            "rel_bias": np.ascontiguousarray(inputs["rel_bias"]),
            "bucket_oh": boh,
            "antiident": np.ascontiguousarray(np.eye(128, dtype=np.float32)[::-1]),
            "hv": np.full((128, 1), float(half), np.float32),
            "ssm_log_dt": np.ascontiguousarray(inputs["ssm_log_dt"][0]),
            "ssm_lambda_re": np.ascontiguousarray(inputs["ssm_lambda_re"][0]),
            "ssm_lambda_im": np.ascontiguousarray(inputs["ssm_lambda_im"][0]),
            "ssm_b_re": np.ascontiguousarray(inputs["ssm_b_re"][0]),
            "ssm_b_im": np.ascontiguousarray(inputs["ssm_b_im"][0]),
            "ssm_c_re": np.ascontiguousarray(inputs["ssm_c_re"][0]),
            "ssm_c_im": np.ascontiguousarray(inputs["ssm_c_im"][0]),
            "ssm_d": np.ascontiguousarray(inputs["ssm_d"][0]),
            "w_glu": np.ascontiguousarray(inputs["w_glu"][0]),
            "b_glu": np.ascontiguousarray(inputs["b_glu"][0]),
            "w_branch_attn": np.ascontiguousarray(inputs["w_branch_attn"][0]),
            "w_branch_ssm": np.ascontiguousarray(inputs["w_branch_ssm"][0]),
            "w_out": np.ascontiguousarray(inputs["w_out"][0]),
            "norm2_g": np.ascontiguousarray(inputs["norm2_g"][0]),
            "w_up": np.ascontiguousarray(inputs["w_up"][0]),
            "conv_w": np.ascontiguousarray(inputs["conv_w"][0]),
            "conv_b": np.ascontiguousarray(inputs["conv_b"][0]),
            "w_down": np.ascontiguousarray(inputs["w_down"][0]),
            "norm_f_g": np.ascontiguousarray(inputs["norm_f_g"]),
        }
        maps.append(m)
    return maps


def kernel(**inputs):
    prog = get_prog(_PHASES)
    maps = make_in_maps(inputs)
    maps = [{k: v for k, v in m.items() if k in prog.din} for m in maps]
    res = run_bass_kernel_spmd(prog.nc, maps, core_ids=list(range(NCORES)))
    R = res.results
    B = 4
    outs = [None] * 14
    for g in range(3):
        W = WINS[g]
        a = np.zeros((1, B, W, 2, 4, 64), np.float32)
        for b in range(B):
            a[0, b] = R[2 * b + 1]["okv%d" % g].reshape(W, 2, 4, 64)
        outs[2 + g] = a
    if "oy" in R[0]:
        y = np.zeros((B, 8192, D), np.float32)
        for c in range(NCORES):
            y[c // 2, (c % 2) * 4096:(c % 2 + 1) * 4096] = R[c]["oy"]
        outs[0] = y
        cv = np.zeros((1, B, 2, DFF), np.float32)
        for b in range(B):
            cv[0, b] = R[2 * b + 1]["ocv"]
        outs[7] = cv
    if "oys" in R[0]:
        outs[1] = np.concatenate([R[c]["oys"].reshape(4, 4, D) for c in range(NCORES)], axis=0)
        outs[13] = np.concatenate([R[c]["ocvs"] for c in range(NCORES)], axis=0)[None]
    if "okvs0" in R[0]:
        for g in range(3):
            outs[8 + g] = np.concatenate([R[c]["okvs%d" % g].reshape(4, 4, 2, 4, 64) for c in range(NCORES)], axis=0)[None]
    if "ossm_s_re" in R[0]:
        outs[11] = np.concatenate([R[c]["ossm_s_re"] for c in range(NCORES)], axis=0)[None]
        outs[12] = np.concatenate([R[c]["ossm_s_im"] for c in range(NCORES)], axis=0)[None]
    if "ossm_re" in R[0]:
        for i, nm in ((5, "ossm_re"), (6, "ossm_im")):
            a = np.zeros((1, B, 32, 64), np.float32)
            for b in range(B):
                a[0, b] = R[2 * b + 1][nm]
            outs[i] = a
    shapes = [(4, 8192, 1024), (32, 4, 1024), None, None, None, (1, 4, 32, 64), (1, 4, 32, 64),
              (1, 4, 2, DFF), (1, 32, 4, 2, 4, 64), (1, 32, 4, 2, 4, 64), (1, 32, 4, 2, 4, 64),
              (1, 32, 32, 64), (1, 32, 32, 64), (1, 32, 2, DFF)]
    for i in range(14):
        if outs[i] is None:
            outs[i] = np.zeros(shapes[i], np.float32)
    return tuple(outs)
```

```python
import math
import os
from contextlib import ExitStack

import numpy as np
import concourse.bass as bass
import concourse.mybir as mybir
from concourse.bass_utils import run_bass_kernel_spmd

F32 = mybir.dt.float32
BF16 = mybir.dt.bfloat16
AF = mybir.ActivationFunctionType
ALU = mybir.AluOpType
AX = mybir.AxisListType

NCORES = 8
D = 1024
INW = 4864
DFF = 2816
TN = 384
NT_ALL = 22
T_ALL = NT_ALL * TN
T_MAIN0 = 11 * TN
T_MAIN = 11 * TN
KV_T0 = 5 * TN
T_KV = T_ALL - KV_T0
EPS = 1e-6
WINS = (128, 512, 2048)
DILS = (1, 4, 16)
LCH = 16
_DBG = int(os.environ.get("K_DBG", "9"))
_DBG2 = int(os.environ.get("K_DBG2", "0"))
_PHASES = set(int(v) for v in os.environ.get("K_PH", "1,2,3,4,5").split(","))
_P3B = int(os.environ.get("K_P3B", "9"))
_ADEPTH = int(os.environ.get("K_ADEPTH", "2"))
_TR = tuple(int(v) for v in os.environ.get("K_TILES", "0,22").split(","))


class Sem:
    _n = 0

    def __init__(self, h):
        self.h = h
        Sem._n += 1
        self.id = Sem._n


class Buf:
    __slots__ = ("name", "w", "r", "sem", "semcnt", "excl")

    def __init__(self, name, excl=False):
        self.name = name
        self.excl = excl
        self.w = None
        self.r = []
        self.sem = None
        self.semcnt = 0


class Sched:
    def __init__(self, nc, es):
        self.nc = nc
        self.es = es
        self.engs = {"pe": nc.tensor, "act": nc.scalar, "dve": nc.vector,
                     "pool": nc.gpsimd, "sp": nc.sync}
        self.ops = {k: [] for k in self.engs}
        self.sem = {k: Sem(es.enter_context(nc.semaphore("sem_" + k)))
                    for k in ("pe", "act", "dve", "pool")}
        self.cnt = {k: 0 for k in self.sem}
        self.seen = {k: {} for k in self.engs}
        self.nsem = 4
        self.pending_noinc = {k: False for k in self.sem}
        self.free_sems = []
        self.dma_bufs = []

    def _need(self, eng, r, w):
        deps = []
        for b in r:
            if b.w is not None:
                deps.append(b.w)
        for b in w:
            if b.w is not None:
                deps.append(b.w)
            deps.extend(b.r)
        waits = {}
        for (s, v) in deps:
            if eng == "pe" and s is self.sem["pe"]:
                continue
            if waits.get(s, (None, 0))[1] < v:
                waits[s] = (s, v)
        need = []
        seen = self.seen[eng]
        for s, v in waits.values():
            if seen.get(s.id, 0) < v:
                seen[s.id] = v
                need.append((s.h, v))
        return need

    def _record(self, tk, r, w):
        for b in r:
            b.r.append(tk)
        for b in w:
            b.w = tk
            b.r = []

    def op(self, eng, fn, r=(), w=(), inc=True):
        if eng != "pe":
            ex = [b for b in r if b.excl]
            if ex:
                r = [b for b in r if not b.excl]
                w = list(w) + ex
        need = self._need(eng, r, w)
        s = self.sem[eng]
        if inc:
            self.cnt[eng] += 1
            tk = (s, self.cnt[eng])
            self.ops[eng].append((need, fn, s.h, 1))
            self.pending_noinc[eng] = False
        else:
            tk = (s, self.cnt[eng] + 1)
            self.ops[eng].append((need, fn, None, 0))
            self.pending_noinc[eng] = True
        self._record(tk, r, w)
        return tk

    def dma(self, q, out, in_, key, r=(), w=(), **kw):
        need = self._need(q, r, w)
        if key.sem is None:
            if self.free_sems:
                key.sem, key.semcnt = self.free_sems.pop()
            else:
                key.sem = Sem(self.es.enter_context(self.nc.semaphore("dsem_%d" % self.nsem)))
                self.nsem += 1
            self.dma_bufs.append(key)
        key.semcnt += 16
        tk = (key.sem, key.semcnt)
        self.ops[q].append((need, lambda e: e.dma_start(out=out, in_=in_, **kw), key.sem.h, 16))
        self._record(tk, r, w)
        return tk

    def drain_dmas(self, eng="sp"):
        need = []
        for b in self.dma_bufs:
            if self.seen[eng].get(b.sem.id, 0) < b.semcnt:
                self.seen[eng][b.sem.id] = b.semcnt
                need.append((b.sem.h, b.semcnt))
            self.free_sems.append((b.sem, b.semcnt))
            b.sem = None
        self.dma_bufs = []
        self.ops[eng].append((need, None, None, 0))

    def wait_all(self, eng, bufs):
        need = self._need(eng, (), bufs)
        self.ops[eng].append((need, None, None, 0))

    def emit(self, block):
        for k in self.pending_noinc:
            assert not self.pending_noinc[k], k

        def run(name):
            lst = self.ops[name]

            def f(e):
                for need, fn, sh, inc in lst:
                    for (h, v) in need:
                        e.wait_ge(h, v)
                    if fn is None:
                        continue
                    ins = fn(e)
                    if sh is not None:
                        ins.then_inc(sh, inc)
            return f
        block.sync(run("sp"))
        block.tensor(run("pe"))
        block.scalar(run("act"))
        block.vector(run("dve"))
        block.gpsimd(run("pool"))
        self.ops = {k: [] for k in self.engs}


def dap(t, off, dims):
    return bass.AP(t, off, [list(d) for d in dims])


class Prog:
    def __init__(self, phases):
        self.phases = phases
        self.nc = bass.Bass("TRN2", target_bir_lowering=False)
        self.es = ExitStack()
        self.S = None
        self.din = {}
        self.dout = {}
        self.outbufs = []
        self.phn = 0

    def inp(self, name, shape, dt=F32):
        t = self.nc.dram_tensor(name, list(shape), dt, kind="ExternalInput")
        self.din[name] = t
        return t

    def outp(self, name, shape, dt=F32):
        t = self.nc.dram_tensor(name, list(shape), dt, kind="ExternalOutput")
        self.dout[name] = t
        return t

    def scr(self, name, shape, dt):
        return self.nc.dram_tensor("d_" + name, list(shape), dt)

    def sb(self, name, shape, dt, glob=False):
        es = self.es if glob else self.pes
        return es.enter_context(self.nc.sbuf_tensor("s%d_%s" % (self.phn, name), list(shape), dt))

    def run_phase(self, fn):
        self.phn += 1
        with ExitStack() as pes:
            self.pes = pes
            fn()
            self.S.drain_dmas("sp")
            with self.nc.Block() as block:
                self.S.emit(block)
        self.pes = self.es

    def ps(self, name, shape, dt):
        return self.es.enter_context(self.nc.psum_tensor("p_" + name, list(shape), dt))

    def build(self):
        nc, es = self.nc, self.es
        with es:
            self._declare_io()
            self.S = Sched(nc, es)
            self.pes = es
            self._consts()
            for ph, fn in ((1, self.phase1), (2, self.phase2), (31, self.phase3a), (32, self.phase3b), (33, self.phase3c), (4, self.phase4), (5, self.phase5)):
                if ph in self.phases:
                    self.run_phase(fn)

            self.run_phase(lambda: None)
        return nc

    def _declare_io(self):
        self.xall = self.inp("xall", [T_ALL, D])
        self.w_in = self.inp("w_in", [D, INW])
        self.norm1_g = self.inp("norm1_g", [D])
        self.ident_in = self.inp("ident", [128, 128])
        self.okv = [self.outp("okv%d" % g, [WINS[g], 2, 256]) for g in range(3)]
        self.rel_bias = self.inp("rel_bias", [32, 12])
        self.bucket_oh = self.inp("bucket_oh", [32, 3 * 129])
        self.hv_in = self.inp("hv", [128, 1])
        self.antiident = self.inp("antiident", [128, 128])
        self.EXT = self.scr("EXT", [12, 385], F32)
        self.log_dt = self.inp("ssm_log_dt", [32])
        self.lam_re = self.inp("ssm_lambda_re", [32, 64])
        self.lam_im = self.inp("ssm_lambda_im", [32, 64])
        self.b_re = self.inp("ssm_b_re", [32, 64, 16])
        self.b_im = self.inp("ssm_b_im", [32, 64, 16])
        self.c_re = self.inp("ssm_c_re", [32, 16, 64])
        self.c_im = self.inp("ssm_c_im", [32, 16, 64])
        self.ssm_d = self.inp("ssm_d", [512])
        self.w_glu = self.inp("w_glu", [512, 512])
        self.b_glu = self.inp("b_glu", [512])
        self.w_ba = self.inp("w_branch_attn", [256, D])
        self.w_bs = self.inp("w_branch_ssm", [512, D])
        self.w_out = self.inp("w_out", [D, D])
        self.norm2_g = self.inp("norm2_g", [D])
        self.w_up = self.inp("w_up", [D, 2 * DFF])
        self.conv_w = self.inp("conv_w", [3, DFF])
        self.conv_b = self.inp("conv_b", [DFF])
        self.w_down = self.inp("w_down", [DFF, D])
        self.norm_f_g = self.inp("norm_f_g", [D])
        self.oy = self.outp("oy", [4096, D])
        self.ocv = self.outp("ocv", [2, DFF])
        if _DBG2:
            self.X1s = self.outp("X1s", [T_MAIN, D], F32)
        else:
            self.X1s = self.scr("X1s", [T_MAIN, D], F32)
        self.xsamp = self.inp("xsamp", [16, D])
        self.XN2s = self.scr("XN2s", [8, 128, T_MAIN], BF16)
        self.XN2ss = self.scr("XN2ss", [8, 128, 16], BF16)
        self.HBs = self.scr("HBs", [2, 128, 16, T_MAIN // LCH], BF16)
        self.st_conv = self.inp("st_conv", [4, 2, DFF])
        self.ocvs = self.outp("ocvs", [4, 2, DFF])
        self.oys = self.outp("oys", [16, D])
        self.X1ss = self.scr("X1ss", [16, D], F32)
        self.st_re = self.inp("st_re", [4, 32, 64])
        self.st_im = self.inp("st_im", [4, 32, 64])
        self.ossm_s_re = self.outp("ossm_s_re", [4, 32, 64])
        self.ossm_s_im = self.outp("ossm_s_im", [4, 32, 64])
        self.YSs = self.scr("YSs", [4, 128, 16], BF16)
        self.caches = [self.inp("cache%d" % g, [4, WINS[g], 512]) for g in range(3)]
        self.ATTs = self.scr("ATTs", [4, 64, 16], BF16)
        self.QTs = self.scr("QTs", [6, 128, 16], BF16)
        self.KTs = self.scr("KTs", [6, 128, 16], BF16)
        self.USs = self.scr("USs", [4, 128, 16], BF16)
        self.GSs = self.scr("GSs", [16, 128, 16], BF16)
        self.VSs = self.scr("VSs", [16, 768], BF16)
        self.okvs = [self.outp("okvs%d" % g, [16, 2, 256]) for g in range(3)]
        self.ossm_re = self.outp("ossm_re", [32, 64])
        self.ossm_im = self.outp("ossm_im", [32, 64])
        if _DBG2:
            self.YS = self.outp("YS", [4, 128, T_MAIN], BF16)
        else:
            self.YS = self.scr("YS", [4, 128, T_MAIN], BF16)
        self.VBs = self.scr("VBs", [2, 128, (LCH + 1) * 16 * 32], BF16)
        self.KBs = self.scr("KBs", [128, LCH, 512], BF16)
        self.WSs = self.scr("WSs", [2, 128, LCH, 512], BF16)
        if _DBG2:
            self.TAB = self.outp("TAB", [128, 4, 16], F32)
        else:
            self.TAB = self.scr("TAB", [128, 4, 16], F32)
        if _DBG2:
            self.ATT = self.outp("ATT", [4, 64, T_MAIN], BF16)
        else:
            self.ATT = self.scr("ATT", [4, 64, T_MAIN], BF16)
        self.QT = self.scr("QT", [6, 128, T_MAIN], BF16)
        self.KT = self.scr("KT", [6, 128, T_KV], BF16)
        self.VS = self.scr("VS", [T_KV, 768], BF16)
        self.US = self.scr("US", [4, 128, T_ALL], BF16)
        self.GS = self.scr("GS", [16, 128, T_MAIN], BF16)

    def _consts(self):
        S = self.S
        self.ident_f = self.sb("ident_f", [128, 128], F32, glob=True)
        self.ident = self.sb("ident", [128, 128], BF16, glob=True)
        self.b_ident = Buf("ident")
        S.dma("sp", self.ident_f[:], self.ident_in.ap(), self.b_ident, w=[self.b_ident])
        S.op("dve", lambda e: e.tensor_copy(self.ident[:], self.ident_f[:]),
             r=[self.b_ident], w=[self.b_ident])
        self.psb = [self.ps("psb%d" % i, [128, 512], F32) for i in range(6)]
        self.b_psb = [Buf("psb%d" % i, True) for i in range(6)]
        self.pst = [self.ps("pst%d" % i, [128, 1024], BF16) for i in range(2)]
        self.b_pst = [Buf("pst%d" % i, True) for i in range(2)]
        self.psi = 0

    def next_ps(self):
        i = self.psi % 6
        self.psi += 1
        return self.psb[i], self.b_psb[i]

    def load_weight_bf16(self, dst, dst_buf, src_dram, nk, ncols, gcol, stg, stg_bufs, nsplit=1, q="sp", row0=0):
        S = self.S
        cw = ncols // nsplit
        i = 0
        for kc in range(nk):
            for sp in range(nsplit):
                sbuf_t, sb_b = stg[i % 2], stg_bufs[i % 2]
                i += 1
                c0 = sp * cw
                src = src_dram.ap()[row0 + kc * 128:row0 + (kc + 1) * 128, c0:c0 + cw]
                S.dma(q, sbuf_t[:, 0:cw], src, sb_b, w=[sb_b])
                eng = "dve" if (i % 2 == 0) else "act"
                if gcol is not None:
                    gc, gb = gcol
                    if eng == "dve":
                        S.op("dve", lambda e, kc=kc, sbuf_t=sbuf_t, gc=gc, c0=c0: e.tensor_scalar(
                            dst[:, kc, c0:c0 + cw], sbuf_t[:, 0:cw], gc[:, kc:kc + 1], None, ALU.mult),
                            r=[sb_b, gb], w=[dst_buf])
                    else:
                        S.op("act", lambda e, kc=kc, sbuf_t=sbuf_t, gc=gc, c0=c0: e.activation(
                            dst[:, kc, c0:c0 + cw], sbuf_t[:, 0:cw], AF.Copy, scale=gc[:, kc:kc + 1]),
                            r=[sb_b, gb], w=[dst_buf])
                else:
                    if eng == "dve":
                        S.op("dve", lambda e, kc=kc, sbuf_t=sbuf_t, c0=c0: e.tensor_copy(
                            dst[:, kc, c0:c0 + cw], sbuf_t[:, 0:cw]), r=[sb_b], w=[dst_buf])
                    else:
                        S.op("act", lambda e, kc=kc, sbuf_t=sbuf_t, c0=c0: e.activation(
                            dst[:, kc, c0:c0 + cw], sbuf_t[:, 0:cw], AF.Copy), r=[sb_b], w=[dst_buf])

    def norm_transpose(self, xt, b_xt, nsub, rows, xnT, b_xnT, tok0=0):
        S = self.S
        ss, b_ss = self.ss, self.b_ss
        for s in range(nsub):
            S.op("act", lambda e, s=s: e.activation(
                self.junk[:rows, :], xt[:rows, s, :], AF.Square, accum_out=ss[:rows, s:s + 1]),
                r=[b_xt], w=[self.b_junk, b_ss])
        S.op("act", lambda e: e.activation(
            self.sq[:rows, 0:nsub], ss[:rows, 0:nsub], AF.Sqrt, bias=self.epsc[:rows, :], scale=1.0 / D),
            r=[b_ss, self.b_epsc], w=[self.b_sq])
        S.op("dve", lambda e: e.reciprocal(self.rstd[:rows, 0:nsub], self.sq[:rows, 0:nsub]),
             r=[self.b_sq], w=[self.b_rstd])
        for s in range(nsub):
            S.op("dve", lambda e, s=s: e.tensor_scalar(
                self.xs[:rows, s, :], xt[:rows, s, :], self.rstd[:rows, s:s + 1], None, ALU.mult),
                r=[b_xt, self.b_rstd], w=[self.b_xs[s]])
        for s in range(nsub):
            pt, b_pt = self.pst[s % 2], self.b_pst[s % 2]
            for kc in range(8):
                S.op("pe", lambda e, s=s, kc=kc, pt=pt: e.transpose(
                    pt[:, kc * 128:kc * 128 + rows], self.xs[:rows, s, kc * 128:(kc + 1) * 128],
                    self.ident[:rows, :rows]),
                    r=[self.b_xs[s], self.b_ident], w=[b_pt], inc=(kc == 7))
            S.op("act", lambda e, s=s, pt=pt: e.activation(
                xnT[:, :, tok0 + s * 128:tok0 + s * 128 + rows],
                pt[:, :].rearrange("p (k t) -> p k t", k=8)[:, :, 0:rows], AF.Copy),
                r=[b_pt], w=[b_xnT])

    def phase1(self):
        S = self.S
        sb = self.sb
        self.g1c = sb("g1c", [128, 8], F32)
        self.b_g1c = Buf("g1c")
        S.dma("sp", self.g1c[:], dap(self.norm1_g, 0, [[1, 128], [128, 8]]), self.b_g1c,
              w=[self.b_g1c], allow_slow_non_contiguous=True)
        self.epsc = sb("epsc", [128, 1], F32)
        self.b_epsc = Buf("epsc")
        S.op("dve", lambda e: e.memset(self.epsc[:], EPS), w=[self.b_epsc])
        Wi = sb("Wi", [128, 8, INW], BF16)
        b_Wi = Buf("Wi")
        stg = [sb("wstg%d" % i, [128, INW // 2], F32) for i in range(2)]
        b_stg = [Buf("wstg%d" % i) for i in range(2)]
        if _DBG >= 1:
            self.load_weight_bf16(Wi, b_Wi, self.w_in, 8, INW, (self.g1c, self.b_g1c), stg, b_stg, nsplit=2)
        self.junk = sb("junk", [128, D], BF16)
        self.b_junk = Buf("junk")
        self.ss = sb("ss", [128, 4], F32)
        self.b_ss = Buf("ss")
        self.sq = sb("sq", [128, 4], F32)
        self.b_sq = Buf("sq")
        self.rstd = sb("rstd", [128, 4], F32)
        self.b_rstd = Buf("rstd")
        self.xs = sb("xs", [128, 3, D], BF16)
        self.b_xs = [Buf("xs%d" % i) for i in range(3)]
        xt = [sb("xt%d" % i, [128, 3, D], F32) for i in range(2)]
        b_xt = [Buf("xt%d" % i) for i in range(2)]
        xnT = [sb("xnT%d" % i, [128, 8, TN], BF16) for i in range(2)]
        b_xnT = [Buf("xnT%d" % i) for i in range(2)]
        fst = [sb("fst%d" % i, [128, 4, TN], BF16) for i in range(2)]
        b_fst = [Buf("fst%d" % i) for i in range(2)]
        vst = [sb("vst%d" % i, [128, 768], BF16) for i in range(2)]
        b_vst = [Buf("vst%d" % i) for i in range(2)]
        kvst = [sb("kvst%d" % i, [128, 2, 768], F32) for i in range(2)]
        b_kvst = [Buf("kvst%d" % i) for i in range(2)]
        b_okv = [Buf("okv%d" % g) for g in range(3)]
        self.outbufs += b_kvst
        fcount = 0
        vcount = 0

        def load_x(ti):
            S.dma("sp", xt[ti % 2][:],
                  self.xall.ap()[ti * TN:(ti + 1) * TN, :].rearrange("(s p) d -> p s d", p=128),
                  b_xt[ti % 2], w=[b_xt[ti % 2]])

        load_x(_TR[0])
        if _TR[0] + 1 < _TR[1]:
            load_x(_TR[0] + 1)
        self.norm_transpose(xt[_TR[0] % 2], b_xt[_TR[0] % 2], 3, 128, xnT[_TR[0] % 2], b_xnT[_TR[0] % 2])
        for ti in range(_TR[0], _TR[1] if _DBG >= 2 else _TR[0]):
            xn, b_xn = xnT[ti % 2], b_xnT[ti % 2]
            did_next = False

            def prep_next(ti=ti):
                if ti + 1 < _TR[1]:
                    if ti + 2 < _TR[1]:
                        load_x(ti + 2)
                    self.norm_transpose(xt[(ti + 1) % 2], b_xt[(ti + 1) % 2], 3, 128,
                                        xnT[(ti + 1) % 2], b_xnT[(ti + 1) % 2])
            is_main = ti >= 11
            has_kv = ti >= 5
            if _DBG < 3:
                continue
            fm = []
            for m in range(4):
                fm.append((2304 + m * 128, self.US, m, ti * TN))
            if has_kv:
                for m in range(6):
                    fm.append((768 + m * 128, self.KT, m, ti * TN - KV_T0))
            if is_main:
                for m in range(6):
                    fm.append((m * 128, self.QT, m, ti * TN - T_MAIN0))
                for m in range(16):
                    fm.append((2816 + m * 128, self.GS, m, ti * TN - T_MAIN0))
            i = 0
            while i < len(fm):
                if not did_next and i >= len(fm) // 2:
                    prep_next()
                    did_next = True
                grp = [fm[i]]
                while len(grp) < 4 and i + len(grp) < len(fm) and fm[i + len(grp)][1] is grp[0][1]:
                    grp.append(fm[i + len(grp)])
                st, b_st = fst[fcount % 2], b_fst[fcount % 2]
                fcount += 1
                for j, (c0, dst, m, t0) in enumerate(grp):
                    pt, b_pt = self.next_ps()
                    for kc in range(8):
                        S.op("pe", lambda e, kc=kc, c0=c0, pt=pt, xn=xn: e.matmul(
                            pt[:, 0:TN], Wi[:, kc, c0:c0 + 128], xn[:, kc, :],
                            start=(kc == 0), stop=(kc == 7)),
                            r=[b_Wi, b_xn], w=[b_pt], inc=(kc == 7))
                    eng = "act" if (j % 2 == 0) else "dve"
                    if dst is self.US:
                        o_ap = st[:, j, :].rearrange("p (t c) -> p c t", t=LCH)
                        i_ap = pt[:, 0:TN].rearrange("p (c t) -> p c t", t=LCH)
                    else:
                        o_ap, i_ap = st[:, j, :], pt[:, 0:TN]
                    if eng == "act":
                        S.op("act", lambda e, o_ap=o_ap, i_ap=i_ap: e.activation(o_ap, i_ap, AF.Copy),
                             r=[b_pt], w=[b_st])
                    else:
                        S.op("dve", lambda e, o_ap=o_ap, i_ap=i_ap: e.tensor_copy(o_ap, i_ap),
                             r=[b_pt], w=[b_st])
                c0, dst, m0, t0 = grp[0]
                n = len(grp)
                if not (os.environ.get("K_NOGS") and (dst is self.GS or dst is self.QT)):
                    S.dma("sp", dst.ap()[m0:m0 + n, :, t0:t0 + TN].rearrange("m p t -> p m t"),
                          st[:, 0:n, :], b_st, r=[b_st])
                i += n
            if not did_next:
                prep_next()
                did_next = True
            if has_kv and _DBG >= 4:
                need_kout = (ti + 1) * TN > T_ALL - 2048
                for s in range(3):
                    tok = ti * TN + s * 128
                    kout = tok >= T_ALL - 2048
                    vt, b_vt = vst[vcount % 2], b_vst[vcount % 2]
                    kt, b_kt = kvst[vcount % 2], b_kvst[vcount % 2]
                    vcount += 1
                    for kv in ((0, 1) if kout else (1,)):
                        for (cc, nn) in ((0, 512), (512, 256)):
                            c0 = 768 * (1 + kv) + cc
                            pt, b_pt = self.next_ps()
                            for kc in range(8):
                                S.op("pe", lambda e, kc=kc, c0=c0, nn=nn, pt=pt, xn=xn, s=s: e.matmul(
                                    pt[:, 0:nn], xn[:, kc, s * 128:(s + 1) * 128], Wi[:, kc, c0:c0 + nn],
                                    start=(kc == 0), stop=(kc == 7)),
                                    r=[b_Wi, b_xn], w=[b_pt], inc=(kc == 7))
                            S.op("dve", lambda e, cc=cc, nn=nn, pt=pt, kt=kt, kv=kv: e.tensor_copy(
                                kt[:, kv, cc:cc + nn], pt[:, 0:nn]), r=[b_pt], w=[b_kt])
                    S.op("act", lambda e, kt=kt, vt=vt: e.activation(
                        vt[:, :], kt[:, 1, :], AF.Copy), r=[b_kt], w=[b_vt])
                    if _DBG >= 5:
                        S.dma("sp", self.VS.ap()[tok - KV_T0:tok - KV_T0 + 128, :], vt[:, :], b_vt, r=[b_vt])
                    if kout and _DBG >= 6:
                        for g in range(3):
                            w0 = T_ALL - WINS[g]
                            if tok >= w0:
                                S.dma("sp", self.okv[g].ap()[tok - w0:tok - w0 + 128, :, :],
                                      kt[:, :, 256 * g:256 * (g + 1)], b_kt, r=[b_kt])


        NS = 16
        xs_t = sb("xsamp", [128, 1, D], F32)
        b_xs_t = Buf("xsamp")
        S.dma("sp", xs_t[0:NS, 0, :], self.xsamp.ap(), b_xs_t, w=[b_xs_t])
        xnS = sb("xnS", [128, 8, NS], BF16)
        b_xnS = Buf("xnS")
        self.norm_transpose(xs_t, b_xs_t, 1, NS, xnS, b_xnS)
        fsS = sb("fsS", [128, 32, NS], BF16)
        b_fsS = Buf("fsS")
        fm = [(2304 + m * 128, self.USs, m) for m in range(4)] + [(768 + m * 128, self.KTs, m) for m in range(6)] \
            + [(m * 128, self.QTs, m) for m in range(6)] + [(2816 + m * 128, self.GSs, m) for m in range(16)]
        for j, (c0, dst, m) in enumerate(fm):
            pt, b_pt = self.next_ps()
            for kc in range(8):
                S.op("pe", lambda e, kc=kc, c0=c0, pt=pt: e.matmul(
                    pt[:, 0:NS], Wi[:, kc, c0:c0 + 128], xnS[:, kc, :], start=(kc == 0), stop=(kc == 7)),
                    r=[b_Wi, b_xnS], w=[b_pt], inc=(kc == 7))
            S.op("act", lambda e, j=j, pt=pt: e.activation(fsS[:, j, :], pt[:, 0:NS], AF.Copy), r=[b_pt], w=[b_fsS])
        for (j0, n, dst) in ((0, 4, self.USs), (4, 6, self.KTs), (10, 6, self.QTs), (16, 16, self.GSs)):
            S.dma("sp", dst.ap().rearrange("m p t -> p m t"), fsS[:, j0:j0 + n, :], b_fsS, r=[b_fsS])
        kvS = sb("kvS", [128, 2, 768], F32)
        vbS = sb("vbS", [128, 768], BF16)
        b_kvS = Buf("kvS")
        for kv in range(2):
            for (cc, nn) in ((0, 512), (512, 256)):
                c0 = 768 * (1 + kv) + cc
                pt, b_pt = self.next_ps()
                for kc in range(8):
                    S.op("pe", lambda e, kc=kc, c0=c0, nn=nn, pt=pt: e.matmul(
                        pt[0:NS, 0:nn], xnS[:, kc, :], Wi[:, kc, c0:c0 + nn], start=(kc == 0), stop=(kc == 7)),
                        r=[b_Wi, b_xnS], w=[b_pt], inc=(kc == 7))
                S.op("dve", lambda e, cc=cc, nn=nn, pt=pt, kv=kv: e.tensor_copy(
                    kvS[0:NS, kv, cc:cc + nn], pt[0:NS, 0:nn]), r=[b_pt], w=[b_kvS])
        S.op("act", lambda e: e.activation(vbS[0:NS, :], kvS[0:NS, 1, :], AF.Copy), r=[b_kvS], w=[b_kvS])
        S.dma("sp", self.VSs.ap(), vbS[0:NS, :], b_kvS, r=[b_kvS])
        for g in range(3):
            S.dma("sp", self.okvs[g].ap().rearrange("t k c -> t k c"), kvS[0:NS, :, 256 * g:256 * (g + 1)], b_kvS, r=[b_kvS])

    def build_expbias(self):
        S, sb = self.S, self.sb
        EB = sb("EB", [128, 12, 2, 128], F32)
        self.EB, self.b_EB = EB, Buf("EB")
        rb = sb("rb", [32, 12], F32)
        oh = sb("oh", [32, 3 * 129], F32)
        b_rb = Buf("rb")
        S.dma("sp", rb[:], self.rel_bias.ap(), b_rb, w=[b_rb])
        S.dma("sp", oh[:], self.bucket_oh.ap(), b_rb, w=[b_rb])
        ebx = sb("ebx", [12, 3, 129], F32)
        b_ebx = Buf("ebx")
        zt = sb("zt", [12, 385], F32)
        b_zt = Buf("zt")
        S.op("dve", lambda e: e.memset(zt[:], 0.0), w=[b_zt])
        b_ext = Buf("ext")
        S.dma("sp", self.EXT.ap(), zt[:], b_zt, r=[b_zt], w=[b_ext])
        pt, b_pt = self.next_ps()
        S.op("pe", lambda e: e.matmul(pt[0:12, 0:387], rb[:, :], oh[:, :], start=True, stop=True),
             r=[b_rb], w=[b_pt])
        S.op("act", lambda e: e.activation(ebx[:, :, :].rearrange("p g j -> p (g j)"), pt[0:12, 0:387], AF.Exp),
             r=[b_pt], w=[b_ebx])
        for g in range(3):
            S.dma("sp", self.EXT.ap()[4 * g:4 * g + 4, 128:257], ebx[4 * g:4 * g + 4, g, :], b_ebx,
                  r=[b_ebx], w=[b_ext])
        TH = sb("TH", [128, 12, 2, 128], F32)
        b_TH = Buf("TH")
        aid = sb("aid", [128, 128], F32)
        b_aid = Buf("aid")
        S.dma("sp", aid[:], self.antiident.ap(), b_aid, w=[b_aid])
        for gh in range(12):
            for bi in range(2):
                S.dma("sp", TH[:, gh, bi, :], dap(self.EXT, gh * 385 + 129 - 128 * bi, [[1, 128], [1, 128]]),
                      b_TH, r=[b_ext], w=[b_TH])
        for gh in range(12):
            pt, b_pt = self.next_ps()
            S.op("pe", lambda e, gh=gh, pt=pt: e.matmul(
                pt[:, 0:256], aid[:, :], TH[:, gh, :, :].rearrange("p b q -> p (b q)"), start=True, stop=True),
                r=[b_aid, b_TH], w=[b_pt])
            S.op("act", lambda e, gh=gh, pt=pt: e.activation(
                EB[:, gh, :, :].rearrange("p b q -> p (b q)"), pt[:, 0:256], AF.Copy),
                r=[b_pt], w=[self.b_EB])

    def attn_unit(self, kp_ap, kc_ap, q_ap, vp_ap, vc_ap, nq, gh, acc_ap, b_acc, first, rb):
        S = self.S
        ps_s, b_ps = self.next_ps()
        nb = len(self.Ebuf)
        E, b_E = self.Ebuf[self.ucnt % nb], self.b_Ebuf[self.ucnt % nb]
        P, b_P = self.Pbuf[self.ucnt % nb], self.b_Pbuf[self.ucnt % nb]
        self.ucnt += 1
        S.op("pe", lambda e: e.matmul(ps_s[:, 0:nq], kp_ap, q_ap, start=True, stop=True),
             r=rb, w=[b_ps], inc=False)
        S.op("pe", lambda e: e.matmul(ps_s[0:nq, 128:128 + nq], kc_ap, q_ap, start=True, stop=True),
             r=rb, w=[b_ps])
        psv = ps_s[:, 0:256].rearrange("p (b q) -> p b q", b=2)
        if nq == 128:
            S.op("act", lambda e: e.activation(E[:, :, :], psv, AF.Exp, scale=0.125), r=[b_ps], w=[b_E])
            S.op("dve", lambda e: e.tensor_tensor(P[:, :, :], E[:, :, :], self.EB[:, gh, :, :], ALU.mult),
                 r=[b_E, self.b_EB], w=[b_P])
        else:
            S.op("act", lambda e: e.activation(E[:, 0, 0:nq], ps_s[:, 0:nq], AF.Exp, scale=0.125),
                 r=[b_ps], w=[b_E])
            S.op("act", lambda e: e.activation(E[0:nq, 1, 0:nq], ps_s[0:nq, 128:128 + nq], AF.Exp, scale=0.125),
                 r=[b_ps], w=[b_E])
            S.op("dve", lambda e: e.tensor_tensor(P[:, 0, 0:nq], E[:, 0, 0:nq], self.EB[:, gh, 0, 0:nq], ALU.mult),
                 r=[b_E, self.b_EB], w=[b_P])
            S.op("dve", lambda e: e.tensor_tensor(P[0:nq, 1, 0:nq], E[0:nq, 1, 0:nq], self.EB[0:nq, gh, 1, 0:nq],
                                                  ALU.mult), r=[b_E, self.b_EB], w=[b_P])

        def stage_b():
            ps_o, b_po = self.next_ps()
            S.op("pe", lambda e: e.matmul(ps_o[0:65, 0:nq], vp_ap, P[:, 0, 0:nq], start=True, stop=False),
                 r=rb + [b_P], w=[b_po], inc=False)
            S.op("pe", lambda e: e.matmul(ps_o[0:65, 0:nq], vc_ap, P[0:nq, 1, 0:nq], start=False, stop=True),
                 r=rb + [b_P], w=[b_po])
            if first:
                S.op("dve", lambda e: e.tensor_copy(acc_ap, ps_o[0:65, 0:nq]), r=[b_po], w=[b_acc])
            else:
                S.op("dve", lambda e: e.tensor_tensor(acc_ap, acc_ap, ps_o[0:65, 0:nq], ALU.add),
                     r=[b_po], w=[b_acc])
        self.pending_b.append(stage_b)
        while len(self.pending_b) > self.attn_depth:
            self.pending_b.pop(0)()

    def attn_flush(self):
        while self.pending_b:
            self.pending_b.pop(0)()

    def sample_attention(self, sel, b_sel, rec, b_rec):
        S, sb = self.S, self.sb
        KnT = sb("KnT", [128, 6, 16], BF16)
        QnT = sb("QnT", [128, 6, 16], BF16)
        b_kq = Buf("knq")
        S.dma("sp", KnT[:], self.KTs.ap().rearrange("m p t -> p m t"), b_kq, w=[b_kq])
        S.dma("sp", QnT[:], self.QTs.ap().rearrange("m p t -> p m t"), b_kq, w=[b_kq])
        accS = sb("accS", [65, 4, 16], F32)
        b_accS = Buf("accS")
        CK = [sb("CK%d" % i, [128, 512], F32) for i in range(2)]
        Kb = [sb("Kb16_%d" % i, [128, 256], BF16) for i in range(2)]
        KcT = [sb("KcT%d" % i, [128, 2, 128], BF16) for i in range(2)]
        VcP = [sb("VcP%d" % i, [128, 4, 65], BF16) for i in range(2)]
        VnC = [sb("VnC%d" % i, [4, 4, 65], BF16) for i in range(2)]
        b_CK = [Buf("CK%d" % i) for i in range(2)]
        b_Kb = [Buf("Kb16_%d" % i) for i in range(2)]
        b_KcT = [Buf("KcT%d" % i) for i in range(2)]
        b_VcP = [Buf("VcP%d" % i) for i in range(2)]
        b_VnC = [Buf("VnC%d" % i) for i in range(2)]
        for i in range(2):
            S.op("pool", lambda e, i=i: e.memset(VcP[i][:, :, 64:65], 1.0), w=[b_VcP[i]])
            S.op("pool", lambda e, i=i: e.memset(VnC[i][:, :, 64:65], 1.0), w=[b_VnC[i]])
        bi = 0
        for s_ in range(4):
            for g in range(3):
                d, W = DILS[g], WINS[g]
                blocks = [(0, 4)] if g == 0 else [(t, 1) for t in range(4)]
                for (t0, nq) in blocks:
                    i = bi % 2
                    bi += 1
                    row0 = 0 if g == 0 else t0
                    S.dma("sp", CK[i][:, :], dap(self.caches[g], (s_ * W + row0) * 512, [[d * 512, 128], [1, 512]]),
                          b_CK[i], w=[b_CK[i]])
                    S.op("dve", lambda e, i=i: e.tensor_copy(Kb[i][:, :], CK[i][:, 0:256]), r=[b_CK[i]], w=[b_Kb[i]])
                    S.op("pool", lambda e, i=i: e.tensor_copy(
                        VcP[i][:, :, 0:64], CK[i][:, 256:512].rearrange("p (h e) -> p h e", h=4)),
                        r=[b_CK[i]], w=[b_VcP[i]])
                    pt, b_pt = self.pst[i], self.b_pst[i]
                    for pair in range(2):
                        S.op("pe", lambda e, pt=pt, pair=pair, i=i: e.transpose(
                            pt[:, pair * 128:(pair + 1) * 128], Kb[i][:, pair * 128:(pair + 1) * 128], self.ident[:, :]),
                            r=[b_Kb[i], self.b_ident], w=[b_pt], inc=(pair == 1))
                    S.op("act", lambda e, pt=pt, i=i: e.activation(
                        KcT[i][:, :, :], pt[:, 0:256].rearrange("p (a k) -> p a k", a=2), AF.Copy),
                        r=[b_pt], w=[b_KcT[i]])
                    tk0 = 4 * s_ + t0
                    S.dma("sp", VnC[i][0:nq, :, 0:64],
                          dap(self.VSs, tk0 * 768 + 256 * g, [[768, nq], [64, 4], [1, 64]]), b_VnC[i], w=[b_VnC[i]])
                    for h in range(4):
                        pair, hh = h // 2, h % 2
                        rw = slice(64 * hh, 64 * hh + 64)
                        self.attn_unit(KcT[i][rw, pair, :], KnT[rw, 2 * g + pair, tk0:tk0 + nq],
                                       QnT[rw, 2 * g + pair, tk0:tk0 + nq], VcP[i][:, h, :], VnC[i][0:nq, h, :],
                                       nq, 4 * g + h, accS[:, h, tk0:tk0 + nq], b_accS, g == 0,
                                       [b_KcT[i], b_kq, b_VcP[i], b_VnC[i]])
        self.attn_flush()
        aS = sb("aS", [64, 4, 16], BF16)
        b_aS = Buf("aS")
        for h in range(4):
            pt, b_pt = self.next_ps()
            S.op("pe", lambda e, pt=pt, h=h: e.matmul(pt[0:64, 0:16], sel[:, :], accS[:, h, :], start=True, stop=True),
                 r=[b_sel, b_accS], w=[b_pt])
            S.op("dve", lambda e, pt=pt: e.tensor_scalar(rec[:, 0:16], pt[0:64, 0:16], 1e-30, None, ALU.max),
                 r=[b_pt], w=[b_rec])
            S.op("dve", lambda e: e.reciprocal(rec[:, 0:16], rec[:, 0:16]), r=[b_rec], w=[b_rec])
            S.op("dve", lambda e, h=h: e.tensor_tensor(aS[:, h, :], accS[0:64, h, :], rec[:, 0:16], ALU.mult),
                 r=[b_accS, b_rec], w=[b_aS])
        S.dma("sp", self.ATTs.ap().rearrange("h p t -> p h t"), aS[:, :, :], b_aS, r=[b_aS])


    def phase2(self):
        S, sb = self.S, self.sb
        self.build_expbias()
        ast = [sb("ast%d" % i, [64, 512], BF16) for i in range(2)]
        b_ast = [Buf("ast%d" % i) for i in range(2)]
        acnt = 0
        hv = sb("hv", [128, 1], F32)
        b_hv = Buf("hv")
        S.dma("sp", hv[:], self.hv_in.ap(), b_hv, w=[b_hv])
        sel = sb("sel", [65, 64], F32)
        b_sel = Buf("sel")
        S.op("dve", lambda e: e.memset(sel[:], 0.0), w=[b_sel])
        S.op("dve", lambda e: e.memset(sel[64:65, :], 1.0), w=[b_sel])
        NEP = _ADEPTH + 1
        self.Ebuf = [sb("E%d" % i, [128, 2, 128], F32) for i in range(NEP)]
        self.b_Ebuf = [Buf("E%d" % i) for i in range(NEP)]
        self.Pbuf = [sb("P%d" % i, [128, 2, 128], BF16) for i in range(NEP)]
        self.b_Pbuf = [Buf("P%d" % i) for i in range(NEP)]
        self.ucnt = 0
        self.pending_b = []
        self.attn_depth = _ADEPTH
        acc = sb("acc", [65, 2, T_MAIN], F32)
        b_acc = Buf("acc")
        NBLK = 3 * 16 + 2 * 16
        Kb = [sb("Kb%d" % i, [128, T_KV], BF16) for i in range(2)]
        Qb = [sb("Qb%d" % i, [128, T_MAIN], BF16) for i in range(2)]
        Vb = [sb("Vb%d" % i, [128, NBLK, 2, 65], BF16) for i in range(2)]
        b_Kb = [Buf("Kb%d" % i) for i in range(2)]
        b_Qb = [Buf("Qb%d" % i) for i in range(2)]
        b_Vb = [Buf("Vb%d" % i) for i in range(2)]
        for i in range(2):
            S.op("pool", lambda e, i=i: e.memset(Vb[i][:, :, :, :], 0.0), w=[b_Vb[i]])
        rec = sb("rec", [64, 512], F32)
        b_rec = Buf("rec")
        H0 = T_MAIN0 + 128
        li = 0
        for pair in range(2):
            for g in range(3):
                d = DILS[g]
                NB = 4096 // (128 * d)
                nqh = 128 // d
                K_, Q_, V_ = Kb[li % 2], Qb[li % 2], Vb[li % 2]
                bK, bQ, bV = b_Kb[li % 2], b_Qb[li % 2], b_Vb[li % 2]
                li += 1
                mt = 2 * g + pair
                S.dma("sp", K_[:, :], self.KT.ap()[mt, :, :], bK, w=[bK])
                S.dma("sp", Q_[:, :], self.QT.ap()[mt, :, :], bQ, w=[bQ])
                nblk = 3 * d + NB * d
                S.op("pool", lambda e, V_=V_, nblk=nblk: e.memset(V_[:, 0:nblk, :, 64:65], 1.0), w=[bV])
                colb = 256 * g + 128 * pair
                for ty, (lt0, npart) in enumerate(((H0 - 128 * d, 128), (T_MAIN0 - 128 * d, 128), (T_MAIN0, nqh))):
                    S.dma("sp", V_[0:npart, ty * d:(ty + 1) * d, :, 0:64],
                          dap(self.VS, (lt0 - KV_T0) * 768 + colb, [[d * 768, npart], [768, d], [64, 2], [1, 64]]),
                          bV, w=[bV])
                for n in range(NB):
                    S.dma("sp", V_[:, 3 * d + n * d:3 * d + (n + 1) * d, :, 0:64],
                          dap(self.VS, (H0 + 128 * n * d - KV_T0) * 768 + colb,
                              [[d * 768, 128], [768, d], [64, 2], [1, 64]]), bV, w=[bV])
                S.op("dve", lambda e, V_=V_, d=d: e.tensor_scalar(
                    V_[:, 0:3 * d, :, :], V_[:, 0:3 * d, :, :], hv[:, 0:1], None, ALU.mult),
                    r=[b_hv], w=[bV])
                for hh in range(2):
                    gh = 4 * g + 2 * pair + hh
                    rw = slice(64 * hh, 64 * hh + 64)
                    rb = [bK, bQ, bV]

                    def cs(st, n, d=d):
                        return slice(st, st + (n - 1) * d + 1, d)
                    for r in range(d):
                        self.attn_unit(K_[rw, cs(T_MAIN0 - 128 * d + r - KV_T0, 128)],
                                       K_[rw, cs(T_MAIN0 + r - KV_T0, nqh)], Q_[rw, cs(r, nqh)],
                                       V_[:, 1 * d + r, hh, :], V_[0:nqh, 2 * d + r, hh, :], nqh, gh,
                                       acc[:, hh, cs(r, nqh)], b_acc, g == 0, rb)
                        for n in range(NB):
                            kp = H0 + 128 * (n - 1) * d + r - KV_T0
                            kc = H0 + 128 * n * d + r - KV_T0
                            q0 = 128 + 128 * n * d + r
                            vp = (0 * d + r) if n == 0 else (3 * d + (n - 1) * d + r)
                            vc = 3 * d + n * d + r
                            self.attn_unit(K_[rw, cs(kp, 128)], K_[rw, cs(kc, 128)], Q_[rw, cs(q0, 128)],
                                           V_[:, vp, hh, :], V_[:, vc, hh, :], 128, gh,
                                           acc[:, hh, cs(q0, 128)], b_acc, g == 0, rb)
            self.attn_flush()
            for hh in range(2):
                h = 2 * pair + hh
                for c0 in range(0, T_MAIN, 512):
                    n = min(512, T_MAIN - c0)
                    pt, b_pt = self.next_ps()
                    S.op("pe", lambda e, pt=pt, hh=hh, c0=c0, n=n: e.matmul(
                        pt[0:64, 0:n], sel[:, :], acc[:, hh, c0:c0 + n], start=True, stop=True),
                        r=[b_sel, b_acc], w=[b_pt])
                    S.op("dve", lambda e, pt=pt, n=n: e.tensor_scalar(
                        rec[:, 0:n], pt[0:64, 0:n], 1e-18, None, ALU.max), r=[b_pt], w=[b_rec])
                    S.op("act", lambda e, n=n: e.activation(rec[:, 0:n], rec[:, 0:n], AF.Ln), r=[b_rec], w=[b_rec])
                    S.op("act", lambda e, n=n: e.activation(rec[:, 0:n], rec[:, 0:n], AF.Exp, scale=-1.0),
                         r=[b_rec], w=[b_rec])
                    a_, b_a = ast[acnt % 2], b_ast[acnt % 2]
                    acnt += 1
                    S.op("dve", lambda e, a_=a_, hh=hh, c0=c0, n=n: e.tensor_tensor(
                        a_[:, 0:n], acc[0:64, hh, c0:c0 + n], rec[:, 0:n], ALU.mult),
                        r=[b_acc, b_rec], w=[b_a])
                    S.dma("sp", self.ATT.ap()[h, :, c0:c0 + n], a_[:, 0:n], b_a, r=[b_a])
        self.sample_attention(sel, b_sel, rec, b_rec)


    def cmul(self, eng, out_r, out_i, ar, ai, br, bi, t, bufs_r, bufs_w):
        S = self.S
        t1, t2 = t
        S.op(eng, lambda e: e.tensor_tensor(t1, ar, br, ALU.mult), r=bufs_r, w=[self.b_ct])
        S.op(eng, lambda e: e.tensor_tensor(t2, ai, bi, ALU.mult), r=bufs_r, w=[self.b_ct])
        S.op(eng, lambda e: e.tensor_tensor(out_r, t1, t2, ALU.subtract), r=[self.b_ct], w=bufs_w)
        S.op(eng, lambda e: e.tensor_tensor(t1, ar, bi, ALU.mult), r=bufs_r + bufs_w, w=[self.b_ct])
        S.op(eng, lambda e: e.tensor_tensor(t2, ai, br, ALU.mult), r=bufs_r, w=[self.b_ct])
        S.op(eng, lambda e: e.tensor_tensor(out_i, t1, t2, ALU.add), r=[self.b_ct], w=bufs_w)

    def phase3a(self):
        S, sb = self.S, self.sb
        L = LCH
        b_in = Buf("ssm_in")
        LR = sb("LR", [128, 16], F32)
        LI = sb("LI", [128, 16], F32)
        LDT = sb("LDT", [128, 16], F32)
        BR = sb("BR", [128, 16, 16], F32)
        BI = sb("BI", [128, 16, 16], F32)
        CR = sb("CR", [128, 16, 16], F32)
        CI = sb("CI", [128, 16, 16], F32)
        Dc = sb("Dc", [128, 4], F32)
        S.dma("sp", LR[:], dap(self.lam_re, 0, [[1, 128], [128, 16]]), b_in, w=[b_in], allow_slow_non_contiguous=True)
        S.dma("sp", LI[:], dap(self.lam_im, 0, [[1, 128], [128, 16]]), b_in, w=[b_in], allow_slow_non_contiguous=True)
        for j in range(2):
            S.dma("sp", LDT[64 * j:64 * j + 64, :], dap(self.log_dt, j, [[0, 64], [2, 16]]), b_in, w=[b_in],
                  allow_slow_non_contiguous=True)
        S.dma("sp", BR[:], dap(self.b_re, 0, [[16, 128], [2048, 16], [1, 16]]), b_in, w=[b_in])
        S.dma("sp", BI[:], dap(self.b_im, 0, [[16, 128], [2048, 16], [1, 16]]), b_in, w=[b_in])
        b_cn = Buf("Cnat")
        for nm, src, dstC in (("r", self.c_re, CR), ("i", self.c_im, CI)):
            CTf = sb("CTf" + nm, [16, 32, 64], F32)
            CTb = sb("CTb" + nm, [16, 32, 64], BF16)
            S.dma("sp", CTf[:], dap(src, 0, [[64, 16], [1024, 32], [1, 64]]), b_cn, w=[b_cn])
            S.op("dve", lambda e, CTf=CTf, CTb=CTb: e.tensor_copy(CTb[:], CTf[:]), r=[b_cn], w=[b_cn])
            pt, b_pt = self.next_ps()
            for pr in range(16):
                for j in range(2):
                    S.op("pe", lambda e, pt=pt, pr=pr, j=j, CTb=CTb: e.matmul(
                        pt[64 * j:64 * j + 64, 16 * pr:16 * pr + 16], CTb[0:16, 2 * pr + j, :],
                        self.ident[0:16, 0:16], start=True, stop=True),
                        r=[b_cn, self.b_ident], w=[b_pt], inc=(pr == 15 and j == 1))
            S.op("act", lambda e, pt=pt, dstC=dstC: e.activation(
                dstC[:].rearrange("p r c -> p (r c)"), pt[:, 0:256], AF.Copy), r=[b_pt], w=[b_in])
        S.dma("sp", Dc[:], dap(self.ssm_d, 0, [[1, 128], [128, 4]]), b_in, w=[b_in], allow_slow_non_contiguous=True)
        halfpi = sb("halfpi", [128, 1], F32)
        b_w = Buf("ssm_work")
        self.b_ct = Buf("ct")
        S.op("dve", lambda e: e.memset(halfpi[:], math.pi / 2), w=[b_w])
        dt = sb("dt", [128, 16], F32)
        S.op("act", lambda e: e.activation(dt[:], LDT[:], AF.Exp), r=[b_in], w=[b_w])
        t1 = sb("t1", [128, 16], F32)
        t2 = sb("t2", [128, 16], F32)
        t3 = sb("t3", [128, 16], F32)
        ar = sb("ar", [128, 16], F32)
        ai = sb("ai", [128, 16], F32)
        wr = sb("wr", [128, 16], F32)
        wi = sb("wi", [128, 16], F32)
        zr = sb("zr", [128, 16], F32)
        zi = sb("zi", [128, 16], F32)
        pr_ = sb("pr_", [128, 16], F32)
        pi_ = sb("pi_", [128, 16], F32)
        qr_ = sb("qr_", [128, 16], F32)
        qi_ = sb("qi_", [128, 16], F32)
        MSQ = 8
        S.op("dve", lambda e: e.scalar_tensor_tensor(zr[:], LR[:], 1.0 / (1 << MSQ), dt[:], ALU.mult, ALU.mult),
             r=[b_in, b_w], w=[b_w])
        S.op("dve", lambda e: e.scalar_tensor_tensor(zi[:], LI[:], 1.0 / (1 << MSQ), dt[:], ALU.mult, ALU.mult),
             r=[b_in, b_w], w=[b_w])
        S.op("dve", lambda e: e.tensor_scalar(pr_[:], zr[:], 1.0 / 5, 1.0, ALU.mult, ALU.add), r=[b_w], w=[b_w])
        S.op("dve", lambda e: e.tensor_scalar(pi_[:], zi[:], 1.0 / 5, None, ALU.mult), r=[b_w], w=[b_w])
        for dv in (4.0, 3.0, 2.0):
            self.cmul("dve", qr_[:], qi_[:], zr[:], zi[:], pr_[:], pi_[:], (t1[:], t2[:]), [b_w], [b_w])
            S.op("dve", lambda e, dv=dv: e.tensor_scalar(pr_[:], qr_[:], 1.0 / dv, 1.0, ALU.mult, ALU.add),
                 r=[b_w], w=[b_w])
            S.op("dve", lambda e, dv=dv: e.tensor_scalar(pi_[:], qi_[:], 1.0 / dv, None, ALU.mult), r=[b_w], w=[b_w])
        self.cmul("dve", wr[:], wi[:], zr[:], zi[:], pr_[:], pi_[:], (t1[:], t2[:]), [b_w], [b_w])
        for _ in range(MSQ):
            S.op("dve", lambda e: e.tensor_tensor(t1[:], wr[:], wr[:], ALU.mult), r=[b_w], w=[self.b_ct])
            S.op("dve", lambda e: e.tensor_tensor(t2[:], wi[:], wi[:], ALU.mult), r=[b_w], w=[self.b_ct])
            S.op("dve", lambda e: e.tensor_tensor(t3[:], wr[:], wi[:], ALU.mult), r=[b_w], w=[self.b_ct])
            S.op("dve", lambda e: e.tensor_tensor(t1[:], t1[:], t2[:], ALU.subtract), r=[self.b_ct], w=[self.b_ct])
            S.op("dve", lambda e: e.tensor_tensor(t3[:], t3[:], wi[:], ALU.add), r=[self.b_ct, b_w], w=[self.b_ct])
            S.op("dve", lambda e: e.scalar_tensor_tensor(wr[:], wr[:], 2.0, t1[:], ALU.mult, ALU.add),
                 r=[self.b_ct, b_w], w=[b_w])
            S.op("dve", lambda e: e.tensor_scalar(wi[:], t3[:], 2.0, None, ALU.mult), r=[self.b_ct], w=[b_w])
        S.op("dve", lambda e: e.tensor_scalar(ar[:], wr[:], 1.0, None, ALU.add), r=[b_w], w=[b_w])
        S.op("dve", lambda e: e.tensor_copy(ai[:], wi[:]), r=[b_w], w=[b_w])
        APr = sb("APr", [128, L + 1, 16], F32)
        APi = sb("APi", [128, L + 1, 16], F32)
        b_ap = Buf("AP")
        S.op("dve", lambda e: e.memset(APr[:, 0, :], 1.0), w=[b_ap])
        S.op("dve", lambda e: e.memset(APi[:, 0, :], 0.0), w=[b_ap])
        for ee in range(1, L + 1):
            self.cmul("dve", APr[:, ee, :], APi[:, ee, :], APr[:, ee - 1, :], APi[:, ee - 1, :], ar[:], ai[:],
                      (t1[:], t2[:]), [b_ap, b_w], [b_ap])
        nr = sb("nr", [128, 16], F32)
        cr = sb("cr", [128, 16], F32)
        ci = sb("ci", [128, 16], F32)
        S.op("dve", lambda e: e.tensor_copy(nr[:], wr[:]), r=[b_w], w=[b_w])
        S.op("dve", lambda e: e.tensor_tensor(t1[:], LR[:], LR[:], ALU.mult), r=[b_in], w=[self.b_ct])
        S.op("dve", lambda e: e.tensor_tensor(t2[:], LI[:], LI[:], ALU.mult), r=[b_in], w=[self.b_ct])
        S.op("dve", lambda e: e.tensor_tensor(t1[:], t1[:], t2[:], ALU.add), r=[self.b_ct], w=[self.b_ct])
        S.op("dve", lambda e: e.reciprocal(t3[:], t1[:]), r=[self.b_ct], w=[self.b_ct])
        S.op("dve", lambda e: e.tensor_tensor(t1[:], nr[:], LR[:], ALU.mult), r=[b_w, b_in], w=[self.b_ct])
        S.op("dve", lambda e: e.tensor_tensor(t2[:], ai[:], LI[:], ALU.mult), r=[b_w, b_in], w=[self.b_ct])
        S.op("dve", lambda e: e.tensor_tensor(t1[:], t1[:], t2[:], ALU.add), r=[self.b_ct], w=[self.b_ct])
        S.op("dve", lambda e: e.tensor_tensor(cr[:], t1[:], t3[:], ALU.mult), r=[self.b_ct], w=[b_w])
        S.op("dve", lambda e: e.tensor_tensor(t1[:], ai[:], LR[:], ALU.mult), r=[b_w, b_in], w=[self.b_ct])
        S.op("dve", lambda e: e.tensor_tensor(t2[:], nr[:], LI[:], ALU.mult), r=[b_w, b_in], w=[self.b_ct])
        S.op("dve", lambda e: e.tensor_tensor(t1[:], t1[:], t2[:], ALU.subtract), r=[self.b_ct], w=[self.b_ct])
        S.op("dve", lambda e: e.tensor_tensor(ci[:], t1[:], t3[:], ALU.mult), r=[self.b_ct], w=[b_w])
        Bbr = sb("Bbr", [128, 16, 16], F32)
        Bbi = sb("Bbi", [128, 16, 16], F32)
        T1 = sb("T1", [128, 16, 16], F32)
        T2 = sb("T2", [128, 16, 16], F32)
        b_bb = Buf("Bb")

        def bc(x):
            return x.unsqueeze(2).to_broadcast([128, 16, 16])
        self.cmul("dve", Bbr[:], Bbi[:], bc(cr[:]), bc(ci[:]), BR[:], BI[:], (T1[:], T2[:]), [b_w, b_in], [b_bb])
        PBr = sb("PBr", [128, L, 16, 32], BF16)
        PBi = sb("PBi", [128, L, 16, 32], BF16)
        CBr = sb("CBr", [128, 16, 32], BF16)
        CBin = sb("CBin", [128, 16, 32], BF16)
        VBr = sb("VBr", [128, L + 1, 16, 32], BF16)
        VBin = sb("VBin", [128, L + 1, 16, 32], BF16)
        b_pb, b_cb, b_vb = Buf("PB"), Buf("CB"), Buf("VB")
        for tt, bb in ((PBr, b_pb), (PBi, b_pb), (CBr, b_cb), (CBin, b_cb), (VBr, b_vb), (VBin, b_vb)):
            S.op("pool", lambda e, tt=tt: e.memset(tt[:], 0.0), w=[bb])
        for j in range(2):
            ps_, cs_ = slice(64 * j, 64 * j + 64), slice(16 * j, 16 * j + 16)
            S.op("dve", lambda e, ps_=ps_, cs_=cs_: e.tensor_copy(CBr[ps_, :, cs_], CR[ps_]), r=[b_in], w=[b_cb])
            S.op("dve", lambda e, ps_=ps_, cs_=cs_: e.tensor_scalar(CBin[ps_, :, cs_], CI[ps_], -1.0, None, ALU.mult),
                 r=[b_in], w=[b_cb])
        for k in range(L):
            akr, aki = bc(APr[:, k, :]), bc(APi[:, k, :])
            S.op("dve", lambda e, akr=akr: e.tensor_tensor(T1[:], akr, Bbr[:], ALU.mult), r=[b_ap, b_bb], w=[self.b_ct])
            S.op("dve", lambda e, aki=aki: e.tensor_tensor(T2[:], aki, Bbi[:], ALU.mult), r=[b_ap, b_bb], w=[self.b_ct])
            for j in range(2):
                ps_, cs_ = slice(64 * j, 64 * j + 64), slice(16 * j, 16 * j + 16)
                S.op("dve", lambda e, ps_=ps_, cs_=cs_, k=k: e.tensor_tensor(
                    PBr[ps_, k, :, cs_], T1[ps_], T2[ps_], ALU.subtract), r=[self.b_ct], w=[b_pb])
            S.op("dve", lambda e, akr=akr: e.tensor_tensor(T1[:], akr, Bbi[:], ALU.mult), r=[b_ap, b_bb, b_pb], w=[self.b_ct])
            S.op("dve", lambda e, aki=aki: e.tensor_tensor(T2[:], aki, Bbr[:], ALU.mult), r=[b_ap, b_bb], w=[self.b_ct])
            for j in range(2):
                ps_, cs_ = slice(64 * j, 64 * j + 64), slice(16 * j, 16 * j + 16)
                S.op("dve", lambda e, ps_=ps_, cs_=cs_, k=k: e.tensor_tensor(
                    PBi[ps_, k, :, cs_], T1[ps_], T2[ps_], ALU.add), r=[self.b_ct], w=[b_pb])
        for ee in range(L + 1):
            akr, aki = bc(APr[:, ee, :]), bc(APi[:, ee, :])
            S.op("dve", lambda e, akr=akr: e.tensor_tensor(T1[:], akr, CR[:], ALU.mult), r=[b_ap, b_in, b_vb, b_pb], w=[self.b_ct])
            S.op("dve", lambda e, aki=aki: e.tensor_tensor(T2[:], aki, CI[:], ALU.mult), r=[b_ap, b_in], w=[self.b_ct])
            for j in range(2):
                ps_, cs_ = slice(64 * j, 64 * j + 64), slice(16 * j, 16 * j + 16)
                S.op("dve", lambda e, ps_=ps_, cs_=cs_, ee=ee: e.tensor_tensor(
                    VBr[ps_, ee, :, cs_], T1[ps_], T2[ps_], ALU.subtract), r=[self.b_ct], w=[b_vb])
            S.op("dve", lambda e, aki=aki: e.tensor_tensor(T1[:], aki, CR[:], ALU.mult), r=[b_ap, b_in, b_vb], w=[self.b_ct])
            S.op("dve", lambda e, akr=akr: e.tensor_tensor(T2[:], akr, CI[:], ALU.mult), r=[b_ap, b_in], w=[self.b_ct])
            S.op("dve", lambda e: e.tensor_tensor(T1[:], T1[:], T2[:], ALU.add), r=[self.b_ct], w=[self.b_ct])
            for j in range(2):
                ps_, cs_ = slice(64 * j, 64 * j + 64), slice(16 * j, 16 * j + 16)
                S.op("dve", lambda e, ps_=ps_, cs_=cs_, ee=ee: e.tensor_scalar(
                    VBin[ps_, ee, :, cs_], T1[ps_], -1.0, None, ALU.mult), r=[self.b_ct], w=[b_vb])
        S.dma("sp", self.VBs.ap()[0], VBr[:].rearrange("p e r c -> p (e r c)"), b_vb, r=[b_vb])
        S.dma("sp", self.VBs.ap()[1], VBin[:].rearrange("p e r c -> p (e r c)"), b_vb, r=[b_vb])
        S.dma("sp", self.TAB.ap()[:, 0, :], APr[:, L, :], b_ap, r=[b_ap])
        S.dma("sp", self.TAB.ap()[:, 1, :], APi[:, L, :], b_ap, r=[b_ap])
        S.dma("sp", self.TAB.ap()[:, 2, :], APr[:, 1, :], b_ap, r=[b_ap])
        S.dma("sp", self.TAB.ap()[:, 3, :], APi[:, 1, :], b_ap, r=[b_ap])
        KBf = sb("KBf", [128, 4, 128], F32)
        b_kbf = Buf("KBf")
        KBo = [sb("KBo%d" % i, [128, 4, 128], BF16) for i in range(2)]
        b_kbo = [Buf("KBo%d" % i) for i in range(2)]
        S.op("pool", lambda e: e.memset(KBf[:], 0.0), w=[b_kbf])
        for lag in range(L):
            for ct in range(4):
                pt, b_pt = self.next_ps()
                for r4 in range(4):
                    pr = 4 * ct + r4
                    o_ = pt[32 * r4:32 * r4 + 32, 32 * r4:32 * r4 + 32]
                    S.op("pe", lambda e, o_=o_, lag=lag, pr=pr, r4=r4: e.matmul(
                        o_, PBr[:, lag, pr, :], CBr[:, pr, :], start=True, stop=False, tile_position=(0, 32 * r4)),
                        r=[b_pb, b_cb], w=[b_pt], inc=False)
                    S.op("pe", lambda e, o_=o_, lag=lag, pr=pr, r4=r4: e.matmul(
                        o_, PBi[:, lag, pr, :], CBin[:, pr, :], start=False, stop=True, tile_position=(0, 32 * r4)),
                        r=[b_pb, b_cb], w=[b_pt], inc=(r4 == 3))
                for r4 in range(4):
                    sl = slice(32 * r4, 32 * r4 + 32)
                    S.op("act", lambda e, sl=sl, ct=ct, pt=pt: e.activation(KBf[sl, ct, sl], pt[sl, sl], AF.Copy),
                         r=[b_pt], w=[b_kbf])
                if lag == 0:
                    S.op("dve", lambda e, ct=ct: e.scalar_tensor_tensor(
                        KBf[:, ct, :], self.ident_f[:, :], Dc[:, ct:ct + 1], KBf[:, ct, :], ALU.mult, ALU.add),
                        r=[b_in, self.b_ident], w=[b_kbf])
            ko, b_ko = KBo[lag % 2], b_kbo[lag % 2]
            S.op("dve", lambda e, ko=ko: e.tensor_copy(ko[:], KBf[:]), r=[b_kbf], w=[b_ko])
            S.dma("sp", self.KBs.ap()[:, lag, :], ko[:].rearrange("p c m -> p (c m)"), b_ko, r=[b_ko])
        WSo = [sb("WSo%d" % i, [128, 4, 128], BF16) for i in range(2)]
        b_wso = [Buf("WSo%d" % i) for i in range(2)]
        cnt = 0
        for ri, PB in enumerate((PBr, PBi)):
            for k in range(L):
                wo, b_wo = WSo[cnt % 2], b_wso[cnt % 2]
                cnt += 1
                for ct in range(4):
                    pt, b_pt = self.next_ps()
                    for r4 in range(4):
                        pr = 4 * ct + r4
                        S.op("pe", lambda e, pt=pt, r4=r4, PB=PB, k=k, pr=pr: e.matmul(
                            pt[32 * r4:32 * r4 + 32, 0:128], PB[:, k, pr, :], self.ident[:, :], start=True, stop=True,
                            tile_position=(0, 32 * r4)),
                            r=[b_pb, self.b_ident], w=[b_pt], inc=(r4 == 3))
                    S.op("act", lambda e, pt=pt, wo=wo, ct=ct: e.activation(wo[:, ct, :], pt[:, 0:128], AF.Copy),
                         r=[b_pt], w=[b_wo])
                S.dma("sp", self.WSs.ap()[ri, :, k, :], wo[:].rearrange("p c m -> p (c m)"), b_wo, r=[b_wo])


    def gelu_tanh(self, dst, src, tmp, b_src, b_tmp, b_dst):
        S = self.S
        S.op("dve", lambda e: e.tensor_tensor(tmp, src, src, ALU.mult), r=[b_src], w=[b_tmp])
        S.op("dve", lambda e: e.tensor_scalar(tmp, tmp, 0.044715, 1.0, ALU.mult, ALU.add), r=[b_tmp], w=[b_tmp])
        S.op("dve", lambda e: e.tensor_tensor(tmp, tmp, src, ALU.mult), r=[b_src, b_tmp], w=[b_tmp])
        S.op("act", lambda e: e.activation(tmp, tmp, AF.Sigmoid, scale=1.5957691216), r=[b_tmp], w=[b_tmp])
        S.op("dve", lambda e: e.tensor_tensor(dst, src, tmp, ALU.mult), r=[b_src, b_tmp], w=[b_dst])

    def sample_ssm(self, WS, VB, TAB, b_wt):
        S, sb = self.S, self.sb
        NS = 16
        b_su = Buf("s_u")
        Us = sb("Us", [128, 4, NS], BF16)
        Dcs = sb("Dcs", [128, 4], F32)
        S.dma("sp", Us[:], self.USs.ap().rearrange("c p t -> p c t"), b_su, w=[b_su])
        S.dma("sp", Dcs[:], dap(self.ssm_d, 0, [[1, 128], [128, 4]]), b_su, w=[b_su], allow_slow_non_contiguous=True)
        BU = [sb("BU%d" % i, [128, 16, NS], F32) for i in range(2)]
        b_BU = Buf("BU")
        for ri in range(2):
            for r4 in range(4):
                pt, b_pt = self.next_ps()
                rows = slice(32 * r4, 32 * r4 + 32)
                for ct in range(4):
                    S.op("pe", lambda e, pt=pt, ct=ct, r4=r4, rows=rows, ri=ri: e.matmul(
                        pt[:, ct * NS:(ct + 1) * NS], WS[ri][rows, 0, ct, :], Us[rows, ct, :],
                        start=True, stop=True, tile_position=(32 * r4, 0)),
                        r=[b_wt, b_su], w=[b_pt], inc=(ct == 3))
                S.op("act", lambda e, pt=pt, ri=ri, r4=r4: e.activation(
                    BU[ri][:, r4:16:4, :], pt[:, 0:4 * NS].rearrange("p (r c) -> p r c", r=4), AF.Copy),
                    r=[b_pt], w=[b_BU])
        H0 = [sb("H0_%d" % i, [128, 16, 4], F32) for i in range(2)]
        b_H0 = Buf("H0")
        for ri, src in enumerate((self.st_re, self.st_im)):
            for s_ in range(4):
                S.dma("sp", H0[ri][:, :, s_], dap(src, s_ * 2048, [[1, 128], [128, 16]]), b_H0, w=[b_H0],
                      allow_slow_non_contiguous=True)
        Hs = [sb("Hs%d" % i, [128, 16, NS], F32) for i in range(2)]
        b_Hs = Buf("Hs")
        tq = [sb("tqs%d" % i, [128, 16, 4], F32) for i in range(4)]
        b_tq = Buf("tqs")
        A1r = TAB[:, 2, :].unsqueeze(2).to_broadcast([128, 16, 4])
        A1i = TAB[:, 3, :].unsqueeze(2).to_broadcast([128, 16, 4])

        def v4(x, t):
            return x[:, :, t:t + 13:4]
        for t in range(4):
            if t == 0:
                pr_, pi_, bp = H0[0][:, :, :], H0[1][:, :, :], b_H0
            else:
                pr_, pi_, bp = v4(Hs[0], t - 1), v4(Hs[1], t - 1), b_Hs
            S.op("pool", lambda e, pr_=pr_: e.tensor_tensor(tq[0][:], A1r, pr_, ALU.mult), r=[bp, b_wt], w=[b_tq])
            S.op("pool", lambda e, pi_=pi_: e.tensor_tensor(tq[1][:], A1i, pi_, ALU.mult), r=[bp], w=[b_tq])
            S.op("pool", lambda e, pi_=pi_: e.tensor_tensor(tq[2][:], A1r, pi_, ALU.mult), r=[bp], w=[b_tq])
            S.op("pool", lambda e, pr_=pr_: e.tensor_tensor(tq[3][:], A1i, pr_, ALU.mult), r=[bp], w=[b_tq])
            S.op("pool", lambda e: e.tensor_tensor(tq[0][:], tq[0][:], tq[1][:], ALU.subtract), r=[b_tq], w=[b_tq])
            S.op("pool", lambda e: e.tensor_tensor(tq[2][:], tq[2][:], tq[3][:], ALU.add), r=[b_tq], w=[b_tq])
            S.op("pool", lambda e, t=t: e.tensor_tensor(v4(Hs[0], t), v4(BU[0], t), tq[0][:], ALU.add),
                 r=[b_tq, b_BU], w=[b_Hs])
            S.op("pool", lambda e, t=t: e.tensor_tensor(v4(Hs[1], t), v4(BU[1], t), tq[2][:], ALU.add),
                 r=[b_tq, b_BU], w=[b_Hs])
        for ri, dst in enumerate((self.ossm_s_re, self.ossm_s_im)):
            for s_ in range(4):
                S.dma("sp", dap(dst, s_ * 2048, [[1, 128], [128, 16]]), Hs[ri][:, :, 4 * s_ + 3], b_Hs, r=[b_Hs],
                      allow_slow_non_contiguous=True)
        Hb = [sb("Hbs%d" % i, [128, 16, NS], BF16) for i in range(2)]
        b_Hb = Buf("Hbs")
        for ri in range(2):
            S.op("act", lambda e, ri=ri: e.activation(Hb[ri][:], Hs[ri][:], AF.Copy), r=[b_Hs], w=[b_Hb])
        yfs = sb("yfs", [128, 4, NS], F32)
        yts = sb("yts", [128, 4, NS], F32)
        ybs = sb("ybs", [128, 4, NS], BF16)
        b_yfs, b_yts, b_ybs = Buf("yfs"), Buf("yts"), Buf("ybs")
        for ct in range(4):
            pt, b_pt = self.next_ps()
            for r4 in range(4):
                pr = 4 * ct + r4
                S.op("pe", lambda e, pt=pt, r4=r4, pr=pr: e.matmul(
                    pt[32 * r4:32 * r4 + 32, 0:NS], VB[0][:, 0, pr, :], Hb[0][:, pr, :], start=True, stop=False,
                    tile_position=(0, 32 * r4)), r=[b_wt, b_Hb], w=[b_pt], inc=False)
                S.op("pe", lambda e, pt=pt, r4=r4, pr=pr: e.matmul(
                    pt[32 * r4:32 * r4 + 32, 0:NS], VB[1][:, 0, pr, :], Hb[1][:, pr, :], start=False, stop=True,
                    tile_position=(0, 32 * r4)), r=[b_wt, b_Hb], w=[b_pt], inc=(r4 == 3))
            S.op("dve", lambda e, pt=pt, ct=ct: e.scalar_tensor_tensor(
                yfs[:, ct, :], Us[:, ct, :], Dcs[:, ct:ct + 1], pt[:, 0:NS], ALU.mult, ALU.add),
                r=[b_pt, b_su], w=[b_yfs])
        self.gelu_tanh(ybs[:], yfs[:], yts[:], b_yfs, b_yts, b_ybs)
        S.dma("sp", self.YSs.ap().rearrange("c p t -> p c t"), ybs[:], b_ybs, r=[b_ybs])


    def cmadd(self, eng, dr, di, ar, ai, xr, xi, tmps, rbufs, wbuf, b_t):
        S = self.S
        t0, t1, t2, t3 = tmps
        S.op(eng, lambda e: e.tensor_tensor(t0, ar, xr, ALU.mult), r=rbufs, w=[b_t])
        S.op(eng, lambda e: e.tensor_tensor(t1, ai, xi, ALU.mult), r=rbufs, w=[b_t])
        S.op(eng, lambda e: e.tensor_tensor(t2, ar, xi, ALU.mult), r=rbufs, w=[b_t])
        S.op(eng, lambda e: e.tensor_tensor(t3, ai, xr, ALU.mult), r=rbufs, w=[b_t])
        S.op(eng, lambda e: e.tensor_tensor(t0, t0, t1, ALU.subtract), r=[b_t], w=[b_t])
        S.op(eng, lambda e: e.tensor_tensor(t2, t2, t3, ALU.add), r=[b_t], w=[b_t])
        S.op(eng, lambda e: e.tensor_tensor(dr, dr, t0, ALU.add), r=[b_t] + rbufs, w=[wbuf])
        S.op(eng, lambda e: e.tensor_tensor(di, di, t2, ALU.add), r=[b_t] + rbufs, w=[wbuf])

    def phase3b(self):
        S, sb = self.S, self.sb
        L = LCH
        NCH = T_ALL // L
        NH = NCH // 2
        TH = T_ALL // 2
        GRP = 8
        NG = NCH // GRP
        b_wt = Buf("ssm_wt")
        WS = [sb("WS%d" % i, [128, L, 4, 128], BF16) for i in range(2)]
        VB = [sb("VB%d" % i, [128, L + 1, 16, 32], BF16) for i in range(2)]
        TAB = sb("TAB", [128, 4, 16], F32)
        for i in range(2):
            S.dma("sp", WS[i][:].rearrange("p l c m -> p l (c m)"), self.WSs.ap()[i], b_wt, w=[b_wt])
            S.dma("sp", VB[i][:].rearrange("p e r c -> p (e r c)"), self.VBs.ap()[i], b_wt, w=[b_wt])
        S.dma("sp", TAB[:], self.TAB.ap(), b_wt, w=[b_wt])
        self.sample_ssm(WS, VB, TAB, b_wt)
        Sall = [sb("Sall%d" % i, [128, 16, NCH], F32) for i in range(2)]
        b_S = Buf("Sall")
        U = sb("Uh", [128, 4, TH], BF16)
        b_U = Buf("Uh")
        for half in range(2):
            S.dma("sp", U[:], self.US.ap()[:, :, half * TH:(half + 1) * TH].rearrange("c p t -> p c t"), b_U, w=[b_U])
            for ri in range(2):
                for pr in range(16):
                    ct, r4 = pr // 4, pr % 4
                    rows = slice(32 * r4, 32 * r4 + 32)
                    pt, b_pt = self.next_ps()
                    for tau in range(L):
                        S.op("pe", lambda e, pt=pt, ct=ct, r4=r4, rows=rows, tau=tau, ri=ri: e.matmul(
                            pt[:, 0:NH], WS[ri][rows, L - 1 - tau, ct, :],
                            U[rows, ct, :].rearrange("p (n t c) -> p n t c", t=L, c=TN // L)[:, :, tau, :],
                            start=(tau == 0), stop=(tau == L - 1), tile_position=(32 * r4, 0)),
                            r=[b_wt, b_U], w=[b_pt], inc=(tau == L - 1))
                    S.op("act", lambda e, pt=pt, ri=ri, pr=pr, half=half: e.activation(
                        Sall[ri][:, pr, half * NH:(half + 1) * NH], pt[:, 0:NH], AF.Copy), r=[b_pt], w=[b_S])
        PW = [sb("PW%d" % i, [128, GRP + 1, 16], F32) for i in range(2)]
        b_PW = Buf("PW")
        tq = [sb("tq%d" % i, [128, 16, NG], F32) for i in range(4)]
        b_tq = Buf("tq")
        self.b_ct = b_tq
        S.op("dve", lambda e: e.tensor_copy(PW[0][:, 1, :], TAB[:, 0, :]), r=[b_wt], w=[b_PW])
        S.op("dve", lambda e: e.tensor_copy(PW[1][:, 1, :], TAB[:, 1, :]), r=[b_wt], w=[b_PW])
        for k in range(2, GRP + 1):
            self.cmul("dve", PW[0][:, k, :], PW[1][:, k, :], PW[0][:, k - 1, :], PW[1][:, k - 1, :],
                      PW[0][:, 1, :], PW[1][:, 1, :], (tq[0][:, :, 0], tq[1][:, :, 0]), [b_PW], [b_PW])

        def bcg(x):
            return x.unsqueeze(2).to_broadcast([128, 16, NG])

        def vw(ri, i):
            return Sall[ri][:, :, i:i + (NG - 1) * GRP + 1:GRP]
        tmps = tuple(t[:, :, :] for t in tq)
        for i in range(1, GRP):
            self.cmadd("dve", vw(0, i), vw(1, i), bcg(PW[0][:, 1, :]), bcg(PW[1][:, 1, :]), vw(0, i - 1), vw(1, i - 1),
                       tmps, [b_PW, b_S], b_S, b_tq)
        t16 = tuple(t[:, :, 0] for t in tq)
        for C in range(1, NG):
            e0, e1 = C * GRP - 1, (C + 1) * GRP - 1
            self.cmadd("dve", Sall[0][:, :, e1], Sall[1][:, :, e1], PW[0][:, GRP, :], PW[1][:, GRP, :],
                       Sall[0][:, :, e0], Sall[1][:, :, e0], t16, [b_PW, b_S], b_S, b_tq)

        def vw1(ri, i):
            return Sall[ri][:, :, GRP + i:GRP + i + (NG - 2) * GRP + 1:GRP]

        def ends(ri):
            return Sall[ri][:, :, GRP - 1:GRP - 1 + (NG - 2) * GRP + 1:GRP]

        def bcg1(x):
            return x.unsqueeze(2).to_broadcast([128, 16, NG - 1])
        tmps1 = tuple(t[:, :, 0:NG - 1] for t in tq)
        for i in range(GRP - 1):
            self.cmadd("dve", vw1(0, i), vw1(1, i), bcg1(PW[0][:, i + 1, :]), bcg1(PW[1][:, i + 1, :]), ends(0), ends(1),
                       tmps1, [b_PW, b_S], b_S, b_tq)
        S.dma("sp", dap(self.ossm_re, 0, [[1, 128], [128, 16]]), Sall[0][:, :, NCH - 1], b_S, r=[b_S],
              allow_slow_non_contiguous=True)
        S.dma("sp", dap(self.ossm_im, 0, [[1, 128], [128, 16]]), Sall[1][:, :, NCH - 1], b_S, r=[b_S],
              allow_slow_non_contiguous=True)
        HBt = sb("HBt", [128, 16, NH], BF16)
        b_HBt = Buf("HBt")
        for ri in range(2):
            S.op("act", lambda e, ri=ri: e.activation(HBt[:, :, :], Sall[ri][:, :, NH - 1:2 * NH - 1], AF.Copy),
                 r=[b_S], w=[b_HBt])
            S.dma("sp", self.HBs.ap()[ri], HBt[:, :, :], b_HBt, r=[b_HBt])

    def phase3c(self):
        S, sb = self.S, self.sb
        L = LCH
        NH = T_MAIN // L
        b_wt = Buf("ssm_wt2")
        KB = sb("KB", [128, L, 4, 128], BF16)
        VB = [sb("VB%d" % i, [128, L + 1, 16, 32], BF16) for i in range(2)]
        HB = [sb("HB%d" % i, [128, 16, NH], BF16) for i in range(2)]
        U = sb("Um", [128, 4, T_MAIN], BF16)
        for i in range(2):
            S.dma("sp", VB[i][:].rearrange("p e r c -> p (e r c)"), self.VBs.ap()[i], b_wt, w=[b_wt])
            S.dma("sp", HB[i][:], self.HBs.ap()[i], b_wt, w=[b_wt])
        S.dma("sp", KB[:].rearrange("p l c m -> p l (c m)"), self.KBs.ap(), b_wt, w=[b_wt])
        S.dma("sp", U[:], self.US.ap()[:, :, T_MAIN0:T_ALL].rearrange("c p t -> p c t"), b_wt, w=[b_wt])
        yf = sb("yf", [128, T_MAIN], F32)
        yt = sb("yt", [128, T_MAIN], F32)
        yb = sb("yb", [128, T_MAIN], BF16)
        b_yf, b_yt, b_yb = Buf("yf"), Buf("yt"), Buf("yb")
        for ct in range(4):
            for tau in range(L):
                pt, b_pt = self.next_ps()
                for lag in range(tau + 1):
                    S.op("pe", lambda e, pt=pt, tau=tau, lag=lag, ct=ct: e.matmul(
                        pt[:, 0:NH], KB[:, lag, ct, :],
                        U[:, ct, :].rearrange("p (n t c) -> p n t c", t=L, c=TN // L)[:, :, tau - lag, :],
                        start=(lag == 0), stop=False, skip_group_check=True), r=[b_wt], w=[b_pt], inc=False)
                for r4 in range(4):
                    pr = 4 * ct + r4
                    for ri in range(2):
                        last = (r4 == 3 and ri == 1)
                        S.op("pe", lambda e, pt=pt, tau=tau, r4=r4, pr=pr, ri=ri, last=last: e.matmul(
                            pt[32 * r4:32 * r4 + 32, 0:NH], VB[ri][:, tau + 1, pr, :], HB[ri][:, pr, :],
                            start=False, stop=last, skip_group_check=True, tile_position=(0, 32 * r4)),
                            r=[b_wt], w=[b_pt], inc=last)
                S.op("act", lambda e, pt=pt, tau=tau: e.activation(
                    yf[:, tau:tau + (NH - 1) * L + 1:L], pt[:, 0:NH], AF.Copy), r=[b_pt], w=[b_yf])
            self.gelu_tanh(yb[:, :], yf[:, :], yt[:, :], b_yf, b_yt, b_yb)
            S.dma("sp", self.YS.ap()[ct, :, :], yb[:, :], b_yb, r=[b_yb])

    def phase4(self):
        S, sb = self.S, self.sb
        stg = [sb("wstg%d" % i, [128, 1024], F32) for i in range(2)]
        b_stg = [Buf("wstg%d" % i) for i in range(2)]
        Wglu = sb("Wglu", [128, 4, 512], BF16)
        Wbs = sb("Wbs", [128, 4, 1024], BF16)
        Wout = sb("Wout", [128, 8, 1024], BF16)
        Wba = sb("Wba", [64, 4, 1024], BF16)
        b_W = Buf("W4")
        self.load_weight_bf16(Wglu, b_W, self.w_glu, 4, 512, None, stg, b_stg)
        self.load_weight_bf16(Wbs, b_W, self.w_bs, 4, 1024, None, stg, b_stg)
        self.load_weight_bf16(Wout, b_W, self.w_out, 8, 1024, None, stg, b_stg)
        for h in range(4):
            st_, bs_ = stg[h % 2], b_stg[h % 2]
            S.dma("sp", st_[0:64, :], self.w_ba.ap()[64 * h:64 * h + 64, :], bs_, w=[bs_])
            S.op("dve", lambda e, st_=st_, h=h: e.tensor_copy(Wba[:, h, :], st_[0:64, :]), r=[bs_], w=[b_W])
        bg = sb("bg", [128, 4], F32)
        b_bg = Buf("bg")
        S.dma("sp", bg[:], dap(self.b_glu, 0, [[1, 128], [128, 4]]), b_bg, w=[b_bg], allow_slow_non_contiguous=True)
        AT = [sb("AT%d" % i, [64, 4, TN], BF16) for i in range(2)]
        YT = [sb("YT%d" % i, [128, 4, TN], BF16) for i in range(2)]
        G = [sb("G%d" % i, [128, 16, TN], BF16) for i in range(2)]
        X = [sb("X%d" % i, [128, 3, D], F32) for i in range(2)]
        b_in = [Buf("in4_%d" % i) for i in range(2)]
        SG = sb("SG", [128, 16, TN], BF16)
        b_SG = Buf("SG")
        sgl = sb("sgl", [128, TN], F32)
        b_sgl = Buf("sgl")
        so = sb("so", [128, 4, TN], BF16)
        b_so = Buf("so")
        mix = sb("mix", [128, 8, TN], BF16)
        b_mix = Buf("mix")
        ta = [sb("ta%d" % i, [128, TN], F32) for i in range(2)]
        b_ta = [Buf("ta%d" % i) for i in range(2)]
        X1 = [sb("X1_%d" % i, [128, 3, D], F32) for i in range(2)]
        b_X1 = [Buf("X1_%d" % i) for i in range(2)]

        def load_tile(k):
            i = k % 2
            m0 = k * TN
            S.dma("sp", AT[i][:], self.ATT.ap()[:, :, m0:m0 + TN].rearrange("h p t -> p h t"), b_in[i], w=[b_in[i]])
            S.dma("sp", YT[i][:], self.YS.ap()[:, :, m0:m0 + TN].rearrange("c p t -> p c t"), b_in[i], w=[b_in[i]])
            S.dma("sp", G[i][:], self.GS.ap()[:, :, m0:m0 + TN].rearrange("c p t -> p c t"), b_in[i], w=[b_in[i]])
            S.dma("sp", X[i][:], self.xall.ap()[T_MAIN0 + m0:T_MAIN0 + m0 + TN, :].rearrange("(s p) d -> p s d", p=128),
                  b_in[i], w=[b_in[i]])
        self.epsc = sb("epsc", [128, 1], F32)
        self.b_epsc = Buf("epsc")
        S.op("dve", lambda e: e.memset(self.epsc[:], EPS), w=[self.b_epsc])
        self.junk = sb("junk", [128, D], BF16)
        self.b_junk = Buf("junk")
        self.ss = sb("ss", [128, 4], F32)
        self.b_ss = Buf("ss")
        self.sq = sb("sq", [128, 4], F32)
        self.b_sq = Buf("sq")
        self.rstd = sb("rstd", [128, 4], F32)
        self.b_rstd = Buf("rstd")
        self.xs = sb("xs", [128, 3, D], BF16)
        self.b_xs = [Buf("xs%d" % i) for i in range(3)]
        XN2 = [sb("XN2_%d" % i, [128, 8, TN], BF16) for i in range(2)]
        b_XN2 = [Buf("XN2_%d" % i) for i in range(2)]

        def body(i, nt, nsub, rows, xo, b_xo, mid_hook=None):
            bi = b_in[i]
            S.op("act", lambda e, i=i: e.activation(SG[:, :, 0:nt], G[i][:, :, 0:nt], AF.Sigmoid), r=[bi], w=[b_SG])
            for mt in range(4):
                pt, b_pt = self.next_ps()
                for kc in range(4):
                    S.op("pe", lambda e, pt=pt, kc=kc, mt=mt, i=i: e.matmul(
                        pt[:, 0:nt], Wglu[:, kc, mt * 128:(mt + 1) * 128], YT[i][:, kc, 0:nt],
                        start=(kc == 0), stop=(kc == 3)), r=[b_W, bi], w=[b_pt], inc=(kc == 3))
                S.op("act", lambda e, pt=pt, mt=mt: e.activation(
                    sgl[:, 0:nt], pt[:, 0:nt], AF.Sigmoid, bias=bg[:, mt:mt + 1]), r=[b_pt, b_bg], w=[b_sgl])
                S.op("dve", lambda e, mt=mt, i=i: e.tensor_tensor(so[:, mt, 0:nt], YT[i][:, mt, 0:nt], sgl[:, 0:nt], ALU.mult),
                     r=[bi, b_sgl], w=[b_so])
            for mt in range(8):
                pa, b_pa = self.next_ps()
                for h in range(4):
                    S.op("pe", lambda e, pa=pa, h=h, mt=mt, i=i: e.matmul(
                        pa[:, 0:nt], Wba[:, h, mt * 128:(mt + 1) * 128], AT[i][:, h, 0:nt],
                        start=(h == 0), stop=(h == 3)), r=[b_W, bi], w=[b_pa], inc=(h == 3))
                pb, b_pb = self.next_ps()
                for kc in range(4):
                    S.op("pe", lambda e, pb=pb, kc=kc, mt=mt: e.matmul(
                        pb[:, 0:nt], Wbs[:, kc, mt * 128:(mt + 1) * 128], so[:, kc, 0:nt],
                        start=(kc == 0), stop=(kc == 3)), r=[b_W, b_so], w=[b_pb], inc=(kc == 3))
                t0_, t1_ = ta[0], ta[1]
                S.op("dve", lambda e, pa=pa, mt=mt: e.tensor_tensor(ta[0][:, 0:nt], pa[:, 0:nt], SG[:, mt, 0:nt], ALU.mult),
                     r=[b_pa, b_SG], w=[b_ta[0]])
                S.op("dve", lambda e, pb=pb, mt=mt: e.tensor_tensor(ta[1][:, 0:nt], pb[:, 0:nt], SG[:, 8 + mt, 0:nt], ALU.mult),
                     r=[b_pb, b_SG], w=[b_ta[1]])
                S.op("dve", lambda e, mt=mt: e.tensor_tensor(mix[:, mt, 0:nt], ta[0][:, 0:nt], ta[1][:, 0:nt], ALU.add),
                     r=[b_ta[0], b_ta[1]], w=[b_mix])
            if mid_hook is not None:
                mid_hook()
            for s_ in range(nsub):
                for nh in range(2):
                    pt, b_pt = self.next_ps()
                    for kc in range(8):
                        S.op("pe", lambda e, pt=pt, kc=kc, s_=s_, nh=nh: e.matmul(
                            pt[0:rows, 0:512], mix[:, kc, s_ * 128:s_ * 128 + rows], Wout[:, kc, nh * 512:(nh + 1) * 512],
                            start=(kc == 0), stop=(kc == 7)), r=[b_W, b_mix], w=[b_pt], inc=(kc == 7))
                    S.op("dve", lambda e, pt=pt, s_=s_, nh=nh, i=i, xo=xo: e.tensor_tensor(
                        xo[0:rows, s_, nh * 512:(nh + 1) * 512], X[i][0:rows, s_, nh * 512:(nh + 1) * 512], pt[0:rows, 0:512], ALU.add),
                        r=[b_pt, bi], w=[b_xo])

        NK = T_MAIN // TN
        load_tile(0)
        pend = None
        for k in range(NK):
            if k + 1 < NK:
                load_tile(k + 1)
            xo, b_xo = X1[k % 2], b_X1[k % 2]
            body(k % 2, TN, 3, 128, xo, b_xo, pend)
            m0 = k * TN
            S.dma("sp", self.X1s.ap()[m0:m0 + TN, :].rearrange("(s p) d -> p s d", p=128), xo[:], b_xo, r=[b_xo])

            def pend(k=k, xo=xo, b_xo=b_xo, m0=m0):
                self.norm_transpose(xo, b_xo, 3, 128, XN2[k % 2], b_XN2[k % 2])
                S.dma("sp", self.XN2s.ap()[:, :, m0:m0 + TN].rearrange("k p t -> p k t"), XN2[k % 2][:, :, :],
                      b_XN2[k % 2], r=[b_XN2[k % 2]])
        i = NK % 2
        NS = 16
        S.dma("sp", AT[i][:, :, 0:NS], self.ATTs.ap().rearrange("h p t -> p h t"), b_in[i], w=[b_in[i]])
        S.dma("sp", YT[i][:, :, 0:NS], self.YSs.ap().rearrange("c p t -> p c t"), b_in[i], w=[b_in[i]])
        S.dma("sp", G[i][:, :, 0:NS], self.GSs.ap().rearrange("c p t -> p c t"), b_in[i], w=[b_in[i]])
        S.dma("sp", X[i][0:NS, 0, :], self.xsamp.ap(), b_in[i], w=[b_in[i]])
        xo, b_xo = X1[NK % 2], b_X1[NK % 2]
        body(i, NS, 1, NS, xo, b_xo, pend)
        S.dma("sp", self.X1ss.ap(), xo[0:NS, 0, :], b_xo, r=[b_xo])
        self.norm_transpose(xo, b_xo, 1, NS, XN2[i], b_XN2[i])
        S.dma("sp", self.XN2ss.ap().rearrange("k p t -> p k t"), XN2[i][:, :, 0:NS], b_XN2[i], r=[b_XN2[i]])

    def phase5(self):
        S, sb = self.S, self.sb
        NM = DFF // 128
        g2c = sb("g2c", [128, 8], F32)
        b_c = Buf("c5")
        S.dma("sp", g2c[:], dap(self.norm2_g, 0, [[1, 128], [128, 8]]), b_c, w=[b_c], allow_slow_non_contiguous=True)
        self.epsc = sb("epsc", [128, 1], F32)
        self.b_epsc = Buf("epsc")
        S.op("dve", lambda e: e.memset(self.epsc[:], EPS), w=[self.b_epsc])
        cw = sb("cw", [128, 3, NM], F32)
        cb = sb("cb", [128, NM], F32)
        for i3 in range(3):
            S.dma("sp", cw[:, i3, :], dap(self.conv_w, i3 * DFF, [[1, 128], [128, NM]]), b_c, w=[b_c],
                  allow_slow_non_contiguous=True)
        S.dma("sp", cb[:], dap(self.conv_b, 0, [[1, 128], [128, NM]]), b_c, w=[b_c], allow_slow_non_contiguous=True)
        gfb = sb("gfb", [128, D], F32)
        S.dma("sp", gfb[:], dap(self.norm_f_g, 0, [[0, 128], [1, D]]), b_c, w=[b_c])
        stg = [sb("wstg%d" % i, [128, 704], F32) for i in range(2)]
        b_stg = [Buf("wstg%d" % i) for i in range(2)]
        Wup = sb("Wup", [128, 8, 2 * DFF], BF16)
        Wdn = sb("Wdn", [128, NM, D], BF16)
        b_W = Buf("W5")
        self.load_weight_bf16(Wup, b_W, self.w_up, 8, 2 * DFF, (g2c, b_c), stg, b_stg, nsplit=8)
        self.load_weight_bf16(Wdn, b_W, self.w_down, NM, D, None, stg, b_stg, nsplit=2)
        self.junk = sb("junk", [128, D], BF16)
        self.b_junk = Buf("junk")
        X1 = [sb("X1_0", [128, 3, D], F32)] * 2
        b_X1 = [Buf("X1_0")] * 2
        xnTs = [sb("xnT%d" % i, [128, 8, TN], BF16) for i in range(2)]
        b_xnTs = [Buf("xnT%d" % i) for i in range(2)]
        carry = sb("carry", [128, NM, 2], F32)
        b_carry = Buf("carry")
        S.op("dve", lambda e: e.memset(carry[:], 0.0), w=[b_carry])
        ab = [sb("ab%d" % i, [128, TN + 2], F32) for i in range(2)]
        b_ab = [Buf("ab%d" % i) for i in range(2)]
        tc_ = [sb("tc%d" % i, [128, TN], F32) for i in range(2)]
        b_tc = [Buf("tc%d" % i) for i in range(2)]
        hT = sb("hT", [128, NM, TN], BF16)
        b_hT = Buf("hT")
        x2 = [sb("x2_0", [128, D], F32)] * 2
        b_x2 = [Buf("x2_0")] * 2
        yo = [sb("yo%d" % i, [128, D], F32) for i in range(2)]
        b_yo = [Buf("yo%d" % i) for i in range(2)]
        ss2 = sb("ss2", [128, 2], F32)
        sq2 = sb("sq2", [128, 2], F32)
        b_n2 = [Buf("n2_%d" % i) for i in range(2)]

        def load_tile(k):
            m0 = k * TN
            S.dma("sp", X1[k % 2][:], self.X1s.ap()[m0:m0 + TN, :].rearrange("(s p) d -> p s d", p=128),
                  b_X1[k % 2], w=[b_X1[k % 2]])

        def load_xn(k):
            m0 = k * TN
            S.dma("sp", xnTs[k % 2][:], self.XN2s.ap()[:, :, m0:m0 + TN].rearrange("k p t -> p k t"),
                  b_xnTs[k % 2], w=[b_xnTs[k % 2]])
        NK = T_MAIN // TN
        load_xn(0)
        cnt = 0
        ocnt = 0
        for k in range(NK):
            if k + 1 < NK:
                load_xn(k + 1)
            load_tile(k)
            xt, b_xt = X1[k % 2], b_X1[k % 2]
            xnT, b_xnT = xnTs[k % 2], b_xnTs[k % 2]
            for mt in range(NM):
                pa, b_pa = self.next_ps()
                pv, b_pv = self.next_ps()
                for (pp, bp, c0) in ((pa, b_pa, mt * 128), (pv, b_pv, DFF + mt * 128)):
                    for kc in range(8):
                        S.op("pe", lambda e, pp=pp, kc=kc, c0=c0, xnT=xnT: e.matmul(
                            pp[:, 0:TN], Wup[:, kc, c0:c0 + 128], xnT[:, kc, :], start=(kc == 0), stop=(kc == 7)),
                            r=[b_W, b_xnT], w=[bp], inc=(kc == 7))
                a_, b_a = ab[cnt % 2], b_ab[cnt % 2]
                t_, b_t = tc_[cnt % 2], b_tc[cnt % 2]
                cnt += 1
                S.op("act", lambda e, a_=a_, mt=mt: e.activation(a_[:, 0:2], carry[:, mt, :], AF.Copy),
                     r=[b_carry], w=[b_a])
                S.op("act", lambda e, a_=a_, pa=pa: e.activation(a_[:, 2:TN + 2], pa[:, 0:TN], AF.Copy),
                     r=[b_pa], w=[b_a])
                S.op("dve", lambda e, a_=a_, t_=t_, mt=mt: e.tensor_scalar(
                    t_[:, :], a_[:, 2:TN + 2], cw[:, 2, mt:mt + 1], cb[:, mt:mt + 1], ALU.mult, ALU.add),
                    r=[b_a, b_c], w=[b_t])
                S.op("dve", lambda e, a_=a_, t_=t_, mt=mt: e.scalar_tensor_tensor(
                    t_[:, :], a_[:, 1:TN + 1], cw[:, 1, mt:mt + 1], t_[:, :], ALU.mult, ALU.add),
                    r=[b_a, b_c], w=[b_t])
                S.op("dve", lambda e, a_=a_, t_=t_, mt=mt: e.scalar_tensor_tensor(
                    t_[:, :], a_[:, 0:TN], cw[:, 0, mt:mt + 1], t_[:, :], ALU.mult, ALU.add),
                    r=[b_a, b_c], w=[b_t])
                S.op("dve", lambda e, a_=a_, mt=mt: e.tensor_copy(carry[:, mt, :], a_[:, TN:TN + 2]),
                     r=[b_a], w=[b_carry])
                S.op("act", lambda e, t_=t_: e.activation(t_[:, :], t_[:, :], AF.Silu), r=[b_t], w=[b_t])
                S.op("dve", lambda e, t_=t_, pv=pv, mt=mt: e.tensor_tensor(hT[:, mt, :], t_[:, :], pv[:, 0:TN], ALU.mult),
                     r=[b_t, b_pv], w=[b_hT])
            for s_ in range(3):
                j = ocnt % 2
                ocnt += 1
                for nh in range(2):
                    pt, b_pt = self.next_ps()
                    for kc in range(NM):
                        S.op("pe", lambda e, pt=pt, kc=kc, s_=s_, nh=nh: e.matmul(
                            pt[:, 0:512], hT[:, kc, s_ * 128:(s_ + 1) * 128], Wdn[:, kc, nh * 512:(nh + 1) * 512],
                            start=(kc == 0), stop=(kc == NM - 1)), r=[b_W, b_hT], w=[b_pt], inc=(kc == NM - 1))
                    S.op("dve", lambda e, pt=pt, s_=s_, nh=nh, j=j, xt=xt: e.tensor_tensor(
                        x2[j][:, nh * 512:(nh + 1) * 512], xt[:, s_, nh * 512:(nh + 1) * 512], pt[:, 0:512], ALU.add),
                        r=[b_pt, b_xt], w=[b_x2[j]])
                tok = k * TN + s_ * 128
                if tok < 128:
                    continue
                S.op("act", lambda e, j=j: e.activation(self.junk[:, :], x2[j][:, :], AF.Square,
                                                        accum_out=ss2[:, j:j + 1]),
                     r=[b_x2[j]], w=[self.b_junk, b_n2[j]])
                S.op("act", lambda e, j=j: e.activation(sq2[:, j:j + 1], ss2[:, j:j + 1], AF.Sqrt,
                                                        bias=self.epsc[:, :], scale=1.0 / D),
                     r=[b_n2[j], self.b_epsc], w=[b_n2[j]])
                S.op("dve", lambda e, j=j: e.reciprocal(sq2[:, j:j + 1], sq2[:, j:j + 1]), r=[b_n2[j]], w=[b_n2[j]])
                S.op("dve", lambda e, j=j: e.scalar_tensor_tensor(
                    yo[j][:, :], x2[j][:, :], sq2[:, j:j + 1], gfb[:, :], ALU.mult, ALU.mult),
                    r=[b_x2[j], b_n2[j], b_c], w=[b_yo[j]])
                S.dma("sp", self.oy.ap()[tok - 128:tok, :], yo[j][:, :], b_yo[j], r=[b_yo[j]])
        for i2 in range(2):
            S.dma("sp", dap(self.ocv, i2 * DFF, [[1, 128], [128, NM]]), carry[:, :, i2], b_carry, r=[b_carry],
                  allow_slow_non_contiguous=True)
        NS = 16
        carS = sb("carS", [128, NM, 4, 2], F32)
        b_carS = Buf("carS")
        for s_ in range(4):
            for i2 in range(2):
                S.dma("sp", carS[:, :, s_, i2], dap(self.st_conv, (s_ * 2 + i2) * DFF, [[1, 128], [128, NM]]),
                      b_carS, w=[b_carS], allow_slow_non_contiguous=True)
        xt, b_xt = X1[0], b_X1[0]
        S.dma("sp", xt[0:NS, 0, :], self.X1ss.ap(), b_xt, w=[b_xt])
        xnT, b_xnT = xnTs[NK % 2], b_xnTs[NK % 2]
        S.dma("sp", xnT[:, :, 0:NS], self.XN2ss.ap().rearrange("k p t -> p k t"), b_xnT, w=[b_xnT])
        a3 = sb("a3", [128, 4, 6], F32)
        b_a3 = Buf("a3")
        for mt in range(NM):
            pa, b_pa = self.next_ps()
            pv, b_pv = self.next_ps()
            for (pp, bp, c0) in ((pa, b_pa, mt * 128), (pv, b_pv, DFF + mt * 128)):
                for kc in range(8):
                    S.op("pe", lambda e, pp=pp, kc=kc, c0=c0, xnT=xnT: e.matmul(
                        pp[:, 0:NS], Wup[:, kc, c0:c0 + 128], xnT[:, kc, 0:NS], start=(kc == 0), stop=(kc == 7)),
                        r=[b_W, b_xnT], w=[bp], inc=(kc == 7))
            t_, b_t = tc_[mt % 2], b_tc[mt % 2]
            t3 = t_[:, 0:NS].rearrange("p (s t) -> p s t", s=4)
            S.op("act", lambda e, mt=mt: e.activation(a3[:, :, 0:2], carS[:, mt, :, :], AF.Copy), r=[b_carS], w=[b_a3])
            S.op("act", lambda e, pa=pa: e.activation(
                a3[:, :, 2:6], pa[:, 0:NS].rearrange("p (s t) -> p s t", s=4), AF.Copy), r=[b_pa], w=[b_a3])
            S.op("dve", lambda e, t3=t3, mt=mt: e.tensor_scalar(
                t3, a3[:, :, 2:6], cw[:, 2, mt:mt + 1], cb[:, mt:mt + 1], ALU.mult, ALU.add), r=[b_a3, b_c], w=[b_t])
            S.op("dve", lambda e, t3=t3, mt=mt: e.scalar_tensor_tensor(
                t3, a3[:, :, 1:5], cw[:, 1, mt:mt + 1], t3, ALU.mult, ALU.add), r=[b_a3, b_c], w=[b_t])
            S.op("dve", lambda e, t3=t3, mt=mt: e.scalar_tensor_tensor(
                t3, a3[:, :, 0:4], cw[:, 0, mt:mt + 1], t3, ALU.mult, ALU.add), r=[b_a3, b_c], w=[b_t])
            S.op("dve", lambda e, mt=mt: e.tensor_copy(carS[:, mt, :, :], a3[:, :, 4:6]), r=[b_a3], w=[b_carS])
            S.op("act", lambda e, t_=t_: e.activation(t_[:, 0:NS], t_[:, 0:NS], AF.Silu), r=[b_t], w=[b_t])
            S.op("dve", lambda e, t_=t_, pv=pv, mt=mt: e.tensor_tensor(hT[:, mt, 0:NS], t_[:, 0:NS], pv[:, 0:NS], ALU.mult),
                 r=[b_t, b_pv], w=[b_hT])
        for s_ in range(4):
            for i2 in range(2):
                S.dma("sp", dap(self.ocvs, (s_ * 2 + i2) * DFF, [[1, 128], [128, NM]]), carS[:, :, s_, i2],
                      b_carS, r=[b_carS], allow_slow_non_contiguous=True)
        j = 0
        for nh in range(2):
            pt, b_pt = self.next_ps()
            for kc in range(NM):
                S.op("pe", lambda e, pt=pt, kc=kc, nh=nh: e.matmul(
                    pt[0:NS, 0:512], hT[:, kc, 0:NS], Wdn[:, kc, nh * 512:(nh + 1) * 512],
                    start=(kc == 0), stop=(kc == NM - 1)), r=[b_W, b_hT], w=[b_pt], inc=(kc == NM - 1))
            S.op("dve", lambda e, pt=pt, nh=nh: e.tensor_tensor(
                x2[j][0:NS, nh * 512:(nh + 1) * 512], xt[0:NS, 0, nh * 512:(nh + 1) * 512], pt[0:NS, 0:512], ALU.add),
                r=[b_pt, b_xt], w=[b_x2[j]])
        S.op("act", lambda e: e.activation(self.junk[0:NS, :], x2[j][0:NS, :], AF.Square, accum_out=ss2[0:NS, j:j + 1]),
             r=[b_x2[j]], w=[self.b_junk, b_n2[j]])
        S.op("act", lambda e: e.activation(sq2[0:NS, j:j + 1], ss2[0:NS, j:j + 1], AF.Sqrt,
                                           bias=self.epsc[0:NS, :], scale=1.0 / D),
             r=[b_n2[j], self.b_epsc], w=[b_n2[j]])
        S.op("dve", lambda e: e.reciprocal(sq2[0:NS, j:j + 1], sq2[0:NS, j:j + 1]), r=[b_n2[j]], w=[b_n2[j]])
        S.op("dve", lambda e: e.scalar_tensor_tensor(
            yo[j][0:NS, :], x2[j][0:NS, :], sq2[0:NS, j:j + 1], gfb[0:NS, :], ALU.mult, ALU.mult),
            r=[b_x2[j], b_n2[j], b_c], w=[b_yo[j]])
        S.dma("sp", self.oys.ap(), yo[j][0:NS, :], b_yo[j], r=[b_yo[j]])


_CACHE = {}


def get_prog(phases):
    key = tuple(sorted(phases))
    if key not in _CACHE:
        phases = set(phases)
        if 3 in phases:
            phases |= {31, 32, 33}
        p = Prog(phases)
        p.build()
        _CACHE[key] = p
    return _CACHE[key]


def _rel_bucket(dist):
    n = np.maximum(dist, 0)
    nf = np.maximum(n, 1).astype(np.float32)
    large = 16 + (np.log(nf / np.float32(16)) / np.float32(math.log(2048 / 16)) * np.float32(16)).astype(np.int32)
    large = np.minimum(large, 31)
    return np.where(n < 16, n, large)


def _bucket_onehot():
    oh = np.zeros((32, 3 * 129), np.float32)
    for g in range(3):
        b = _rel_bucket(np.arange(129) * DILS[g])
        oh[b, g * 129 + np.arange(129)] = 1.0
    return oh


def make_in_maps(inputs):
    xp = np.asarray(inputs["x_prompt"], dtype=np.float32)
    maps = []
    ident = np.eye(128, dtype=np.float32)
    boh = _bucket_onehot()
    for c in range(NCORES):
        b, half = c // 2, c % 2
        s0 = half * 4096
        lo = s0 - (T_ALL - 4096)
        xall = np.zeros((T_ALL, D), np.float32)
        src_lo = max(lo, 0)
        xall[src_lo - lo:, :] = xp[b, src_lo:s0 + 4096, :]
        m = {
            "xall": xall,
            "w_in": np.ascontiguousarray(inputs["w_in"][0]),
            "norm1_g": np.ascontiguousarray(inputs["norm1_g"][0]),
            "ident": ident,
            "st_conv": np.ascontiguousarray(inputs["state_ffn_conv"][0, 4 * c:4 * c + 4]),
            "st_re": np.ascontiguousarray(inputs["state_ssm_re"][0, 4 * c:4 * c + 4]),
            "st_im": np.ascontiguousarray(inputs["state_ssm_im"][0, 4 * c:4 * c + 4]),
            "cache0": np.ascontiguousarray(inputs["cache_kv_w128"][0, 4 * c:4 * c + 4].reshape(4, 128, 512)),
            "cache1": np.ascontiguousarray(inputs["cache_kv_w512"][0, 4 * c:4 * c + 4].reshape(4, 512, 512)),
            "cache2": np.ascontiguousarray(inputs["cache_kv_w2048"][0, 4 * c:4 * c + 4].reshape(4, 2048, 512)),
            "xsamp": np.ascontiguousarray(inputs["x_sample"][4 * c:4 * c + 4].reshape(16, D)),
            "rel_bias": np.ascontiguousarray(inputs["rel_bias"]),
            "bucket_oh": boh,
            "antiident": np.ascontiguousarray(np.eye(128, dtype=np.float32)[::-1]),
            "hv": np.full((128, 1), float(half), np.float32),
            "ssm_log_dt": np.ascontiguousarray(inputs["ssm_log_dt"][0]),
            "ssm_lambda_re": np.ascontiguousarray(inputs["ssm_lambda_re"][0]),
            "ssm_lambda_im": np.ascontiguousarray(inputs["ssm_lambda_im"][0]),
            "ssm_b_re": np.ascontiguousarray(inputs["ssm_b_re"][0]),
            "ssm_b_im": np.ascontiguousarray(inputs["ssm_b_im"][0]),
            "ssm_c_re": np.ascontiguousarray(inputs["ssm_c_re"][0]),
            "ssm_c_im": np.ascontiguousarray(inputs["ssm_c_im"][0]),
            "ssm_d": np.ascontiguousarray(inputs["ssm_d"][0]),
            "w_glu": np.ascontiguousarray(inputs["w_glu"][0]),
            "b_glu": np.ascontiguousarray(inputs["b_glu"][0]),
            "w_branch_attn": np.ascontiguousarray(inputs["w_branch_attn"][0]),
            "w_branch_ssm": np.ascontiguousarray(inputs["w_branch_ssm"][0]),
            "w_out": np.ascontiguousarray(inputs["w_out"][0]),
            "norm2_g": np.ascontiguousarray(inputs["norm2_g"][0]),
            "w_up": np.ascontiguousarray(inputs["w_up"][0]),
            "conv_w": np.ascontiguousarray(inputs["conv_w"][0]),
            "conv_b": np.ascontiguousarray(inputs["conv_b"][0]),
            "w_down": np.ascontiguousarray(inputs["w_down"][0]),
            "norm_f_g": np.ascontiguousarray(inputs["norm_f_g"]),
        }
        maps.append(m)
    return maps


def kernel(**inputs):
    prog = get_prog(_PHASES)
    maps = make_in_maps(inputs)
    maps = [{k: v for k, v in m.items() if k in prog.din} for m in maps]
    res = run_bass_kernel_spmd(prog.nc, maps, core_ids=list(range(NCORES)))
    R = res.results
    B = 4
    outs = [None] * 14
    for g in range(3):
        W = WINS[g]
        a = np.zeros((1, B, W, 2, 4, 64), np.float32)
        for b in range(B):
            a[0, b] = R[2 * b + 1]["okv%d" % g].reshape(W, 2, 4, 64)
        outs[2 + g] = a
    if "oy" in R[0]:
        y = np.zeros((B, 8192, D), np.float32)
        for c in range(NCORES):
            y[c // 2, (c % 2) * 4096:(c % 2 + 1) * 4096] = R[c]["oy"]
        outs[0] = y
        cv = np.zeros((1, B, 2, DFF), np.float32)
        for b in range(B):
            cv[0, b] = R[2 * b + 1]["ocv"]
        outs[7] = cv
    if "oys" in R[0]:
        outs[1] = np.concatenate([R[c]["oys"].reshape(4, 4, D) for c in range(NCORES)], axis=0)
        outs[13] = np.concatenate([R[c]["ocvs"] for c in range(NCORES)], axis=0)[None]
    if "okvs0" in R[0]:
        for g in range(3):
            outs[8 + g] = np.concatenate([R[c]["okvs%d" % g].reshape(4, 4, 2, 4, 64) for c in range(NCORES)], axis=0)[None]
    if "ossm_s_re" in R[0]:
        outs[11] = np.concatenate([R[c]["ossm_s_re"] for c in range(NCORES)], axis=0)[None]
        outs[12] = np.concatenate([R[c]["ossm_s_im"] for c in range(NCORES)], axis=0)[None]
    if "ossm_re" in R[0]:
        for i, nm in ((5, "ossm_re"), (6, "ossm_im")):
            a = np.zeros((1, B, 32, 64), np.float32)
            for b in range(B):
                a[0, b] = R[2 * b + 1][nm]
            outs[i] = a
    shapes = [(4, 8192, 1024), (32, 4, 1024), None, None, None, (1, 4, 32, 64), (1, 4, 32, 64),
              (1, 4, 2, DFF), (1, 32, 4, 2, 4, 64), (1, 32, 4, 2, 4, 64), (1, 32, 4, 2, 4, 64),
              (1, 32, 32, 64), (1, 32, 32, 64), (1, 32, 2, DFF)]
    for i in range(14):
        if outs[i] is None:
            outs[i] = np.zeros(shapes[i], np.float32)
    return tuple(outs)
```

```python
import math
import os
from contextlib import ExitStack

import numpy as np
import concourse.bass as bass
import concourse.mybir as mybir
from concourse.bass_utils import run_bass_kernel_spmd

F32 = mybir.dt.float32
BF16 = mybir.dt.bfloat16
AF = mybir.ActivationFunctionType
ALU = mybir.AluOpType
AX = mybir.AxisListType

NCORES = 8
D = 1024
INW = 4864
DFF = 2816
TN = 384
NT_ALL = 22
T_ALL = NT_ALL * TN
T_MAIN0 = 11 * TN
T_MAIN = 11 * TN
KV_T0 = 5 * TN
T_KV = T_ALL - KV_T0
EPS = 1e-6
WINS = (128, 512, 2048)
DILS = (1, 4, 16)
LCH = 16
_DBG = int(os.environ.get("K_DBG", "9"))
_DBG2 = int(os.environ.get("K_DBG2", "0"))
_PHASES = set(int(v) for v in os.environ.get("K_PH", "1,2,3,4,5").split(","))
_P3B = int(os.environ.get("K_P3B", "9"))
_ADEPTH = int(os.environ.get("K_ADEPTH", "2"))
_TR = tuple(int(v) for v in os.environ.get("K_TILES", "0,22").split(","))


class Sem:
    _n = 0

    def __init__(self, h):
        self.h = h
        Sem._n += 1
        self.id = Sem._n


class Buf:
    __slots__ = ("name", "w", "r", "sem", "semcnt", "excl")

    def __init__(self, name, excl=False):
        self.name = name
        self.excl = excl
        self.w = None
        self.r = []
        self.sem = None
        self.semcnt = 0


class Sched:
    def __init__(self, nc, es):
        self.nc = nc
        self.es = es
        self.engs = {"pe": nc.tensor, "act": nc.scalar, "dve": nc.vector,
                     "pool": nc.gpsimd, "sp": nc.sync}
        self.ops = {k: [] for k in self.engs}
        self.sem = {k: Sem(es.enter_context(nc.semaphore("sem_" + k)))
                    for k in ("pe", "act", "dve", "pool")}
        self.cnt = {k: 0 for k in self.sem}
        self.seen = {k: {} for k in self.engs}
        self.nsem = 4
        self.pending_noinc = {k: False for k in self.sem}
        self.free_sems = []
        self.dma_bufs = []

    def _need(self, eng, r, w):
        deps = []
        for b in r:
            if b.w is not None:
                deps.append(b.w)
        for b in w:
            if b.w is not None:
                deps.append(b.w)
            deps.extend(b.r)
        waits = {}
        for (s, v) in deps:
            if eng == "pe" and s is self.sem["pe"]:
                continue
            if waits.get(s, (None, 0))[1] < v:
                waits[s] = (s, v)
        need = []
        seen = self.seen[eng]
        for s, v in waits.values():
            if seen.get(s.id, 0) < v:
                seen[s.id] = v
                need.append((s.h, v))
        return need

    def _record(self, tk, r, w):
        for b in r:
            b.r.append(tk)
        for b in w:
            b.w = tk
            b.r = []

    def op(self, eng, fn, r=(), w=(), inc=True):
        if eng != "pe":
            ex = [b for b in r if b.excl]
            if ex:
                r = [b for b in r if not b.excl]
                w = list(w) + ex
        need = self._need(eng, r, w)
        s = self.sem[eng]
        if inc:
            self.cnt[eng] += 1
            tk = (s, self.cnt[eng])
            self.ops[eng].append((need, fn, s.h, 1))
            self.pending_noinc[eng] = False
        else:
            tk = (s, self.cnt[eng] + 1)
            self.ops[eng].append((need, fn, None, 0))
            self.pending_noinc[eng] = True
        self._record(tk, r, w)
        return tk

    def dma(self, q, out, in_, key, r=(), w=(), **kw):
        need = self._need(q, r, w)
        if key.sem is None:
            if self.free_sems:
                key.sem, key.semcnt = self.free_sems.pop()
            else:
                key.sem = Sem(self.es.enter_context(self.nc.semaphore("dsem_%d" % self.nsem)))
                self.nsem += 1
            self.dma_bufs.append(key)
        key.semcnt += 16
        tk = (key.sem, key.semcnt)
        self.ops[q].append((need, lambda e: e.dma_start(out=out, in_=in_, **kw), key.sem.h, 16))
        self._record(tk, r, w)
        return tk

    def drain_dmas(self, eng="sp"):
        need = []
        for b in self.dma_bufs:
            if self.seen[eng].get(b.sem.id, 0) < b.semcnt:
                self.seen[eng][b.sem.id] = b.semcnt
                need.append((b.sem.h, b.semcnt))
            self.free_sems.append((b.sem, b.semcnt))
            b.sem = None
        self.dma_bufs = []
        self.ops[eng].append((need, None, None, 0))

    def wait_all(self, eng, bufs):
        need = self._need(eng, (), bufs)
        self.ops[eng].append((need, None, None, 0))

    def emit(self, block):
        for k in self.pending_noinc:
            assert not self.pending_noinc[k], k

        def run(name):
            lst = self.ops[name]

            def f(e):
                for need, fn, sh, inc in lst:
                    for (h, v) in need:
                        e.wait_ge(h, v)
                    if fn is None:
                        continue
                    ins = fn(e)
                    if sh is not None:
                        ins.then_inc(sh, inc)
            return f
        block.sync(run("sp"))
        block.tensor(run("pe"))
        block.scalar(run("act"))
        block.vector(run("dve"))
        block.gpsimd(run("pool"))
        self.ops = {k: [] for k in self.engs}


def dap(t, off, dims):
    return bass.AP(t, off, [list(d) for d in dims])


class Prog:
    def __init__(self, phases):
        self.phases = phases
        self.nc = bass.Bass("TRN2", target_bir_lowering=False)
        self.es = ExitStack()
        self.S = None
        self.din = {}
        self.dout = {}
        self.outbufs = []
        self.phn = 0

    def inp(self, name, shape, dt=F32):
        t = self.nc.dram_tensor(name, list(shape), dt, kind="ExternalInput")
        self.din[name] = t
        return t

    def outp(self, name, shape, dt=F32):
        t = self.nc.dram_tensor(name, list(shape), dt, kind="ExternalOutput")
        self.dout[name] = t
        return t

    def scr(self, name, shape, dt):
        return self.nc.dram_tensor("d_" + name, list(shape), dt)

    def sb(self, name, shape, dt, glob=False):
        es = self.es if glob else self.pes
        return es.enter_context(self.nc.sbuf_tensor("s%d_%s" % (self.phn, name), list(shape), dt))

    def run_phase(self, fn):
        self.phn += 1
        with ExitStack() as pes:
            self.pes = pes
            fn()
            self.S.drain_dmas("sp")
            with self.nc.Block() as block:
                self.S.emit(block)
        self.pes = self.es

    def ps(self, name, shape, dt):
        return self.es.enter_context(self.nc.psum_tensor("p_" + name, list(shape), dt))

    def build(self):
        nc, es = self.nc, self.es
        with es:
            self._declare_io()
            self.S = Sched(nc, es)
            self.pes = es
            self._consts()
            for ph, fn in ((1, self.phase1), (2, self.phase2), (31, self.phase3a), (32, self.phase3b), (33, self.phase3c), (4, self.phase4), (5, self.phase5)):
                if ph in self.phases:
                    self.run_phase(fn)

            self.run_phase(lambda: None)
        return nc

    def _declare_io(self):
        self.xall = self.inp("xall", [T_ALL, D])
        self.w_in = self.inp("w_in", [D, INW])
        self.norm1_g = self.inp("norm1_g", [D])
        self.ident_in = self.inp("ident", [128, 128])
        self.okv = [self.outp("okv%d" % g, [WINS[g], 2, 256]) for g in range(3)]
        self.rel_bias = self.inp("rel_bias", [32, 12])
        self.bucket_oh = self.inp("bucket_oh", [32, 3 * 129])
        self.hv_in = self.inp("hv", [128, 1])
        self.antiident = self.inp("antiident", [128, 128])
        self.EXT = self.scr("EXT", [12, 385], F32)
        self.log_dt = self.inp("ssm_log_dt", [32])
        self.lam_re = self.inp("ssm_lambda_re", [32, 64])
        self.lam_im = self.inp("ssm_lambda_im", [32, 64])
        self.b_re = self.inp("ssm_b_re", [32, 64, 16])
        self.b_im = self.inp("ssm_b_im", [32, 64, 16])
        self.c_re = self.inp("ssm_c_re", [32, 16, 64])
        self.c_im = self.inp("ssm_c_im", [32, 16, 64])
        self.ssm_d = self.inp("ssm_d", [512])
        self.w_glu = self.inp("w_glu", [512, 512])
        self.b_glu = self.inp("b_glu", [512])
        self.w_ba = self.inp("w_branch_attn", [256, D])
        self.w_bs = self.inp("w_branch_ssm", [512, D])
        self.w_out = self.inp("w_out", [D, D])
        self.norm2_g = self.inp("norm2_g", [D])
        self.w_up = self.inp("w_up", [D, 2 * DFF])
        self.conv_w = self.inp("conv_w", [3, DFF])
        self.conv_b = self.inp("conv_b", [DFF])
        self.w_down = self.inp("w_down", [DFF, D])
        self.norm_f_g = self.inp("norm_f_g", [D])
        self.oy = self.outp("oy", [4096, D])
        self.ocv = self.outp("ocv", [2, DFF])
        if _DBG2:
            self.X1s = self.outp("X1s", [T_MAIN, D], F32)
        else:
            self.X1s = self.scr("X1s", [T_MAIN, D], F32)
        self.xsamp = self.inp("xsamp", [16, D])
        self.XN2s = self.scr("XN2s", [8, 128, T_MAIN], BF16)
        self.XN2ss = self.scr("XN2ss", [8, 128, 16], BF16)
        self.HBs = self.scr("HBs", [2, 128, 16, T_MAIN // LCH], BF16)
        self.st_conv = self.inp("st_conv", [4, 2, DFF])
        self.ocvs = self.outp("ocvs", [4, 2, DFF])
        self.oys = self.outp("oys", [16, D])
        self.X1ss = self.scr("X1ss", [16, D], F32)
        self.st_re = self.inp("st_re", [4, 32, 64])
        self.st_im = self.inp("st_im", [4, 32, 64])
        self.ossm_s_re = self.outp("ossm_s_re", [4, 32, 64])
        self.ossm_s_im = self.outp("ossm_s_im", [4, 32, 64])
        self.YSs = self.scr("YSs", [4, 128, 16], BF16)
        self.caches = [self.inp("cache%d" % g, [4, WINS[g], 512]) for g in range(3)]
        self.ATTs = self.scr("ATTs", [4, 64, 16], BF16)
        self.QTs = self.scr("QTs", [6, 128, 16], BF16)
        self.KTs = self.scr("KTs", [6, 128, 16], BF16)
        self.USs = self.scr("USs", [4, 128, 16], BF16)
        self.GSs = self.scr("GSs", [16, 128, 16], BF16)
        self.VSs = self.scr("VSs", [16, 768], BF16)
        self.okvs = [self.outp("okvs%d" % g, [16, 2, 256]) for g in range(3)]
        self.ossm_re = self.outp("ossm_re", [32, 64])
        self.ossm_im = self.outp("ossm_im", [32, 64])
        if _DBG2:
            self.YS = self.outp("YS", [4, 128, T_MAIN], BF16)
        else:
            self.YS = self.scr("YS", [4, 128, T_MAIN], BF16)
        self.VBs = self.scr("VBs", [2, 128, (LCH + 1) * 16 * 32], BF16)
        self.KBs = self.scr("KBs", [128, LCH, 512], BF16)
        self.WSs = self.scr("WSs", [2, 128, LCH, 512], BF16)
        if _DBG2:
            self.TAB = self.outp("TAB", [128, 4, 16], F32)
        else:
            self.TAB = self.scr("TAB", [128, 4, 16], F32)
        if _DBG2:
            self.ATT = self.outp("ATT", [4, 64, T_MAIN], BF16)
        else:
            self.ATT = self.scr("ATT", [4, 64, T_MAIN], BF16)
        self.QT = self.scr("QT", [6, 128, T_MAIN], BF16)
        self.KT = self.scr("KT", [6, 128, T_KV], BF16)
        self.VS = self.scr("VS", [T_KV, 768], BF16)
        self.US = self.scr("US", [4, 128, T_ALL], BF16)
        self.GS = self.scr("GS", [16, 128, T_MAIN], BF16)

    def _consts(self):
        S = self.S
        self.ident_f = self.sb("ident_f", [128, 128], F32, glob=True)
        self.ident = self.sb("ident", [128, 128], BF16, glob=True)
        self.b_ident = Buf("ident")
        S.dma("sp", self.ident_f[:], self.ident_in.ap(), self.b_ident, w=[self.b_ident])
        S.op("dve", lambda e: e.tensor_copy(self.ident[:], self.ident_f[:]),
             r=[self.b_ident], w=[self.b_ident])
        self.psb = [self.ps("psb%d" % i, [128, 512], F32) for i in range(6)]
        self.b_psb = [Buf("psb%d" % i, True) for i in range(6)]
        self.pst = [self.ps("pst%d" % i, [128, 1024], BF16) for i in range(2)]
        self.b_pst = [Buf("pst%d" % i, True) for i in range(2)]
        self.psi = 0

    def next_ps(self):
        i = self.psi % 6
        self.psi += 1
        return self.psb[i], self.b_psb[i]

    def load_weight_bf16(self, dst, dst_buf, src_dram, nk, ncols, gcol, stg, stg_bufs, nsplit=1, q="sp", row0=0,
                         col0=0):
        S = self.S
        cw = ncols // nsplit
        i = 0
        for kc in range(nk):
            for sp in range(nsplit):
                sbuf_t, sb_b = stg[i % 2], stg_bufs[i % 2]
                i += 1
                c0 = col0 + sp * cw
                src = src_dram.ap()[row0 + kc * 128:row0 + (kc + 1) * 128, c0:c0 + cw]
                S.dma(q, sbuf_t[:, 0:cw], src, sb_b, w=[sb_b])
                eng = "dve" if (i % 2 == 0) else "act"
                if gcol is not None:
                    gc, gb = gcol
                    if eng == "dve":
                        S.op("dve", lambda e, kc=kc, sbuf_t=sbuf_t, gc=gc, c0=c0: e.tensor_scalar(
                            dst[:, kc, c0:c0 + cw], sbuf_t[:, 0:cw], gc[:, kc:kc + 1], None, ALU.mult),
                            r=[sb_b, gb], w=[dst_buf])
                    else:
                        S.op("act", lambda e, kc=kc, sbuf_t=sbuf_t, gc=gc, c0=c0: e.activation(
                            dst[:, kc, c0:c0 + cw], sbuf_t[:, 0:cw], AF.Copy, scale=gc[:, kc:kc + 1]),
                            r=[sb_b, gb], w=[dst_buf])
                else:
                    if eng == "dve":
                        S.op("dve", lambda e, kc=kc, sbuf_t=sbuf_t, c0=c0: e.tensor_copy(
                            dst[:, kc, c0:c0 + cw], sbuf_t[:, 0:cw]), r=[sb_b], w=[dst_buf])
                    else:
                        S.op("act", lambda e, kc=kc, sbuf_t=sbuf_t, c0=c0: e.activation(
                            dst[:, kc, c0:c0 + cw], sbuf_t[:, 0:cw], AF.Copy), r=[sb_b], w=[dst_buf])

    def norm_transpose(self, xt, b_xt, nsub, rows, xnT, b_xnT, tok0=0):
        S = self.S
        ss, b_ss = self.ss, self.b_ss
        for s in range(nsub):
            S.op("act", lambda e, s=s: e.activation(
                self.junk[:rows, :], xt[:rows, s, :], AF.Square, accum_out=ss[:rows, s:s + 1]),
                r=[b_xt], w=[self.b_junk, b_ss])
        S.op("act", lambda e: e.activation(
            self.sq[:rows, 0:nsub], ss[:rows, 0:nsub], AF.Sqrt, bias=self.epsc[:rows, :], scale=1.0 / D),
            r=[b_ss, self.b_epsc], w=[self.b_sq])
        S.op("dve", lambda e: e.reciprocal(self.rstd[:rows, 0:nsub], self.sq[:rows, 0:nsub]),
             r=[self.b_sq], w=[self.b_rstd])
        for s in range(nsub):
            S.op("dve", lambda e, s=s: e.tensor_scalar(
                self.xs[:rows, s, :], xt[:rows, s, :], self.rstd[:rows, s:s + 1], None, ALU.mult),
                r=[b_xt, self.b_rstd], w=[self.b_xs[s]])
        for s in range(nsub):
            pt, b_pt = self.pst[s % 2], self.b_pst[s % 2]
            for kc in range(8):
                S.op("pe", lambda e, s=s, kc=kc, pt=pt: e.transpose(
                    pt[:, kc * 128:kc * 128 + rows], self.xs[:rows, s, kc * 128:(kc + 1) * 128],
                    self.ident[:rows, :rows]),
                    r=[self.b_xs[s], self.b_ident], w=[b_pt], inc=(kc == 7))
            S.op("act", lambda e, s=s, pt=pt: e.activation(
                xnT[:, :, tok0 + s * 128:tok0 + s * 128 + rows],
                pt[:, :].rearrange("p (k t) -> p k t", k=8)[:, :, 0:rows], AF.Copy),
                r=[b_pt], w=[b_xnT])

    def phase1(self):
        S = self.S
        sb = self.sb
        self.g1c = sb("g1c", [128, 8], F32)
        self.b_g1c = Buf("g1c")
        S.dma("sp", self.g1c[:], dap(self.norm1_g, 0, [[1, 128], [128, 8]]), self.b_g1c,
              w=[self.b_g1c], allow_slow_non_contiguous=True)
        self.epsc = sb("epsc", [128, 1], F32)
        self.b_epsc = Buf("epsc")
        S.op("dve", lambda e: e.memset(self.epsc[:], EPS), w=[self.b_epsc])
        Wi = sb("Wi", [128, 8, INW], BF16)
        b_Wi = Buf("Wi")
        b_Wu = Buf("Wu")
        stg = [sb("wstg%d" % i, [128, INW // 2], F32) for i in range(2)]
        b_stg = [Buf("wstg%d" % i) for i in range(2)]
        if _DBG >= 1:
            self.load_weight_bf16(Wi, b_Wu, self.w_in, 8, 512, (self.g1c, self.b_g1c), stg, b_stg, col0=2304)
            self.load_weight_bf16(Wi, b_Wi, self.w_in, 8, 2304, (self.g1c, self.b_g1c), stg, b_stg, col0=0)
            self.load_weight_bf16(Wi, b_Wi, self.w_in, 8, 2048, (self.g1c, self.b_g1c), stg, b_stg, col0=2816)
        self.junk = sb("junk", [128, D], BF16)
        self.b_junk = Buf("junk")
        self.ss = sb("ss", [128, 4], F32)
        self.b_ss = Buf("ss")
        self.sq = sb("sq", [128, 4], F32)
        self.b_sq = Buf("sq")
        self.rstd = sb("rstd", [128, 4], F32)
        self.b_rstd = Buf("rstd")
        self.xs = sb("xs", [128, 3, D], BF16)
        self.b_xs = [Buf("xs%d" % i) for i in range(3)]
        xt = [sb("xt%d" % i, [128, 3, D], F32) for i in range(2)]
        b_xt = [Buf("xt%d" % i) for i in range(2)]
        xnT = [sb("xnT%d" % i, [128, 8, TN], BF16) for i in range(2)]
        b_xnT = [Buf("xnT%d" % i) for i in range(2)]
        fst = [sb("fst%d" % i, [128, 4, TN], BF16) for i in range(2)]
        b_fst = [Buf("fst%d" % i) for i in range(2)]
        vst = [sb("vst%d" % i, [128, 768], BF16) for i in range(2)]
        b_vst = [Buf("vst%d" % i) for i in range(2)]
        kvst = [sb("kvst%d" % i, [128, 2, 768], F32) for i in range(2)]
        b_kvst = [Buf("kvst%d" % i) for i in range(2)]
        b_okv = [Buf("okv%d" % g) for g in range(3)]
        self.outbufs += b_kvst
        fcount = 0
        vcount = 0

        def load_x(ti):
            S.dma("sp", xt[ti % 2][:],
                  self.xall.ap()[ti * TN:(ti + 1) * TN, :].rearrange("(s p) d -> p s d", p=128),
                  b_xt[ti % 2], w=[b_xt[ti % 2]])

        load_x(_TR[0])
        if _TR[0] + 1 < _TR[1]:
            load_x(_TR[0] + 1)
        self.norm_transpose(xt[_TR[0] % 2], b_xt[_TR[0] % 2], 3, 128, xnT[_TR[0] % 2], b_xnT[_TR[0] % 2])
        for ti in range(_TR[0], _TR[1] if _DBG >= 2 else _TR[0]):
            xn, b_xn = xnT[ti % 2], b_xnT[ti % 2]
            did_next = False

            def prep_next(ti=ti):
                if ti + 1 < _TR[1]:
                    if ti + 2 < _TR[1]:
                        load_x(ti + 2)
                    self.norm_transpose(xt[(ti + 1) % 2], b_xt[(ti + 1) % 2], 3, 128,
                                        xnT[(ti + 1) % 2], b_xnT[(ti + 1) % 2])
            is_main = ti >= 11
            has_kv = ti >= 5
            if _DBG < 3:
                continue
            fm = []
            for m in range(4):
                fm.append((2304 + m * 128, self.US, m, ti * TN))
            if has_kv:
                for m in range(6):
                    fm.append((768 + m * 128, self.KT, m, ti * TN - KV_T0))
            if is_main:
                for m in range(6):
                    fm.append((m * 128, self.QT, m, ti * TN - T_MAIN0))
                for m in range(16):
                    fm.append((2816 + m * 128, self.GS, m, ti * TN - T_MAIN0))
            i = 0
            while i < len(fm):
                if not did_next and i >= len(fm) // 2:
                    prep_next()
                    did_next = True
                grp = [fm[i]]
                while len(grp) < 4 and i + len(grp) < len(fm) and fm[i + len(grp)][1] is grp[0][1]:
                    grp.append(fm[i + len(grp)])
                st, b_st = fst[fcount % 2], b_fst[fcount % 2]
                fcount += 1
                for j, (c0, dst, m, t0) in enumerate(grp):
                    pt, b_pt = self.next_ps()
                    for kc in range(8):
                        S.op("pe", lambda e, kc=kc, c0=c0, pt=pt, xn=xn: e.matmul(
                            pt[:, 0:TN], Wi[:, kc, c0:c0 + 128], xn[:, kc, :],
                            start=(kc == 0), stop=(kc == 7)),
                            r=[b_Wu if dst is self.US else b_Wi, b_xn], w=[b_pt], inc=(kc == 7))
                    eng = "act" if (j % 2 == 0) else "dve"
                    if dst is self.US:
                        o_ap = st[:, j, :].rearrange("p (t c) -> p c t", t=LCH)
                        i_ap = pt[:, 0:TN].rearrange("p (c t) -> p c t", t=LCH)
                    else:
                        o_ap, i_ap = st[:, j, :], pt[:, 0:TN]
                    if eng == "act":
                        S.op("act", lambda e, o_ap=o_ap, i_ap=i_ap: e.activation(o_ap, i_ap, AF.Copy),
                             r=[b_pt], w=[b_st])
                    else:
                        S.op("dve", lambda e, o_ap=o_ap, i_ap=i_ap: e.tensor_copy(o_ap, i_ap),
                             r=[b_pt], w=[b_st])
                c0, dst, m0, t0 = grp[0]
                n = len(grp)
                if not (os.environ.get("K_NOGS") and (dst is self.GS or dst is self.QT)):
                    S.dma("sp", dst.ap()[m0:m0 + n, :, t0:t0 + TN].rearrange("m p t -> p m t"),
                          st[:, 0:n, :], b_st, r=[b_st])
                i += n
            if not did_next:
                prep_next()
                did_next = True
            if has_kv and _DBG >= 4:
                need_kout = (ti + 1) * TN > T_ALL - 2048
                for s in range(3):
                    tok = ti * TN + s * 128
                    kout = tok >= T_ALL - 2048
                    vt, b_vt = vst[vcount % 2], b_vst[vcount % 2]
                    kt, b_kt = kvst[vcount % 2], b_kvst[vcount % 2]
                    vcount += 1
                    for kv in ((0, 1) if kout else (1,)):
                        for (cc, nn) in ((0, 512), (512, 256)):
                            c0 = 768 * (1 + kv) + cc
                            pt, b_pt = self.next_ps()
                            for kc in range(8):
                                S.op("pe", lambda e, kc=kc, c0=c0, nn=nn, pt=pt, xn=xn, s=s: e.matmul(
                                    pt[:, 0:nn], xn[:, kc, s * 128:(s + 1) * 128], Wi[:, kc, c0:c0 + nn],
                                    start=(kc == 0), stop=(kc == 7)),
                                    r=[b_Wi, b_xn], w=[b_pt], inc=(kc == 7))
                            S.op("dve", lambda e, cc=cc, nn=nn, pt=pt, kt=kt, kv=kv: e.tensor_copy(
                                kt[:, kv, cc:cc + nn], pt[:, 0:nn]), r=[b_pt], w=[b_kt])
                    S.op("act", lambda e, kt=kt, vt=vt: e.activation(
                        vt[:, :], kt[:, 1, :], AF.Copy), r=[b_kt], w=[b_vt])
                    if _DBG >= 5:
                        S.dma("sp", self.VS.ap()[tok - KV_T0:tok - KV_T0 + 128, :], vt[:, :], b_vt, r=[b_vt])
                    if kout and _DBG >= 6:
                        for g in range(3):
                            w0 = T_ALL - WINS[g]
                            if tok >= w0:
                                S.dma("sp", self.okv[g].ap()[tok - w0:tok - w0 + 128, :, :],
                                      kt[:, :, 256 * g:256 * (g + 1)], b_kt, r=[b_kt])


        NS = 16
        xs_t = sb("xsamp", [128, 1, D], F32)
        b_xs_t = Buf("xsamp")
        S.dma("sp", xs_t[0:NS, 0, :], self.xsamp.ap(), b_xs_t, w=[b_xs_t])
        xnS = sb("xnS", [128, 8, NS], BF16)
        b_xnS = Buf("xnS")
        self.norm_transpose(xs_t, b_xs_t, 1, NS, xnS, b_xnS)
        fsS = sb("fsS", [128, 32, NS], BF16)
        b_fsS = Buf("fsS")
        fm = [(2304 + m * 128, self.USs, m) for m in range(4)] + [(768 + m * 128, self.KTs, m) for m in range(6)] \
            + [(m * 128, self.QTs, m) for m in range(6)] + [(2816 + m * 128, self.GSs, m) for m in range(16)]
        for j, (c0, dst, m) in enumerate(fm):
            pt, b_pt = self.next_ps()
            for kc in range(8):
                S.op("pe", lambda e, kc=kc, c0=c0, pt=pt: e.matmul(
                    pt[:, 0:NS], Wi[:, kc, c0:c0 + 128], xnS[:, kc, :], start=(kc == 0), stop=(kc == 7)),
                    r=[b_Wi, b_Wu, b_xnS], w=[b_pt], inc=(kc == 7))
            S.op("act", lambda e, j=j, pt=pt: e.activation(fsS[:, j, :], pt[:, 0:NS], AF.Copy), r=[b_pt], w=[b_fsS])
        for (j0, n, dst) in ((0, 4, self.USs), (4, 6, self.KTs), (10, 6, self.QTs), (16, 16, self.GSs)):
            S.dma("sp", dst.ap().rearrange("m p t -> p m t"), fsS[:, j0:j0 + n, :], b_fsS, r=[b_fsS])
        kvS = sb("kvS", [128, 2, 768], F32)
        vbS = sb("vbS", [128, 768], BF16)
        b_kvS = Buf("kvS")
        for kv in range(2):
            for (cc, nn) in ((0, 512), (512, 256)):
                c0 = 768 * (1 + kv) + cc
                pt, b_pt = self.next_ps()
                for kc in range(8):
                    S.op("pe", lambda e, kc=kc, c0=c0, nn=nn, pt=pt: e.matmul(
                        pt[0:NS, 0:nn], xnS[:, kc, :], Wi[:, kc, c0:c0 + nn], start=(kc == 0), stop=(kc == 7)),
                        r=[b_Wi, b_xnS], w=[b_pt], inc=(kc == 7))
                S.op("dve", lambda e, cc=cc, nn=nn, pt=pt, kv=kv: e.tensor_copy(
                    kvS[0:NS, kv, cc:cc + nn], pt[0:NS, 0:nn]), r=[b_pt], w=[b_kvS])
        S.op("act", lambda e: e.activation(vbS[0:NS, :], kvS[0:NS, 1, :], AF.Copy), r=[b_kvS], w=[b_kvS])
        S.dma("sp", self.VSs.ap(), vbS[0:NS, :], b_kvS, r=[b_kvS])
        for g in range(3):
            S.dma("sp", self.okvs[g].ap().rearrange("t k c -> t k c"), kvS[0:NS, :, 256 * g:256 * (g + 1)], b_kvS, r=[b_kvS])

    def build_expbias(self):
        S, sb = self.S, self.sb
        EB = sb("EB", [128, 12, 2, 128], F32)
        self.EB, self.b_EB = EB, Buf("EB")
        rb = sb("rb", [32, 12], F32)
        oh = sb("oh", [32, 3 * 129], F32)
        b_rb = Buf("rb")
        S.dma("sp", rb[:], self.rel_bias.ap(), b_rb, w=[b_rb])
        S.dma("sp", oh[:], self.bucket_oh.ap(), b_rb, w=[b_rb])
        ebx = sb("ebx", [12, 3, 129], F32)
        b_ebx = Buf("ebx")
        zt = sb("zt", [12, 385], F32)
        b_zt = Buf("zt")
        S.op("dve", lambda e: e.memset(zt[:], 0.0), w=[b_zt])
        b_ext = Buf("ext")
        S.dma("sp", self.EXT.ap(), zt[:], b_zt, r=[b_zt], w=[b_ext])
        pt, b_pt = self.next_ps()
        S.op("pe", lambda e: e.matmul(pt[0:12, 0:387], rb[:, :], oh[:, :], start=True, stop=True),
             r=[b_rb], w=[b_pt])
        S.op("act", lambda e: e.activation(ebx[:, :, :].rearrange("p g j -> p (g j)"), pt[0:12, 0:387], AF.Exp),
             r=[b_pt], w=[b_ebx])
        for g in range(3):
            S.dma("sp", self.EXT.ap()[4 * g:4 * g + 4, 128:257], ebx[4 * g:4 * g + 4, g, :], b_ebx,
                  r=[b_ebx], w=[b_ext])
        TH = sb("TH", [128, 12, 2, 128], F32)
        b_TH = Buf("TH")
        aid = sb("aid", [128, 128], F32)
        b_aid = Buf("aid")
        S.dma("sp", aid[:], self.antiident.ap(), b_aid, w=[b_aid])
        for gh in range(12):
            for bi in range(2):
                S.dma("sp", TH[:, gh, bi, :], dap(self.EXT, gh * 385 + 129 - 128 * bi, [[1, 128], [1, 128]]),
                      b_TH, r=[b_ext], w=[b_TH])
        for gh in range(12):
            pt, b_pt = self.next_ps()
            S.op("pe", lambda e, gh=gh, pt=pt: e.matmul(
                pt[:, 0:256], aid[:, :], TH[:, gh, :, :].rearrange("p b q -> p (b q)"), start=True, stop=True),
                r=[b_aid, b_TH], w=[b_pt])
            S.op("act", lambda e, gh=gh, pt=pt: e.activation(
                EB[:, gh, :, :].rearrange("p b q -> p (b q)"), pt[:, 0:256], AF.Copy),
                r=[b_pt], w=[self.b_EB])

    def attn_unit(self, kp_ap, kc_ap, q_ap, vp_ap, vc_ap, nq, gh, acc_ap, b_acc, first, rb):
        S = self.S
        ps_s, b_ps = self.next_ps()
        nb = len(self.Ebuf)
        E, b_E = self.Ebuf[self.ucnt % nb], self.b_Ebuf[self.ucnt % nb]
        P, b_P = self.Pbuf[self.ucnt % nb], self.b_Pbuf[self.ucnt % nb]
        self.ucnt += 1
        S.op("pe", lambda e: e.matmul(ps_s[:, 0:nq], kp_ap, q_ap, start=True, stop=True),
             r=rb, w=[b_ps], inc=False)
        S.op("pe", lambda e: e.matmul(ps_s[0:nq, 128:128 + nq], kc_ap, q_ap, start=True, stop=True),
             r=rb, w=[b_ps])
        psv = ps_s[:, 0:256].rearrange("p (b q) -> p b q", b=2)
        if nq == 128:
            S.op("act", lambda e: e.activation(E[:, :, :], psv, AF.Exp, scale=0.125), r=[b_ps], w=[b_E])
            S.op("dve", lambda e: e.tensor_tensor(P[:, :, :], E[:, :, :], self.EB[:, gh, :, :], ALU.mult),
                 r=[b_E, self.b_EB], w=[b_P])
        else:
            S.op("act", lambda e: e.activation(E[:, 0, 0:nq], ps_s[:, 0:nq], AF.Exp, scale=0.125),
                 r=[b_ps], w=[b_E])
            S.op("act", lambda e: e.activation(E[0:nq, 1, 0:nq], ps_s[0:nq, 128:128 + nq], AF.Exp, scale=0.125),
                 r=[b_ps], w=[b_E])
            S.op("dve", lambda e: e.tensor_tensor(P[:, 0, 0:nq], E[:, 0, 0:nq], self.EB[:, gh, 0, 0:nq], ALU.mult),
                 r=[b_E, self.b_EB], w=[b_P])
            S.op("dve", lambda e: e.tensor_tensor(P[0:nq, 1, 0:nq], E[0:nq, 1, 0:nq], self.EB[0:nq, gh, 1, 0:nq],
                                                  ALU.mult), r=[b_E, self.b_EB], w=[b_P])

        def stage_b():
            ps_o, b_po = self.next_ps()
            S.op("pe", lambda e: e.matmul(ps_o[0:65, 0:nq], vp_ap, P[:, 0, 0:nq], start=True, stop=False),
                 r=rb + [b_P], w=[b_po], inc=False)
            S.op("pe", lambda e: e.matmul(ps_o[0:65, 0:nq], vc_ap, P[0:nq, 1, 0:nq], start=False, stop=True),
                 r=rb + [b_P], w=[b_po])
            if first:
                S.op("dve", lambda e: e.tensor_copy(acc_ap, ps_o[0:65, 0:nq]), r=[b_po], w=[b_acc])
            else:
                S.op("dve", lambda e: e.tensor_tensor(acc_ap, acc_ap, ps_o[0:65, 0:nq], ALU.add),
                     r=[b_po], w=[b_acc])
        self.pending_b.append(stage_b)
        while len(self.pending_b) > self.attn_depth:
            self.pending_b.pop(0)()

    def attn_flush(self):
        while self.pending_b:
            self.pending_b.pop(0)()

    def sample_attention(self, sel, b_sel, rec, b_rec):
        S, sb = self.S, self.sb
        KnT = sb("KnT", [128, 6, 16], BF16)
        QnT = sb("QnT", [128, 6, 16], BF16)
        b_kq = Buf("knq")
        S.dma("sp", KnT[:], self.KTs.ap().rearrange("m p t -> p m t"), b_kq, w=[b_kq])
        S.dma("sp", QnT[:], self.QTs.ap().rearrange("m p t -> p m t"), b_kq, w=[b_kq])
        accS = sb("accS", [65, 4, 16], F32)
        b_accS = Buf("accS")
        CK = [sb("CK%d" % i, [128, 512], F32) for i in range(2)]
        Kb = [sb("Kb16_%d" % i, [128, 256], BF16) for i in range(2)]
        KcT = [sb("KcT%d" % i, [128, 2, 128], BF16) for i in range(2)]
        VcP = [sb("VcP%d" % i, [128, 4, 65], BF16) for i in range(2)]
        VnC = [sb("VnC%d" % i, [4, 4, 65], BF16) for i in range(2)]
        b_CK = [Buf("CK%d" % i) for i in range(2)]
        b_Kb = [Buf("Kb16_%d" % i) for i in range(2)]
        b_KcT = [Buf("KcT%d" % i) for i in range(2)]
        b_VcP = [Buf("VcP%d" % i) for i in range(2)]
        b_VnC = [Buf("VnC%d" % i) for i in range(2)]
        for i in range(2):
            S.op("pool", lambda e, i=i: e.memset(VcP[i][:, :, 64:65], 1.0), w=[b_VcP[i]])
            S.op("pool", lambda e, i=i: e.memset(VnC[i][:, :, 64:65], 1.0), w=[b_VnC[i]])
        bi = 0
        for s_ in range(4):
            for g in range(3):
                d, W = DILS[g], WINS[g]
                blocks = [(0, 4)] if g == 0 else [(t, 1) for t in range(4)]
                for (t0, nq) in blocks:
                    i = bi % 2
                    bi += 1
                    row0 = 0 if g == 0 else t0
                    S.dma("sp", CK[i][:, :], dap(self.caches[g], (s_ * W + row0) * 512, [[d * 512, 128], [1, 512]]),
                          b_CK[i], w=[b_CK[i]])
                    S.op("dve", lambda e, i=i: e.tensor_copy(Kb[i][:, :], CK[i][:, 0:256]), r=[b_CK[i]], w=[b_Kb[i]])
                    S.op("pool", lambda e, i=i: e.tensor_copy(
                        VcP[i][:, :, 0:64], CK[i][:, 256:512].rearrange("p (h e) -> p h e", h=4)),
                        r=[b_CK[i]], w=[b_VcP[i]])
                    pt, b_pt = self.pst[i], self.b_pst[i]
                    for pair in range(2):
                        S.op("pe", lambda e, pt=pt, pair=pair, i=i: e.transpose(
                            pt[:, pair * 128:(pair + 1) * 128], Kb[i][:, pair * 128:(pair + 1) * 128], self.ident[:, :]),
                            r=[b_Kb[i], self.b_ident], w=[b_pt], inc=(pair == 1))
                    S.op("act", lambda e, pt=pt, i=i: e.activation(
                        KcT[i][:, :, :], pt[:, 0:256].rearrange("p (a k) -> p a k", a=2), AF.Copy),
                        r=[b_pt], w=[b_KcT[i]])
                    tk0 = 4 * s_ + t0
                    S.dma("sp", VnC[i][0:nq, :, 0:64],
                          dap(self.VSs, tk0 * 768 + 256 * g, [[768, nq], [64, 4], [1, 64]]), b_VnC[i], w=[b_VnC[i]])
                    for h in range(4):
                        pair, hh = h // 2, h % 2
                        rw = slice(64 * hh, 64 * hh + 64)
                        self.attn_unit(KcT[i][rw, pair, :], KnT[rw, 2 * g + pair, tk0:tk0 + nq],
                                       QnT[rw, 2 * g + pair, tk0:tk0 + nq], VcP[i][:, h, :], VnC[i][0:nq, h, :],
                                       nq, 4 * g + h, accS[:, h, tk0:tk0 + nq], b_accS, g == 0,
                                       [b_KcT[i], b_kq, b_VcP[i], b_VnC[i]])
        self.attn_flush()
        aS = sb("aS", [64, 4, 16], BF16)
        b_aS = Buf("aS")
        for h in range(4):
            pt, b_pt = self.next_ps()
            S.op("pe", lambda e, pt=pt, h=h: e.matmul(pt[0:64, 0:16], sel[:, :], accS[:, h, :], start=True, stop=True),
                 r=[b_sel, b_accS], w=[b_pt])
            S.op("dve", lambda e, pt=pt: e.tensor_scalar(rec[:, 0:16], pt[0:64, 0:16], 1e-30, None, ALU.max),
                 r=[b_pt], w=[b_rec])
            S.op("dve", lambda e: e.reciprocal(rec[:, 0:16], rec[:, 0:16]), r=[b_rec], w=[b_rec])
            S.op("dve", lambda e, h=h: e.tensor_tensor(aS[:, h, :], accS[0:64, h, :], rec[:, 0:16], ALU.mult),
                 r=[b_accS, b_rec], w=[b_aS])
        S.dma("sp", self.ATTs.ap().rearrange("h p t -> p h t"), aS[:, :, :], b_aS, r=[b_aS])


    def phase2(self):
        S, sb = self.S, self.sb
        self.build_expbias()
        ast = [sb("ast%d" % i, [64, 512], BF16) for i in range(2)]
        b_ast = [Buf("ast%d" % i) for i in range(2)]
        acnt = 0
        hv = sb("hv", [128, 1], F32)
        b_hv = Buf("hv")
        S.dma("sp", hv[:], self.hv_in.ap(), b_hv, w=[b_hv])
        sel = sb("sel", [65, 64], F32)
        b_sel = Buf("sel")
        S.op("dve", lambda e: e.memset(sel[:], 0.0), w=[b_sel])
        S.op("dve", lambda e: e.memset(sel[64:65, :], 1.0), w=[b_sel])
        NEP = _ADEPTH + 1
        self.Ebuf = [sb("E%d" % i, [128, 2, 128], F32) for i in range(NEP)]
        self.b_Ebuf = [Buf("E%d" % i) for i in range(NEP)]
        self.Pbuf = [sb("P%d" % i, [128, 2, 128], BF16) for i in range(NEP)]
        self.b_Pbuf = [Buf("P%d" % i) for i in range(NEP)]
        self.ucnt = 0
        self.pending_b = []
        self.attn_depth = _ADEPTH
        acc = sb("acc", [65, 2, T_MAIN], F32)
        b_acc = Buf("acc")
        NBLK = 3 * 16 + 2 * 16
        Kb = [sb("Kb%d" % i, [128, T_KV], BF16) for i in range(2)]
        Qb = [sb("Qb%d" % i, [128, T_MAIN], BF16) for i in range(2)]
        Vb = [sb("Vb%d" % i, [128, NBLK, 2, 65], BF16) for i in range(2)]
        b_Kb = [Buf("Kb%d" % i) for i in range(2)]
        b_Qb = [Buf("Qb%d" % i) for i in range(2)]
        b_Vb = [Buf("Vb%d" % i) for i in range(2)]
        for i in range(2):
            S.op("pool", lambda e, i=i: e.memset(Vb[i][:, :, :, :], 0.0), w=[b_Vb[i]])
        rec = sb("rec", [64, 512], F32)
        b_rec = Buf("rec")
        H0 = T_MAIN0 + 128
        li = 0
        for pair in range(2):
            for g in range(3):
                d = DILS[g]
                NB = 4096 // (128 * d)
                nqh = 128 // d
                K_, Q_, V_ = Kb[li % 2], Qb[li % 2], Vb[li % 2]
                bK, bQ, bV = b_Kb[li % 2], b_Qb[li % 2], b_Vb[li % 2]
                li += 1
                mt = 2 * g + pair
                S.dma("sp", K_[:, :], self.KT.ap()[mt, :, :], bK, w=[bK])
                S.dma("sp", Q_[:, :], self.QT.ap()[mt, :, :], bQ, w=[bQ])
                nblk = 3 * d + NB * d
                S.op("pool", lambda e, V_=V_, nblk=nblk: e.memset(V_[:, 0:nblk, :, 64:65], 1.0), w=[bV])
                colb = 256 * g + 128 * pair
                for ty, (lt0, npart) in enumerate(((H0 - 128 * d, 128), (T_MAIN0 - 128 * d, 128), (T_MAIN0, nqh))):
                    S.dma("sp", V_[0:npart, ty * d:(ty + 1) * d, :, 0:64],
                          dap(self.VS, (lt0 - KV_T0) * 768 + colb, [[d * 768, npart], [768, d], [64, 2], [1, 64]]),
                          bV, w=[bV])
                for n in range(NB):
                    S.dma("sp", V_[:, 3 * d + n * d:3 * d + (n + 1) * d, :, 0:64],
                          dap(self.VS, (H0 + 128 * n * d - KV_T0) * 768 + colb,
                              [[d * 768, 128], [768, d], [64, 2], [1, 64]]), bV, w=[bV])
                S.op("dve", lambda e, V_=V_, d=d: e.tensor_scalar(
                    V_[:, 0:3 * d, :, :], V_[:, 0:3 * d, :, :], hv[:, 0:1], None, ALU.mult),
                    r=[b_hv], w=[bV])
                for hh in range(2):
                    gh = 4 * g + 2 * pair + hh
                    rw = slice(64 * hh, 64 * hh + 64)
                    rb = [bK, bQ, bV]

                    def cs(st, n, d=d):
                        return slice(st, st + (n - 1) * d + 1, d)
                    for r in range(d):
                        self.attn_unit(K_[rw, cs(T_MAIN0 - 128 * d + r - KV_T0, 128)],
                                       K_[rw, cs(T_MAIN0 + r - KV_T0, nqh)], Q_[rw, cs(r, nqh)],
                                       V_[:, 1 * d + r, hh, :], V_[0:nqh, 2 * d + r, hh, :], nqh, gh,
                                       acc[:, hh, cs(r, nqh)], b_acc, g == 0, rb)
                        for n in range(NB):
                            kp = H0 + 128 * (n - 1) * d + r - KV_T0
                            kc = H0 + 128 * n * d + r - KV_T0
                            q0 = 128 + 128 * n * d + r
                            vp = (0 * d + r) if n == 0 else (3 * d + (n - 1) * d + r)
                            vc = 3 * d + n * d + r
                            self.attn_unit(K_[rw, cs(kp, 128)], K_[rw, cs(kc, 128)], Q_[rw, cs(q0, 128)],
                                           V_[:, vp, hh, :], V_[:, vc, hh, :], 128, gh,
                                           acc[:, hh, cs(q0, 128)], b_acc, g == 0, rb)
            self.attn_flush()
            for hh in range(2):
                h = 2 * pair + hh
                for c0 in range(0, T_MAIN, 512):
                    n = min(512, T_MAIN - c0)
                    pt, b_pt = self.next_ps()
                    S.op("pe", lambda e, pt=pt, hh=hh, c0=c0, n=n: e.matmul(
                        pt[0:64, 0:n], sel[:, :], acc[:, hh, c0:c0 + n], start=True, stop=True),
                        r=[b_sel, b_acc], w=[b_pt])
                    S.op("dve", lambda e, pt=pt, n=n: e.tensor_scalar(
                        rec[:, 0:n], pt[0:64, 0:n], 1e-18, None, ALU.max), r=[b_pt], w=[b_rec])
                    S.op("act", lambda e, n=n: e.activation(rec[:, 0:n], rec[:, 0:n], AF.Ln), r=[b_rec], w=[b_rec])
                    S.op("act", lambda e, n=n: e.activation(rec[:, 0:n], rec[:, 0:n], AF.Exp, scale=-1.0),
                         r=[b_rec], w=[b_rec])
                    a_, b_a = ast[acnt % 2], b_ast[acnt % 2]
                    acnt += 1
                    S.op("dve", lambda e, a_=a_, hh=hh, c0=c0, n=n: e.tensor_tensor(
                        a_[:, 0:n], acc[0:64, hh, c0:c0 + n], rec[:, 0:n], ALU.mult),
                        r=[b_acc, b_rec], w=[b_a])
                    S.dma("sp", self.ATT.ap()[h, :, c0:c0 + n], a_[:, 0:n], b_a, r=[b_a])
        self.sample_attention(sel, b_sel, rec, b_rec)


    def cmul(self, eng, out_r, out_i, ar, ai, br, bi, t, bufs_r, bufs_w):
        S = self.S
        t1, t2 = t
        S.op(eng, lambda e: e.tensor_tensor(t1, ar, br, ALU.mult), r=bufs_r, w=[self.b_ct])
        S.op(eng, lambda e: e.tensor_tensor(t2, ai, bi, ALU.mult), r=bufs_r, w=[self.b_ct])
        S.op(eng, lambda e: e.tensor_tensor(out_r, t1, t2, ALU.subtract), r=[self.b_ct], w=bufs_w)
        S.op(eng, lambda e: e.tensor_tensor(t1, ar, bi, ALU.mult), r=bufs_r + bufs_w, w=[self.b_ct])
        S.op(eng, lambda e: e.tensor_tensor(t2, ai, br, ALU.mult), r=bufs_r, w=[self.b_ct])
        S.op(eng, lambda e: e.tensor_tensor(out_i, t1, t2, ALU.add), r=[self.b_ct], w=bufs_w)

    def phase3a(self):
        S, sb = self.S, self.sb
        L = LCH
        b_in = Buf("ssm_in")
        LR = sb("LR", [128, 16], F32)
        LI = sb("LI", [128, 16], F32)
        LDT = sb("LDT", [128, 16], F32)
        BR = sb("BR", [128, 16, 16], F32)
        BI = sb("BI", [128, 16, 16], F32)
        CR = sb("CR", [128, 16, 16], F32)
        CI = sb("CI", [128, 16, 16], F32)
        Dc = sb("Dc", [128, 4], F32)
        S.dma("sp", LR[:], dap(self.lam_re, 0, [[1, 128], [128, 16]]), b_in, w=[b_in], allow_slow_non_contiguous=True)
        S.dma("sp", LI[:], dap(self.lam_im, 0, [[1, 128], [128, 16]]), b_in, w=[b_in], allow_slow_non_contiguous=True)
        for j in range(2):
            S.dma("sp", LDT[64 * j:64 * j + 64, :], dap(self.log_dt, j, [[0, 64], [2, 16]]), b_in, w=[b_in],
                  allow_slow_non_contiguous=True)
        S.dma("sp", BR[:], dap(self.b_re, 0, [[16, 128], [2048, 16], [1, 16]]), b_in, w=[b_in])
        S.dma("sp", BI[:], dap(self.b_im, 0, [[16, 128], [2048, 16], [1, 16]]), b_in, w=[b_in])
        b_cn = Buf("Cnat")
        for nm, src, dstC in (("r", self.c_re, CR), ("i", self.c_im, CI)):
            CTf = sb("CTf" + nm, [16, 32, 64], F32)
            CTb = sb("CTb" + nm, [16, 32, 64], BF16)
            S.dma("sp", CTf[:], dap(src, 0, [[64, 16], [1024, 32], [1, 64]]), b_cn, w=[b_cn])
            S.op("dve", lambda e, CTf=CTf, CTb=CTb: e.tensor_copy(CTb[:], CTf[:]), r=[b_cn], w=[b_cn])
            pt, b_pt = self.next_ps()
            for pr in range(16):
                for j in range(2):
                    S.op("pe", lambda e, pt=pt, pr=pr, j=j, CTb=CTb: e.matmul(
                        pt[64 * j:64 * j + 64, 16 * pr:16 * pr + 16], CTb[0:16, 2 * pr + j, :],
                        self.ident[0:16, 0:16], start=True, stop=True),
                        r=[b_cn, self.b_ident], w=[b_pt], inc=(pr == 15 and j == 1))
            S.op("act", lambda e, pt=pt, dstC=dstC: e.activation(
                dstC[:].rearrange("p r c -> p (r c)"), pt[:, 0:256], AF.Copy), r=[b_pt], w=[b_in])
        S.dma("sp", Dc[:], dap(self.ssm_d, 0, [[1, 128], [128, 4]]), b_in, w=[b_in], allow_slow_non_contiguous=True)
        halfpi = sb("halfpi", [128, 1], F32)
        b_w = Buf("ssm_work")
        self.b_ct = Buf("ct")
        S.op("dve", lambda e: e.memset(halfpi[:], math.pi / 2), w=[b_w])
        dt = sb("dt", [128, 16], F32)
        S.op("act", lambda e: e.activation(dt[:], LDT[:], AF.Exp), r=[b_in], w=[b_w])
        t1 = sb("t1", [128, 16], F32)
        t2 = sb("t2", [128, 16], F32)
        t3 = sb("t3", [128, 16], F32)
        ar = sb("ar", [128, 16], F32)
        ai = sb("ai", [128, 16], F32)
        wr = sb("wr", [128, 16], F32)
        wi = sb("wi", [128, 16], F32)
        zr = sb("zr", [128, 16], F32)
        zi = sb("zi", [128, 16], F32)
        pr_ = sb("pr_", [128, 16], F32)
        pi_ = sb("pi_", [128, 16], F32)
        qr_ = sb("qr_", [128, 16], F32)
        qi_ = sb("qi_", [128, 16], F32)
        MSQ = 8
        S.op("dve", lambda e: e.scalar_tensor_tensor(zr[:], LR[:], 1.0 / (1 << MSQ), dt[:], ALU.mult, ALU.mult),
             r=[b_in, b_w], w=[b_w])
        S.op("dve", lambda e: e.scalar_tensor_tensor(zi[:], LI[:], 1.0 / (1 << MSQ), dt[:], ALU.mult, ALU.mult),
             r=[b_in, b_w], w=[b_w])
        S.op("dve", lambda e: e.tensor_scalar(pr_[:], zr[:], 1.0 / 5, 1.0, ALU.mult, ALU.add), r=[b_w], w=[b_w])
        S.op("dve", lambda e: e.tensor_scalar(pi_[:], zi[:], 1.0 / 5, None, ALU.mult), r=[b_w], w=[b_w])
        for dv in (4.0, 3.0, 2.0):
            self.cmul("dve", qr_[:], qi_[:], zr[:], zi[:], pr_[:], pi_[:], (t1[:], t2[:]), [b_w], [b_w])
            S.op("dve", lambda e, dv=dv: e.tensor_scalar(pr_[:], qr_[:], 1.0 / dv, 1.0, ALU.mult, ALU.add),
                 r=[b_w], w=[b_w])
            S.op("dve", lambda e, dv=dv: e.tensor_scalar(pi_[:], qi_[:], 1.0 / dv, None, ALU.mult), r=[b_w], w=[b_w])
        self.cmul("dve", wr[:], wi[:], zr[:], zi[:], pr_[:], pi_[:], (t1[:], t2[:]), [b_w], [b_w])
        for _ in range(MSQ):
            S.op("dve", lambda e: e.tensor_tensor(t1[:], wr[:], wr[:], ALU.mult), r=[b_w], w=[self.b_ct])
            S.op("dve", lambda e: e.tensor_tensor(t2[:], wi[:], wi[:], ALU.mult), r=[b_w], w=[self.b_ct])
            S.op("dve", lambda e: e.tensor_tensor(t3[:], wr[:], wi[:], ALU.mult), r=[b_w], w=[self.b_ct])
            S.op("dve", lambda e: e.tensor_tensor(t1[:], t1[:], t2[:], ALU.subtract), r=[self.b_ct], w=[self.b_ct])
            S.op("dve", lambda e: e.tensor_tensor(t3[:], t3[:], wi[:], ALU.add), r=[self.b_ct, b_w], w=[self.b_ct])
            S.op("dve", lambda e: e.scalar_tensor_tensor(wr[:], wr[:], 2.0, t1[:], ALU.mult, ALU.add),
                 r=[self.b_ct, b_w], w=[b_w])
            S.op("dve", lambda e: e.tensor_scalar(wi[:], t3[:], 2.0, None, ALU.mult), r=[self.b_ct], w=[b_w])
        S.op("dve", lambda e: e.tensor_scalar(ar[:], wr[:], 1.0, None, ALU.add), r=[b_w], w=[b_w])
        S.op("dve", lambda e: e.tensor_copy(ai[:], wi[:]), r=[b_w], w=[b_w])
        APr = sb("APr", [128, L + 1, 16], F32)
        APi = sb("APi", [128, L + 1, 16], F32)
        b_ap = Buf("AP")
        S.op("dve", lambda e: e.memset(APr[:, 0, :], 1.0), w=[b_ap])
        S.op("dve", lambda e: e.memset(APi[:, 0, :], 0.0), w=[b_ap])
        for ee in range(1, L + 1):
            self.cmul("dve", APr[:, ee, :], APi[:, ee, :], APr[:, ee - 1, :], APi[:, ee - 1, :], ar[:], ai[:],
                      (t1[:], t2[:]), [b_ap, b_w], [b_ap])
        nr = sb("nr", [128, 16], F32)
        cr = sb("cr", [128, 16], F32)
        ci = sb("ci", [128, 16], F32)
        S.op("dve", lambda e: e.tensor_copy(nr[:], wr[:]), r=[b_w], w=[b_w])
        S.op("dve", lambda e: e.tensor_tensor(t1[:], LR[:], LR[:], ALU.mult), r=[b_in], w=[self.b_ct])
        S.op("dve", lambda e: e.tensor_tensor(t2[:], LI[:], LI[:], ALU.mult), r=[b_in], w=[self.b_ct])
        S.op("dve", lambda e: e.tensor_tensor(t1[:], t1[:], t2[:], ALU.add), r=[self.b_ct], w=[self.b_ct])
        S.op("dve", lambda e: e.reciprocal(t3[:], t1[:]), r=[self.b_ct], w=[self.b_ct])
        S.op("dve", lambda e: e.tensor_tensor(t1[:], nr[:], LR[:], ALU.mult), r=[b_w, b_in], w=[self.b_ct])
        S.op("dve", lambda e: e.tensor_tensor(t2[:], ai[:], LI[:], ALU.mult), r=[b_w, b_in], w=[self.b_ct])
        S.op("dve", lambda e: e.tensor_tensor(t1[:], t1[:], t2[:], ALU.add), r=[self.b_ct], w=[self.b_ct])
        S.op("dve", lambda e: e.tensor_tensor(cr[:], t1[:], t3[:], ALU.mult), r=[self.b_ct], w=[b_w])
        S.op("dve", lambda e: e.tensor_tensor(t1[:], ai[:], LR[:], ALU.mult), r=[b_w, b_in], w=[self.b_ct])
        S.op("dve", lambda e: e.tensor_tensor(t2[:], nr[:], LI[:], ALU.mult), r=[b_w, b_in], w=[self.b_ct])
        S.op("dve", lambda e: e.tensor_tensor(t1[:], t1[:], t2[:], ALU.subtract), r=[self.b_ct], w=[self.b_ct])
        S.op("dve", lambda e: e.tensor_tensor(ci[:], t1[:], t3[:], ALU.mult), r=[self.b_ct], w=[b_w])
        Bbr = sb("Bbr", [128, 16, 16], F32)
        Bbi = sb("Bbi", [128, 16, 16], F32)
        T1 = sb("T1", [128, 16, 16], F32)
        T2 = sb("T2", [128, 16, 16], F32)
        b_bb = Buf("Bb")

        def bc(x):
            return x.unsqueeze(2).to_broadcast([128, 16, 16])
        self.cmul("dve", Bbr[:], Bbi[:], bc(cr[:]), bc(ci[:]), BR[:], BI[:], (T1[:], T2[:]), [b_w, b_in], [b_bb])
        PBr = sb("PBr", [128, L, 16, 32], BF16)
        PBi = sb("PBi", [128, L, 16, 32], BF16)
        CBr = sb("CBr", [128, 16, 32], BF16)
        CBin = sb("CBin", [128, 16, 32], BF16)
        VBr = sb("VBr", [128, L + 1, 16, 32], BF16)
        VBin = sb("VBin", [128, L + 1, 16, 32], BF16)
        b_pb, b_cb, b_vb = Buf("PB"), Buf("CB"), Buf("VB")
        for tt, bb in ((PBr, b_pb), (PBi, b_pb), (CBr, b_cb), (CBin, b_cb), (VBr, b_vb), (VBin, b_vb)):
            S.op("pool", lambda e, tt=tt: e.memset(tt[:], 0.0), w=[bb])
        for j in range(2):
            ps_, cs_ = slice(64 * j, 64 * j + 64), slice(16 * j, 16 * j + 16)
            S.op("dve", lambda e, ps_=ps_, cs_=cs_: e.tensor_copy(CBr[ps_, :, cs_], CR[ps_]), r=[b_in], w=[b_cb])
            S.op("dve", lambda e, ps_=ps_, cs_=cs_: e.tensor_scalar(CBin[ps_, :, cs_], CI[ps_], -1.0, None, ALU.mult),
                 r=[b_in], w=[b_cb])
        for k in range(L):
            akr, aki = bc(APr[:, k, :]), bc(APi[:, k, :])
            S.op("dve", lambda e, akr=akr: e.tensor_tensor(T1[:], akr, Bbr[:], ALU.mult), r=[b_ap, b_bb], w=[self.b_ct])
            S.op("dve", lambda e, aki=aki: e.tensor_tensor(T2[:], aki, Bbi[:], ALU.mult), r=[b_ap, b_bb], w=[self.b_ct])
            for j in range(2):
                ps_, cs_ = slice(64 * j, 64 * j + 64), slice(16 * j, 16 * j + 16)
                S.op("dve", lambda e, ps_=ps_, cs_=cs_, k=k: e.tensor_tensor(
                    PBr[ps_, k, :, cs_], T1[ps_], T2[ps_], ALU.subtract), r=[self.b_ct], w=[b_pb])
            S.op("dve", lambda e, akr=akr: e.tensor_tensor(T1[:], akr, Bbi[:], ALU.mult), r=[b_ap, b_bb, b_pb], w=[self.b_ct])
            S.op("dve", lambda e, aki=aki: e.tensor_tensor(T2[:], aki, Bbr[:], ALU.mult), r=[b_ap, b_bb], w=[self.b_ct])
            for j in range(2):
                ps_, cs_ = slice(64 * j, 64 * j + 64), slice(16 * j, 16 * j + 16)
                S.op("dve", lambda e, ps_=ps_, cs_=cs_, k=k: e.tensor_tensor(
                    PBi[ps_, k, :, cs_], T1[ps_], T2[ps_], ALU.add), r=[self.b_ct], w=[b_pb])
        for ee in range(L + 1):
            akr, aki = bc(APr[:, ee, :]), bc(APi[:, ee, :])
            S.op("dve", lambda e, akr=akr: e.tensor_tensor(T1[:], akr, CR[:], ALU.mult), r=[b_ap, b_in, b_vb, b_pb], w=[self.b_ct])
            S.op("dve", lambda e, aki=aki: e.tensor_tensor(T2[:], aki, CI[:], ALU.mult), r=[b_ap, b_in], w=[self.b_ct])
            for j in range(2):
                ps_, cs_ = slice(64 * j, 64 * j + 64), slice(16 * j, 16 * j + 16)
                S.op("dve", lambda e, ps_=ps_, cs_=cs_, ee=ee: e.tensor_tensor(
                    VBr[ps_, ee, :, cs_], T1[ps_], T2[ps_], ALU.subtract), r=[self.b_ct], w=[b_vb])
            S.op("dve", lambda e, aki=aki: e.tensor_tensor(T1[:], aki, CR[:], ALU.mult), r=[b_ap, b_in, b_vb], w=[self.b_ct])
            S.op("dve", lambda e, akr=akr: e.tensor_tensor(T2[:], akr, CI[:], ALU.mult), r=[b_ap, b_in], w=[self.b_ct])
            S.op("dve", lambda e: e.tensor_tensor(T1[:], T1[:], T2[:], ALU.add), r=[self.b_ct], w=[self.b_ct])
            for j in range(2):
                ps_, cs_ = slice(64 * j, 64 * j + 64), slice(16 * j, 16 * j + 16)
                S.op("dve", lambda e, ps_=ps_, cs_=cs_, ee=ee: e.tensor_scalar(
                    VBin[ps_, ee, :, cs_], T1[ps_], -1.0, None, ALU.mult), r=[self.b_ct], w=[b_vb])
        S.dma("sp", self.VBs.ap()[0], VBr[:].rearrange("p e r c -> p (e r c)"), b_vb, r=[b_vb])
        S.dma("sp", self.VBs.ap()[1], VBin[:].rearrange("p e r c -> p (e r c)"), b_vb, r=[b_vb])
        S.dma("sp", self.TAB.ap()[:, 0, :], APr[:, L, :], b_ap, r=[b_ap])
        S.dma("sp", self.TAB.ap()[:, 1, :], APi[:, L, :], b_ap, r=[b_ap])
        S.dma("sp", self.TAB.ap()[:, 2, :], APr[:, 1, :], b_ap, r=[b_ap])
        S.dma("sp", self.TAB.ap()[:, 3, :], APi[:, 1, :], b_ap, r=[b_ap])
        KBf = sb("KBf", [128, 4, 128], F32)
        b_kbf = Buf("KBf")
        KBo = [sb("KBo%d" % i, [128, 4, 128], BF16) for i in range(2)]
        b_kbo = [Buf("KBo%d" % i) for i in range(2)]
        S.op("pool", lambda e: e.memset(KBf[:], 0.0), w=[b_kbf])
        for lag in range(L):
            for ct in range(4):
                pt, b_pt = self.next_ps()
                for r4 in range(4):
                    pr = 4 * ct + r4
                    o_ = pt[32 * r4:32 * r4 + 32, 32 * r4:32 * r4 + 32]
                    S.op("pe", lambda e, o_=o_, lag=lag, pr=pr, r4=r4: e.matmul(
                        o_, PBr[:, lag, pr, :], CBr[:, pr, :], start=True, stop=False, tile_position=(0, 32 * r4)),
                        r=[b_pb, b_cb], w=[b_pt], inc=False)
                    S.op("pe", lambda e, o_=o_, lag=lag, pr=pr, r4=r4: e.matmul(
                        o_, PBi[:, lag, pr, :], CBin[:, pr, :], start=False, stop=True, tile_position=(0, 32 * r4)),
                        r=[b_pb, b_cb], w=[b_pt], inc=(r4 == 3))
                for r4 in range(4):
                    sl = slice(32 * r4, 32 * r4 + 32)
                    S.op("act", lambda e, sl=sl, ct=ct, pt=pt: e.activation(KBf[sl, ct, sl], pt[sl, sl], AF.Copy),
                         r=[b_pt], w=[b_kbf])
                if lag == 0:
                    S.op("dve", lambda e, ct=ct: e.scalar_tensor_tensor(
                        KBf[:, ct, :], self.ident_f[:, :], Dc[:, ct:ct + 1], KBf[:, ct, :], ALU.mult, ALU.add),
                        r=[b_in, self.b_ident], w=[b_kbf])
            ko, b_ko = KBo[lag % 2], b_kbo[lag % 2]
            S.op("dve", lambda e, ko=ko: e.tensor_copy(ko[:], KBf[:]), r=[b_kbf], w=[b_ko])
            S.dma("sp", self.KBs.ap()[:, lag, :], ko[:].rearrange("p c m -> p (c m)"), b_ko, r=[b_ko])
        WSo = [sb("WSo%d" % i, [128, 4, 128], BF16) for i in range(2)]
        b_wso = [Buf("WSo%d" % i) for i in range(2)]
        cnt = 0
        for ri, PB in enumerate((PBr, PBi)):
            for k in range(L):
                wo, b_wo = WSo[cnt % 2], b_wso[cnt % 2]
                cnt += 1
                for ct in range(4):
                    pt, b_pt = self.next_ps()
                    for r4 in range(4):
                        pr = 4 * ct + r4
                        S.op("pe", lambda e, pt=pt, r4=r4, PB=PB, k=k, pr=pr: e.matmul(
                            pt[32 * r4:32 * r4 + 32, 0:128], PB[:, k, pr, :], self.ident[:, :], start=True, stop=True,
                            tile_position=(0, 32 * r4)),
                            r=[b_pb, self.b_ident], w=[b_pt], inc=(r4 == 3))
                    S.op("act", lambda e, pt=pt, wo=wo, ct=ct: e.activation(wo[:, ct, :], pt[:, 0:128], AF.Copy),
                         r=[b_pt], w=[b_wo])
                S.dma("sp", self.WSs.ap()[ri, :, k, :], wo[:].rearrange("p c m -> p (c m)"), b_wo, r=[b_wo])


    def gelu_tanh(self, dst, src, tmp, b_src, b_tmp, b_dst):
        S = self.S
        S.op("dve", lambda e: e.tensor_tensor(tmp, src, src, ALU.mult), r=[b_src], w=[b_tmp])
        S.op("dve", lambda e: e.tensor_scalar(tmp, tmp, 0.044715, 1.0, ALU.mult, ALU.add), r=[b_tmp], w=[b_tmp])
        S.op("dve", lambda e: e.tensor_tensor(tmp, tmp, src, ALU.mult), r=[b_src, b_tmp], w=[b_tmp])
        S.op("act", lambda e: e.activation(tmp, tmp, AF.Sigmoid, scale=1.5957691216), r=[b_tmp], w=[b_tmp])
        S.op("dve", lambda e: e.tensor_tensor(dst, src, tmp, ALU.mult), r=[b_src, b_tmp], w=[b_dst])

    def sample_ssm(self, WS, VB, TAB, b_wt):
        S, sb = self.S, self.sb
        NS = 16
        b_su = Buf("s_u")
        Us = sb("Us", [128, 4, NS], BF16)
        Dcs = sb("Dcs", [128, 4], F32)
        S.dma("sp", Us[:], self.USs.ap().rearrange("c p t -> p c t"), b_su, w=[b_su])
        S.dma("sp", Dcs[:], dap(self.ssm_d, 0, [[1, 128], [128, 4]]), b_su, w=[b_su], allow_slow_non_contiguous=True)
        BU = [sb("BU%d" % i, [128, 16, NS], F32) for i in range(2)]
        b_BU = Buf("BU")
        for ri in range(2):
            for r4 in range(4):
                pt, b_pt = self.next_ps()
                rows = slice(32 * r4, 32 * r4 + 32)
                for ct in range(4):
                    S.op("pe", lambda e, pt=pt, ct=ct, r4=r4, rows=rows, ri=ri: e.matmul(
                        pt[:, ct * NS:(ct + 1) * NS], WS[ri][rows, 0, ct, :], Us[rows, ct, :],
                        start=True, stop=True, tile_position=(32 * r4, 0)),
                        r=[b_wt, b_su], w=[b_pt], inc=(ct == 3))
                S.op("act", lambda e, pt=pt, ri=ri, r4=r4: e.activation(
                    BU[ri][:, r4:16:4, :], pt[:, 0:4 * NS].rearrange("p (r c) -> p r c", r=4), AF.Copy),
                    r=[b_pt], w=[b_BU])
        H0 = [sb("H0_%d" % i, [128, 16, 4], F32) for i in range(2)]
        b_H0 = Buf("H0")
        for ri, src in enumerate((self.st_re, self.st_im)):
            for s_ in range(4):
                S.dma("sp", H0[ri][:, :, s_], dap(src, s_ * 2048, [[1, 128], [128, 16]]), b_H0, w=[b_H0],
                      allow_slow_non_contiguous=True)
        Hs = [sb("Hs%d" % i, [128, 16, NS], F32) for i in range(2)]
        b_Hs = Buf("Hs")
        tq = [sb("tqs%d" % i, [128, 16, 4], F32) for i in range(4)]
        b_tq = Buf("tqs")
        A1r = TAB[:, 2, :].unsqueeze(2).to_broadcast([128, 16, 4])
        A1i = TAB[:, 3, :].unsqueeze(2).to_broadcast([128, 16, 4])

        def v4(x, t):
            return x[:, :, t:t + 13:4]
        for t in range(4):
            if t == 0:
                pr_, pi_, bp = H0[0][:, :, :], H0[1][:, :, :], b_H0
            else:
                pr_, pi_, bp = v4(Hs[0], t - 1), v4(Hs[1], t - 1), b_Hs
            S.op("pool", lambda e, pr_=pr_: e.tensor_tensor(tq[0][:], A1r, pr_, ALU.mult), r=[bp, b_wt], w=[b_tq])
            S.op("pool", lambda e, pi_=pi_: e.tensor_tensor(tq[1][:], A1i, pi_, ALU.mult), r=[bp], w=[b_tq])
            S.op("pool", lambda e, pi_=pi_: e.tensor_tensor(tq[2][:], A1r, pi_, ALU.mult), r=[bp], w=[b_tq])
            S.op("pool", lambda e, pr_=pr_: e.tensor_tensor(tq[3][:], A1i, pr_, ALU.mult), r=[bp], w=[b_tq])
            S.op("pool", lambda e: e.tensor_tensor(tq[0][:], tq[0][:], tq[1][:], ALU.subtract), r=[b_tq], w=[b_tq])
            S.op("pool", lambda e: e.tensor_tensor(tq[2][:], tq[2][:], tq[3][:], ALU.add), r=[b_tq], w=[b_tq])
            S.op("pool", lambda e, t=t: e.tensor_tensor(v4(Hs[0], t), v4(BU[0], t), tq[0][:], ALU.add),
                 r=[b_tq, b_BU], w=[b_Hs])
            S.op("pool", lambda e, t=t: e.tensor_tensor(v4(Hs[1], t), v4(BU[1], t), tq[2][:], ALU.add),
                 r=[b_tq, b_BU], w=[b_Hs])
        for ri, dst in enumerate((self.ossm_s_re, self.ossm_s_im)):
            for s_ in range(4):
                S.dma("sp", dap(dst, s_ * 2048, [[1, 128], [128, 16]]), Hs[ri][:, :, 4 * s_ + 3], b_Hs, r=[b_Hs],
                      allow_slow_non_contiguous=True)
        Hb = [sb("Hbs%d" % i, [128, 16, NS], BF16) for i in range(2)]
        b_Hb = Buf("Hbs")
        for ri in range(2):
            S.op("act", lambda e, ri=ri: e.activation(Hb[ri][:], Hs[ri][:], AF.Copy), r=[b_Hs], w=[b_Hb])
        yfs = sb("yfs", [128, 4, NS], F32)
        yts = sb("yts", [128, 4, NS], F32)
        ybs = sb("ybs", [128, 4, NS], BF16)
        b_yfs, b_yts, b_ybs = Buf("yfs"), Buf("yts"), Buf("ybs")
        for ct in range(4):
            pt, b_pt = self.next_ps()
            for r4 in range(4):
                pr = 4 * ct + r4
                S.op("pe", lambda e, pt=pt, r4=r4, pr=pr: e.matmul(
                    pt[32 * r4:32 * r4 + 32, 0:NS], VB[0][:, 0, pr, :], Hb[0][:, pr, :], start=True, stop=False,
                    tile_position=(0, 32 * r4)), r=[b_wt, b_Hb], w=[b_pt], inc=False)
                S.op("pe", lambda e, pt=pt, r4=r4, pr=pr: e.matmul(
                    pt[32 * r4:32 * r4 + 32, 0:NS], VB[1][:, 0, pr, :], Hb[1][:, pr, :], start=False, stop=True,
                    tile_position=(0, 32 * r4)), r=[b_wt, b_Hb], w=[b_pt], inc=(r4 == 3))
            S.op("dve", lambda e, pt=pt, ct=ct: e.scalar_tensor_tensor(
                yfs[:, ct, :], Us[:, ct, :], Dcs[:, ct:ct + 1], pt[:, 0:NS], ALU.mult, ALU.add),
                r=[b_pt, b_su], w=[b_yfs])
        self.gelu_tanh(ybs[:], yfs[:], yts[:], b_yfs, b_yts, b_ybs)
        S.dma("sp", self.YSs.ap().rearrange("c p t -> p c t"), ybs[:], b_ybs, r=[b_ybs])


    def cmadd(self, eng, dr, di, ar, ai, xr, xi, tmps, rbufs, wbuf, b_t):
        S = self.S
        t0, t1, t2, t3 = tmps
        S.op(eng, lambda e: e.tensor_tensor(t0, ar, xr, ALU.mult), r=rbufs, w=[b_t])
        S.op(eng, lambda e: e.tensor_tensor(t1, ai, xi, ALU.mult), r=rbufs, w=[b_t])
        S.op(eng, lambda e: e.tensor_tensor(t2, ar, xi, ALU.mult), r=rbufs, w=[b_t])
        S.op(eng, lambda e: e.tensor_tensor(t3, ai, xr, ALU.mult), r=rbufs, w=[b_t])
        S.op(eng, lambda e: e.tensor_tensor(t0, t0, t1, ALU.subtract), r=[b_t], w=[b_t])
        S.op(eng, lambda e: e.tensor_tensor(t2, t2, t3, ALU.add), r=[b_t], w=[b_t])
        S.op(eng, lambda e: e.tensor_tensor(dr, dr, t0, ALU.add), r=[b_t] + rbufs, w=[wbuf])
        S.op(eng, lambda e: e.tensor_tensor(di, di, t2, ALU.add), r=[b_t] + rbufs, w=[wbuf])

    def phase3b(self):
        S, sb = self.S, self.sb
        L = LCH
        NCH = T_ALL // L
        NH = NCH // 2
        TH = T_ALL // 2
        GRP = 8
        NG = NCH // GRP
        b_wt = Buf("ssm_wt")
        WS = [sb("WS%d" % i, [128, L, 4, 128], BF16) for i in range(2)]
        VB = [sb("VB%d" % i, [128, L + 1, 16, 32], BF16) for i in range(2)]
        TAB = sb("TAB", [128, 4, 16], F32)
        for i in range(2):
            S.dma("sp", WS[i][:].rearrange("p l c m -> p l (c m)"), self.WSs.ap()[i], b_wt, w=[b_wt])
            S.dma("sp", VB[i][:].rearrange("p e r c -> p (e r c)"), self.VBs.ap()[i], b_wt, w=[b_wt])
        S.dma("sp", TAB[:], self.TAB.ap(), b_wt, w=[b_wt])
        self.sample_ssm(WS, VB, TAB, b_wt)
        Sall = [sb("Sall%d" % i, [128, 16, NCH], F32) for i in range(2)]
        b_S = Buf("Sall")
        U = sb("Uh", [128, 4, TH], BF16)
        b_U = Buf("Uh")
        for half in range(2):
            S.dma("sp", U[:], self.US.ap()[:, :, half * TH:(half + 1) * TH].rearrange("c p t -> p c t"), b_U, w=[b_U])
            for ri in range(2):
                for pr in range(16):
                    ct, r4 = pr // 4, pr % 4
                    rows = slice(32 * r4, 32 * r4 + 32)
                    pt, b_pt = self.next_ps()
                    for tau in range(L):
                        S.op("pe", lambda e, pt=pt, ct=ct, r4=r4, rows=rows, tau=tau, ri=ri: e.matmul(
                            pt[:, 0:NH], WS[ri][rows, L - 1 - tau, ct, :],
                            U[rows, ct, :].rearrange("p (n t c) -> p n t c", t=L, c=TN // L)[:, :, tau, :],
                            start=(tau == 0), stop=(tau == L - 1), tile_position=(32 * r4, 0)),
                            r=[b_wt, b_U], w=[b_pt], inc=(tau == L - 1))
                    S.op("act", lambda e, pt=pt, ri=ri, pr=pr, half=half: e.activation(
                        Sall[ri][:, pr, half * NH:(half + 1) * NH], pt[:, 0:NH], AF.Copy), r=[b_pt], w=[b_S])
        PW = [sb("PW%d" % i, [128, GRP + 1, 16], F32) for i in range(2)]
        b_PW = Buf("PW")
        tq = [sb("tq%d" % i, [128, 16, NG], F32) for i in range(4)]
        b_tq = Buf("tq")
        self.b_ct = b_tq
        S.op("dve", lambda e: e.tensor_copy(PW[0][:, 1, :], TAB[:, 0, :]), r=[b_wt], w=[b_PW])
        S.op("dve", lambda e: e.tensor_copy(PW[1][:, 1, :], TAB[:, 1, :]), r=[b_wt], w=[b_PW])
        for k in range(2, GRP + 1):
            self.cmul("dve", PW[0][:, k, :], PW[1][:, k, :], PW[0][:, k - 1, :], PW[1][:, k - 1, :],
                      PW[0][:, 1, :], PW[1][:, 1, :], (tq[0][:, :, 0], tq[1][:, :, 0]), [b_PW], [b_PW])

        def bcg(x):
            return x.unsqueeze(2).to_broadcast([128, 16, NG])

        def vw(ri, i):
            return Sall[ri][:, :, i:i + (NG - 1) * GRP + 1:GRP]
        tmps = tuple(t[:, :, :] for t in tq)
        for i in range(1, GRP):
            self.cmadd("dve", vw(0, i), vw(1, i), bcg(PW[0][:, 1, :]), bcg(PW[1][:, 1, :]), vw(0, i - 1), vw(1, i - 1),
                       tmps, [b_PW, b_S], b_S, b_tq)
        t16 = tuple(t[:, :, 0] for t in tq)
        for C in range(1, NG):
            e0, e1 = C * GRP - 1, (C + 1) * GRP - 1
            self.cmadd("dve", Sall[0][:, :, e1], Sall[1][:, :, e1], PW[0][:, GRP, :], PW[1][:, GRP, :],
                       Sall[0][:, :, e0], Sall[1][:, :, e0], t16, [b_PW, b_S], b_S, b_tq)

        def vw1(ri, i):
            return Sall[ri][:, :, GRP + i:GRP + i + (NG - 2) * GRP + 1:GRP]

        def ends(ri):
            return Sall[ri][:, :, GRP - 1:GRP - 1 + (NG - 2) * GRP + 1:GRP]

        def bcg1(x):
            return x.unsqueeze(2).to_broadcast([128, 16, NG - 1])
        tmps1 = tuple(t[:, :, 0:NG - 1] for t in tq)
        for i in range(GRP - 1):
            self.cmadd("dve", vw1(0, i), vw1(1, i), bcg1(PW[0][:, i + 1, :]), bcg1(PW[1][:, i + 1, :]), ends(0), ends(1),
                       tmps1, [b_PW, b_S], b_S, b_tq)
        S.dma("sp", dap(self.ossm_re, 0, [[1, 128], [128, 16]]), Sall[0][:, :, NCH - 1], b_S, r=[b_S],
              allow_slow_non_contiguous=True)
        S.dma("sp", dap(self.ossm_im, 0, [[1, 128], [128, 16]]), Sall[1][:, :, NCH - 1], b_S, r=[b_S],
              allow_slow_non_contiguous=True)
        HBt = sb("HBt", [128, 16, NH], BF16)
        b_HBt = Buf("HBt")
        for ri in range(2):
            S.op("act", lambda e, ri=ri: e.activation(HBt[:, :, :], Sall[ri][:, :, NH - 1:2 * NH - 1], AF.Copy),
                 r=[b_S], w=[b_HBt])
            S.dma("sp", self.HBs.ap()[ri], HBt[:, :, :], b_HBt, r=[b_HBt])

    def phase3c(self):
        S, sb = self.S, self.sb
        L = LCH
        NH = T_MAIN // L
        b_wt = Buf("ssm_wt2")
        KB = sb("KB", [128, L, 4, 128], BF16)
        VB = [sb("VB%d" % i, [128, L + 1, 16, 32], BF16) for i in range(2)]
        HB = [sb("HB%d" % i, [128, 16, NH], BF16) for i in range(2)]
        U = sb("Um", [128, 4, T_MAIN], BF16)
        for i in range(2):
            S.dma("sp", VB[i][:].rearrange("p e r c -> p (e r c)"), self.VBs.ap()[i], b_wt, w=[b_wt])
            S.dma("sp", HB[i][:], self.HBs.ap()[i], b_wt, w=[b_wt])
        S.dma("sp", KB[:].rearrange("p l c m -> p l (c m)"), self.KBs.ap(), b_wt, w=[b_wt])
        S.dma("sp", U[:], self.US.ap()[:, :, T_MAIN0:T_ALL].rearrange("c p t -> p c t"), b_wt, w=[b_wt])
        yf = sb("yf", [128, T_MAIN], F32)
        yt = sb("yt", [128, T_MAIN], F32)
        yb = sb("yb", [128, T_MAIN], BF16)
        b_yf, b_yt, b_yb = Buf("yf"), Buf("yt"), Buf("yb")
        for ct in range(4):
            for tau in range(L):
                pt, b_pt = self.next_ps()
                for lag in range(tau + 1):
                    S.op("pe", lambda e, pt=pt, tau=tau, lag=lag, ct=ct: e.matmul(
                        pt[:, 0:NH], KB[:, lag, ct, :],
                        U[:, ct, :].rearrange("p (n t c) -> p n t c", t=L, c=TN // L)[:, :, tau - lag, :],
                        start=(lag == 0), stop=False, skip_group_check=True), r=[b_wt], w=[b_pt], inc=False)
                for r4 in range(4):
                    pr = 4 * ct + r4
                    for ri in range(2):
                        last = (r4 == 3 and ri == 1)
                        S.op("pe", lambda e, pt=pt, tau=tau, r4=r4, pr=pr, ri=ri, last=last: e.matmul(
                            pt[32 * r4:32 * r4 + 32, 0:NH], VB[ri][:, tau + 1, pr, :], HB[ri][:, pr, :],
                            start=False, stop=last, skip_group_check=True, tile_position=(0, 32 * r4)),
                            r=[b_wt], w=[b_pt], inc=last)
                S.op("act", lambda e, pt=pt, tau=tau: e.activation(
                    yf[:, tau:tau + (NH - 1) * L + 1:L], pt[:, 0:NH], AF.Copy), r=[b_pt], w=[b_yf])
            self.gelu_tanh(yb[:, :], yf[:, :], yt[:, :], b_yf, b_yt, b_yb)
            S.dma("sp", self.YS.ap()[ct, :, :], yb[:, :], b_yb, r=[b_yb])

    def phase4(self):
        S, sb = self.S, self.sb
        stg = [sb("wstg%d" % i, [128, 1024], F32) for i in range(2)]
        b_stg = [Buf("wstg%d" % i) for i in range(2)]
        Wglu = sb("Wglu", [128, 4, 512], BF16)
        Wbs = sb("Wbs", [128, 4, 1024], BF16)
        Wout = sb("Wout", [128, 8, 1024], BF16)
        Wba = sb("Wba", [64, 4, 1024], BF16)
        b_W = Buf("W4")
        b_Wg, b_Wo = Buf("W4g"), Buf("W4o")
        self.load_weight_bf16(Wglu, b_Wg, self.w_glu, 4, 512, None, stg, b_stg)
        self.load_weight_bf16(Wbs, b_W, self.w_bs, 4, 1024, None, stg, b_stg)
        for h in range(4):
            st_, bs_ = stg[h % 2], b_stg[h % 2]
            S.dma("sp", st_[0:64, :], self.w_ba.ap()[64 * h:64 * h + 64, :], bs_, w=[bs_])
            S.op("dve", lambda e, st_=st_, h=h: e.tensor_copy(Wba[:, h, :], st_[0:64, :]), r=[bs_], w=[b_W])
        self.load_weight_bf16(Wout, b_Wo, self.w_out, 8, 1024, None, stg, b_stg)
        bg = sb("bg", [128, 4], F32)
        b_bg = Buf("bg")
        S.dma("sp", bg[:], dap(self.b_glu, 0, [[1, 128], [128, 4]]), b_bg, w=[b_bg], allow_slow_non_contiguous=True)
        AT = [sb("AT%d" % i, [64, 4, TN], BF16) for i in range(2)]
        YT = [sb("YT%d" % i, [128, 4, TN], BF16) for i in range(2)]
        G = [sb("G%d" % i, [128, 16, TN], BF16) for i in range(2)]
        X = [sb("X%d" % i, [128, 3, D], F32) for i in range(2)]
        b_in = [Buf("in4_%d" % i) for i in range(2)]
        SG = sb("SG", [128, 16, TN], BF16)
        b_SG = Buf("SG")
        sgl = sb("sgl", [128, TN], F32)
        b_sgl = Buf("sgl")
        so = sb("so", [128, 4, TN], BF16)
        b_so = Buf("so")
        mix = sb("mix", [128, 8, TN], BF16)
        b_mix = Buf("mix")
        ta = [sb("ta%d" % i, [128, TN], F32) for i in range(2)]
        b_ta = [Buf("ta%d" % i) for i in range(2)]
        X1 = [sb("X1_%d" % i, [128, 3, D], F32) for i in range(2)]
        b_X1 = [Buf("X1_%d" % i) for i in range(2)]

        def load_tile(k):
            i = k % 2
            m0 = k * TN
            S.dma("sp", AT[i][:], self.ATT.ap()[:, :, m0:m0 + TN].rearrange("h p t -> p h t"), b_in[i], w=[b_in[i]])
            S.dma("sp", YT[i][:], self.YS.ap()[:, :, m0:m0 + TN].rearrange("c p t -> p c t"), b_in[i], w=[b_in[i]])
            S.dma("sp", G[i][:], self.GS.ap()[:, :, m0:m0 + TN].rearrange("c p t -> p c t"), b_in[i], w=[b_in[i]])
            S.dma("sp", X[i][:], self.xall.ap()[T_MAIN0 + m0:T_MAIN0 + m0 + TN, :].rearrange("(s p) d -> p s d", p=128),
                  b_in[i], w=[b_in[i]])
        self.epsc = sb("epsc", [128, 1], F32)
        self.b_epsc = Buf("epsc")
        S.op("dve", lambda e: e.memset(self.epsc[:], EPS), w=[self.b_epsc])
        self.junk = sb("junk", [128, D], BF16)
        self.b_junk = Buf("junk")
        self.ss = sb("ss", [128, 4], F32)
        self.b_ss = Buf("ss")
        self.sq = sb("sq", [128, 4], F32)
        self.b_sq = Buf("sq")
        self.rstd = sb("rstd", [128, 4], F32)
        self.b_rstd = Buf("rstd")
        self.xs = sb("xs", [128, 3, D], BF16)
        self.b_xs = [Buf("xs%d" % i) for i in range(3)]
        XN2 = [sb("XN2_%d" % i, [128, 8, TN], BF16) for i in range(2)]
        b_XN2 = [Buf("XN2_%d" % i) for i in range(2)]

        def body(i, nt, nsub, rows, xo, b_xo, mid_hook=None):
            bi = b_in[i]
            S.op("act", lambda e, i=i: e.activation(SG[:, :, 0:nt], G[i][:, :, 0:nt], AF.Sigmoid), r=[bi], w=[b_SG])
            for mt in range(4):
                pt, b_pt = self.next_ps()
                for kc in range(4):
                    S.op("pe", lambda e, pt=pt, kc=kc, mt=mt, i=i: e.matmul(
                        pt[:, 0:nt], Wglu[:, kc, mt * 128:(mt + 1) * 128], YT[i][:, kc, 0:nt],
                        start=(kc == 0), stop=(kc == 3)), r=[b_Wg, bi], w=[b_pt], inc=(kc == 3))
                S.op("act", lambda e, pt=pt, mt=mt: e.activation(
                    sgl[:, 0:nt], pt[:, 0:nt], AF.Sigmoid, bias=bg[:, mt:mt + 1]), r=[b_pt, b_bg], w=[b_sgl])
                S.op("dve", lambda e, mt=mt, i=i: e.tensor_tensor(so[:, mt, 0:nt], YT[i][:, mt, 0:nt], sgl[:, 0:nt], ALU.mult),
                     r=[bi, b_sgl], w=[b_so])
            for mt in range(8):
                pa, b_pa = self.next_ps()
                for h in range(4):
                    S.op("pe", lambda e, pa=pa, h=h, mt=mt, i=i: e.matmul(
                        pa[:, 0:nt], Wba[:, h, mt * 128:(mt + 1) * 128], AT[i][:, h, 0:nt],
                        start=(h == 0), stop=(h == 3)), r=[b_W, bi], w=[b_pa], inc=(h == 3))
                pb, b_pb = self.next_ps()
                for kc in range(4):
                    S.op("pe", lambda e, pb=pb, kc=kc, mt=mt: e.matmul(
                        pb[:, 0:nt], Wbs[:, kc, mt * 128:(mt + 1) * 128], so[:, kc, 0:nt],
                        start=(kc == 0), stop=(kc == 3)), r=[b_W, b_so], w=[b_pb], inc=(kc == 3))
                t0_, t1_ = ta[0], ta[1]
                S.op("dve", lambda e, pa=pa, mt=mt: e.tensor_tensor(ta[0][:, 0:nt], pa[:, 0:nt], SG[:, mt, 0:nt], ALU.mult),
                     r=[b_pa, b_SG], w=[b_ta[0]])
                S.op("dve", lambda e, pb=pb, mt=mt: e.tensor_tensor(ta[1][:, 0:nt], pb[:, 0:nt], SG[:, 8 + mt, 0:nt], ALU.mult),
                     r=[b_pb, b_SG], w=[b_ta[1]])
                S.op("dve", lambda e, mt=mt: e.tensor_tensor(mix[:, mt, 0:nt], ta[0][:, 0:nt], ta[1][:, 0:nt], ALU.add),
                     r=[b_ta[0], b_ta[1]], w=[b_mix])
            if mid_hook is not None:
                mid_hook()
            for s_ in range(nsub):
                for nh in range(2):
                    pt, b_pt = self.next_ps()
                    for kc in range(8):
                        S.op("pe", lambda e, pt=pt, kc=kc, s_=s_, nh=nh: e.matmul(
                            pt[0:rows, 0:512], mix[:, kc, s_ * 128:s_ * 128 + rows], Wout[:, kc, nh * 512:(nh + 1) * 512],
                            start=(kc == 0), stop=(kc == 7)), r=[b_Wo, b_mix], w=[b_pt], inc=(kc == 7))
                    S.op("dve", lambda e, pt=pt, s_=s_, nh=nh, i=i, xo=xo: e.tensor_tensor(
                        xo[0:rows, s_, nh * 512:(nh + 1) * 512], X[i][0:rows, s_, nh * 512:(nh + 1) * 512], pt[0:rows, 0:512], ALU.add),
                        r=[b_pt, bi], w=[b_xo])

        NK = T_MAIN // TN
        load_tile(0)
        pend = None
        for k in range(NK):
            if k + 1 < NK:
                load_tile(k + 1)
            xo, b_xo = X1[k % 2], b_X1[k % 2]
            body(k % 2, TN, 3, 128, xo, b_xo, pend)
            m0 = k * TN
            S.dma("sp", self.X1s.ap()[m0:m0 + TN, :].rearrange("(s p) d -> p s d", p=128), xo[:], b_xo, r=[b_xo])

            def pend(k=k, xo=xo, b_xo=b_xo, m0=m0):
                self.norm_transpose(xo, b_xo, 3, 128, XN2[k % 2], b_XN2[k % 2])
                S.dma("sp", self.XN2s.ap()[:, :, m0:m0 + TN].rearrange("k p t -> p k t"), XN2[k % 2][:, :, :],
                      b_XN2[k % 2], r=[b_XN2[k % 2]])
        i = NK % 2
        NS = 16
        S.dma("sp", AT[i][:, :, 0:NS], self.ATTs.ap().rearrange("h p t -> p h t"), b_in[i], w=[b_in[i]])
        S.dma("sp", YT[i][:, :, 0:NS], self.YSs.ap().rearrange("c p t -> p c t"), b_in[i], w=[b_in[i]])
        S.dma("sp", G[i][:, :, 0:NS], self.GSs.ap().rearrange("c p t -> p c t"), b_in[i], w=[b_in[i]])
        S.dma("sp", X[i][0:NS, 0, :], self.xsamp.ap(), b_in[i], w=[b_in[i]])
        xo, b_xo = X1[NK % 2], b_X1[NK % 2]
        body(i, NS, 1, NS, xo, b_xo, pend)
        S.dma("sp", self.X1ss.ap(), xo[0:NS, 0, :], b_xo, r=[b_xo])
        self.norm_transpose(xo, b_xo, 1, NS, XN2[i], b_XN2[i])
        S.dma("sp", self.XN2ss.ap().rearrange("k p t -> p k t"), XN2[i][:, :, 0:NS], b_XN2[i], r=[b_XN2[i]])

    def phase5(self):
        S, sb = self.S, self.sb
        NM = DFF // 128
        g2c = sb("g2c", [128, 8], F32)
        b_c = Buf("c5")
        S.dma("sp", g2c[:], dap(self.norm2_g, 0, [[1, 128], [128, 8]]), b_c, w=[b_c], allow_slow_non_contiguous=True)
        self.epsc = sb("epsc", [128, 1], F32)
        self.b_epsc = Buf("epsc")
        S.op("dve", lambda e: e.memset(self.epsc[:], EPS), w=[self.b_epsc])
        cw = sb("cw", [128, 3, NM], F32)
        cb = sb("cb", [128, NM], F32)
        for i3 in range(3):
            S.dma("sp", cw[:, i3, :], dap(self.conv_w, i3 * DFF, [[1, 128], [128, NM]]), b_c, w=[b_c],
                  allow_slow_non_contiguous=True)
        S.dma("sp", cb[:], dap(self.conv_b, 0, [[1, 128], [128, NM]]), b_c, w=[b_c], allow_slow_non_contiguous=True)
        gfb = sb("gfb", [128, D], F32)
        S.dma("sp", gfb[:], dap(self.norm_f_g, 0, [[0, 128], [1, D]]), b_c, w=[b_c])
        stg = [sb("wstg%d" % i, [128, 704], F32) for i in range(2)]
        b_stg = [Buf("wstg%d" % i) for i in range(2)]
        Wup = sb("Wup", [128, 8, 2 * DFF], BF16)
        Wdn = sb("Wdn", [128, NM, D], BF16)
        b_W = Buf("W5")
        b_Wdn = Buf("Wdn")
        self.load_weight_bf16(Wup, b_W, self.w_up, 8, 2 * DFF, (g2c, b_c), stg, b_stg, nsplit=8)
        self.load_weight_bf16(Wdn, b_Wdn, self.w_down, NM, D, None, stg, b_stg, nsplit=2)
        self.junk = sb("junk", [128, D], BF16)
        self.b_junk = Buf("junk")
        X1 = [sb("X1_0", [128, 3, D], F32)] * 2
        b_X1 = [Buf("X1_0")] * 2
        xnTs = [sb("xnT%d" % i, [128, 8, TN], BF16) for i in range(2)]
        b_xnTs = [Buf("xnT%d" % i) for i in range(2)]
        carry = sb("carry", [128, NM, 2], F32)
        b_carry = Buf("carry")
        S.op("dve", lambda e: e.memset(carry[:], 0.0), w=[b_carry])
        ab = [sb("ab%d" % i, [128, TN + 2], F32) for i in range(2)]
        b_ab = [Buf("ab%d" % i) for i in range(2)]
        tc_ = [sb("tc%d" % i, [128, TN], F32) for i in range(2)]
        b_tc = [Buf("tc%d" % i) for i in range(2)]
        hT = sb("hT", [128, NM, TN], BF16)
        b_hT = Buf("hT")
        x2 = [sb("x2_0", [128, D], F32)] * 2
        b_x2 = [Buf("x2_0")] * 2
        yo = [sb("yo%d" % i, [128, D], F32) for i in range(2)]
        b_yo = [Buf("yo%d" % i) for i in range(2)]
        ss2 = sb("ss2", [128, 2], F32)
        sq2 = sb("sq2", [128, 2], F32)
        b_n2 = [Buf("n2_%d" % i) for i in range(2)]

        def load_tile(k):
            m0 = k * TN
            S.dma("sp", X1[k % 2][:], self.X1s.ap()[m0:m0 + TN, :].rearrange("(s p) d -> p s d", p=128),
                  b_X1[k % 2], w=[b_X1[k % 2]])

        def load_xn(k):
            m0 = k * TN
            S.dma("sp", xnTs[k % 2][:], self.XN2s.ap()[:, :, m0:m0 + TN].rearrange("k p t -> p k t"),
                  b_xnTs[k % 2], w=[b_xnTs[k % 2]])
        NK = T_MAIN // TN
        load_xn(0)
        cnt = 0
        ocnt = 0
        for k in range(NK):
            if k + 1 < NK:
                load_xn(k + 1)
            load_tile(k)
            xt, b_xt = X1[k % 2], b_X1[k % 2]
            xnT, b_xnT = xnTs[k % 2], b_xnTs[k % 2]
            for mt in range(NM):
                pa, b_pa = self.next_ps()
                pv, b_pv = self.next_ps()
                for (pp, bp, c0) in ((pa, b_pa, mt * 128), (pv, b_pv, DFF + mt * 128)):
                    for kc in range(8):
                        S.op("pe", lambda e, pp=pp, kc=kc, c0=c0, xnT=xnT: e.matmul(
                            pp[:, 0:TN], Wup[:, kc, c0:c0 + 128], xnT[:, kc, :], start=(kc == 0), stop=(kc == 7)),
                            r=[b_W, b_xnT], w=[bp], inc=(kc == 7))
                a_, b_a = ab[cnt % 2], b_ab[cnt % 2]
                t_, b_t = tc_[cnt % 2], b_tc[cnt % 2]
                cnt += 1
                S.op("act", lambda e, a_=a_, mt=mt: e.activation(a_[:, 0:2], carry[:, mt, :], AF.Copy),
                     r=[b_carry], w=[b_a])
                S.op("act", lambda e, a_=a_, pa=pa: e.activation(a_[:, 2:TN + 2], pa[:, 0:TN], AF.Copy),
                     r=[b_pa], w=[b_a])
                S.op("dve", lambda e, a_=a_, t_=t_, mt=mt: e.tensor_scalar(
                    t_[:, :], a_[:, 2:TN + 2], cw[:, 2, mt:mt + 1], cb[:, mt:mt + 1], ALU.mult, ALU.add),
                    r=[b_a, b_c], w=[b_t])
                S.op("dve", lambda e, a_=a_, t_=t_, mt=mt: e.scalar_tensor_tensor(
                    t_[:, :], a_[:, 1:TN + 1], cw[:, 1, mt:mt + 1], t_[:, :], ALU.mult, ALU.add),
                    r=[b_a, b_c], w=[b_t])
                S.op("dve", lambda e, a_=a_, t_=t_, mt=mt: e.scalar_tensor_tensor(
                    t_[:, :], a_[:, 0:TN], cw[:, 0, mt:mt + 1], t_[:, :], ALU.mult, ALU.add),
                    r=[b_a, b_c], w=[b_t])
                S.op("dve", lambda e, a_=a_, mt=mt: e.tensor_copy(carry[:, mt, :], a_[:, TN:TN + 2]),
                     r=[b_a], w=[b_carry])
                S.op("act", lambda e, t_=t_: e.activation(t_[:, :], t_[:, :], AF.Silu), r=[b_t], w=[b_t])
                S.op("dve", lambda e, t_=t_, pv=pv, mt=mt: e.tensor_tensor(hT[:, mt, :], t_[:, :], pv[:, 0:TN], ALU.mult),
                     r=[b_t, b_pv], w=[b_hT])
            for s_ in range(3):
                j = ocnt % 2
                ocnt += 1
                for nh in range(2):
                    pt, b_pt = self.next_ps()
                    for kc in range(NM):
                        S.op("pe", lambda e, pt=pt, kc=kc, s_=s_, nh=nh: e.matmul(
                            pt[:, 0:512], hT[:, kc, s_ * 128:(s_ + 1) * 128], Wdn[:, kc, nh * 512:(nh + 1) * 512],
                            start=(kc == 0), stop=(kc == NM - 1)), r=[b_Wdn, b_hT], w=[b_pt], inc=(kc == NM - 1))
                    S.op("dve", lambda e, pt=pt, s_=s_, nh=nh, j=j, xt=xt: e.tensor_tensor(
                        x2[j][:, nh * 512:(nh + 1) * 512], xt[:, s_, nh * 512:(nh + 1) * 512], pt[:, 0:512], ALU.add),
                        r=[b_pt, b_xt], w=[b_x2[j]])
                tok = k * TN + s_ * 128
                if tok < 128:
                    continue
                S.op("act", lambda e, j=j: e.activation(self.junk[:, :], x2[j][:, :], AF.Square,
                                                        accum_out=ss2[:, j:j + 1]),
                     r=[b_x2[j]], w=[self.b_junk, b_n2[j]])
                S.op("act", lambda e, j=j: e.activation(sq2[:, j:j + 1], ss2[:, j:j + 1], AF.Sqrt,
                                                        bias=self.epsc[:, :], scale=1.0 / D),
                     r=[b_n2[j], self.b_epsc], w=[b_n2[j]])
                S.op("dve", lambda e, j=j: e.reciprocal(sq2[:, j:j + 1], sq2[:, j:j + 1]), r=[b_n2[j]], w=[b_n2[j]])
                S.op("dve", lambda e, j=j: e.scalar_tensor_tensor(
                    yo[j][:, :], x2[j][:, :], sq2[:, j:j + 1], gfb[:, :], ALU.mult, ALU.mult),
                    r=[b_x2[j], b_n2[j], b_c], w=[b_yo[j]])
                S.dma("sp", self.oy.ap()[tok - 128:tok, :], yo[j][:, :], b_yo[j], r=[b_yo[j]])
        for i2 in range(2):
            S.dma("sp", dap(self.ocv, i2 * DFF, [[1, 128], [128, NM]]), carry[:, :, i2], b_carry, r=[b_carry],
                  allow_slow_non_contiguous=True)
        NS = 16
        carS = sb("carS", [128, NM, 4, 2], F32)
        b_carS = Buf("carS")
        for s_ in range(4):
            for i2 in range(2):
                S.dma("sp", carS[:, :, s_, i2], dap(self.st_conv, (s_ * 2 + i2) * DFF, [[1, 128], [128, NM]]),
                      b_carS, w=[b_carS], allow_slow_non_contiguous=True)
        xt, b_xt = X1[0], b_X1[0]
        S.dma("sp", xt[0:NS, 0, :], self.X1ss.ap(), b_xt, w=[b_xt])
        xnT, b_xnT = xnTs[NK % 2], b_xnTs[NK % 2]
        S.dma("sp", xnT[:, :, 0:NS], self.XN2ss.ap().rearrange("k p t -> p k t"), b_xnT, w=[b_xnT])
        a3 = sb("a3", [128, 4, 6], F32)
        b_a3 = Buf("a3")
        for mt in range(NM):
            pa, b_pa = self.next_ps()
            pv, b_pv = self.next_ps()
            for (pp, bp, c0) in ((pa, b_pa, mt * 128), (pv, b_pv, DFF + mt * 128)):
                for kc in range(8):
                    S.op("pe", lambda e, pp=pp, kc=kc, c0=c0, xnT=xnT: e.matmul(
                        pp[:, 0:NS], Wup[:, kc, c0:c0 + 128], xnT[:, kc, 0:NS], start=(kc == 0), stop=(kc == 7)),
                        r=[b_W, b_xnT], w=[bp], inc=(kc == 7))
            t_, b_t = tc_[mt % 2], b_tc[mt % 2]
            t3 = t_[:, 0:NS].rearrange("p (s t) -> p s t", s=4)
            S.op("act", lambda e, mt=mt: e.activation(a3[:, :, 0:2], carS[:, mt, :, :], AF.Copy), r=[b_carS], w=[b_a3])
            S.op("act", lambda e, pa=pa: e.activation(
                a3[:, :, 2:6], pa[:, 0:NS].rearrange("p (s t) -> p s t", s=4), AF.Copy), r=[b_pa], w=[b_a3])
            S.op("dve", lambda e, t3=t3, mt=mt: e.tensor_scalar(
                t3, a3[:, :, 2:6], cw[:, 2, mt:mt + 1], cb[:, mt:mt + 1], ALU.mult, ALU.add), r=[b_a3, b_c], w=[b_t])
            S.op("dve", lambda e, t3=t3, mt=mt: e.scalar_tensor_tensor(
                t3, a3[:, :, 1:5], cw[:, 1, mt:mt + 1], t3, ALU.mult, ALU.add), r=[b_a3, b_c], w=[b_t])
            S.op("dve", lambda e, t3=t3, mt=mt: e.scalar_tensor_tensor(
                t3, a3[:, :, 0:4], cw[:, 0, mt:mt + 1], t3, ALU.mult, ALU.add), r=[b_a3, b_c], w=[b_t])
            S.op("dve", lambda e, mt=mt: e.tensor_copy(carS[:, mt, :, :], a3[:, :, 4:6]), r=[b_a3], w=[b_carS])
            S.op("act", lambda e, t_=t_: e.activation(t_[:, 0:NS], t_[:, 0:NS], AF.Silu), r=[b_t], w=[b_t])
            S.op("dve", lambda e, t_=t_, pv=pv, mt=mt: e.tensor_tensor(hT[:, mt, 0:NS], t_[:, 0:NS], pv[:, 0:NS], ALU.mult),
                 r=[b_t, b_pv], w=[b_hT])
        for s_ in range(4):
            for i2 in range(2):
                S.dma("sp", dap(self.ocvs, (s_ * 2 + i2) * DFF, [[1, 128], [128, NM]]), carS[:, :, s_, i2],
                      b_carS, r=[b_carS], allow_slow_non_contiguous=True)
        j = 0
        for nh in range(2):
            pt, b_pt = self.next_ps()
            for kc in range(NM):
                S.op("pe", lambda e, pt=pt, kc=kc, nh=nh: e.matmul(
                    pt[0:NS, 0:512], hT[:, kc, 0:NS], Wdn[:, kc, nh * 512:(nh + 1) * 512],
                    start=(kc == 0), stop=(kc == NM - 1)), r=[b_Wdn, b_hT], w=[b_pt], inc=(kc == NM - 1))
            S.op("dve", lambda e, pt=pt, nh=nh: e.tensor_tensor(
                x2[j][0:NS, nh * 512:(nh + 1) * 512], xt[0:NS, 0, nh * 512:(nh + 1) * 512], pt[0:NS, 0:512], ALU.add),
                r=[b_pt, b_xt], w=[b_x2[j]])
        S.op("act", lambda e: e.activation(self.junk[0:NS, :], x2[j][0:NS, :], AF.Square, accum_out=ss2[0:NS, j:j + 1]),
             r=[b_x2[j]], w=[self.b_junk, b_n2[j]])
        S.op("act", lambda e: e.activation(sq2[0:NS, j:j + 1], ss2[0:NS, j:j + 1], AF.Sqrt,
                                           bias=self.epsc[0:NS, :], scale=1.0 / D),
             r=[b_n2[j], self.b_epsc], w=[b_n2[j]])
        S.op("dve", lambda e: e.reciprocal(sq2[0:NS, j:j + 1], sq2[0:NS, j:j + 1]), r=[b_n2[j]], w=[b_n2[j]])
        S.op("dve", lambda e: e.scalar_tensor_tensor(
            yo[j][0:NS, :], x2[j][0:NS, :], sq2[0:NS, j:j + 1], gfb[0:NS, :], ALU.mult, ALU.mult),
            r=[b_x2[j], b_n2[j], b_c], w=[b_yo[j]])
        S.dma("sp", self.oys.ap(), yo[j][0:NS, :], b_yo[j], r=[b_yo[j]])


_CACHE = {}


def get_prog(phases):
    key = tuple(sorted(phases))
    if key not in _CACHE:
        phases = set(phases)
        if 3 in phases:
            phases |= {31, 32, 33}
        p = Prog(phases)
        p.build()
        _CACHE[key] = p
    return _CACHE[key]


def _rel_bucket(dist):
    n = np.maximum(dist, 0)
    nf = np.maximum(n, 1).astype(np.float32)
    large = 16 + (np.log(nf / np.float32(16)) / np.float32(math.log(2048 / 16)) * np.float32(16)).astype(np.int32)
    large = np.minimum(large, 31)
    return np.where(n < 16, n, large)


def _bucket_onehot():
    oh = np.zeros((32, 3 * 129), np.float32)
    for g in range(3):
        b = _rel_bucket(np.arange(129) * DILS[g])
        oh[b, g * 129 + np.arange(129)] = 1.0
    return oh


def make_in_maps(inputs):
    xp = np.asarray(inputs["x_prompt"], dtype=np.float32)
    maps = []
    ident = np.eye(128, dtype=np.float32)
    boh = _bucket_onehot()
    for c in range(NCORES):
        b, half = c // 2, c % 2
        s0 = half * 4096
        lo = s0 - (T_ALL - 4096)
        xall = np.zeros((T_ALL, D), np.float32)
        src_lo = max(lo, 0)
        xall[src_lo - lo:, :] = xp[b, src_lo:s0 + 4096, :]
        m = {
            "xall": xall,
            "w_in": np.ascontiguousarray(inputs["w_in"][0]),
            "norm1_g": np.ascontiguousarray(inputs["norm1_g"][0]),
            "ident": ident,
            "st_conv": np.ascontiguousarray(inputs["state_ffn_conv"][0, 4 * c:4 * c + 4]),
            "st_re": np.ascontiguousarray(inputs["state_ssm_re"][0, 4 * c:4 * c + 4]),
            "st_im": np.ascontiguousarray(inputs["state_ssm_im"][0, 4 * c:4 * c + 4]),
            "cache0": np.ascontiguousarray(inputs["cache_kv_w128"][0, 4 * c:4 * c + 4].reshape(4, 128, 512)),
            "cache1": np.ascontiguousarray(inputs["cache_kv_w512"][0, 4 * c:4 * c + 4].reshape(4, 512, 512)),
            "cache2": np.ascontiguousarray(inputs["cache_kv_w2048"][0, 4 * c:4 * c + 4].reshape(4, 2048, 512)),
            "xsamp": np.ascontiguousarray(inputs["x_sample"][4 * c:4 * c + 4].reshape(16, D)),
            "rel_bias": np.ascontiguousarray(inputs["rel_bias"]),
            "bucket_oh": boh,
            "antiident": np.ascontiguousarray(np.eye(128, dtype=np.float32)[::-1]),
            "hv": np.full((128, 1), float(half), np.float32),
            "ssm_log_dt": np.ascontiguousarray(inputs["ssm_log_dt"][0]),
            "ssm_lambda_re": np.ascontiguousarray(inputs["ssm_lambda_re"][0]),
            "ssm_lambda_im": np.ascontiguousarray(inputs["ssm_lambda_im"][0]),
            "ssm_b_re": np.ascontiguousarray(inputs["ssm_b_re"][0]),
            "ssm_b_im": np.ascontiguousarray(inputs["ssm_b_im"][0]),
            "ssm_c_re": np.ascontiguousarray(inputs["ssm_c_re"][0]),
            "ssm_c_im": np.ascontiguousarray(inputs["ssm_c_im"][0]),
            "ssm_d": np.ascontiguousarray(inputs["ssm_d"][0]),
            "w_glu": np.ascontiguousarray(inputs["w_glu"][0]),
            "b_glu": np.ascontiguousarray(inputs["b_glu"][0]),
            "w_branch_attn": np.ascontiguousarray(inputs["w_branch_attn"][0]),
            "w_branch_ssm": np.ascontiguousarray(inputs["w_branch_ssm"][0]),
            "w_out": np.ascontiguousarray(inputs["w_out"][0]),
            "norm2_g": np.ascontiguousarray(inputs["norm2_g"][0]),
            "w_up": np.ascontiguousarray(inputs["w_up"][0]),
            "conv_w": np.ascontiguousarray(inputs["conv_w"][0]),
            "conv_b": np.ascontiguousarray(inputs["conv_b"][0]),
            "w_down": np.ascontiguousarray(inputs["w_down"][0]),
            "norm_f_g": np.ascontiguousarray(inputs["norm_f_g"]),
        }
        maps.append(m)
    return maps


def kernel(**inputs):
    prog = get_prog(_PHASES)
    maps = make_in_maps(inputs)
    maps = [{k: v for k, v in m.items() if k in prog.din} for m in maps]
    res = run_bass_kernel_spmd(prog.nc, maps, core_ids=list(range(NCORES)))
    R = res.results
    B = 4
    outs = [None] * 14
    for g in range(3):
        W = WINS[g]
        a = np.zeros((1, B, W, 2, 4, 64), np.float32)
        for b in range(B):
            a[0, b] = R[2 * b + 1]["okv%d" % g].reshape(W, 2, 4, 64)
        outs[2 + g] = a
    if "oy" in R[0]:
        y = np.zeros((B, 8192, D), np.float32)
        for c in range(NCORES):
            y[c // 2, (c % 2) * 4096:(c % 2 + 1) * 4096] = R[c]["oy"]
        outs[0] = y
        cv = np.zeros((1, B, 2, DFF), np.float32)
        for b in range(B):
            cv[0, b] = R[2 * b + 1]["ocv"]
        outs[7] = cv
    if "oys" in R[0]:
        outs[1] = np.concatenate([R[c]["oys"].reshape(4, 4, D) for c in range(NCORES)], axis=0)
        outs[13] = np.concatenate([R[c]["ocvs"] for c in range(NCORES)], axis=0)[None]
    if "okvs0" in R[0]:
        for g in range(3):
            outs[8 + g] = np.concatenate([R[c]["okvs%d" % g].reshape(4, 4, 2, 4, 64) for c in range(NCORES)], axis=0)[None]
    if "ossm_s_re" in R[0]:
        outs[11] = np.concatenate([R[c]["ossm_s_re"] for c in range(NCORES)], axis=0)[None]
        outs[12] = np.concatenate([R[c]["ossm_s_im"] for c in range(NCORES)], axis=0)[None]
    if "ossm_re" in R[0]:
        for i, nm in ((5, "ossm_re"), (6, "ossm_im")):
            a = np.zeros((1, B, 32, 64), np.float32)
            for b in range(B):
                a[0, b] = R[2 * b + 1][nm]
            outs[i] = a
    shapes = [(4, 8192, 1024), (32, 4, 1024), None, None, None, (1, 4, 32, 64), (1, 4, 32, 64),
              (1, 4, 2, DFF), (1, 32, 4, 2, 4, 64), (1, 32, 4, 2, 4, 64), (1, 32, 4, 2, 4, 64),
              (1, 32, 32, 64), (1, 32, 32, 64), (1, 32, 2, DFF)]
    for i in range(14):
        if outs[i] is None:
            outs[i] = np.zeros(shapes[i], np.float32)
    return tuple(outs)
```

```python
import math
import os
from contextlib import ExitStack

import numpy as np
import concourse.bass as bass
import concourse.mybir as mybir
from concourse.bass_utils import run_bass_kernel_spmd

F32 = mybir.dt.float32
BF16 = mybir.dt.bfloat16
AF = mybir.ActivationFunctionType
ALU = mybir.AluOpType
AX = mybir.AxisListType

NCORES = 8
D = 1024
INW = 4864
DFF = 2816
TN = 384
NT_ALL = 22
T_ALL = NT_ALL * TN
T_MAIN0 = 11 * TN
T_MAIN = 11 * TN
KV_T0 = 5 * TN
T_KV = T_ALL - KV_T0
EPS = 1e-6
WINS = (128, 512, 2048)
DILS = (1, 4, 16)
LCH = 16
_DBG = int(os.environ.get("K_DBG", "9"))
_DBG2 = int(os.environ.get("K_DBG2", "0"))
_PHASES = set(int(v) for v in os.environ.get("K_PH", "1,2,3,4,5").split(","))
_P3B = int(os.environ.get("K_P3B", "9"))
_ADEPTH = int(os.environ.get("K_ADEPTH", "2"))
_TR = tuple(int(v) for v in os.environ.get("K_TILES", "0,22").split(","))


class Sem:
    _n = 0

    def __init__(self, h):
        self.h = h
        Sem._n += 1
        self.id = Sem._n


class Buf:
    __slots__ = ("name", "w", "r", "sem", "semcnt", "excl")

    def __init__(self, name, excl=False):
        self.name = name
        self.excl = excl
        self.w = None
        self.r = []
        self.sem = None
        self.semcnt = 0


class Sched:
    def __init__(self, nc, es):
        self.nc = nc
        self.es = es
        self.engs = {"pe": nc.tensor, "act": nc.scalar, "dve": nc.vector,
                     "pool": nc.gpsimd, "sp": nc.sync}
        self.ops = {k: [] for k in self.engs}
        self.sem = {k: Sem(es.enter_context(nc.semaphore("sem_" + k)))
                    for k in ("pe", "act", "dve", "pool")}
        self.cnt = {k: 0 for k in self.sem}
        self.seen = {k: {} for k in self.engs}
        self.nsem = 4
        self.pending_noinc = {k: False for k in self.sem}
        self.free_sems = []
        self.dma_bufs = []

    def _need(self, eng, r, w):
        deps = []
        for b in r:
            if b.w is not None:
                deps.append(b.w)
        for b in w:
            if b.w is not None:
                deps.append(b.w)
            deps.extend(b.r)
        waits = {}
        for (s, v) in deps:
            if eng == "pe" and s is self.sem["pe"]:
                continue
            if waits.get(s, (None, 0))[1] < v:
                waits[s] = (s, v)
        need = []
        seen = self.seen[eng]
        for s, v in waits.values():
            if seen.get(s.id, 0) < v:
                seen[s.id] = v
                need.append((s.h, v))
        return need

    def _record(self, tk, r, w):
        for b in r:
            b.r.append(tk)
        for b in w:
            b.w = tk
            b.r = []

    def op(self, eng, fn, r=(), w=(), inc=True):
        if eng != "pe":
            ex = [b for b in r if b.excl]
            if ex:
                r = [b for b in r if not b.excl]
                w = list(w) + ex
        need = self._need(eng, r, w)
        s = self.sem[eng]
        if inc:
            self.cnt[eng] += 1
            tk = (s, self.cnt[eng])
            self.ops[eng].append((need, fn, s.h, 1))
            self.pending_noinc[eng] = False
        else:
            tk = (s, self.cnt[eng] + 1)
            self.ops[eng].append((need, fn, None, 0))
            self.pending_noinc[eng] = True
        self._record(tk, r, w)
        return tk

    def dma(self, q, out, in_, key, r=(), w=(), **kw):
        need = self._need(q, r, w)
        if key.sem is None:
            if self.free_sems:
                key.sem, key.semcnt = self.free_sems.pop()
            else:
                key.sem = Sem(self.es.enter_context(self.nc.semaphore("dsem_%d" % self.nsem)))
                self.nsem += 1
            self.dma_bufs.append(key)
        key.semcnt += 16
        tk = (key.sem, key.semcnt)
        self.ops[q].append((need, lambda e: e.dma_start(out=out, in_=in_, **kw), key.sem.h, 16))
        self._record(tk, r, w)
        return tk

    def drain_dmas(self, eng="sp"):
        need = []
        for b in self.dma_bufs:
            if self.seen[eng].get(b.sem.id, 0) < b.semcnt:
                self.seen[eng][b.sem.id] = b.semcnt
                need.append((b.sem.h, b.semcnt))
            self.free_sems.append((b.sem, b.semcnt))
            b.sem = None
        self.dma_bufs = []
        self.ops[eng].append((need, None, None, 0))

    def wait_all(self, eng, bufs):
        need = self._need(eng, (), bufs)
        self.ops[eng].append((need, None, None, 0))

    def emit(self, block):
        for k in self.pending_noinc:
            assert not self.pending_noinc[k], k

        def run(name):
            lst = self.ops[name]

            def f(e):
                for need, fn, sh, inc in lst:
                    for (h, v) in need:
                        e.wait_ge(h, v)
                    if fn is None:
                        continue
                    ins = fn(e)
                    if sh is not None:
                        ins.then_inc(sh, inc)
            return f
        block.sync(run("sp"))
        block.tensor(run("pe"))
        block.scalar(run("act"))
        block.vector(run("dve"))
        block.gpsimd(run("pool"))
        self.ops = {k: [] for k in self.engs}


def dap(t, off, dims):
    return bass.AP(t, off, [list(d) for d in dims])


class Prog:
    def __init__(self, phases):
        self.phases = phases
        self.nc = bass.Bass("TRN2", target_bir_lowering=False)
        self.es = ExitStack()
        self.S = None
        self.din = {}
        self.dout = {}
        self.outbufs = []
        self.phn = 0

    def inp(self, name, shape, dt=F32):
        t = self.nc.dram_tensor(name, list(shape), dt, kind="ExternalInput")
        self.din[name] = t
        return t

    def outp(self, name, shape, dt=F32):
        t = self.nc.dram_tensor(name, list(shape), dt, kind="ExternalOutput")
        self.dout[name] = t
        return t

    def scr(self, name, shape, dt):
        return self.nc.dram_tensor("d_" + name, list(shape), dt)

    def sb(self, name, shape, dt, glob=False):
        es = self.es if glob else self.pes
        return es.enter_context(self.nc.sbuf_tensor("s%d_%s" % (self.phn, name), list(shape), dt))

    def run_phase(self, fn):
        self.phn += 1
        with ExitStack() as pes:
            self.pes = pes
            fn()
            self.S.drain_dmas("sp")
            with self.nc.Block() as block:
                self.S.emit(block)
        self.pes = self.es

    def ps(self, name, shape, dt):
        return self.es.enter_context(self.nc.psum_tensor("p_" + name, list(shape), dt))

    def build(self):
        nc, es = self.nc, self.es
        with es:
            self._declare_io()
            self.S = Sched(nc, es)
            self.pes = es
            self._consts()
            for ph, fn in ((1, self.phase1), (2, self.phase2), (31, self.phase3a), (32, self.phase3b), (33, self.phase3c), (4, self.phase4), (5, self.phase5)):
                if ph in self.phases:
                    self.run_phase(fn)

            self.run_phase(lambda: None)
        return nc

    def _declare_io(self):
        self.xall = self.inp("xall", [T_ALL, D])
        self.w_in = self.inp("w_in", [D, INW])
        self.norm1_g = self.inp("norm1_g", [D])
        self.ident_in = self.inp("ident", [128, 128])
        self.okv = [self.outp("okv%d" % g, [WINS[g], 2, 256]) for g in range(3)]
        self.rel_bias = self.inp("rel_bias", [32, 12])
        self.bucket_oh = self.inp("bucket_oh", [32, 3 * 129])
        self.hv_in = self.inp("hv", [128, 1])
        self.antiident = self.inp("antiident", [128, 128])
        self.EXT = self.scr("EXT", [12, 385], F32)
        self.log_dt = self.inp("ssm_log_dt", [32])
        self.lam_re = self.inp("ssm_lambda_re", [32, 64])
        self.lam_im = self.inp("ssm_lambda_im", [32, 64])
        self.b_re = self.inp("ssm_b_re", [32, 64, 16])
        self.b_im = self.inp("ssm_b_im", [32, 64, 16])
        self.c_re = self.inp("ssm_c_re", [32, 16, 64])
        self.c_im = self.inp("ssm_c_im", [32, 16, 64])
        self.ssm_d = self.inp("ssm_d", [512])
        self.w_glu = self.inp("w_glu", [512, 512])
        self.b_glu = self.inp("b_glu", [512])
        self.w_ba = self.inp("w_branch_attn", [256, D])
        self.w_bs = self.inp("w_branch_ssm", [512, D])
        self.w_out = self.inp("w_out", [D, D])
        self.norm2_g = self.inp("norm2_g", [D])
        self.w_up = self.inp("w_up", [D, 2 * DFF])
        self.conv_w = self.inp("conv_w", [3, DFF])
        self.conv_b = self.inp("conv_b", [DFF])
        self.w_down = self.inp("w_down", [DFF, D])
        self.norm_f_g = self.inp("norm_f_g", [D])
        self.oy = self.outp("oy", [4096, D])
        self.ocv = self.outp("ocv", [2, DFF])
        if _DBG2:
            self.X1s = self.outp("X1s", [T_MAIN, D], F32)
        else:
            self.X1s = self.scr("X1s", [T_MAIN, D], F32)
        self.xsamp = self.inp("xsamp", [16, D])
        self.XN2s = self.scr("XN2s", [8, 128, T_MAIN], BF16)
        self.XN2ss = self.scr("XN2ss", [8, 128, 16], BF16)
        self.HBs = self.scr("HBs", [2, 128, 16, T_MAIN // LCH], BF16)
        self.st_conv = self.inp("st_conv", [4, 2, DFF])
        self.ocvs = self.outp("ocvs", [4, 2, DFF])
        self.oys = self.outp("oys", [16, D])
        self.X1ss = self.scr("X1ss", [16, D], F32)
        self.st_re = self.inp("st_re", [4, 32, 64])
        self.st_im = self.inp("st_im", [4, 32, 64])
        self.ossm_s_re = self.outp("ossm_s_re", [4, 32, 64])
        self.ossm_s_im = self.outp("ossm_s_im", [4, 32, 64])
        self.YSs = self.scr("YSs", [4, 128, 16], BF16)
        self.caches = [self.inp("cache%d" % g, [4, WINS[g], 512]) for g in range(3)]
        self.ATTs = self.scr("ATTs", [4, 64, 16], BF16)
        self.QTs = self.scr("QTs", [6, 128, 16], BF16)
        self.KTs = self.scr("KTs", [6, 128, 16], BF16)
        self.USs = self.scr("USs", [4, 128, 16], BF16)
        self.GSs = self.scr("GSs", [16, 128, 16], BF16)
        self.VSs = self.scr("VSs", [16, 768], BF16)
        self.okvs = [self.outp("okvs%d" % g, [16, 2, 256]) for g in range(3)]
        self.ossm_re = self.outp("ossm_re", [32, 64])
        self.ossm_im = self.outp("ossm_im", [32, 64])
        if _DBG2:
            self.YS = self.outp("YS", [4, 128, T_MAIN], BF16)
        else:
            self.YS = self.scr("YS", [4, 128, T_MAIN], BF16)
        self.VBs = self.scr("VBs", [2, 128, (LCH + 1) * 16 * 32], BF16)
        self.KBs = self.scr("KBs", [128, LCH, 512], BF16)
        self.WSs = self.scr("WSs", [2, 128, LCH, 512], BF16)
        if _DBG2:
            self.TAB = self.outp("TAB", [128, 4, 16], F32)
        else:
            self.TAB = self.scr("TAB", [128, 4, 16], F32)
        if _DBG2:
            self.ATT = self.outp("ATT", [4, 64, T_MAIN], BF16)
        else:
            self.ATT = self.scr("ATT", [4, 64, T_MAIN], BF16)
        self.QT = self.scr("QT", [6, 128, T_MAIN], BF16)
        self.KT = self.scr("KT", [6, 128, T_KV], BF16)
        self.VS = self.scr("VS", [T_KV, 768], BF16)
        self.US = self.scr("US", [4, 128, T_ALL], BF16)
        self.GS = self.scr("GS", [16, 128, T_MAIN], BF16)

    def _consts(self):
        S = self.S
        self.ident_f = self.sb("ident_f", [128, 128], F32, glob=True)
        self.ident = self.sb("ident", [128, 128], BF16, glob=True)
        self.b_ident = Buf("ident")
        S.dma("sp", self.ident_f[:], self.ident_in.ap(), self.b_ident, w=[self.b_ident])
        S.op("dve", lambda e: e.tensor_copy(self.ident[:], self.ident_f[:]),
             r=[self.b_ident], w=[self.b_ident])
        self.psb = [self.ps("psb%d" % i, [128, 512], F32) for i in range(6)]
        self.b_psb = [Buf("psb%d" % i, True) for i in range(6)]
        self.pst = [self.ps("pst%d" % i, [128, 1024], BF16) for i in range(2)]
        self.b_pst = [Buf("pst%d" % i, True) for i in range(2)]
        self.psi = 0

    def next_ps(self):
        i = self.psi % 6
        self.psi += 1
        return self.psb[i], self.b_psb[i]

    def load_weight_bf16(self, dst, dst_buf, src_dram, nk, ncols, gcol, stg, stg_bufs, nsplit=1, q="sp", row0=0):
        S = self.S
        cw = ncols // nsplit
        i = 0
        for kc in range(nk):
            for sp in range(nsplit):
                sbuf_t, sb_b = stg[i % 2], stg_bufs[i % 2]
                i += 1
                c0 = sp * cw
                src = src_dram.ap()[row0 + kc * 128:row0 + (kc + 1) * 128, c0:c0 + cw]
                S.dma(q, sbuf_t[:, 0:cw], src, sb_b, w=[sb_b])
                eng = "dve" if (i % 2 == 0) else "act"
                if gcol is not None:
                    gc, gb = gcol
                    if eng == "dve":
                        S.op("dve", lambda e, kc=kc, sbuf_t=sbuf_t, gc=gc, c0=c0: e.tensor_scalar(
                            dst[:, kc, c0:c0 + cw], sbuf_t[:, 0:cw], gc[:, kc:kc + 1], None, ALU.mult),
                            r=[sb_b, gb], w=[dst_buf])
                    else:
                        S.op("act", lambda e, kc=kc, sbuf_t=sbuf_t, gc=gc, c0=c0: e.activation(
                            dst[:, kc, c0:c0 + cw], sbuf_t[:, 0:cw], AF.Copy, scale=gc[:, kc:kc + 1]),
                            r=[sb_b, gb], w=[dst_buf])
                else:
                    if eng == "dve":
                        S.op("dve", lambda e, kc=kc, sbuf_t=sbuf_t, c0=c0: e.tensor_copy(
                            dst[:, kc, c0:c0 + cw], sbuf_t[:, 0:cw]), r=[sb_b], w=[dst_buf])
                    else:
                        S.op("act", lambda e, kc=kc, sbuf_t=sbuf_t, c0=c0: e.activation(
                            dst[:, kc, c0:c0 + cw], sbuf_t[:, 0:cw], AF.Copy), r=[sb_b], w=[dst_buf])

    def norm_transpose(self, xt, b_xt, nsub, rows, xnT, b_xnT, tok0=0):
        S = self.S
        ss, b_ss = self.ss, self.b_ss
        for s in range(nsub):
            S.op("act", lambda e, s=s: e.activation(
                self.junk[:rows, :], xt[:rows, s, :], AF.Square, accum_out=ss[:rows, s:s + 1]),
                r=[b_xt], w=[self.b_junk, b_ss])
        S.op("act", lambda e: e.activation(
            self.sq[:rows, 0:nsub], ss[:rows, 0:nsub], AF.Sqrt, bias=self.epsc[:rows, :], scale=1.0 / D),
            r=[b_ss, self.b_epsc], w=[self.b_sq])
        S.op("dve", lambda e: e.reciprocal(self.rstd[:rows, 0:nsub], self.sq[:rows, 0:nsub]),
             r=[self.b_sq], w=[self.b_rstd])
        for s in range(nsub):
            S.op("dve", lambda e, s=s: e.tensor_scalar(
                self.xs[:rows, s, :], xt[:rows, s, :], self.rstd[:rows, s:s + 1], None, ALU.mult),
                r=[b_xt, self.b_rstd], w=[self.b_xs[s]])
        for s in range(nsub):
            pt, b_pt = self.pst[s % 2], self.b_pst[s % 2]
            for kc in range(8):
                S.op("pe", lambda e, s=s, kc=kc, pt=pt: e.transpose(
                    pt[:, kc * 128:kc * 128 + rows], self.xs[:rows, s, kc * 128:(kc + 1) * 128],
                    self.ident[:rows, :rows]),
                    r=[self.b_xs[s], self.b_ident], w=[b_pt], inc=(kc == 7))
            S.op("act", lambda e, s=s, pt=pt: e.activation(
                xnT[:, :, tok0 + s * 128:tok0 + s * 128 + rows],
                pt[:, :].rearrange("p (k t) -> p k t", k=8)[:, :, 0:rows], AF.Copy),
                r=[b_pt], w=[b_xnT])

    def phase1(self):
        S = self.S
        sb = self.sb
        self.g1c = sb("g1c", [128, 8], F32)
        self.b_g1c = Buf("g1c")
        S.dma("sp", self.g1c[:], dap(self.norm1_g, 0, [[1, 128], [128, 8]]), self.b_g1c,
              w=[self.b_g1c], allow_slow_non_contiguous=True)
        self.epsc = sb("epsc", [128, 1], F32)
        self.b_epsc = Buf("epsc")
        S.op("dve", lambda e: e.memset(self.epsc[:], EPS), w=[self.b_epsc])
        Wi = sb("Wi", [128, 8, INW], BF16)
        b_Wi = Buf("Wi")
        stg = [sb("wstg%d" % i, [128, INW // 2], F32) for i in range(2)]
        b_stg = [Buf("wstg%d" % i) for i in range(2)]
        if _DBG >= 1:
            self.load_weight_bf16(Wi, b_Wi, self.w_in, 8, INW, (self.g1c, self.b_g1c), stg, b_stg, nsplit=2)
        self.junk = sb("junk", [128, D], BF16)
        self.b_junk = Buf("junk")
        self.ss = sb("ss", [128, 4], F32)
        self.b_ss = Buf("ss")
        self.sq = sb("sq", [128, 4], F32)
        self.b_sq = Buf("sq")
        self.rstd = sb("rstd", [128, 4], F32)
        self.b_rstd = Buf("rstd")
        self.xs = sb("xs", [128, 3, D], BF16)
        self.b_xs = [Buf("xs%d" % i) for i in range(3)]
        xt = [sb("xt%d" % i, [128, 3, D], F32) for i in range(2)]
        b_xt = [Buf("xt%d" % i) for i in range(2)]
        xnT = [sb("xnT%d" % i, [128, 8, TN], BF16) for i in range(2)]
        b_xnT = [Buf("xnT%d" % i) for i in range(2)]
        fst = [sb("fst%d" % i, [128, 4, TN], BF16) for i in range(2)]
        b_fst = [Buf("fst%d" % i) for i in range(2)]
        vst = [sb("vst%d" % i, [128, 768], BF16) for i in range(2)]
        b_vst = [Buf("vst%d" % i) for i in range(2)]
        kvst = [sb("kvst%d" % i, [128, 2, 768], F32) for i in range(2)]
        b_kvst = [Buf("kvst%d" % i) for i in range(2)]
        b_okv = [Buf("okv%d" % g) for g in range(3)]
        self.outbufs += b_kvst
        fcount = 0
        vcount = 0

        def load_x(ti):
            S.dma("sp", xt[ti % 2][:],
                  self.xall.ap()[ti * TN:(ti + 1) * TN, :].rearrange("(s p) d -> p s d", p=128),
                  b_xt[ti % 2], w=[b_xt[ti % 2]])

        load_x(_TR[0])
        if _TR[0] + 1 < _TR[1]:
            load_x(_TR[0] + 1)
        self.norm_transpose(xt[_TR[0] % 2], b_xt[_TR[0] % 2], 3, 128, xnT[_TR[0] % 2], b_xnT[_TR[0] % 2])
        for ti in range(_TR[0], _TR[1] if _DBG >= 2 else _TR[0]):
            xn, b_xn = xnT[ti % 2], b_xnT[ti % 2]
            did_next = False

            def prep_next(ti=ti):
                if ti + 1 < _TR[1]:
                    if ti + 2 < _TR[1]:
                        load_x(ti + 2)
                    self.norm_transpose(xt[(ti + 1) % 2], b_xt[(ti + 1) % 2], 3, 128,
                                        xnT[(ti + 1) % 2], b_xnT[(ti + 1) % 2])
            is_main = ti >= 11
            has_kv = ti >= 5
            if _DBG < 3:
                continue
            fm = []
            for m in range(4):
                fm.append((2304 + m * 128, self.US, m, ti * TN))
            if has_kv:
                for m in range(6):
                    fm.append((768 + m * 128, self.KT, m, ti * TN - KV_T0))
            if is_main:
                for m in range(6):
                    fm.append((m * 128, self.QT, m, ti * TN - T_MAIN0))
                for m in range(16):
                    fm.append((2816 + m * 128, self.GS, m, ti * TN - T_MAIN0))
            i = 0
            while i < len(fm):
                if not did_next and i >= len(fm) // 2:
                    prep_next()
                    did_next = True
                grp = [fm[i]]
                while len(grp) < 4 and i + len(grp) < len(fm) and fm[i + len(grp)][1] is grp[0][1]:
                    grp.append(fm[i + len(grp)])
                st, b_st = fst[fcount % 2], b_fst[fcount % 2]
                fcount += 1
                for j, (c0, dst, m, t0) in enumerate(grp):
                    pt, b_pt = self.next_ps()
                    for kc in range(8):
                        S.op("pe", lambda e, kc=kc, c0=c0, pt=pt, xn=xn: e.matmul(
                            pt[:, 0:TN], Wi[:, kc, c0:c0 + 128], xn[:, kc, :],
                            start=(kc == 0), stop=(kc == 7)),
                            r=[b_Wi, b_xn], w=[b_pt], inc=(kc == 7))
                    eng = "act" if (j % 2 == 0) else "dve"
                    if dst is self.US:
                        o_ap = st[:, j, :].rearrange("p (t c) -> p c t", t=LCH)
                        i_ap = pt[:, 0:TN].rearrange("p (c t) -> p c t", t=LCH)
                    else:
                        o_ap, i_ap = st[:, j, :], pt[:, 0:TN]
                    if eng == "act":
                        S.op("act", lambda e, o_ap=o_ap, i_ap=i_ap: e.activation(o_ap, i_ap, AF.Copy),
                             r=[b_pt], w=[b_st])
                    else:
                        S.op("dve", lambda e, o_ap=o_ap, i_ap=i_ap: e.tensor_copy(o_ap, i_ap),
                             r=[b_pt], w=[b_st])
                c0, dst, m0, t0 = grp[0]
                n = len(grp)
                if not (os.environ.get("K_NOGS") and (dst is self.GS or dst is self.QT)):
                    S.dma("sp", dst.ap()[m0:m0 + n, :, t0:t0 + TN].rearrange("m p t -> p m t"),
                          st[:, 0:n, :], b_st, r=[b_st])
                i += n
            if not did_next:
                prep_next()
                did_next = True
            if has_kv and _DBG >= 4:
                need_kout = (ti + 1) * TN > T_ALL - 2048
                for s in range(3):
                    tok = ti * TN + s * 128
                    kout = tok >= T_ALL - 2048
                    vt, b_vt = vst[vcount % 2], b_vst[vcount % 2]
                    kt, b_kt = kvst[vcount % 2], b_kvst[vcount % 2]
                    vcount += 1
                    for kv in ((0, 1) if kout else (1,)):
                        for (cc, nn) in ((0, 512), (512, 256)):
                            c0 = 768 * (1 + kv) + cc
                            pt, b_pt = self.next_ps()
                            for kc in range(8):
                                S.op("pe", lambda e, kc=kc, c0=c0, nn=nn, pt=pt, xn=xn, s=s: e.matmul(
                                    pt[:, 0:nn], xn[:, kc, s * 128:(s + 1) * 128], Wi[:, kc, c0:c0 + nn],
                                    start=(kc == 0), stop=(kc == 7)),
                                    r=[b_Wi, b_xn], w=[b_pt], inc=(kc == 7))
                            S.op("dve", lambda e, cc=cc, nn=nn, pt=pt, kt=kt, kv=kv: e.tensor_copy(
                                kt[:, kv, cc:cc + nn], pt[:, 0:nn]), r=[b_pt], w=[b_kt])
                    S.op("act", lambda e, kt=kt, vt=vt: e.activation(
                        vt[:, :], kt[:, 1, :], AF.Copy), r=[b_kt], w=[b_vt])
                    if _DBG >= 5:
                        S.dma("sp", self.VS.ap()[tok - KV_T0:tok - KV_T0 + 128, :], vt[:, :], b_vt, r=[b_vt])
                    if kout and _DBG >= 6:
                        for g in range(3):
                            w0 = T_ALL - WINS[g]
                            if tok >= w0:
                                S.dma("sp", self.okv[g].ap()[tok - w0:tok - w0 + 128, :, :],
                                      kt[:, :, 256 * g:256 * (g + 1)], b_kt, r=[b_kt])


        NS = 16
        xs_t = sb("xsamp", [128, 1, D], F32)
        b_xs_t = Buf("xsamp")
        S.dma("sp", xs_t[0:NS, 0, :], self.xsamp.ap(), b_xs_t, w=[b_xs_t])
        xnS = sb("xnS", [128, 8, NS], BF16)
        b_xnS = Buf("xnS")
        self.norm_transpose(xs_t, b_xs_t, 1, NS, xnS, b_xnS)
        fsS = sb("fsS", [128, 32, NS], BF16)
        b_fsS = Buf("fsS")
        fm = [(2304 + m * 128, self.USs, m) for m in range(4)] + [(768 + m * 128, self.KTs, m) for m in range(6)] \
            + [(m * 128, self.QTs, m) for m in range(6)] + [(2816 + m * 128, self.GSs, m) for m in range(16)]
        for j, (c0, dst, m) in enumerate(fm):
            pt, b_pt = self.next_ps()
            for kc in range(8):
                S.op("pe", lambda e, kc=kc, c0=c0, pt=pt: e.matmul(
                    pt[:, 0:NS], Wi[:, kc, c0:c0 + 128], xnS[:, kc, :], start=(kc == 0), stop=(kc == 7)),
                    r=[b_Wi, b_xnS], w=[b_pt], inc=(kc == 7))
            S.op("act", lambda e, j=j, pt=pt: e.activation(fsS[:, j, :], pt[:, 0:NS], AF.Copy), r=[b_pt], w=[b_fsS])
        for (j0, n, dst) in ((0, 4, self.USs), (4, 6, self.KTs), (10, 6, self.QTs), (16, 16, self.GSs)):
            S.dma("sp", dst.ap().rearrange("m p t -> p m t"), fsS[:, j0:j0 + n, :], b_fsS, r=[b_fsS])
        kvS = sb("kvS", [128, 2, 768], F32)
        vbS = sb("vbS", [128, 768], BF16)
        b_kvS = Buf("kvS")
        for kv in range(2):
            for (cc, nn) in ((0, 512), (512, 256)):
                c0 = 768 * (1 + kv) + cc
                pt, b_pt = self.next_ps()
                for kc in range(8):
                    S.op("pe", lambda e, kc=kc, c0=c0, nn=nn, pt=pt: e.matmul(
                        pt[0:NS, 0:nn], xnS[:, kc, :], Wi[:, kc, c0:c0 + nn], start=(kc == 0), stop=(kc == 7)),
                        r=[b_Wi, b_xnS], w=[b_pt], inc=(kc == 7))
                S.op("dve", lambda e, cc=cc, nn=nn, pt=pt, kv=kv: e.tensor_copy(
                    kvS[0:NS, kv, cc:cc + nn], pt[0:NS, 0:nn]), r=[b_pt], w=[b_kvS])
        S.op("act", lambda e: e.activation(vbS[0:NS, :], kvS[0:NS, 1, :], AF.Copy), r=[b_kvS], w=[b_kvS])
        S.dma("sp", self.VSs.ap(), vbS[0:NS, :], b_kvS, r=[b_kvS])
        for g in range(3):
            S.dma("sp", self.okvs[g].ap().rearrange("t k c -> t k c"), kvS[0:NS, :, 256 * g:256 * (g + 1)], b_kvS, r=[b_kvS])

    def build_expbias(self):
        S, sb = self.S, self.sb
        EB = sb("EB", [128, 12, 2, 128], F32)
        self.EB, self.b_EB = EB, Buf("EB")
        rb = sb("rb", [32, 12], F32)
        oh = sb("oh", [32, 3 * 129], F32)
        b_rb = Buf("rb")
        S.dma("act", rb[:], self.rel_bias.ap(), b_rb, w=[b_rb])
        S.dma("act", oh[:], self.bucket_oh.ap(), b_rb, w=[b_rb])
        ebx = sb("ebx", [12, 3, 129], F32)
        b_ebx = Buf("ebx")
        zt = sb("zt", [12, 385], F32)
        b_zt = Buf("zt")
        S.op("dve", lambda e: e.memset(zt[:], 0.0), w=[b_zt])
        b_ext = Buf("ext")
        S.dma("act", self.EXT.ap(), zt[:], b_zt, r=[b_zt], w=[b_ext])
        pt, b_pt = self.next_ps()
        S.op("pe", lambda e: e.matmul(pt[0:12, 0:387], rb[:, :], oh[:, :], start=True, stop=True),
             r=[b_rb], w=[b_pt])
        S.op("act", lambda e: e.activation(ebx[:, :, :].rearrange("p g j -> p (g j)"), pt[0:12, 0:387], AF.Exp),
             r=[b_pt], w=[b_ebx])
        for g in range(3):
            S.dma("act", self.EXT.ap()[4 * g:4 * g + 4, 128:257], ebx[4 * g:4 * g + 4, g, :], b_ebx,
                  r=[b_ebx], w=[b_ext])
        TH = sb("TH", [128, 12, 2, 128], F32)
        b_TH = Buf("TH")
        aid = sb("aid", [128, 128], F32)
        b_aid = Buf("aid")
        S.dma("act", aid[:], self.antiident.ap(), b_aid, w=[b_aid])
        for gh in range(12):
            for bi in range(2):
                S.dma("act", TH[:, gh, bi, :], dap(self.EXT, gh * 385 + 129 - 128 * bi, [[1, 128], [1, 128]]),
                      b_TH, r=[b_ext], w=[b_TH])
        for gh in range(12):
            pt, b_pt = self.next_ps()
            S.op("pe", lambda e, gh=gh, pt=pt: e.matmul(
                pt[:, 0:256], aid[:, :], TH[:, gh, :, :].rearrange("p b q -> p (b q)"), start=True, stop=True),
                r=[b_aid, b_TH], w=[b_pt])
            S.op("act", lambda e, gh=gh, pt=pt: e.activation(
                EB[:, gh, :, :].rearrange("p b q -> p (b q)"), pt[:, 0:256], AF.Copy),
                r=[b_pt], w=[self.b_EB])

    def attn_unit(self, kp_ap, kc_ap, q_ap, vp_ap, vc_ap, nq, gh, acc_ap, b_acc, first, rb):
        S = self.S
        ps_s, b_ps = self.next_ps()
        nb = len(self.Ebuf)
        E, b_E = self.Ebuf[self.ucnt % nb], self.b_Ebuf[self.ucnt % nb]
        P, b_P = self.Pbuf[self.ucnt % nb], self.b_Pbuf[self.ucnt % nb]
        self.ucnt += 1
        S.op("pe", lambda e: e.matmul(ps_s[:, 0:nq], kp_ap, q_ap, start=True, stop=True),
             r=rb, w=[b_ps], inc=False)
        S.op("pe", lambda e: e.matmul(ps_s[0:nq, 128:128 + nq], kc_ap, q_ap, start=True, stop=True),
             r=rb, w=[b_ps])
        psv = ps_s[:, 0:256].rearrange("p (b q) -> p b q", b=2)
        if nq == 128:
            S.op("act", lambda e: e.activation(E[:, :, :], psv, AF.Exp, scale=0.125), r=[b_ps], w=[b_E])
            S.op("dve", lambda e: e.tensor_tensor(P[:, :, :], E[:, :, :], self.EB[:, gh, :, :], ALU.mult),
                 r=[b_E, self.b_EB], w=[b_P])
        else:
            S.op("act", lambda e: e.activation(E[:, 0, 0:nq], ps_s[:, 0:nq], AF.Exp, scale=0.125),
                 r=[b_ps], w=[b_E])
            S.op("act", lambda e: e.activation(E[0:nq, 1, 0:nq], ps_s[0:nq, 128:128 + nq], AF.Exp, scale=0.125),
                 r=[b_ps], w=[b_E])
            S.op("dve", lambda e: e.tensor_tensor(P[:, 0, 0:nq], E[:, 0, 0:nq], self.EB[:, gh, 0, 0:nq], ALU.mult),
                 r=[b_E, self.b_EB], w=[b_P])
            S.op("dve", lambda e: e.tensor_tensor(P[0:nq, 1, 0:nq], E[0:nq, 1, 0:nq], self.EB[0:nq, gh, 1, 0:nq],
                                                  ALU.mult), r=[b_E, self.b_EB], w=[b_P])

        def stage_b():
            ps_o, b_po = self.next_ps()
            S.op("pe", lambda e: e.matmul(ps_o[0:65, 0:nq], vp_ap, P[:, 0, 0:nq], start=True, stop=False),
                 r=rb + [b_P], w=[b_po], inc=False)
            S.op("pe", lambda e: e.matmul(ps_o[0:65, 0:nq], vc_ap, P[0:nq, 1, 0:nq], start=False, stop=True),
                 r=rb + [b_P], w=[b_po])
            if first:
                S.op("dve", lambda e: e.tensor_copy(acc_ap, ps_o[0:65, 0:nq]), r=[b_po], w=[b_acc])
            else:
                S.op("dve", lambda e: e.tensor_tensor(acc_ap, acc_ap, ps_o[0:65, 0:nq], ALU.add),
                     r=[b_po], w=[b_acc])
        self.pending_b.append(stage_b)
        while len(self.pending_b) > self.attn_depth:
            self.pending_b.pop(0)()

    def attn_flush(self):
        while self.pending_b:
            self.pending_b.pop(0)()

    def sample_attention(self, sel, b_sel, rec, b_rec):
        S, sb = self.S, self.sb
        KnT = sb("KnT", [128, 6, 16], BF16)
        QnT = sb("QnT", [128, 6, 16], BF16)
        b_kq = Buf("knq")
        S.dma("sp", KnT[:], self.KTs.ap().rearrange("m p t -> p m t"), b_kq, w=[b_kq])
        S.dma("sp", QnT[:], self.QTs.ap().rearrange("m p t -> p m t"), b_kq, w=[b_kq])
        accS = sb("accS", [65, 4, 16], F32)
        b_accS = Buf("accS")
        CK = [sb("CK%d" % i, [128, 512], F32) for i in range(2)]
        Kb = [sb("Kb16_%d" % i, [128, 256], BF16) for i in range(2)]
        KcT = [sb("KcT%d" % i, [128, 2, 128], BF16) for i in range(2)]
        VcP = [sb("VcP%d" % i, [128, 4, 65], BF16) for i in range(2)]
        VnC = [sb("VnC%d" % i, [4, 4, 65], BF16) for i in range(2)]
        b_CK = [Buf("CK%d" % i) for i in range(2)]
        b_Kb = [Buf("Kb16_%d" % i) for i in range(2)]
        b_KcT = [Buf("KcT%d" % i) for i in range(2)]
        b_VcP = [Buf("VcP%d" % i) for i in range(2)]
        b_VnC = [Buf("VnC%d" % i) for i in range(2)]
        for i in range(2):
            S.op("pool", lambda e, i=i: e.memset(VcP[i][:, :, 64:65], 1.0), w=[b_VcP[i]])
            S.op("pool", lambda e, i=i: e.memset(VnC[i][:, :, 64:65], 1.0), w=[b_VnC[i]])
        bi = 0
        for s_ in range(4):
            for g in range(3):
                d, W = DILS[g], WINS[g]
                blocks = [(0, 4)] if g == 0 else [(t, 1) for t in range(4)]
                for (t0, nq) in blocks:
                    i = bi % 2
                    bi += 1
                    row0 = 0 if g == 0 else t0
                    S.dma("sp", CK[i][:, :], dap(self.caches[g], (s_ * W + row0) * 512, [[d * 512, 128], [1, 512]]),
                          b_CK[i], w=[b_CK[i]])
                    S.op("dve", lambda e, i=i: e.tensor_copy(Kb[i][:, :], CK[i][:, 0:256]), r=[b_CK[i]], w=[b_Kb[i]])
                    S.op("pool", lambda e, i=i: e.tensor_copy(
                        VcP[i][:, :, 0:64], CK[i][:, 256:512].rearrange("p (h e) -> p h e", h=4)),
                        r=[b_CK[i]], w=[b_VcP[i]])
                    pt, b_pt = self.pst[i], self.b_pst[i]
                    for pair in range(2):
                        S.op("pe", lambda e, pt=pt, pair=pair, i=i: e.transpose(
                            pt[:, pair * 128:(pair + 1) * 128], Kb[i][:, pair * 128:(pair + 1) * 128], self.ident[:, :]),
                            r=[b_Kb[i], self.b_ident], w=[b_pt], inc=(pair == 1))
                    S.op("act", lambda e, pt=pt, i=i: e.activation(
                        KcT[i][:, :, :], pt[:, 0:256].rearrange("p (a k) -> p a k", a=2), AF.Copy),
                        r=[b_pt], w=[b_KcT[i]])
                    tk0 = 4 * s_ + t0
                    S.dma("sp", VnC[i][0:nq, :, 0:64],
                          dap(self.VSs, tk0 * 768 + 256 * g, [[768, nq], [64, 4], [1, 64]]), b_VnC[i], w=[b_VnC[i]])
                    for h in range(4):
                        pair, hh = h // 2, h % 2
                        rw = slice(64 * hh, 64 * hh + 64)
                        self.attn_unit(KcT[i][rw, pair, :], KnT[rw, 2 * g + pair, tk0:tk0 + nq],
                                       QnT[rw, 2 * g + pair, tk0:tk0 + nq], VcP[i][:, h, :], VnC[i][0:nq, h, :],
                                       nq, 4 * g + h, accS[:, h, tk0:tk0 + nq], b_accS, g == 0,
                                       [b_KcT[i], b_kq, b_VcP[i], b_VnC[i]])
        self.attn_flush()
        aS = sb("aS", [64, 4, 16], BF16)
        b_aS = Buf("aS")
        for h in range(4):
            pt, b_pt = self.next_ps()
            S.op("pe", lambda e, pt=pt, h=h: e.matmul(pt[0:64, 0:16], sel[:, :], accS[:, h, :], start=True, stop=True),
                 r=[b_sel, b_accS], w=[b_pt])
            S.op("dve", lambda e, pt=pt: e.tensor_scalar(rec[:, 0:16], pt[0:64, 0:16], 1e-30, None, ALU.max),
                 r=[b_pt], w=[b_rec])
            S.op("dve", lambda e: e.reciprocal(rec[:, 0:16], rec[:, 0:16]), r=[b_rec], w=[b_rec])
            S.op("dve", lambda e, h=h: e.tensor_tensor(aS[:, h, :], accS[0:64, h, :], rec[:, 0:16], ALU.mult),
                 r=[b_accS, b_rec], w=[b_aS])
        S.dma("sp", self.ATTs.ap().rearrange("h p t -> p h t"), aS[:, :, :], b_aS, r=[b_aS])


    def phase2(self):
        S, sb = self.S, self.sb
        self.build_expbias()
        ast = [sb("ast%d" % i, [64, 512], BF16) for i in range(2)]
        b_ast = [Buf("ast%d" % i) for i in range(2)]
        acnt = 0
        hv = sb("hv", [128, 1], F32)
        b_hv = Buf("hv")
        S.dma("sp", hv[:], self.hv_in.ap(), b_hv, w=[b_hv])
        sel = sb("sel", [65, 64], F32)
        b_sel = Buf("sel")
        S.op("dve", lambda e: e.memset(sel[:], 0.0), w=[b_sel])
        S.op("dve", lambda e: e.memset(sel[64:65, :], 1.0), w=[b_sel])
        NEP = _ADEPTH + 1
        self.Ebuf = [sb("E%d" % i, [128, 2, 128], F32) for i in range(NEP)]
        self.b_Ebuf = [Buf("E%d" % i) for i in range(NEP)]
        self.Pbuf = [sb("P%d" % i, [128, 2, 128], BF16) for i in range(NEP)]
        self.b_Pbuf = [Buf("P%d" % i) for i in range(NEP)]
        self.ucnt = 0
        self.pending_b = []
        self.attn_depth = _ADEPTH
        acc = sb("acc", [65, 2, T_MAIN], F32)
        b_acc = Buf("acc")
        NBLK = 3 * 16 + 2 * 16
        Kb = [sb("Kb%d" % i, [128, T_KV], BF16) for i in range(2)]
        Qb = [sb("Qb%d" % i, [128, T_MAIN], BF16) for i in range(2)]
        Vb = [sb("Vb%d" % i, [128, NBLK, 2, 65], BF16) for i in range(2)]
        b_Kb = [Buf("Kb%d" % i) for i in range(2)]
        b_Qb = [Buf("Qb%d" % i) for i in range(2)]
        b_Vb = [Buf("Vb%d" % i) for i in range(2)]
        for i in range(2):
            S.op("pool", lambda e, i=i: e.memset(Vb[i][:, :, :, :], 0.0), w=[b_Vb[i]])
        rec = sb("rec", [64, 512], F32)
        b_rec = Buf("rec")
        H0 = T_MAIN0 + 128
        li = 0
        for pair in range(2):
            for g in range(3):
                d = DILS[g]
                NB = 4096 // (128 * d)
                nqh = 128 // d
                K_, Q_, V_ = Kb[li % 2], Qb[li % 2], Vb[li % 2]
                bK, bQ, bV = b_Kb[li % 2], b_Qb[li % 2], b_Vb[li % 2]
                li += 1
                mt = 2 * g + pair
                S.dma("sp", K_[:, :], self.KT.ap()[mt, :, :], bK, w=[bK])
                S.dma("sp", Q_[:, :], self.QT.ap()[mt, :, :], bQ, w=[bQ])
                nblk = 3 * d + NB * d
                S.op("pool", lambda e, V_=V_, nblk=nblk: e.memset(V_[:, 0:nblk, :, 64:65], 1.0), w=[bV])
                colb = 256 * g + 128 * pair
                for ty, (lt0, npart) in enumerate(((H0 - 128 * d, 128), (T_MAIN0 - 128 * d, 128), (T_MAIN0, nqh))):
                    S.dma("sp", V_[0:npart, ty * d:(ty + 1) * d, :, 0:64],
                          dap(self.VS, (lt0 - KV_T0) * 768 + colb, [[d * 768, npart], [768, d], [64, 2], [1, 64]]),
                          bV, w=[bV])
                for n in range(NB):
                    S.dma("sp", V_[:, 3 * d + n * d:3 * d + (n + 1) * d, :, 0:64],
                          dap(self.VS, (H0 + 128 * n * d - KV_T0) * 768 + colb,
                              [[d * 768, 128], [768, d], [64, 2], [1, 64]]), bV, w=[bV])
                S.op("dve", lambda e, V_=V_, d=d: e.tensor_scalar(
                    V_[:, 0:3 * d, :, :], V_[:, 0:3 * d, :, :], hv[:, 0:1], None, ALU.mult),
                    r=[b_hv], w=[bV])
                for hh in range(2):
                    gh = 4 * g + 2 * pair + hh
                    rw = slice(64 * hh, 64 * hh + 64)
                    rb = [bK, bQ, bV]

                    def cs(st, n, d=d):
                        return slice(st, st + (n - 1) * d + 1, d)
                    for r in range(d):
                        self.attn_unit(K_[rw, cs(T_MAIN0 - 128 * d + r - KV_T0, 128)],
                                       K_[rw, cs(T_MAIN0 + r - KV_T0, nqh)], Q_[rw, cs(r, nqh)],
                                       V_[:, 1 * d + r, hh, :], V_[0:nqh, 2 * d + r, hh, :], nqh, gh,
                                       acc[:, hh, cs(r, nqh)], b_acc, g == 0, rb)
                        for n in range(NB):
                            kp = H0 + 128 * (n - 1) * d + r - KV_T0
                            kc = H0 + 128 * n * d + r - KV_T0
                            q0 = 128 + 128 * n * d + r
                            vp = (0 * d + r) if n == 0 else (3 * d + (n - 1) * d + r)
                            vc = 3 * d + n * d + r
                            self.attn_unit(K_[rw, cs(kp, 128)], K_[rw, cs(kc, 128)], Q_[rw, cs(q0, 128)],
                                           V_[:, vp, hh, :], V_[:, vc, hh, :], 128, gh,
                                           acc[:, hh, cs(q0, 128)], b_acc, g == 0, rb)
            self.attn_flush()
            for hh in range(2):
                h = 2 * pair + hh
                for c0 in range(0, T_MAIN, 512):
                    n = min(512, T_MAIN - c0)
                    pt, b_pt = self.next_ps()
                    S.op("pe", lambda e, pt=pt, hh=hh, c0=c0, n=n: e.matmul(
                        pt[0:64, 0:n], sel[:, :], acc[:, hh, c0:c0 + n], start=True, stop=True),
                        r=[b_sel, b_acc], w=[b_pt])
                    S.op("dve", lambda e, pt=pt, n=n: e.tensor_scalar(
                        rec[:, 0:n], pt[0:64, 0:n], 1e-18, None, ALU.max), r=[b_pt], w=[b_rec])
                    S.op("act", lambda e, n=n: e.activation(rec[:, 0:n], rec[:, 0:n], AF.Ln), r=[b_rec], w=[b_rec])
                    S.op("act", lambda e, n=n: e.activation(rec[:, 0:n], rec[:, 0:n], AF.Exp, scale=-1.0),
                         r=[b_rec], w=[b_rec])
                    a_, b_a = ast[acnt % 2], b_ast[acnt % 2]
                    acnt += 1
                    S.op("dve", lambda e, a_=a_, hh=hh, c0=c0, n=n: e.tensor_tensor(
                        a_[:, 0:n], acc[0:64, hh, c0:c0 + n], rec[:, 0:n], ALU.mult),
                        r=[b_acc, b_rec], w=[b_a])
                    S.dma("sp", self.ATT.ap()[h, :, c0:c0 + n], a_[:, 0:n], b_a, r=[b_a])
        self.sample_attention(sel, b_sel, rec, b_rec)


    def cmul(self, eng, out_r, out_i, ar, ai, br, bi, t, bufs_r, bufs_w):
        S = self.S
        t1, t2 = t
        S.op(eng, lambda e: e.tensor_tensor(t1, ar, br, ALU.mult), r=bufs_r, w=[self.b_ct])
        S.op(eng, lambda e: e.tensor_tensor(t2, ai, bi, ALU.mult), r=bufs_r, w=[self.b_ct])
        S.op(eng, lambda e: e.tensor_tensor(out_r, t1, t2, ALU.subtract), r=[self.b_ct], w=bufs_w)
        S.op(eng, lambda e: e.tensor_tensor(t1, ar, bi, ALU.mult), r=bufs_r + bufs_w, w=[self.b_ct])
        S.op(eng, lambda e: e.tensor_tensor(t2, ai, br, ALU.mult), r=bufs_r, w=[self.b_ct])
        S.op(eng, lambda e: e.tensor_tensor(out_i, t1, t2, ALU.add), r=[self.b_ct], w=bufs_w)

    def phase3a(self):
        S, sb = self.S, self.sb
        L = LCH
        b_in = Buf("ssm_in")
        LR = sb("LR", [128, 16], F32)
        LI = sb("LI", [128, 16], F32)
        LDT = sb("LDT", [128, 16], F32)
        BR = sb("BR", [128, 16, 16], F32)
        BI = sb("BI", [128, 16, 16], F32)
        CR = sb("CR", [128, 16, 16], F32)
        CI = sb("CI", [128, 16, 16], F32)
        Dc = sb("Dc", [128, 4], F32)
        S.dma("sp", LR[:], dap(self.lam_re, 0, [[1, 128], [128, 16]]), b_in, w=[b_in], allow_slow_non_contiguous=True)
        S.dma("sp", LI[:], dap(self.lam_im, 0, [[1, 128], [128, 16]]), b_in, w=[b_in], allow_slow_non_contiguous=True)
        for j in range(2):
            S.dma("sp", LDT[64 * j:64 * j + 64, :], dap(self.log_dt, j, [[0, 64], [2, 16]]), b_in, w=[b_in],
                  allow_slow_non_contiguous=True)
        S.dma("sp", BR[:], dap(self.b_re, 0, [[16, 128], [2048, 16], [1, 16]]), b_in, w=[b_in])
        S.dma("sp", BI[:], dap(self.b_im, 0, [[16, 128], [2048, 16], [1, 16]]), b_in, w=[b_in])
        b_cn = Buf("Cnat")
        for nm, src, dstC in (("r", self.c_re, CR), ("i", self.c_im, CI)):
            CTf = sb("CTf" + nm, [16, 32, 64], F32)
            CTb = sb("CTb" + nm, [16, 32, 64], BF16)
            S.dma("sp", CTf[:], dap(src, 0, [[64, 16], [1024, 32], [1, 64]]), b_cn, w=[b_cn])
            S.op("dve", lambda e, CTf=CTf, CTb=CTb: e.tensor_copy(CTb[:], CTf[:]), r=[b_cn], w=[b_cn])
            pt, b_pt = self.next_ps()
            for pr in range(16):
                for j in range(2):
                    S.op("pe", lambda e, pt=pt, pr=pr, j=j, CTb=CTb: e.matmul(
                        pt[64 * j:64 * j + 64, 16 * pr:16 * pr + 16], CTb[0:16, 2 * pr + j, :],
                        self.ident[0:16, 0:16], start=True, stop=True),
                        r=[b_cn, self.b_ident], w=[b_pt], inc=(pr == 15 and j == 1))
            S.op("act", lambda e, pt=pt, dstC=dstC: e.activation(
                dstC[:].rearrange("p r c -> p (r c)"), pt[:, 0:256], AF.Copy), r=[b_pt], w=[b_in])
        S.dma("sp", Dc[:], dap(self.ssm_d, 0, [[1, 128], [128, 4]]), b_in, w=[b_in], allow_slow_non_contiguous=True)
        halfpi = sb("halfpi", [128, 1], F32)
        b_w = Buf("ssm_work")
        self.b_ct = Buf("ct")
        S.op("dve", lambda e: e.memset(halfpi[:], math.pi / 2), w=[b_w])
        dt = sb("dt", [128, 16], F32)
        S.op("act", lambda e: e.activation(dt[:], LDT[:], AF.Exp), r=[b_in], w=[b_w])
        t1 = sb("t1", [128, 16], F32)
        t2 = sb("t2", [128, 16], F32)
        t3 = sb("t3", [128, 16], F32)
        ar = sb("ar", [128, 16], F32)
        ai = sb("ai", [128, 16], F32)
        wr = sb("wr", [128, 16], F32)
        wi = sb("wi", [128, 16], F32)
        zr = sb("zr", [128, 16], F32)
        zi = sb("zi", [128, 16], F32)
        pr_ = sb("pr_", [128, 16], F32)
        pi_ = sb("pi_", [128, 16], F32)
        qr_ = sb("qr_", [128, 16], F32)
        qi_ = sb("qi_", [128, 16], F32)
        MSQ = 8
        S.op("dve", lambda e: e.scalar_tensor_tensor(zr[:], LR[:], 1.0 / (1 << MSQ), dt[:], ALU.mult, ALU.mult),
             r=[b_in, b_w], w=[b_w])
        S.op("dve", lambda e: e.scalar_tensor_tensor(zi[:], LI[:], 1.0 / (1 << MSQ), dt[:], ALU.mult, ALU.mult),
             r=[b_in, b_w], w=[b_w])
        S.op("dve", lambda e: e.tensor_scalar(pr_[:], zr[:], 1.0 / 5, 1.0, ALU.mult, ALU.add), r=[b_w], w=[b_w])
        S.op("dve", lambda e: e.tensor_scalar(pi_[:], zi[:], 1.0 / 5, None, ALU.mult), r=[b_w], w=[b_w])
        for dv in (4.0, 3.0, 2.0):
            self.cmul("dve", qr_[:], qi_[:], zr[:], zi[:], pr_[:], pi_[:], (t1[:], t2[:]), [b_w], [b_w])
            S.op("dve", lambda e, dv=dv: e.tensor_scalar(pr_[:], qr_[:], 1.0 / dv, 1.0, ALU.mult, ALU.add),
                 r=[b_w], w=[b_w])
            S.op("dve", lambda e, dv=dv: e.tensor_scalar(pi_[:], qi_[:], 1.0 / dv, None, ALU.mult), r=[b_w], w=[b_w])
        self.cmul("dve", wr[:], wi[:], zr[:], zi[:], pr_[:], pi_[:], (t1[:], t2[:]), [b_w], [b_w])
        for _ in range(MSQ):
            S.op("dve", lambda e: e.tensor_tensor(t1[:], wr[:], wr[:], ALU.mult), r=[b_w], w=[self.b_ct])
            S.op("dve", lambda e: e.tensor_tensor(t2[:], wi[:], wi[:], ALU.mult), r=[b_w], w=[self.b_ct])
            S.op("dve", lambda e: e.tensor_tensor(t3[:], wr[:], wi[:], ALU.mult), r=[b_w], w=[self.b_ct])
            S.op("dve", lambda e: e.tensor_tensor(t1[:], t1[:], t2[:], ALU.subtract), r=[self.b_ct], w=[self.b_ct])
            S.op("dve", lambda e: e.tensor_tensor(t3[:], t3[:], wi[:], ALU.add), r=[self.b_ct, b_w], w=[self.b_ct])
            S.op("dve", lambda e: e.scalar_tensor_tensor(wr[:], wr[:], 2.0, t1[:], ALU.mult, ALU.add),
                 r=[self.b_ct, b_w], w=[b_w])
            S.op("dve", lambda e: e.tensor_scalar(wi[:], t3[:], 2.0, None, ALU.mult), r=[self.b_ct], w=[b_w])
        S.op("dve", lambda e: e.tensor_scalar(ar[:], wr[:], 1.0, None, ALU.add), r=[b_w], w=[b_w])
        S.op("dve", lambda e: e.tensor_copy(ai[:], wi[:]), r=[b_w], w=[b_w])
        APr = sb("APr", [128, L + 1, 16], F32)
        APi = sb("APi", [128, L + 1, 16], F32)
        b_ap = Buf("AP")
        S.op("dve", lambda e: e.memset(APr[:, 0, :], 1.0), w=[b_ap])
        S.op("dve", lambda e: e.memset(APi[:, 0, :], 0.0), w=[b_ap])
        for ee in range(1, L + 1):
            self.cmul("dve", APr[:, ee, :], APi[:, ee, :], APr[:, ee - 1, :], APi[:, ee - 1, :], ar[:], ai[:],
                      (t1[:], t2[:]), [b_ap, b_w], [b_ap])
        nr = sb("nr", [128, 16], F32)
        cr = sb("cr", [128, 16], F32)
        ci = sb("ci", [128, 16], F32)
        S.op("dve", lambda e: e.tensor_copy(nr[:], wr[:]), r=[b_w], w=[b_w])
        S.op("dve", lambda e: e.tensor_tensor(t1[:], LR[:], LR[:], ALU.mult), r=[b_in], w=[self.b_ct])
        S.op("dve", lambda e: e.tensor_tensor(t2[:], LI[:], LI[:], ALU.mult), r=[b_in], w=[self.b_ct])
        S.op("dve", lambda e: e.tensor_tensor(t1[:], t1[:], t2[:], ALU.add), r=[self.b_ct], w=[self.b_ct])
        S.op("dve", lambda e: e.reciprocal(t3[:], t1[:]), r=[self.b_ct], w=[self.b_ct])
        S.op("dve", lambda e: e.tensor_tensor(t1[:], nr[:], LR[:], ALU.mult), r=[b_w, b_in], w=[self.b_ct])
        S.op("dve", lambda e: e.tensor_tensor(t2[:], ai[:], LI[:], ALU.mult), r=[b_w, b_in], w=[self.b_ct])
        S.op("dve", lambda e: e.tensor_tensor(t1[:], t1[:], t2[:], ALU.add), r=[self.b_ct], w=[self.b_ct])
        S.op("dve", lambda e: e.tensor_tensor(cr[:], t1[:], t3[:], ALU.mult), r=[self.b_ct], w=[b_w])
        S.op("dve", lambda e: e.tensor_tensor(t1[:], ai[:], LR[:], ALU.mult), r=[b_w, b_in], w=[self.b_ct])
        S.op("dve", lambda e: e.tensor_tensor(t2[:], nr[:], LI[:], ALU.mult), r=[b_w, b_in], w=[self.b_ct])
        S.op("dve", lambda e: e.tensor_tensor(t1[:], t1[:], t2[:], ALU.subtract), r=[self.b_ct], w=[self.b_ct])
        S.op("dve", lambda e: e.tensor_tensor(ci[:], t1[:], t3[:], ALU.mult), r=[self.b_ct], w=[b_w])
        Bbr = sb("Bbr", [128, 16, 16], F32)
        Bbi = sb("Bbi", [128, 16, 16], F32)
        T1 = sb("T1", [128, 16, 16], F32)
        T2 = sb("T2", [128, 16, 16], F32)
        b_bb = Buf("Bb")

        def bc(x):
            return x.unsqueeze(2).to_broadcast([128, 16, 16])
        self.cmul("dve", Bbr[:], Bbi[:], bc(cr[:]), bc(ci[:]), BR[:], BI[:], (T1[:], T2[:]), [b_w, b_in], [b_bb])
        PBr = sb("PBr", [128, L, 16, 32], BF16)
        PBi = sb("PBi", [128, L, 16, 32], BF16)
        CBr = sb("CBr", [128, 16, 32], BF16)
        CBin = sb("CBin", [128, 16, 32], BF16)
        VBr = sb("VBr", [128, L + 1, 16, 32], BF16)
        VBin = sb("VBin", [128, L + 1, 16, 32], BF16)
        b_pb, b_cb, b_vb = Buf("PB"), Buf("CB"), Buf("VB")
        for tt, bb in ((PBr, b_pb), (PBi, b_pb), (CBr, b_cb), (CBin, b_cb), (VBr, b_vb), (VBin, b_vb)):
            S.op("pool", lambda e, tt=tt: e.memset(tt[:], 0.0), w=[bb])
        for j in range(2):
            ps_, cs_ = slice(64 * j, 64 * j + 64), slice(16 * j, 16 * j + 16)
            S.op("dve", lambda e, ps_=ps_, cs_=cs_: e.tensor_copy(CBr[ps_, :, cs_], CR[ps_]), r=[b_in], w=[b_cb])
            S.op("dve", lambda e, ps_=ps_, cs_=cs_: e.tensor_scalar(CBin[ps_, :, cs_], CI[ps_], -1.0, None, ALU.mult),
                 r=[b_in], w=[b_cb])
        for k in range(L):
            akr, aki = bc(APr[:, k, :]), bc(APi[:, k, :])
            S.op("dve", lambda e, akr=akr: e.tensor_tensor(T1[:], akr, Bbr[:], ALU.mult), r=[b_ap, b_bb], w=[self.b_ct])
            S.op("dve", lambda e, aki=aki: e.tensor_tensor(T2[:], aki, Bbi[:], ALU.mult), r=[b_ap, b_bb], w=[self.b_ct])
            for j in range(2):
                ps_, cs_ = slice(64 * j, 64 * j + 64), slice(16 * j, 16 * j + 16)
                S.op("dve", lambda e, ps_=ps_, cs_=cs_, k=k: e.tensor_tensor(
                    PBr[ps_, k, :, cs_], T1[ps_], T2[ps_], ALU.subtract), r=[self.b_ct], w=[b_pb])
            S.op("dve", lambda e, akr=akr: e.tensor_tensor(T1[:], akr, Bbi[:], ALU.mult), r=[b_ap, b_bb, b_pb], w=[self.b_ct])
            S.op("dve", lambda e, aki=aki: e.tensor_tensor(T2[:], aki, Bbr[:], ALU.mult), r=[b_ap, b_bb], w=[self.b_ct])
            for j in range(2):
                ps_, cs_ = slice(64 * j, 64 * j + 64), slice(16 * j, 16 * j + 16)
                S.op("dve", lambda e, ps_=ps_, cs_=cs_, k=k: e.tensor_tensor(
                    PBi[ps_, k, :, cs_], T1[ps_], T2[ps_], ALU.add), r=[self.b_ct], w=[b_pb])
        for ee in range(L + 1):
            akr, aki = bc(APr[:, ee, :]), bc(APi[:, ee, :])
            S.op("dve", lambda e, akr=akr: e.tensor_tensor(T1[:], akr, CR[:], ALU.mult), r=[b_ap, b_in, b_vb, b_pb], w=[self.b_ct])
            S.op("dve", lambda e, aki=aki: e.tensor_tensor(T2[:], aki, CI[:], ALU.mult), r=[b_ap, b_in], w=[self.b_ct])
            for j in range(2):
                ps_, cs_ = slice(64 * j, 64 * j + 64), slice(16 * j, 16 * j + 16)
                S.op("dve", lambda e, ps_=ps_, cs_=cs_, ee=ee: e.tensor_tensor(
                    VBr[ps_, ee, :, cs_], T1[ps_], T2[ps_], ALU.subtract), r=[self.b_ct], w=[b_vb])
            S.op("dve", lambda e, aki=aki: e.tensor_tensor(T1[:], aki, CR[:], ALU.mult), r=[b_ap, b_in, b_vb], w=[self.b_ct])
            S.op("dve", lambda e, akr=akr: e.tensor_tensor(T2[:], akr, CI[:], ALU.mult), r=[b_ap, b_in], w=[self.b_ct])
            S.op("dve", lambda e: e.tensor_tensor(T1[:], T1[:], T2[:], ALU.add), r=[self.b_ct], w=[self.b_ct])
            for j in range(2):
                ps_, cs_ = slice(64 * j, 64 * j + 64), slice(16 * j, 16 * j + 16)
                S.op("dve", lambda e, ps_=ps_, cs_=cs_, ee=ee: e.tensor_scalar(
                    VBin[ps_, ee, :, cs_], T1[ps_], -1.0, None, ALU.mult), r=[self.b_ct], w=[b_vb])
        S.dma("sp", self.VBs.ap()[0], VBr[:].rearrange("p e r c -> p (e r c)"), b_vb, r=[b_vb])
        S.dma("sp", self.VBs.ap()[1], VBin[:].rearrange("p e r c -> p (e r c)"), b_vb, r=[b_vb])
        S.dma("sp", self.TAB.ap()[:, 0, :], APr[:, L, :], b_ap, r=[b_ap])
        S.dma("sp", self.TAB.ap()[:, 1, :], APi[:, L, :], b_ap, r=[b_ap])
        S.dma("sp", self.TAB.ap()[:, 2, :], APr[:, 1, :], b_ap, r=[b_ap])
        S.dma("sp", self.TAB.ap()[:, 3, :], APi[:, 1, :], b_ap, r=[b_ap])
        KBf = sb("KBf", [128, 4, 128], F32)
        b_kbf = Buf("KBf")
        KBo = [sb("KBo%d" % i, [128, 4, 128], BF16) for i in range(2)]
        b_kbo = [Buf("KBo%d" % i) for i in range(2)]
        S.op("pool", lambda e: e.memset(KBf[:], 0.0), w=[b_kbf])
        for lag in range(L):
            for ct in range(4):
                pt, b_pt = self.next_ps()
                for r4 in range(4):
                    pr = 4 * ct + r4
                    o_ = pt[32 * r4:32 * r4 + 32, 32 * r4:32 * r4 + 32]
                    S.op("pe", lambda e, o_=o_, lag=lag, pr=pr, r4=r4: e.matmul(
                        o_, PBr[:, lag, pr, :], CBr[:, pr, :], start=True, stop=False, tile_position=(0, 32 * r4)),
                        r=[b_pb, b_cb], w=[b_pt], inc=False)
                    S.op("pe", lambda e, o_=o_, lag=lag, pr=pr, r4=r4: e.matmul(
                        o_, PBi[:, lag, pr, :], CBin[:, pr, :], start=False, stop=True, tile_position=(0, 32 * r4)),
                        r=[b_pb, b_cb], w=[b_pt], inc=(r4 == 3))
                for r4 in range(4):
                    sl = slice(32 * r4, 32 * r4 + 32)
                    S.op("act", lambda e, sl=sl, ct=ct, pt=pt: e.activation(KBf[sl, ct, sl], pt[sl, sl], AF.Copy),
                         r=[b_pt], w=[b_kbf])
                if lag == 0:
                    S.op("dve", lambda e, ct=ct: e.scalar_tensor_tensor(
                        KBf[:, ct, :], self.ident_f[:, :], Dc[:, ct:ct + 1], KBf[:, ct, :], ALU.mult, ALU.add),
                        r=[b_in, self.b_ident], w=[b_kbf])
            ko, b_ko = KBo[lag % 2], b_kbo[lag % 2]
            S.op("dve", lambda e, ko=ko: e.tensor_copy(ko[:], KBf[:]), r=[b_kbf], w=[b_ko])
            S.dma("sp", self.KBs.ap()[:, lag, :], ko[:].rearrange("p c m -> p (c m)"), b_ko, r=[b_ko])
        WSo = [sb("WSo%d" % i, [128, 4, 128], BF16) for i in range(2)]
        b_wso = [Buf("WSo%d" % i) for i in range(2)]
        cnt = 0
        for ri, PB in enumerate((PBr, PBi)):
            for k in range(L):
                wo, b_wo = WSo[cnt % 2], b_wso[cnt % 2]
                cnt += 1
                for ct in range(4):
                    pt, b_pt = self.next_ps()
                    for r4 in range(4):
                        pr = 4 * ct + r4
                        S.op("pe", lambda e, pt=pt, r4=r4, PB=PB, k=k, pr=pr: e.matmul(
                            pt[32 * r4:32 * r4 + 32, 0:128], PB[:, k, pr, :], self.ident[:, :], start=True, stop=True,
                            tile_position=(0, 32 * r4)),
                            r=[b_pb, self.b_ident], w=[b_pt], inc=(r4 == 3))
                    S.op("act", lambda e, pt=pt, wo=wo, ct=ct: e.activation(wo[:, ct, :], pt[:, 0:128], AF.Copy),
                         r=[b_pt], w=[b_wo])
                S.dma("sp", self.WSs.ap()[ri, :, k, :], wo[:].rearrange("p c m -> p (c m)"), b_wo, r=[b_wo])


    def gelu_tanh(self, dst, src, tmp, b_src, b_tmp, b_dst):
        S = self.S
        S.op("dve", lambda e: e.tensor_tensor(tmp, src, src, ALU.mult), r=[b_src], w=[b_tmp])
        S.op("dve", lambda e: e.tensor_scalar(tmp, tmp, 0.044715, 1.0, ALU.mult, ALU.add), r=[b_tmp], w=[b_tmp])
        S.op("dve", lambda e: e.tensor_tensor(tmp, tmp, src, ALU.mult), r=[b_src, b_tmp], w=[b_tmp])
        S.op("act", lambda e: e.activation(tmp, tmp, AF.Sigmoid, scale=1.5957691216), r=[b_tmp], w=[b_tmp])
        S.op("dve", lambda e: e.tensor_tensor(dst, src, tmp, ALU.mult), r=[b_src, b_tmp], w=[b_dst])

    def sample_ssm(self, WS, VB, TAB, b_wt):
        S, sb = self.S, self.sb
        NS = 16
        b_su = Buf("s_u")
        Us = sb("Us", [128, 4, NS], BF16)
        Dcs = sb("Dcs", [128, 4], F32)
        S.dma("sp", Us[:], self.USs.ap().rearrange("c p t -> p c t"), b_su, w=[b_su])
        S.dma("sp", Dcs[:], dap(self.ssm_d, 0, [[1, 128], [128, 4]]), b_su, w=[b_su], allow_slow_non_contiguous=True)
        BU = [sb("BU%d" % i, [128, 16, NS], F32) for i in range(2)]
        b_BU = Buf("BU")
        for ri in range(2):
            for r4 in range(4):
                pt, b_pt = self.next_ps()
                rows = slice(32 * r4, 32 * r4 + 32)
                for ct in range(4):
                    S.op("pe", lambda e, pt=pt, ct=ct, r4=r4, rows=rows, ri=ri: e.matmul(
                        pt[:, ct * NS:(ct + 1) * NS], WS[ri][rows, 0, ct, :], Us[rows, ct, :],
                        start=True, stop=True, tile_position=(32 * r4, 0)),
                        r=[b_wt, b_su], w=[b_pt], inc=(ct == 3))
                S.op("act", lambda e, pt=pt, ri=ri, r4=r4: e.activation(
                    BU[ri][:, r4:16:4, :], pt[:, 0:4 * NS].rearrange("p (r c) -> p r c", r=4), AF.Copy),
                    r=[b_pt], w=[b_BU])
        H0 = [sb("H0_%d" % i, [128, 16, 4], F32) for i in range(2)]
        b_H0 = Buf("H0")
        for ri, src in enumerate((self.st_re, self.st_im)):
            for s_ in range(4):
                S.dma("sp", H0[ri][:, :, s_], dap(src, s_ * 2048, [[1, 128], [128, 16]]), b_H0, w=[b_H0],
                      allow_slow_non_contiguous=True)
        Hs = [sb("Hs%d" % i, [128, 16, NS], F32) for i in range(2)]
        b_Hs = Buf("Hs")
        tq = [sb("tqs%d" % i, [128, 16, 4], F32) for i in range(4)]
        b_tq = Buf("tqs")
        A1r = TAB[:, 2, :].unsqueeze(2).to_broadcast([128, 16, 4])
        A1i = TAB[:, 3, :].unsqueeze(2).to_broadcast([128, 16, 4])

        def v4(x, t):
            return x[:, :, t:t + 13:4]
        for t in range(4):
            if t == 0:
                pr_, pi_, bp = H0[0][:, :, :], H0[1][:, :, :], b_H0
            else:
                pr_, pi_, bp = v4(Hs[0], t - 1), v4(Hs[1], t - 1), b_Hs
            S.op("pool", lambda e, pr_=pr_: e.tensor_tensor(tq[0][:], A1r, pr_, ALU.mult), r=[bp, b_wt], w=[b_tq])
            S.op("pool", lambda e, pi_=pi_: e.tensor_tensor(tq[1][:], A1i, pi_, ALU.mult), r=[bp], w=[b_tq])
            S.op("pool", lambda e, pi_=pi_: e.tensor_tensor(tq[2][:], A1r, pi_, ALU.mult), r=[bp], w=[b_tq])
            S.op("pool", lambda e, pr_=pr_: e.tensor_tensor(tq[3][:], A1i, pr_, ALU.mult), r=[bp], w=[b_tq])
            S.op("pool", lambda e: e.tensor_tensor(tq[0][:], tq[0][:], tq[1][:], ALU.subtract), r=[b_tq], w=[b_tq])
            S.op("pool", lambda e: e.tensor_tensor(tq[2][:], tq[2][:], tq[3][:], ALU.add), r=[b_tq], w=[b_tq])
            S.op("pool", lambda e, t=t: e.tensor_tensor(v4(Hs[0], t), v4(BU[0], t), tq[0][:], ALU.add),
                 r=[b_tq, b_BU], w=[b_Hs])
            S.op("pool", lambda e, t=t: e.tensor_tensor(v4(Hs[1], t), v4(BU[1], t), tq[2][:], ALU.add),
                 r=[b_tq, b_BU], w=[b_Hs])
        for ri, dst in enumerate((self.ossm_s_re, self.ossm_s_im)):
            for s_ in range(4):
                S.dma("sp", dap(dst, s_ * 2048, [[1, 128], [128, 16]]), Hs[ri][:, :, 4 * s_ + 3], b_Hs, r=[b_Hs],
                      allow_slow_non_contiguous=True)
        Hb = [sb("Hbs%d" % i, [128, 16, NS], BF16) for i in range(2)]
        b_Hb = Buf("Hbs")
        for ri in range(2):
            S.op("act", lambda e, ri=ri: e.activation(Hb[ri][:], Hs[ri][:], AF.Copy), r=[b_Hs], w=[b_Hb])
        yfs = sb("yfs", [128, 4, NS], F32)
        yts = sb("yts", [128, 4, NS], F32)
        ybs = sb("ybs", [128, 4, NS], BF16)
        b_yfs, b_yts, b_ybs = Buf("yfs"), Buf("yts"), Buf("ybs")
        for ct in range(4):
            pt, b_pt = self.next_ps()
            for r4 in range(4):
                pr = 4 * ct + r4
                S.op("pe", lambda e, pt=pt, r4=r4, pr=pr: e.matmul(
                    pt[32 * r4:32 * r4 + 32, 0:NS], VB[0][:, 0, pr, :], Hb[0][:, pr, :], start=True, stop=False,
                    tile_position=(0, 32 * r4)), r=[b_wt, b_Hb], w=[b_pt], inc=False)
                S.op("pe", lambda e, pt=pt, r4=r4, pr=pr: e.matmul(
                    pt[32 * r4:32 * r4 + 32, 0:NS], VB[1][:, 0, pr, :], Hb[1][:, pr, :], start=False, stop=True,
                    tile_position=(0, 32 * r4)), r=[b_wt, b_Hb], w=[b_pt], inc=(r4 == 3))
            S.op("dve", lambda e, pt=pt, ct=ct: e.scalar_tensor_tensor(
                yfs[:, ct, :], Us[:, ct, :], Dcs[:, ct:ct + 1], pt[:, 0:NS], ALU.mult, ALU.add),
                r=[b_pt, b_su], w=[b_yfs])
        self.gelu_tanh(ybs[:], yfs[:], yts[:], b_yfs, b_yts, b_ybs)
        S.dma("sp", self.YSs.ap().rearrange("c p t -> p c t"), ybs[:], b_ybs, r=[b_ybs])


    def cmadd(self, eng, dr, di, ar, ai, xr, xi, tmps, rbufs, wbuf, b_t):
        S = self.S
        t0, t1, t2, t3 = tmps
        S.op(eng, lambda e: e.tensor_tensor(t0, ar, xr, ALU.mult), r=rbufs, w=[b_t])
        S.op(eng, lambda e: e.tensor_tensor(t1, ai, xi, ALU.mult), r=rbufs, w=[b_t])
        S.op(eng, lambda e: e.tensor_tensor(t2, ar, xi, ALU.mult), r=rbufs, w=[b_t])
        S.op(eng, lambda e: e.tensor_tensor(t3, ai, xr, ALU.mult), r=rbufs, w=[b_t])
        S.op(eng, lambda e: e.tensor_tensor(t0, t0, t1, ALU.subtract), r=[b_t], w=[b_t])
        S.op(eng, lambda e: e.tensor_tensor(t2, t2, t3, ALU.add), r=[b_t], w=[b_t])
        S.op(eng, lambda e: e.tensor_tensor(dr, dr, t0, ALU.add), r=[b_t] + rbufs, w=[wbuf])
        S.op(eng, lambda e: e.tensor_tensor(di, di, t2, ALU.add), r=[b_t] + rbufs, w=[wbuf])

    def phase3b(self):
        S, sb = self.S, self.sb
        L = LCH
        NCH = T_ALL // L
        NH = NCH // 2
        TH = T_ALL // 2
        GRP = 16
        NG = NCH // GRP
        b_wt = Buf("ssm_wt")
        WS = [sb("WS%d" % i, [128, L, 4, 128], BF16) for i in range(2)]
        VB = [sb("VB%d" % i, [128, L + 1, 16, 32], BF16) for i in range(2)]
        TAB = sb("TAB", [128, 4, 16], F32)
        for i in range(2):
            S.dma("sp", WS[i][:].rearrange("p l c m -> p l (c m)"), self.WSs.ap()[i], b_wt, w=[b_wt])
            S.dma("sp", VB[i][:].rearrange("p e r c -> p (e r c)"), self.VBs.ap()[i], b_wt, w=[b_wt])
        S.dma("sp", TAB[:], self.TAB.ap(), b_wt, w=[b_wt])
        self.sample_ssm(WS, VB, TAB, b_wt)
        Sall = [sb("Sall%d" % i, [128, 16, NCH], F32) for i in range(2)]
        b_S = Buf("Sall")
        U = sb("Uh", [128, 4, TH], BF16)
        b_U = Buf("Uh")
        for half in range(2):
            S.dma("sp", U[:], self.US.ap()[:, :, half * TH:(half + 1) * TH].rearrange("c p t -> p c t"), b_U, w=[b_U])
            for ri in range(2):
                for pr in range(16):
                    ct, r4 = pr // 4, pr % 4
                    rows = slice(32 * r4, 32 * r4 + 32)
                    pt, b_pt = self.next_ps()
                    for tau in range(L):
                        S.op("pe", lambda e, pt=pt, ct=ct, r4=r4, rows=rows, tau=tau, ri=ri: e.matmul(
                            pt[:, 0:NH], WS[ri][rows, L - 1 - tau, ct, :],
                            U[rows, ct, :].rearrange("p (n t c) -> p n t c", t=L, c=TN // L)[:, :, tau, :],
                            start=(tau == 0), stop=(tau == L - 1), tile_position=(32 * r4, 0)),
                            r=[b_wt, b_U], w=[b_pt], inc=(tau == L - 1))
                    S.op("act", lambda e, pt=pt, ri=ri, pr=pr, half=half: e.activation(
                        Sall[ri][:, pr, half * NH:(half + 1) * NH], pt[:, 0:NH], AF.Copy), r=[b_pt], w=[b_S])
        PW = [sb("PW%d" % i, [128, GRP + 1, 16], F32) for i in range(2)]
        b_PW = Buf("PW")
        tq = [sb("tq%d" % i, [128, 16, NG], F32) for i in range(4)]
        b_tq = Buf("tq")
        self.b_ct = b_tq
        S.op("dve", lambda e: e.tensor_copy(PW[0][:, 1, :], TAB[:, 0, :]), r=[b_wt], w=[b_PW])
        S.op("dve", lambda e: e.tensor_copy(PW[1][:, 1, :], TAB[:, 1, :]), r=[b_wt], w=[b_PW])
        for k in range(2, GRP + 1):
            self.cmul("dve", PW[0][:, k, :], PW[1][:, k, :], PW[0][:, k - 1, :], PW[1][:, k - 1, :],
                      PW[0][:, 1, :], PW[1][:, 1, :], (tq[0][:, :, 0], tq[1][:, :, 0]), [b_PW], [b_PW])

        def bcg(x):
            return x.unsqueeze(2).to_broadcast([128, 16, NG])

        def vw(ri, i):
            return Sall[ri][:, :, i:i + (NG - 1) * GRP + 1:GRP]
        tmps = tuple(t[:, :, :] for t in tq)
        for i in range(1, GRP):
            self.cmadd("dve", vw(0, i), vw(1, i), bcg(PW[0][:, 1, :]), bcg(PW[1][:, 1, :]), vw(0, i - 1), vw(1, i - 1),
                       tmps, [b_PW, b_S], b_S, b_tq)
        t16 = tuple(t[:, :, 0] for t in tq)
        for C in range(1, NG):
            e0, e1 = C * GRP - 1, (C + 1) * GRP - 1
            self.cmadd("dve", Sall[0][:, :, e1], Sall[1][:, :, e1], PW[0][:, GRP, :], PW[1][:, GRP, :],
                       Sall[0][:, :, e0], Sall[1][:, :, e0], t16, [b_PW, b_S], b_S, b_tq)

        def vw1(ri, i):
            return Sall[ri][:, :, GRP + i:GRP + i + (NG - 2) * GRP + 1:GRP]

        def ends(ri):
            return Sall[ri][:, :, GRP - 1:GRP - 1 + (NG - 2) * GRP + 1:GRP]

        def bcg1(x):
            return x.unsqueeze(2).to_broadcast([128, 16, NG - 1])
        tmps1 = tuple(t[:, :, 0:NG - 1] for t in tq)
        for i in range(GRP - 1):
            self.cmadd("dve", vw1(0, i), vw1(1, i), bcg1(PW[0][:, i + 1, :]), bcg1(PW[1][:, i + 1, :]), ends(0), ends(1),
                       tmps1, [b_PW, b_S], b_S, b_tq)
        S.dma("sp", dap(self.ossm_re, 0, [[1, 128], [128, 16]]), Sall[0][:, :, NCH - 1], b_S, r=[b_S],
              allow_slow_non_contiguous=True)
        S.dma("sp", dap(self.ossm_im, 0, [[1, 128], [128, 16]]), Sall[1][:, :, NCH - 1], b_S, r=[b_S],
              allow_slow_non_contiguous=True)
        HBt = sb("HBt", [128, 16, NH], BF16)
        b_HBt = Buf("HBt")
        for ri in range(2):
            S.op("act", lambda e, ri=ri: e.activation(HBt[:, :, :], Sall[ri][:, :, NH - 1:2 * NH - 1], AF.Copy),
                 r=[b_S], w=[b_HBt])
            S.dma("sp", self.HBs.ap()[ri], HBt[:, :, :], b_HBt, r=[b_HBt])

    def phase3c(self):
        S, sb = self.S, self.sb
        L = LCH
        NH = T_MAIN // L
        b_wt = Buf("ssm_wt2")
        KB = sb("KB", [128, L, 4, 128], BF16)
        VB = [sb("VB%d" % i, [128, L + 1, 16, 32], BF16) for i in range(2)]
        HB = [sb("HB%d" % i, [128, 16, NH], BF16) for i in range(2)]
        U = sb("Um", [128, 4, T_MAIN], BF16)
        for i in range(2):
            S.dma("sp", VB[i][:].rearrange("p e r c -> p (e r c)"), self.VBs.ap()[i], b_wt, w=[b_wt])
            S.dma("sp", HB[i][:], self.HBs.ap()[i], b_wt, w=[b_wt])
        S.dma("sp", KB[:].rearrange("p l c m -> p l (c m)"), self.KBs.ap(), b_wt, w=[b_wt])
        S.dma("sp", U[:], self.US.ap()[:, :, T_MAIN0:T_ALL].rearrange("c p t -> p c t"), b_wt, w=[b_wt])
        yf = sb("yf", [128, T_MAIN], F32)
        yt = sb("yt", [128, T_MAIN], F32)
        yb = sb("yb", [128, T_MAIN], BF16)
        b_yf, b_yt, b_yb = Buf("yf"), Buf("yt"), Buf("yb")
        for ct in range(4):
            for tau in range(L):
                pt, b_pt = self.next_ps()
                for lag in range(tau + 1):
                    S.op("pe", lambda e, pt=pt, tau=tau, lag=lag, ct=ct: e.matmul(
                        pt[:, 0:NH], KB[:, lag, ct, :],
                        U[:, ct, :].rearrange("p (n t c) -> p n t c", t=L, c=TN // L)[:, :, tau - lag, :],
                        start=(lag == 0), stop=False, skip_group_check=True), r=[b_wt], w=[b_pt], inc=False)
                for r4 in range(4):
                    pr = 4 * ct + r4
                    for ri in range(2):
                        last = (r4 == 3 and ri == 1)
                        S.op("pe", lambda e, pt=pt, tau=tau, r4=r4, pr=pr, ri=ri, last=last: e.matmul(
                            pt[32 * r4:32 * r4 + 32, 0:NH], VB[ri][:, tau + 1, pr, :], HB[ri][:, pr, :],
                            start=False, stop=last, skip_group_check=True, tile_position=(0, 32 * r4)),
                            r=[b_wt], w=[b_pt], inc=last)
                S.op("act", lambda e, pt=pt, tau=tau: e.activation(
                    yf[:, tau:tau + (NH - 1) * L + 1:L], pt[:, 0:NH], AF.Copy), r=[b_pt], w=[b_yf])
            self.gelu_tanh(yb[:, :], yf[:, :], yt[:, :], b_yf, b_yt, b_yb)
            S.dma("sp", self.YS.ap()[ct, :, :], yb[:, :], b_yb, r=[b_yb])

    def phase4(self):
        S, sb = self.S, self.sb
        stg = [sb("wstg%d" % i, [128, 1024], F32) for i in range(2)]
        b_stg = [Buf("wstg%d" % i) for i in range(2)]
        Wglu = sb("Wglu", [128, 4, 512], BF16)
        Wbs = sb("Wbs", [128, 4, 1024], BF16)
        Wout = sb("Wout", [128, 8, 1024], BF16)
        Wba = sb("Wba", [64, 4, 1024], BF16)
        b_W = Buf("W4")
        self.load_weight_bf16(Wglu, b_W, self.w_glu, 4, 512, None, stg, b_stg)
        self.load_weight_bf16(Wbs, b_W, self.w_bs, 4, 1024, None, stg, b_stg)
        self.load_weight_bf16(Wout, b_W, self.w_out, 8, 1024, None, stg, b_stg)
        for h in range(4):
            st_, bs_ = stg[h % 2], b_stg[h % 2]
            S.dma("sp", st_[0:64, :], self.w_ba.ap()[64 * h:64 * h + 64, :], bs_, w=[bs_])
            S.op("dve", lambda e, st_=st_, h=h: e.tensor_copy(Wba[:, h, :], st_[0:64, :]), r=[bs_], w=[b_W])
        bg = sb("bg", [128, 4], F32)
        b_bg = Buf("bg")
        S.dma("sp", bg[:], dap(self.b_glu, 0, [[1, 128], [128, 4]]), b_bg, w=[b_bg], allow_slow_non_contiguous=True)
        AT = [sb("AT%d" % i, [64, 4, TN], BF16) for i in range(2)]
        YT = [sb("YT%d" % i, [128, 4, TN], BF16) for i in range(2)]
        G = [sb("G%d" % i, [128, 16, TN], BF16) for i in range(2)]
        X = [sb("X%d" % i, [128, 3, D], F32) for i in range(2)]
        b_in = [Buf("in4_%d" % i) for i in range(2)]
        SG = sb("SG", [128, 16, TN], BF16)
        b_SG = Buf("SG")
        sgl = sb("sgl", [128, TN], F32)
        b_sgl = Buf("sgl")
        so = sb("so", [128, 4, TN], BF16)
        b_so = Buf("so")
        mix = sb("mix", [128, 8, TN], BF16)
        b_mix = Buf("mix")
        ta = [sb("ta%d" % i, [128, TN], F32) for i in range(2)]
        b_ta = [Buf("ta%d" % i) for i in range(2)]
        X1 = [sb("X1_%d" % i, [128, 3, D], F32) for i in range(2)]
        b_X1 = [Buf("X1_%d" % i) for i in range(2)]

        def load_tile(k):
            i = k % 2
            m0 = k * TN
            S.dma("sp", AT[i][:], self.ATT.ap()[:, :, m0:m0 + TN].rearrange("h p t -> p h t"), b_in[i], w=[b_in[i]])
            S.dma("sp", YT[i][:], self.YS.ap()[:, :, m0:m0 + TN].rearrange("c p t -> p c t"), b_in[i], w=[b_in[i]])
            S.dma("sp", G[i][:], self.GS.ap()[:, :, m0:m0 + TN].rearrange("c p t -> p c t"), b_in[i], w=[b_in[i]])
            S.dma("sp", X[i][:], self.xall.ap()[T_MAIN0 + m0:T_MAIN0 + m0 + TN, :].rearrange("(s p) d -> p s d", p=128),
                  b_in[i], w=[b_in[i]])
        self.epsc = sb("epsc", [128, 1], F32)
        self.b_epsc = Buf("epsc")
        S.op("dve", lambda e: e.memset(self.epsc[:], EPS), w=[self.b_epsc])
        self.junk = sb("junk", [128, D], BF16)
        self.b_junk = Buf("junk")
        self.ss = sb("ss", [128, 4], F32)
        self.b_ss = Buf("ss")
        self.sq = sb("sq", [128, 4], F32)
        self.b_sq = Buf("sq")
        self.rstd = sb("rstd", [128, 4], F32)
        self.b_rstd = Buf("rstd")
        self.xs = sb("xs", [128, 3, D], BF16)
        self.b_xs = [Buf("xs%d" % i) for i in range(3)]
        XN2 = [sb("XN2_%d" % i, [128, 8, TN], BF16) for i in range(2)]
        b_XN2 = [Buf("XN2_%d" % i) for i in range(2)]

        def body(i, nt, nsub, rows, xo, b_xo, mid_hook=None):
            bi = b_in[i]
            S.op("act", lambda e, i=i: e.activation(SG[:, :, 0:nt], G[i][:, :, 0:nt], AF.Sigmoid), r=[bi], w=[b_SG])
            for mt in range(4):
                pt, b_pt = self.next_ps()
                for kc in range(4):
                    S.op("pe", lambda e, pt=pt, kc=kc, mt=mt, i=i: e.matmul(
                        pt[:, 0:nt], Wglu[:, kc, mt * 128:(mt + 1) * 128], YT[i][:, kc, 0:nt],
                        start=(kc == 0), stop=(kc == 3)), r=[b_W, bi], w=[b_pt], inc=(kc == 3))
                S.op("act", lambda e, pt=pt, mt=mt: e.activation(
                    sgl[:, 0:nt], pt[:, 0:nt], AF.Sigmoid, bias=bg[:, mt:mt + 1]), r=[b_pt, b_bg], w=[b_sgl])
                S.op("dve", lambda e, mt=mt, i=i: e.tensor_tensor(so[:, mt, 0:nt], YT[i][:, mt, 0:nt], sgl[:, 0:nt], ALU.mult),
                     r=[bi, b_sgl], w=[b_so])
            for mt in range(8):
                pa, b_pa = self.next_ps()
                for h in range(4):
                    S.op("pe", lambda e, pa=pa, h=h, mt=mt, i=i: e.matmul(
                        pa[:, 0:nt], Wba[:, h, mt * 128:(mt + 1) * 128], AT[i][:, h, 0:nt],
                        start=(h == 0), stop=(h == 3)), r=[b_W, bi], w=[b_pa], inc=(h == 3))
                pb, b_pb = self.next_ps()
                for kc in range(4):
                    S.op("pe", lambda e, pb=pb, kc=kc, mt=mt: e.matmul(
                        pb[:, 0:nt], Wbs[:, kc, mt * 128:(mt + 1) * 128], so[:, kc, 0:nt],
                        start=(kc == 0), stop=(kc == 3)), r=[b_W, b_so], w=[b_pb], inc=(kc == 3))
                t0_, t1_ = ta[0], ta[1]
                S.op("dve", lambda e, pa=pa, mt=mt: e.tensor_tensor(ta[0][:, 0:nt], pa[:, 0:nt], SG[:, mt, 0:nt], ALU.mult),
                     r=[b_pa, b_SG], w=[b_ta[0]])
                S.op("dve", lambda e, pb=pb, mt=mt: e.tensor_tensor(ta[1][:, 0:nt], pb[:, 0:nt], SG[:, 8 + mt, 0:nt], ALU.mult),
                     r=[b_pb, b_SG], w=[b_ta[1]])
                S.op("dve", lambda e, mt=mt: e.tensor_tensor(mix[:, mt, 0:nt], ta[0][:, 0:nt], ta[1][:, 0:nt], ALU.add),
                     r=[b_ta[0], b_ta[1]], w=[b_mix])
            if mid_hook is not None:
                mid_hook()
            for s_ in range(nsub):
                for nh in range(2):
                    pt, b_pt = self.next_ps()
                    for kc in range(8):
                        S.op("pe", lambda e, pt=pt, kc=kc, s_=s_, nh=nh: e.matmul(
                            pt[0:rows, 0:512], mix[:, kc, s_ * 128:s_ * 128 + rows], Wout[:, kc, nh * 512:(nh + 1) * 512],
                            start=(kc == 0), stop=(kc == 7)), r=[b_W, b_mix], w=[b_pt], inc=(kc == 7))
                    S.op("dve", lambda e, pt=pt, s_=s_, nh=nh, i=i, xo=xo: e.tensor_tensor(
                        xo[0:rows, s_, nh * 512:(nh + 1) * 512], X[i][0:rows, s_, nh * 512:(nh + 1) * 512], pt[0:rows, 0:512], ALU.add),
                        r=[b_pt, bi], w=[b_xo])

        NK = T_MAIN // TN
        load_tile(0)
        pend = None
        for k in range(NK):
            if k + 1 < NK:
                load_tile(k + 1)
            xo, b_xo = X1[k % 2], b_X1[k % 2]
            body(k % 2, TN, 3, 128, xo, b_xo, pend)
            m0 = k * TN
            S.dma("sp", self.X1s.ap()[m0:m0 + TN, :].rearrange("(s p) d -> p s d", p=128), xo[:], b_xo, r=[b_xo])

            def pend(k=k, xo=xo, b_xo=b_xo, m0=m0):
                self.norm_transpose(xo, b_xo, 3, 128, XN2[k % 2], b_XN2[k % 2])
                S.dma("sp", self.XN2s.ap()[:, :, m0:m0 + TN].rearrange("k p t -> p k t"), XN2[k % 2][:, :, :],
                      b_XN2[k % 2], r=[b_XN2[k % 2]])
        i = NK % 2
        NS = 16
        S.dma("sp", AT[i][:, :, 0:NS], self.ATTs.ap().rearrange("h p t -> p h t"), b_in[i], w=[b_in[i]])
        S.dma("sp", YT[i][:, :, 0:NS], self.YSs.ap().rearrange("c p t -> p c t"), b_in[i], w=[b_in[i]])
        S.dma("sp", G[i][:, :, 0:NS], self.GSs.ap().rearrange("c p t -> p c t"), b_in[i], w=[b_in[i]])
        S.dma("sp", X[i][0:NS, 0, :], self.xsamp.ap(), b_in[i], w=[b_in[i]])
        xo, b_xo = X1[NK % 2], b_X1[NK % 2]
        body(i, NS, 1, NS, xo, b_xo, pend)
        S.dma("sp", self.X1ss.ap(), xo[0:NS, 0, :], b_xo, r=[b_xo])
        self.norm_transpose(xo, b_xo, 1, NS, XN2[i], b_XN2[i])
        S.dma("sp", self.XN2ss.ap().rearrange("k p t -> p k t"), XN2[i][:, :, 0:NS], b_XN2[i], r=[b_XN2[i]])

    def phase5(self):
        S, sb = self.S, self.sb
        NM = DFF // 128
        g2c = sb("g2c", [128, 8], F32)
        b_c = Buf("c5")
        S.dma("sp", g2c[:], dap(self.norm2_g, 0, [[1, 128], [128, 8]]), b_c, w=[b_c], allow_slow_non_contiguous=True)
        self.epsc = sb("epsc", [128, 1], F32)
        self.b_epsc = Buf("epsc")
        S.op("dve", lambda e: e.memset(self.epsc[:], EPS), w=[self.b_epsc])
        cw = sb("cw", [128, 3, NM], F32)
        cb = sb("cb", [128, NM], F32)
        for i3 in range(3):
            S.dma("sp", cw[:, i3, :], dap(self.conv_w, i3 * DFF, [[1, 128], [128, NM]]), b_c, w=[b_c],
                  allow_slow_non_contiguous=True)
        S.dma("sp", cb[:], dap(self.conv_b, 0, [[1, 128], [128, NM]]), b_c, w=[b_c], allow_slow_non_contiguous=True)
        gfb = sb("gfb", [128, D], F32)
        S.dma("sp", gfb[:], dap(self.norm_f_g, 0, [[0, 128], [1, D]]), b_c, w=[b_c])
        stg = [sb("wstg%d" % i, [128, 704], F32) for i in range(2)]
        b_stg = [Buf("wstg%d" % i) for i in range(2)]
        Wup = sb("Wup", [128, 8, 2 * DFF], BF16)
        Wdn = sb("Wdn", [128, NM, D], BF16)
        b_W = Buf("W5")
        self.load_weight_bf16(Wup, b_W, self.w_up, 8, 2 * DFF, (g2c, b_c), stg, b_stg, nsplit=8)
        self.load_weight_bf16(Wdn, b_W, self.w_down, NM, D, None, stg, b_stg, nsplit=2)
        self.junk = sb("junk", [128, D], BF16)
        self.b_junk = Buf("junk")
        X1 = [sb("X1_0", [128, 3, D], F32)] * 2
        b_X1 = [Buf("X1_0")] * 2
        xnTs = [sb("xnT%d" % i, [128, 8, TN], BF16) for i in range(2)]
        b_xnTs = [Buf("xnT%d" % i) for i in range(2)]
        carry = sb("carry", [128, NM, 2], F32)
        b_carry = Buf("carry")
        S.op("dve", lambda e: e.memset(carry[:], 0.0), w=[b_carry])
        ab = [sb("ab%d" % i, [128, TN + 2], F32) for i in range(2)]
        b_ab = [Buf("ab%d" % i) for i in range(2)]
        tc_ = [sb("tc%d" % i, [128, TN], F32) for i in range(2)]
        b_tc = [Buf("tc%d" % i) for i in range(2)]
        hT = sb("hT", [128, NM, TN], BF16)
        b_hT = Buf("hT")
        x2 = [sb("x2_0", [128, D], F32)] * 2
        b_x2 = [Buf("x2_0")] * 2
        yo = [sb("yo%d" % i, [128, D], F32) for i in range(2)]
        b_yo = [Buf("yo%d" % i) for i in range(2)]
        ss2 = sb("ss2", [128, 2], F32)
        sq2 = sb("sq2", [128, 2], F32)
        b_n2 = [Buf("n2_%d" % i) for i in range(2)]

        def load_tile(k):
            m0 = k * TN
            S.dma("sp", X1[k % 2][:], self.X1s.ap()[m0:m0 + TN, :].rearrange("(s p) d -> p s d", p=128),
                  b_X1[k % 2], w=[b_X1[k % 2]])

        def load_xn(k):
            m0 = k * TN
            S.dma("sp", xnTs[k % 2][:], self.XN2s.ap()[:, :, m0:m0 + TN].rearrange("k p t -> p k t"),
                  b_xnTs[k % 2], w=[b_xnTs[k % 2]])
        NK = T_MAIN // TN
        load_xn(0)
        cnt = 0
        ocnt = 0
        for k in range(NK):
            if k + 1 < NK:
                load_xn(k + 1)
            load_tile(k)
            xt, b_xt = X1[k % 2], b_X1[k % 2]
            xnT, b_xnT = xnTs[k % 2], b_xnTs[k % 2]
            for mt in range(NM):
                pa, b_pa = self.next_ps()
                pv, b_pv = self.next_ps()
                for (pp, bp, c0) in ((pa, b_pa, mt * 128), (pv, b_pv, DFF + mt * 128)):
                    for kc in range(8):
                        S.op("pe", lambda e, pp=pp, kc=kc, c0=c0, xnT=xnT: e.matmul(
                            pp[:, 0:TN], Wup[:, kc, c0:c0 + 128], xnT[:, kc, :], start=(kc == 0), stop=(kc == 7)),
                            r=[b_W, b_xnT], w=[bp], inc=(kc == 7))
                a_, b_a = ab[cnt % 2], b_ab[cnt % 2]
                t_, b_t = tc_[cnt % 2], b_tc[cnt % 2]
                cnt += 1
                S.op("act", lambda e, a_=a_, mt=mt: e.activation(a_[:, 0:2], carry[:, mt, :], AF.Copy),
                     r=[b_carry], w=[b_a])
                S.op("act", lambda e, a_=a_, pa=pa: e.activation(a_[:, 2:TN + 2], pa[:, 0:TN], AF.Copy),
                     r=[b_pa], w=[b_a])
                S.op("dve", lambda e, a_=a_, t_=t_, mt=mt: e.tensor_scalar(
                    t_[:, :], a_[:, 2:TN + 2], cw[:, 2, mt:mt + 1], cb[:, mt:mt + 1], ALU.mult, ALU.add),
                    r=[b_a, b_c], w=[b_t])
                S.op("dve", lambda e, a_=a_, t_=t_, mt=mt: e.scalar_tensor_tensor(
                    t_[:, :], a_[:, 1:TN + 1], cw[:, 1, mt:mt + 1], t_[:, :], ALU.mult, ALU.add),
                    r=[b_a, b_c], w=[b_t])
                S.op("dve", lambda e, a_=a_, t_=t_, mt=mt: e.scalar_tensor_tensor(
                    t_[:, :], a_[:, 0:TN], cw[:, 0, mt:mt + 1], t_[:, :], ALU.mult, ALU.add),
                    r=[b_a, b_c], w=[b_t])
                S.op("dve", lambda e, a_=a_, mt=mt: e.tensor_copy(carry[:, mt, :], a_[:, TN:TN + 2]),
                     r=[b_a], w=[b_carry])
                S.op("act", lambda e, t_=t_: e.activation(t_[:, :], t_[:, :], AF.Silu), r=[b_t], w=[b_t])
                S.op("dve", lambda e, t_=t_, pv=pv, mt=mt: e.tensor_tensor(hT[:, mt, :], t_[:, :], pv[:, 0:TN], ALU.mult),
                     r=[b_t, b_pv], w=[b_hT])
            for s_ in range(3):
                j = ocnt % 2
                ocnt += 1
                for nh in range(2):
                    pt, b_pt = self.next_ps()
                    for kc in range(NM):
                        S.op("pe", lambda e, pt=pt, kc=kc, s_=s_, nh=nh: e.matmul(
                            pt[:, 0:512], hT[:, kc, s_ * 128:(s_ + 1) * 128], Wdn[:, kc, nh * 512:(nh + 1) * 512],
                            start=(kc == 0), stop=(kc == NM - 1)), r=[b_W, b_hT], w=[b_pt], inc=(kc == NM - 1))
                    S.op("dve", lambda e, pt=pt, s_=s_, nh=nh, j=j, xt=xt: e.tensor_tensor(
                        x2[j][:, nh * 512:(nh + 1) * 512], xt[:, s_, nh * 512:(nh + 1) * 512], pt[:, 0:512], ALU.add),
                        r=[b_pt, b_xt], w=[b_x2[j]])
                tok = k * TN + s_ * 128
                if tok < 128:
                    continue
                S.op("act", lambda e, j=j: e.activation(self.junk[:, :], x2[j][:, :], AF.Square,
                                                        accum_out=ss2[:, j:j + 1]),
                     r=[b_x2[j]], w=[self.b_junk, b_n2[j]])
                S.op("act", lambda e, j=j: e.activation(sq2[:, j:j + 1], ss2[:, j:j + 1], AF.Sqrt,
                                                        bias=self.epsc[:, :], scale=1.0 / D),
                     r=[b_n2[j], self.b_epsc], w=[b_n2[j]])
                S.op("dve", lambda e, j=j: e.reciprocal(sq2[:, j:j + 1], sq2[:, j:j + 1]), r=[b_n2[j]], w=[b_n2[j]])
                S.op("dve", lambda e, j=j: e.scalar_tensor_tensor(
                    yo[j][:, :], x2[j][:, :], sq2[:, j:j + 1], gfb[:, :], ALU.mult, ALU.mult),
                    r=[b_x2[j], b_n2[j], b_c], w=[b_yo[j]])
                S.dma("sp", self.oy.ap()[tok - 128:tok, :], yo[j][:, :], b_yo[j], r=[b_yo[j]])
        for i2 in range(2):
            S.dma("sp", dap(self.ocv, i2 * DFF, [[1, 128], [128, NM]]), carry[:, :, i2], b_carry, r=[b_carry],
                  allow_slow_non_contiguous=True)
        NS = 16
        carS = sb("carS", [128, NM, 4, 2], F32)
        b_carS = Buf("carS")
        for s_ in range(4):
            for i2 in range(2):
                S.dma("sp", carS[:, :, s_, i2], dap(self.st_conv, (s_ * 2 + i2) * DFF, [[1, 128], [128, NM]]),
                      b_carS, w=[b_carS], allow_slow_non_contiguous=True)
        xt, b_xt = X1[0], b_X1[0]
        S.dma("sp", xt[0:NS, 0, :], self.X1ss.ap(), b_xt, w=[b_xt])
        xnT, b_xnT = xnTs[NK % 2], b_xnTs[NK % 2]
        S.dma("sp", xnT[:, :, 0:NS], self.XN2ss.ap().rearrange("k p t -> p k t"), b_xnT, w=[b_xnT])
        a3 = sb("a3", [128, 4, 6], F32)
        b_a3 = Buf("a3")
        for mt in range(NM):
            pa, b_pa = self.next_ps()
            pv, b_pv = self.next_ps()
            for (pp, bp, c0) in ((pa, b_pa, mt * 128), (pv, b_pv, DFF + mt * 128)):
                for kc in range(8):
                    S.op("pe", lambda e, pp=pp, kc=kc, c0=c0, xnT=xnT: e.matmul(
                        pp[:, 0:NS], Wup[:, kc, c0:c0 + 128], xnT[:, kc, 0:NS], start=(kc == 0), stop=(kc == 7)),
                        r=[b_W, b_xnT], w=[bp], inc=(kc == 7))
            t_, b_t = tc_[mt % 2], b_tc[mt % 2]
            t3 = t_[:, 0:NS].rearrange("p (s t) -> p s t", s=4)
            S.op("act", lambda e, mt=mt: e.activation(a3[:, :, 0:2], carS[:, mt, :, :], AF.Copy), r=[b_carS], w=[b_a3])
            S.op("act", lambda e, pa=pa: e.activation(
                a3[:, :, 2:6], pa[:, 0:NS].rearrange("p (s t) -> p s t", s=4), AF.Copy), r=[b_pa], w=[b_a3])
            S.op("dve", lambda e, t3=t3, mt=mt: e.tensor_scalar(
                t3, a3[:, :, 2:6], cw[:, 2, mt:mt + 1], cb[:, mt:mt + 1], ALU.mult, ALU.add), r=[b_a3, b_c], w=[b_t])
            S.op("dve", lambda e, t3=t3, mt=mt: e.scalar_tensor_tensor(
                t3, a3[:, :, 1:5], cw[:, 1, mt:mt + 1], t3, ALU.mult, ALU.add), r=[b_a3, b_c], w=[b_t])
            S.op("dve", lambda e, t3=t3, mt=mt: e.scalar_tensor_tensor(
                t3, a3[:, :, 0:4], cw[:, 0, mt:mt + 1], t3, ALU.mult, ALU.add), r=[b_a3, b_c], w=[b_t])
            S.op("dve", lambda e, mt=mt: e.tensor_copy(carS[:, mt, :, :], a3[:, :, 4:6]), r=[b_a3], w=[b_carS])
            S.op("act", lambda e, t_=t_: e.activation(t_[:, 0:NS], t_[:, 0:NS], AF.Silu), r=[b_t], w=[b_t])
            S.op("dve", lambda e, t_=t_, pv=pv, mt=mt: e.tensor_tensor(hT[:, mt, 0:NS], t_[:, 0:NS], pv[:, 0:NS], ALU.mult),
                 r=[b_t, b_pv], w=[b_hT])
        for s_ in range(4):
            for i2 in range(2):
                S.dma("sp", dap(self.ocvs, (s_ * 2 + i2) * DFF, [[1, 128], [128, NM]]), carS[:, :, s_, i2],
                      b_carS, r=[b_carS], allow_slow_non_contiguous=True)
        j = 0
        for nh in range(2):
            pt, b_pt = self.next_ps()
            for kc in range(NM):
                S.op("pe", lambda e, pt=pt, kc=kc, nh=nh: e.matmul(
                    pt[0:NS, 0:512], hT[:, kc, 0:NS], Wdn[:, kc, nh * 512:(nh + 1) * 512],
                    start=(kc == 0), stop=(kc == NM - 1)), r=[b_W, b_hT], w=[b_pt], inc=(kc == NM - 1))
            S.op("dve", lambda e, pt=pt, nh=nh: e.tensor_tensor(
                x2[j][0:NS, nh * 512:(nh + 1) * 512], xt[0:NS, 0, nh * 512:(nh + 1) * 512], pt[0:NS, 0:512], ALU.add),
                r=[b_pt, b_xt], w=[b_x2[j]])
        S.op("act", lambda e: e.activation(self.junk[0:NS, :], x2[j][0:NS, :], AF.Square, accum_out=ss2[0:NS, j:j + 1]),
             r=[b_x2[j]], w=[self.b_junk, b_n2[j]])
        S.op("act", lambda e: e.activation(sq2[0:NS, j:j + 1], ss2[0:NS, j:j + 1], AF.Sqrt,
                                           bias=self.epsc[0:NS, :], scale=1.0 / D),
             r=[b_n2[j], self.b_epsc], w=[b_n2[j]])
        S.op("dve", lambda e: e.reciprocal(sq2[0:NS, j:j + 1], sq2[0:NS, j:j + 1]), r=[b_n2[j]], w=[b_n2[j]])
        S.op("dve", lambda e: e.scalar_tensor_tensor(
            yo[j][0:NS, :], x2[j][0:NS, :], sq2[0:NS, j:j + 1], gfb[0:NS, :], ALU.mult, ALU.mult),
            r=[b_x2[j], b_n2[j], b_c], w=[b_yo[j]])
        S.dma("sp", self.oys.ap(), yo[j][0:NS, :], b_yo[j], r=[b_yo[j]])


_CACHE = {}


def get_prog(phases):
    key = tuple(sorted(phases))
    if key not in _CACHE:
        phases = set(phases)
        if 3 in phases:
            phases |= {31, 32, 33}
        p = Prog(phases)
        p.build()
        _CACHE[key] = p
    return _CACHE[key]


def _rel_bucket(dist):
    n = np.maximum(dist, 0)
    nf = np.maximum(n, 1).astype(np.float32)
    large = 16 + (np.log(nf / np.float32(16)) / np.float32(math.log(2048 / 16)) * np.float32(16)).astype(np.int32)
    large = np.minimum(large, 31)
    return np.where(n < 16, n, large)


def _bucket_onehot():
    oh = np.zeros((32, 3 * 129), np.float32)
    for g in range(3):
        b = _rel_bucket(np.arange(129) * DILS[g])
        oh[b, g * 129 + np.arange(129)] = 1.0
    return oh


def make_in_maps(inputs):
    xp = np.asarray(inputs["x_prompt"], dtype=np.float32)
    maps = []
    ident = np.eye(128, dtype=np.float32)
    boh = _bucket_onehot()
    for c in range(NCORES):
        b, half = c // 2, c % 2
        s0 = half * 4096
        lo = s0 - (T_ALL - 4096)
        xall = np.zeros((T_ALL, D), np.float32)
        src_lo = max(lo, 0)
        xall[src_lo - lo:, :] = xp[b, src_lo:s0 + 4096, :]
        m = {
            "xall": xall,
            "w_in": np.ascontiguousarray(inputs["w_in"][0]),
            "norm1_g": np.ascontiguousarray(inputs["norm1_g"][0]),
            "ident": ident,
            "st_conv": np.ascontiguousarray(inputs["state_ffn_conv"][0, 4 * c:4 * c + 4]),
            "st_re": np.ascontiguousarray(inputs["state_ssm_re"][0, 4 * c:4 * c + 4]),
            "st_im": np.ascontiguousarray(inputs["state_ssm_im"][0, 4 * c:4 * c + 4]),
            "cache0": np.ascontiguousarray(inputs["cache_kv_w128"][0, 4 * c:4 * c + 4].reshape(4, 128, 512)),
            "cache1": np.ascontiguousarray(inputs["cache_kv_w512"][0, 4 * c:4 * c + 4].reshape(4, 512, 512)),
            "cache2": np.ascontiguousarray(inputs["cache_kv_w2048"][0, 4 * c:4 * c + 4].reshape(4, 2048, 512)),
            "xsamp": np.ascontiguousarray(inputs["x_sample"][4 * c:4 * c + 4].reshape(16, D)),
            "rel_bias": np.ascontiguousarray(inputs["rel_bias"]),
            "bucket_oh": boh,
            "antiident": np.ascontiguousarray(np.eye(128, dtype=np.float32)[::-1]),
            "hv": np.full((128, 1), float(half), np.float32),
            "ssm_log_dt": np.ascontiguousarray(inputs["ssm_log_dt"][0]),
            "ssm_lambda_re": np.ascontiguousarray(inputs["ssm_lambda_re"][0]),
            "ssm_lambda_im": np.ascontiguousarray(inputs["ssm_lambda_im"][0]),
            "ssm_b_re": np.ascontiguousarray(inputs["ssm_b_re"][0]),
            "ssm_b_im": np.ascontiguousarray(inputs["ssm_b_im"][0]),
            "ssm_c_re": np.ascontiguousarray(inputs["ssm_c_re"][0]),
            "ssm_c_im": np.ascontiguousarray(inputs["ssm_c_im"][0]),
            "ssm_d": np.ascontiguousarray(inputs["ssm_d"][0]),
            "w_glu": np.ascontiguousarray(inputs["w_glu"][0]),
            "b_glu": np.ascontiguousarray(inputs["b_glu"][0]),
            "w_branch_attn": np.ascontiguousarray(inputs["w_branch_attn"][0]),
            "w_branch_ssm": np.ascontiguousarray(inputs["w_branch_ssm"][0]),
            "w_out": np.ascontiguousarray(inputs["w_out"][0]),
            "norm2_g": np.ascontiguousarray(inputs["norm2_g"][0]),
            "w_up": np.ascontiguousarray(inputs["w_up"][0]),
            "conv_w": np.ascontiguousarray(inputs["conv_w"][0]),
            "conv_b": np.ascontiguousarray(inputs["conv_b"][0]),
            "w_down": np.ascontiguousarray(inputs["w_down"][0]),
            "norm_f_g": np.ascontiguousarray(inputs["norm_f_g"]),
        }
        maps.append(m)
    return maps


def kernel(**inputs):
    prog = get_prog(_PHASES)
    maps = make_in_maps(inputs)
    maps = [{k: v for k, v in m.items() if k in prog.din} for m in maps]
    res = run_bass_kernel_spmd(prog.nc, maps, core_ids=list(range(NCORES)))
    R = res.results
    B = 4
    outs = [None] * 14
    for g in range(3):
        W = WINS[g]
        a = np.zeros((1, B, W, 2, 4, 64), np.float32)
        for b in range(B):
            a[0, b] = R[2 * b + 1]["okv%d" % g].reshape(W, 2, 4, 64)
        outs[2 + g] = a
    if "oy" in R[0]:
        y = np.zeros((B, 8192, D), np.float32)
        for c in range(NCORES):
            y[c // 2, (c % 2) * 4096:(c % 2 + 1) * 4096] = R[c]["oy"]
        outs[0] = y
        cv = np.zeros((1, B, 2, DFF), np.float32)
        for b in range(B):
            cv[0, b] = R[2 * b + 1]["ocv"]
        outs[7] = cv
    if "oys" in R[0]:
        outs[1] = np.concatenate([R[c]["oys"].reshape(4, 4, D) for c in range(NCORES)], axis=0)
        outs[13] = np.concatenate([R[c]["ocvs"] for c in range(NCORES)], axis=0)[None]
    if "okvs0" in R[0]:
        for g in range(3):
            outs[8 + g] = np.concatenate([R[c]["okvs%d" % g].reshape(4, 4, 2, 4, 64) for c in range(NCORES)], axis=0)[None]
    if "ossm_s_re" in R[0]:
        outs[11] = np.concatenate([R[c]["ossm_s_re"] for c in range(NCORES)], axis=0)[None]
        outs[12] = np.concatenate([R[c]["ossm_s_im"] for c in range(NCORES)], axis=0)[None]
    if "ossm_re" in R[0]:
        for i, nm in ((5, "ossm_re"), (6, "ossm_im")):
            a = np.zeros((1, B, 32, 64), np.float32)
            for b in range(B):
                a[0, b] = R[2 * b + 1][nm]
            outs[i] = a
    shapes = [(4, 8192, 1024), (32, 4, 1024), None, None, None, (1, 4, 32, 64), (1, 4, 32, 64),
              (1, 4, 2, DFF), (1, 32, 4, 2, 4, 64), (1, 32, 4, 2, 4, 64), (1, 32, 4, 2, 4, 64),
              (1, 32, 32, 64), (1, 32, 32, 64), (1, 32, 2, DFF)]
    for i in range(14):
        if outs[i] is None:
            outs[i] = np.zeros(shapes[i], np.float32)
    return tuple(outs)
```
